# Optimizing a Trainium2 kernel written in Bass

```python
import jax
import jax.numpy as jnp
from jax import lax
import numpy as np

D_MODEL = 2048
BATCH = 1
SEQ = 8192
DEPTH = 4

GRID_W = 64
CTX_LEN = 256
N_EVEN = (DEPTH + 1) // 2
N_ODD = DEPTH // 2
D_FF = ((8 * D_MODEL // 3 + 255) // 256) * 256
NORM_EPS = 1e-6
ROPE_THETA = 10000.0
NA_HD = 128
A_W = D_MODEL // 2
NA_HEADS = A_W // NA_HD
NA_ROWS_MAX = 8
NA_COLS = 16
HG_DK = 128
B_W = D_MODEL // 2
HG_HEADS = B_W // HG_DK
HG_CHUNK = 64
C_HD = 128
C_HEADS = D_MODEL // C_HD
C_KV_HEADS = C_HEADS // 4
C_GROUP = C_HEADS // C_KV_HEADS
C_WINDOW = 128
C_BLOCK = 128
C_QKV = (C_HEADS + 2 * C_KV_HEADS) * C_HD
EVEN_IN = 3 * A_W + 5 * B_W
EVEN_SPLITS = (A_W, 2 * A_W, 3 * A_W, 3 * A_W + B_W, 3 * A_W + 2 * B_W, 3 * A_W + 3 * B_W, 3 * A_W + 4 * B_W)
C_SPLITS = (C_HEADS * C_HD, (C_HEADS + C_KV_HEADS) * C_HD)

kernel_name = 'hybrid_na_hgrn2_swa_diffusion_trunk'


def rms_norm(x, g):
    x32 = x.astype(jnp.float32)
    y = x32 * lax.rsqrt(jnp.mean(x32 * x32, axis=-1, keepdims=True) + NORM_EPS)
    return (y * g.astype(jnp.float32)).astype(x.dtype)


def adaln(x, g, shift, scale):
    return rms_norm(x, g) * (1 + scale) + shift


def swiglu(u, w_in, w_out):
    gate, up = jnp.split(u @ w_in, 2, axis=-1)
    return (jax.nn.silu(gate) * up) @ w_out


def rope_1d(x, pos):
    d = x.shape[-1]
    inv = ROPE_THETA ** (-jnp.arange(0, d, 2, dtype=jnp.float32) / d)
    ang = pos.astype(jnp.float32)[:, None] * inv[None, :]
    cos = jnp.cos(ang)[None, :, None, :].astype(x.dtype)
    sin = jnp.sin(ang)[None, :, None, :].astype(x.dtype)
    x1, x2 = jnp.split(x, 2, axis=-1)
    return jnp.concatenate([x1 * cos - x2 * sin, x1 * sin + x2 * cos], axis=-1)


def axial_rope(x, rows, cols):
    xr, xc = jnp.split(x, 2, axis=-1)
    return jnp.concatenate([rope_1d(xr, rows), rope_1d(xc, cols)], axis=-1)


def context_attention(q, k, v, sink=None):
    bsz, seq, kvh, grp, hd = q.shape
    s = jnp.einsum('blkgd,bmkd->bkglm', q, k).astype(jnp.float32) * (hd ** -0.5)
    if sink is not None:
        s_sink = jnp.broadcast_to(sink.astype(jnp.float32).reshape(1, kvh, grp, 1, 1), s.shape[:-1] + (1,))
        p = jax.nn.softmax(jnp.concatenate([s, s_sink], axis=-1), axis=-1)[..., :-1]
    else:
        p = jax.nn.softmax(s, axis=-1)
    o = jnp.einsum('bkglm,bmkd->blkgd', p.astype(v.dtype), v)
    return o.reshape(bsz, seq, kvh * grp * hd)


def neighbourhood_attention(q, k, v, kc, vc, rpb):
    bsz, n, nh, hd = q.shape
    rows = n // GRID_W
    kr = min(NA_ROWS_MAX, rows)
    scale = hd ** -0.5
    kg = k.reshape(bsz, rows, GRID_W, nh, hd)
    vg = v.reshape(bsz, rows, GRID_W, nh, hd)
    q_rows = jnp.moveaxis(q.reshape(bsz, rows, GRID_W, nh, hd), 1, 0)
    col = jnp.arange(GRID_W)
    col_idx = jnp.clip(col - NA_COLS // 2, 0, GRID_W - NA_COLS)[:, None] + jnp.arange(NA_COLS)[None, :]
    rpb_cols = rpb[:, :, col_idx - col[:, None] + NA_COLS - 1]
    n_loc = kr * NA_COLS

    def one_row(args):
        q_r, r = args
        r0 = jnp.clip(r - kr // 2, 0, rows - kr)
        k_win = lax.dynamic_slice_in_dim(kg, r0, kr, axis=1)[:, :, col_idx]
        v_win = lax.dynamic_slice_in_dim(vg, r0, kr, axis=1)[:, :, col_idx]
        row_bias_idx = r0 + jnp.arange(kr) - r + NA_ROWS_MAX - 1
        bias = jnp.transpose(rpb_cols[:, row_bias_idx], (0, 2, 1, 3)).astype(jnp.float32)
        s_loc = jnp.einsum('bqhd,bkqjhd->bhqkj', q_r, k_win).astype(jnp.float32) * scale + bias[None]
        s_ctx = jnp.einsum('bqhd,blhd->bhql', q_r, kc).astype(jnp.float32) * scale
        s = jnp.concatenate([s_loc.reshape(bsz, nh, GRID_W, n_loc), s_ctx], axis=-1)
        p = jax.nn.softmax(s, axis=-1).astype(v.dtype)
        p_loc = p[..., :n_loc].reshape(bsz, nh, GRID_W, kr, NA_COLS)
        return (jnp.einsum('bhqkj,bkqjhd->bqhd', p_loc, v_win)
                + jnp.einsum('bhql,blhd->bqhd', p[..., n_loc:], vc))

    out = lax.map(one_row, (q_rows, jnp.arange(rows)))
    return jnp.moveaxis(out, 0, 1).reshape(bsz, n, nh * hd)


def hgrn_gates(fx, lb):
    fx32 = fx.astype(jnp.float32)
    lb = lb.reshape(HG_HEADS, HG_DK)
    log_f = jnp.logaddexp(jnp.log(lb), jnp.log1p(-lb) + jax.nn.log_sigmoid(fx32))
    k = (1.0 - lb) * jax.nn.sigmoid(-fx32)
    return log_f, k


def gated_chunk_scan(q, k, v, log_f, s0, with_out):
    bsz, t_len, nh, _ = k.shape
    nc = t_len // HG_CHUNK

    def chunks(t):
        return jnp.transpose(t.astype(jnp.float32).reshape(bsz, nc, HG_CHUNK, nh, t.shape[-1]), (1, 0, 3, 2, 4))

    tri = jnp.tril(jnp.ones((HG_CHUNK, HG_CHUNK), dtype=bool))

    def step(state, inp):
        kc, vc, gc = inp[:3]
        b = jnp.cumsum(gc, axis=2)
        b_last = b[:, :, -1:, :]
        new_state = (jnp.exp(b_last[:, :, 0, :])[..., None] * state
                     + jnp.einsum('bhcd,bhce->bhde', kc * jnp.exp(b_last - b), vc))
        if not with_out:
            return new_state, None
        qc = inp[3]
        o_inter = jnp.einsum('bhcd,bhde->bhce', qc * jnp.exp(b), state)
        diff = b[:, :, :, None, :] - b[:, :, None, :, :]
        decay = jnp.exp(jnp.where(tri[None, None, :, :, None], diff, -jnp.inf))
        attn = jnp.einsum('bhtsd,bhtd->bhts', decay * kc[:, :, None, :, :], qc)
        return new_state, o_inter + jnp.einsum('bhts,bhse->bhte', attn, vc)

    xs = (chunks(k), chunks(v), chunks(log_f)) + ((chunks(q),) if with_out else ())
    s_fin, o = lax.scan(step, s0, xs)
    if with_out:
        o = jnp.transpose(o, (1, 0, 3, 2, 4)).reshape(bsz, t_len, nh, -1)
    return s_fin, o


def hgrn_output(o, g, norm_g):
    bsz, t_len = o.shape[:2]
    o = o * lax.rsqrt(jnp.mean(o * o, axis=-1, keepdims=True) + NORM_EPS)
    o = o.reshape(bsz, t_len, -1) * norm_g.astype(jnp.float32)
    return (o * jax.nn.silu(g.astype(jnp.float32))).astype(g.dtype)


def hgrn2_bidirectional(lat, ctx_in, lb, norm_g, need_ctx):
    def heads(t):
        return t.reshape(t.shape[0], t.shape[1], HG_HEADS, HG_DK)
    q, f_fw, f_bw, i, g = lat
    qc, fc_fw, fc_bw, ic, gc = ctx_in
    qh, ih, qch, ich = heads(q), heads(i), heads(qc), heads(ic)
    bsz = q.shape[0]
    o_lat = 0.0
    o_ctx = 0.0
    for d, (f_l, f_c) in enumerate(((f_fw, fc_fw), (f_bw, fc_bw))):
        rev = (lambda t: t) if d == 0 else (lambda t: jnp.flip(t, axis=1))
        log_f, k = hgrn_gates(heads(f_l), lb[d])
        log_fc, kc = hgrn_gates(heads(f_c), lb[d])
        s0 = jnp.zeros((bsz, HG_HEADS, HG_DK, HG_DK), jnp.float32)
        s_ctx, oc_d = gated_chunk_scan(rev(qch), rev(kc), rev(ich), rev(log_fc), s0, need_ctx)
        _, o_d = gated_chunk_scan(rev(qh), rev(k), rev(ih), rev(log_f), s_ctx, True)
        o_lat = o_lat + rev(o_d)
        if need_ctx:
            o_ctx = o_ctx + rev(oc_d)
    y = hgrn_output(o_lat, g, norm_g)
    yc = hgrn_output(o_ctx, gc, norm_g) if need_ctx else None
    return y, yc


def parallel_na_hgrn(u, uc, w_in, w_out, rpb, lb, hg_norm_g, need_ctx):
    qa, ka, va, *lat_b = jnp.split(u @ w_in, EVEN_SPLITS, axis=-1)
    qa_c, ka_c, va_c, *ctx_b = jnp.split(uc @ w_in, EVEN_SPLITS, axis=-1)

    def heads(t):
        return t.reshape(t.shape[0], t.shape[1], NA_HEADS, NA_HD)
    ya = neighbourhood_attention(heads(qa), heads(ka), heads(va), heads(ka_c), heads(va_c), rpb)
    yb, yb_c = hgrn2_bidirectional(tuple(lat_b), tuple(ctx_b), lb, hg_norm_g, need_ctx)
    y = jnp.concatenate([ya, yb], axis=-1) @ w_out
    if not need_ctx:
        return y, None
    ya_c = context_attention(heads(qa_c)[:, :, :, None, :], heads(ka_c), heads(va_c))
    return y, jnp.concatenate([ya_c, yb_c], axis=-1) @ w_out


def banded_window_attention(q, k, v, kc, vc, sink):
    bsz, n, kvh, grp, hd = q.shape
    nb = n // C_BLOCK
    scale = hd ** -0.5
    qb = q.reshape(bsz, nb, C_BLOCK, kvh, grp, hd)

    def band(t):
        tp = jnp.pad(t, ((0, 0), (C_BLOCK, C_BLOCK), (0, 0), (0, 0))).reshape(bsz, nb + 2, C_BLOCK, kvh, hd)
        return jnp.concatenate([tp[:, :-2], tp[:, 1:-1], tp[:, 2:]], axis=2)

    kb, vb = band(k), band(v)
    blk = jnp.arange(nb)[:, None] * C_BLOCK
    qpos = blk + jnp.arange(C_BLOCK)[None, :]
    kpos = blk - C_BLOCK + jnp.arange(3 * C_BLOCK)[None, :]
    rel = kpos[:, None, :] - qpos[:, :, None]
    valid = (jnp.abs(rel) <= C_WINDOW) & (kpos[:, None, :] >= 0) & (kpos[:, None, :] < n)
    s_loc = jnp.einsum('bnqkgd,bnskd->bnkgqs', qb, kb).astype(jnp.float32) * scale
    s_loc = jnp.where(valid[None, :, None, None], s_loc, -jnp.inf)
    s_ctx = jnp.einsum('bnqkgd,blkd->bnkgql', qb, kc).astype(jnp.float32) * scale
    s_sink = jnp.broadcast_to(sink.astype(jnp.float32).reshape(1, 1, kvh, grp, 1, 1), s_loc.shape[:-1] + (1,))
    p = jax.nn.softmax(jnp.concatenate([s_loc, s_ctx, s_sink], axis=-1), axis=-1).astype(v.dtype)
    n_loc = 3 * C_BLOCK
    n_ctx = kc.shape[1]
    o = (jnp.einsum('bnkgqs,bnskd->bnqkgd', p[..., :n_loc], vb)
         + jnp.einsum('bnkgql,blkd->bnqkgd', p[..., n_loc:n_loc + n_ctx], vc))
    return o.reshape(bsz, n, kvh * grp * hd)


def windowed_gqa_sink(u, uc, w_qkv, w_o, sink, need_ctx):
    bsz, n, _ = u.shape
    n_ctx = uc.shape[1]
    q, k, v = jnp.split(u @ w_qkv, C_SPLITS, axis=-1)
    qc, kc, vc = jnp.split(uc @ w_qkv, C_SPLITS, axis=-1)
    t = jnp.arange(n)
    rows, cols = t // GRID_W, t % GRID_W
    q = axial_rope(q.reshape(bsz, n, C_HEADS, C_HD), rows, cols).reshape(bsz, n, C_KV_HEADS, C_GROUP, C_HD)
    k = axial_rope(k.reshape(bsz, n, C_KV_HEADS, C_HD), rows, cols)
    v = v.reshape(bsz, n, C_KV_HEADS, C_HD)
    kc = kc.reshape(bsz, n_ctx, C_KV_HEADS, C_HD)
    vc = vc.reshape(bsz, n_ctx, C_KV_HEADS, C_HD)
    y = banded_window_attention(q, k, v, kc, vc, sink) @ w_o
    if not need_ctx:
        return y, None
    yc = context_attention(qc.reshape(bsz, n_ctx, C_KV_HEADS, C_GROUP, C_HD), kc, vc, sink) @ w_o
    return y, yc


def setup_inputs(seed: int = 0) -> dict:
    key = jax.random.key(seed)
    ks = jax.random.split(key, 18)

    def nrm(k, shape, s):
        return jax.random.normal(k, shape, jnp.float32) * s

    return {
        'x': nrm(ks[0], (BATCH, SEQ, D_MODEL), 1.0),
        'c': nrm(ks[1], (BATCH, D_MODEL), 1.0),
        'ctx': nrm(ks[2], (BATCH, CTX_LEN, D_MODEL), 1.0),
        'c_ctx': nrm(ks[3], (D_MODEL,), 1.0),
        'w_mod': nrm(ks[4], (DEPTH, D_MODEL, 9 * D_MODEL), 0.5 * D_MODEL ** -0.5),
        'b_mod': nrm(ks[5], (DEPTH, 9 * D_MODEL), 0.02),
        'norm_g': 1.0 + nrm(ks[6], (DEPTH, 3, D_MODEL), 0.05),
        'w_ff_in': nrm(ks[7], (DEPTH, 2, D_MODEL, 2 * D_FF), D_MODEL ** -0.5),
        'w_ff_out': nrm(ks[8], (DEPTH, 2, D_FF, D_MODEL), D_FF ** -0.5),
        'w_in_even': nrm(ks[9], (N_EVEN, D_MODEL, EVEN_IN), D_MODEL ** -0.5),
        'w_out_even': nrm(ks[10], (N_EVEN, A_W + B_W, D_MODEL), (A_W + B_W) ** -0.5),
        'na_rpb': nrm(ks[11], (N_EVEN, NA_HEADS, 2 * NA_ROWS_MAX - 1, 2 * NA_COLS - 1), 0.1),
        'hg_lb_logits': nrm(ks[12], (2, N_EVEN, B_W), 1.0),
        'hg_norm_g': 1.0 + nrm(ks[13], (N_EVEN, B_W), 0.05),
        'w_qkv_odd': nrm(ks[14], (N_ODD, D_MODEL, C_QKV), D_MODEL ** -0.5),
        'w_o_odd': nrm(ks[15], (N_ODD, C_HEADS * C_HD, D_MODEL), (C_HEADS * C_HD) ** -0.5),
        'sink_odd': nrm(ks[16], (N_ODD, C_HEADS), 0.5),
        'final_norm_g': 1.0 + nrm(ks[17], (D_MODEL,), 0.05),
    }


def reference(x, c, ctx, c_ctx, w_mod, b_mod, norm_g, w_ff_in, w_ff_out, w_in_even, w_out_even,
              na_rpb, hg_lb_logits, hg_norm_g, w_qkv_odd, w_o_odd, sink_odd, final_norm_g):
    bsz = x.shape[0]
    h, hc = x, ctx
    lb_all = jnp.cumsum(jax.nn.softmax(hg_lb_logits.astype(jnp.float32), axis=1), axis=1)
    lb_all = lb_all - lb_all[:, :1]
    sc = jax.nn.silu(c)
    scc = jax.nn.silu(c_ctx)
    for l in range(DEPTH):
        need_ctx = l < DEPTH - 1
        mod = (sc @ w_mod[l] + b_mod[l]).reshape(bsz, 3, 3, D_MODEL)[:, :, :, None, :]
        modc = (scc @ w_mod[l] + b_mod[l]).reshape(3, 3, D_MODEL)
        h = h + 0.5 * mod[:, 0, 2] * swiglu(adaln(h, norm_g[l, 0], mod[:, 0, 0], mod[:, 0, 1]), w_ff_in[l, 0], w_ff_out[l, 0])
        hc = hc + 0.5 * modc[0, 2] * swiglu(adaln(hc, norm_g[l, 0], modc[0, 0], modc[0, 1]), w_ff_in[l, 0], w_ff_out[l, 0])
        u = adaln(h, norm_g[l, 1], mod[:, 1, 0], mod[:, 1, 1])
        uc = adaln(hc, norm_g[l, 1], modc[1, 0], modc[1, 1])
        if l % 2 == 0:
            e = l // 2
            y, yc = parallel_na_hgrn(u, uc, w_in_even[e], w_out_even[e], na_rpb[e], lb_all[:, e], hg_norm_g[e], need_ctx)
        else:
            o = l // 2
            y, yc = windowed_gqa_sink(u, uc, w_qkv_odd[o], w_o_odd[o], sink_odd[o], need_ctx)
        h = h + mod[:, 1, 2] * y
        h = h + 0.5 * mod[:, 2, 2] * swiglu(adaln(h, norm_g[l, 2], mod[:, 2, 0], mod[:, 2, 1]), w_ff_in[l, 1], w_ff_out[l, 1])
        if need_ctx:
            hc = hc + modc[1, 2] * yc
            hc = hc + 0.5 * modc[2, 2] * swiglu(adaln(hc, norm_g[l, 2], modc[2, 0], modc[2, 1]), w_ff_in[l, 1], w_ff_out[l, 1])
    return rms_norm(h, final_norm_g)
```

```python
import numpy as np
import ml_dtypes
from contextlib import ExitStack
import concourse.bass as bass
import concourse.mybir as mybir
from concourse.bass_utils import run_bass_kernel_spmd

F32 = mybir.dt.float32
BF16 = mybir.dt.bfloat16
AF = mybir.ActivationFunctionType
ALU = mybir.AluOpType
NPBF = ml_dtypes.bfloat16

D = 2048
KC = 16
DFF = 5632
JC = 44
NCORE = 8
SEQ = 8192
CTX = 256
TLAT = SEQ // NCORE
TCTX = CTX // NCORE
TT = TLAT + TCTX
TW = 352
NT = 3
EPS = 1e-6
DEPTH = 4


class Prog:
    ENGS = {'pe': 'tensor', 'act': 'scalar', 'dve': 'vector', 'pool': 'gpsimd', 'sp': 'sync'}

    def __init__(self, nc, es):
        self.nc = nc
        self.es = es
        self.q = {e: [] for e in self.ENGS}
        self.sems = {}
        self.val = {}
        self.w = {}
        self.r = {}
        self.waited = {}
        self.limit = None
        self.nops = 0

    def sem(self, key):
        if key not in self.sems:
            self.sems[key] = self.es.enter_context(self.nc.semaphore("s_" + key))
            self.val[key] = 0
        return self.sems[key]

    def op(self, eng, fn, reads=(), writes=(), pwrites=(), chan=None):
        self.nops += 1
        if self.limit is not None and self.nops > self.limit:
            return None
        waits = {}

        def need(evs):
            for k, (v, e) in evs.items():
                if e == 'pe' and eng == 'pe':
                    continue
                if waits.get(k, 0) < v:
                    waits[k] = v
        for t in reads:
            need(self.w.get(t, {}))
        for t in list(writes) + list(pwrites):
            need(self.w.get(t, {}))
            need(self.r.get(t, {}))
        wl = []
        for k, v in waits.items():
            if self.waited.get((eng, k), 0) >= v:
                continue
            self.waited[(eng, k)] = v
            wl.append((k, v))
        if chan is None:
            key, inc, ee = eng, 1, eng
        else:
            key, inc, ee = 'd_' + chan, 16, 'dma'
        self.sem(key)
        self.val[key] += inc
        v = self.val[key]
        for t in reads:
            self.r.setdefault(t, {})[key] = (v, ee)
        for t in writes:
            self.w[t] = {key: (v, ee)}
            self.r[t] = {}
        for t in pwrites:
            self.w.setdefault(t, {})[key] = (v, ee)
            self.r[t] = {}
        self.q[eng].append((wl, fn, key, inc))
        return (key, v)

    def finish(self, eng='sp'):
        wl = [(k, v) for k, v in self.val.items() if k.startswith('d_')]
        self.q[eng].append((wl, None, None, 0))

    def emit(self):
        nc = self.nc
        with nc.Block() as block:
            for e, attr in self.ENGS.items():
                items = self.q[e]

                def body(engine, items=items):
                    for wl, fn, key, inc in items:
                        for k, v in wl:
                            engine.wait_ge(self.sems[k], v)
                        if fn is not None:
                            ins = fn(engine)
                            ins.then_inc(self.sems[key], inc)
                getattr(block, attr)(body)


def bc_mid(ap, n):
    a = [list(x) for x in ap.ap]
    return bass.AP(ap.tensor, ap.offset, [a[0], [0, n]] + a[1:])


def bc_last(ap, n):
    a = [list(x) for x in ap.ap]
    return bass.AP(ap.tensor, ap.offset, a + [[0, n]])


class Ring:
    def __init__(self, bufs, name):
        self.bufs = bufs
        self.name = name
        self.i = 0

    def next(self):
        k = self.i % len(self.bufs)
        self.i += 1
        return self.bufs[k], f"{self.name}{k}"


def build_token_prog(stages, want_u=False, final=False):
    nc = bass.Bass("TRN2", target_bir_lowering=False)
    nv = 0
    for s in stages:
        nv += 2 if s[0] == 'proj' else 7
    if want_u:
        nv += 5
    if final:
        nv += 1
    h_in = nc.dram_tensor("h_in", [128, KC, TT], F32, kind="ExternalInput").ap()
    vecs_d = nc.dram_tensor("vecs", [128, nv, KC], F32, kind="ExternalInput").ap()
    wd = {}
    y_in = None
    for si, s in enumerate(stages):
        if s[0] == 'proj':
            wd[si] = nc.dram_tensor(f"wo{si}", [KC, 128, KC, 128], F32, kind="ExternalInput").ap()
            y_in = nc.dram_tensor("y_in", [128, KC, TT], BF16, kind="ExternalInput").ap()
        else:
            wd[si] = (nc.dram_tensor(f"wi{si}", [JC, 128, KC, 256], F32, kind="ExternalInput").ap(),
                      nc.dram_tensor(f"wf{si}", [KC, 128, JC, 128], F32, kind="ExternalInput").ap())
    hd = [h_in]
    for si in range(len(stages)):
        last = si == len(stages) - 1
        if last and not final:
            hd.append(nc.dram_tensor("h_out", [128, KC, TT], F32, kind="ExternalOutput").ap())
        else:
            hd.append(nc.dram_tensor(f"h_s{si}", [128, KC, TT], F32).ap())
    u_out = nc.dram_tensor("u_out", [128, KC, TT], BF16, kind="ExternalOutput").ap() if want_u else None
    o_out = nc.dram_tensor("o_out", [128, KC, TT], F32, kind="ExternalOutput").ap() if final else None

    tiles = [(i * TW, TW) for i in range(NT)]
    segs = []
    for (t0, w) in tiles:
        sg = []
        lat_end = min(max(TLAT - t0, 0), w)
        if lat_end > 0:
            sg.append((0, lat_end, 0))
        if lat_end < w:
            sg.append((lat_end, w, 1))
        segs.append(sg)

    with ExitStack() as es:
        P = Prog(nc, es)
        sb = lambda name, shape, dt: es.enter_context(nc.sbuf_tensor(name, shape, dt))
        vecs = sb("vecs_sb", [128, nv, KC], F32)
        dvec = sb("dvec", [128, 8, KC], F32)
        ones = sb("ones", [128, 128], F32)
        uT = sb("uT", [128, KC, TT], BF16)
        big = sb("big", [128, JC, TT], BF16)
        rstd = sb("rstd", [128, TW], F32)
        tmpn = [sb(f"tmpn{i}", [128, TW], F32) for i in range(3)]
        wib = [sb(f"wib{i}", [128, KC, 256], BF16) for i in range(3)]
        wob = [sb(f"wob{i}", [128, JC, 128], BF16) for i in range(2)]
        sgb = [sb(f"sgb{i}", [128, TW], F32) for i in range(3)]
        hres = [sb(f"hres{i}", [128, TW], F32) for i in range(3)]
        hout = [sb(f"hout{i}", [128, TW], F32) for i in range(3)]
        ps = [es.enter_context(nc.psum_tensor(f"ps{i}", [128, 512], F32)) for i in range(8)]
        r_tmpn = Ring(tmpn, "tmpn")
        r_wib = Ring(wib, "wib")
        r_wob = Ring(wob, "wob")
        r_sgb = Ring(sgb, "sgb")
        r_hres = Ring(hres, "hres")
        r_hout = Ring(hout, "hout")
        r_psA = Ring(ps[0:6], "ps")
        r_psN = Ring(ps[6:8], "psn")

        P.op('sp', lambda e: e.dma_start(out=vecs[:], in_=vecs_d), writes=['vecs'], chan='vecs')
        P.op('pool', lambda e: e.memset(ones[:], 1.0), writes=['ones'])

        def norm_stage(src, vbase, has_mod, dst_dram):
            gi, shl, scl, shc, scc = vbase
            if has_mod:
                for which, sc_i in ((0, scl), (1, scc)):
                    P.op('dve', lambda e, which=which, sc_i=sc_i: e.scalar_tensor_tensor(
                        out=dvec[:, which, :], in0=vecs[:, sc_i, :], scalar=1.0, in1=vecs[:, gi, :],
                        op0=ALU.add, op1=ALU.mult), reads=['vecs'], writes=[f'dvA{which}'])
            for ti, (t0, w) in enumerate(tiles):
                pb, pbt = r_psN.next()
                for k in range(KC):
                    hr, hrt = r_hres.next()
                    P.op('sp', lambda e, hr=hr, k=k, t0=t0, w=w: e.dma_start(out=hr[:, 0:w], in_=src[:, k, t0:t0 + w]),
                         reads=[src.tensor.name], writes=[hrt], chan=hrt)
                    sq, sqt = r_sgb.next()
                    P.op('act', lambda e, hr=hr, sq=sq, w=w: e.activation(out=sq[:, 0:w], in_=hr[:, 0:w], func=AF.Square),
                         reads=[hrt], writes=[sqt])
                    P.op('pe', lambda e, k=k, pb=pb, sq=sq, w=w: e.matmul(pb[:, 0:w], ones[:, :], sq[:, 0:w],
                                                                        start=(k == 0), stop=(k == KC - 1)),
                         reads=[sqt, 'ones'], writes=[pbt])
                P.op('dve', lambda e, pb=pb, w=w: e.tensor_scalar(out=rstd[:, 0:w], in0=pb[:, 0:w], scalar1=1.0 / D,
                                                                 scalar2=EPS, op0=ALU.mult, op1=ALU.add),
                     reads=[pbt], writes=['rstd'])
                P.op('act', lambda e, w=w: e.activation(out=rstd[:, 0:w], in_=rstd[:, 0:w], func=AF.Sqrt),
                     reads=['rstd'], writes=['rstd'])
                P.op('dve', lambda e, w=w: e.reciprocal(out=rstd[:, 0:w], in_=rstd[:, 0:w]),
                     reads=['rstd'], writes=['rstd'])
                for k in range(KC):
                    hr, hrt = r_hres.next()
                    P.op('sp', lambda e, hr=hr, k=k, t0=t0, w=w: e.dma_start(out=hr[:, 0:w], in_=src[:, k, t0:t0 + w]),
                         reads=[src.tensor.name], writes=[hrt], chan=hrt)
                    if has_mod:
                        tb, tbt = r_tmpn.next()
                        for (c0, c1, which) in segs[ti]:
                            a_ap = dvec[:, which, k:k + 1]
                            b_ap = vecs[:, (shl if which == 0 else shc), k:k + 1]
                            P.op('dve', lambda e, tb=tb, hr=hr, c0=c0, c1=c1, a_ap=a_ap: e.scalar_tensor_tensor(
                                out=tb[:, c0:c1], in0=hr[:, c0:c1], scalar=a_ap, in1=rstd[:, c0:c1],
                                op0=ALU.mult, op1=ALU.mult), reads=[hrt, 'rstd', f'dvA{which}'], pwrites=[tbt])
                            P.op('act', lambda e, tb=tb, c0=c0, c1=c1, b_ap=b_ap, k=k, t0=t0: e.activation(
                                out=uT[:, k, t0 + c0:t0 + c1], in_=tb[:, c0:c1], func=AF.Identity, bias=b_ap, scale=1.0),
                                reads=[tbt, 'vecs'], pwrites=[f'u_t{ti}'])
                    else:
                        ho, hot = r_hout.next()
                        P.op('dve', lambda e, ho=ho, hr=hr, k=k, w=w: e.scalar_tensor_tensor(
                            out=ho[:, 0:w], in0=hr[:, 0:w], scalar=vecs[:, gi, k:k + 1], in1=rstd[:, 0:w],
                            op0=ALU.mult, op1=ALU.mult), reads=[hrt, 'rstd', 'vecs'], writes=[hot])
                        P.op('sp', lambda e, ho=ho, k=k, t0=t0, w=w: e.dma_start(out=dst_dram[:, k, t0:t0 + w], in_=ho[:, 0:w]),
                             reads=[hot], pwrites=[dst_dram.tensor.name], chan='st_' + hot)
                if has_mod and dst_dram is not None:
                    P.op('sp', lambda e, t0=t0, w=w: e.dma_start(out=dst_dram[:, :, t0:t0 + w], in_=uT[:, :, t0:t0 + w]),
                         reads=[f'u_t{ti}'], pwrites=[dst_dram.tensor.name], chan=f'ust{ti}')

        def ffn1_stage(w_in_d):
            for j in range(JC):
                wb, wbt = r_wib.next()
                P.op('pool', lambda e, j=j, wb=wb: e.dma_start(out=wb[:], in_=w_in_d[j]), writes=[wbt], chan=wbt)
                for ti, (t0, w) in enumerate(tiles):
                    pg, pgt = r_psA.next()
                    pu, put = r_psA.next()
                    for half, (pp, ppt) in enumerate(((pg, pgt), (pu, put))):
                        for k in range(KC):
                            P.op('pe', lambda e, k=k, pp=pp, wb=wb, half=half, t0=t0, w=w: e.matmul(
                                pp[:, 0:w], wb[:, k, half * 128:(half + 1) * 128], uT[:, k, t0:t0 + w],
                                start=(k == 0), stop=(k == KC - 1)),
                                reads=[wbt, f'u_t{ti}'], writes=[ppt])
                    sg, sgt = r_sgb.next()
                    P.op('act', lambda e, sg=sg, pg=pg, w=w: e.activation(out=sg[:, 0:w], in_=pg[:, 0:w], func=AF.Silu),
                         reads=[pgt], writes=[sgt])
                    P.op('dve', lambda e, sg=sg, pu=pu, j=j, t0=t0, w=w: e.tensor_tensor(
                        out=big[:, j, t0:t0 + w], in0=sg[:, 0:w], in1=pu[:, 0:w], op=ALU.mult),
                        reads=[sgt, put], pwrites=[f'big_t{ti}'])

        def proj_stage(w_d, kc, src, dst, gl, gc, half):
            mul = 0.5 if half else 1.0
            for which, gi_ in ((0, gl), (1, gc)):
                P.op('dve', lambda e, which=which, gi_=gi_: e.tensor_scalar(
                    out=dvec[:, 2 + which, :], in0=vecs[:, gi_, :], scalar1=mul, scalar2=None, op0=ALU.mult),
                    reads=['vecs'], writes=[f'dvG{which}'])
            for dc in range(KC):
                wb, wbt = r_wob.next()
                P.op('pool', lambda e, dc=dc, wb=wb: e.dma_start(out=wb[:, 0:kc, :], in_=w_d[dc]), writes=[wbt], chan=wbt)
                for ti, (t0, w) in enumerate(tiles):
                    hr, hrt = r_hres.next()
                    P.op('sp', lambda e, hr=hr, dc=dc, t0=t0, w=w: e.dma_start(out=hr[:, 0:w], in_=src[:, dc, t0:t0 + w]),
                         reads=[src.tensor.name], writes=[hrt], chan=hrt)
                    pp, ppt = r_psA.next()
                    for j in range(kc):
                        P.op('pe', lambda e, j=j, pp=pp, wb=wb, t0=t0, w=w: e.matmul(
                            pp[:, 0:w], wb[:, j, :], big[:, j, t0:t0 + w], start=(j == 0), stop=(j == kc - 1)),
                            reads=[wbt, f'big_t{ti}'], writes=[ppt])
                    ho, hot = r_hout.next()
                    for (c0, c1, which) in segs[ti]:
                        P.op('dve', lambda e, ho=ho, pp=pp, hr=hr, c0=c0, c1=c1, which=which, dc=dc: e.scalar_tensor_tensor(
                            out=ho[:, c0:c1], in0=pp[:, c0:c1], scalar=dvec[:, 2 + which, dc:dc + 1], in1=hr[:, c0:c1],
                            op0=ALU.mult, op1=ALU.add), reads=[ppt, hrt, f'dvG{which}'], pwrites=[hot])
                    P.op('sp', lambda e, ho=ho, dc=dc, t0=t0, w=w: e.dma_start(out=dst[:, dc, t0:t0 + w], in_=ho[:, 0:w]),
                         reads=[hot], pwrites=[dst.tensor.name], chan='st_' + hot)

        vb = 0
        for si, s in enumerate(stages):
            src, dst = hd[si], hd[si + 1]
            if s[0] == 'proj':
                for ti, (t0, w) in enumerate(tiles):
                    P.op('sp', lambda e, t0=t0, w=w: e.dma_start(out=big[:, 0:KC, t0:t0 + w], in_=y_in[:, :, t0:t0 + w]),
                         writes=[f'big_t{ti}'], chan=f'yin{ti}')
                proj_stage(wd[si], KC, src, dst, vb, vb + 1, False)
                vb += 2
            else:
                g, shl, scl, gl, shc, scc, gc = range(vb, vb + 7)
                norm_stage(src, (g, shl, scl, shc, scc), True, None)
                ffn1_stage(wd[si][0])
                proj_stage(wd[si][1], JC, src, dst, gl, gc, True)
                vb += 7
        if want_u:
            g, shl, scl, shc, scc = range(vb, vb + 5)
            norm_stage(hd[-1], (g, shl, scl, shc, scc), True, u_out)
            vb += 5
        if final:
            norm_stage(hd[-1], (vb, 0, 0, 0, 0), False, o_out)
            vb += 1
        P.finish('sp')
        P.emit()
    return nc


MODC = 9 * D // NCORE
MCH = 384
NMCH = MODC // MCH


def build_mod_prog():
    nc = bass.Bass("TRN2", target_bir_lowering=False)
    cvec_d = nc.dram_tensor("cvec", [128, KC, 2], F32, kind="ExternalInput").ap()
    w_d = nc.dram_tensor("wm", [DEPTH * NMCH, 128, KC, MCH], F32, kind="ExternalInput").ap()
    b_d = nc.dram_tensor("bm", [2, DEPTH * MODC], F32, kind="ExternalInput").ap()
    o_d = nc.dram_tensor("mod_out", [2, DEPTH * MODC], F32, kind="ExternalOutput").ap()
    with ExitStack() as es:
        P = Prog(nc, es)
        sb = lambda name, shape, dt: es.enter_context(nc.sbuf_tensor(name, shape, dt))
        cv = sb("cv", [128, KC, 2], F32)
        sc = sb("sc", [128, KC, 2], F32)
        bsb = sb("bsb", [2, DEPTH * MODC], F32)
        mo = sb("mo", [2, DEPTH * MODC], F32)
        wbs = [sb(f"wb{i}", [128, KC, MCH], F32) for i in range(3)]
        ps = [es.enter_context(nc.psum_tensor(f"ps{i}", [128, 512], F32)) for i in range(4)]
        r_wb = Ring(wbs, "wb")
        r_ps = Ring(ps, "ps")
        P.op('sp', lambda e: e.dma_start(out=cv[:], in_=cvec_d), writes=['cv'], chan='cv')
        P.op('sp', lambda e: e.dma_start(out=bsb[:], in_=b_d), writes=['bsb'], chan='bsb')
        P.op('act', lambda e: e.activation(out=sc[:], in_=cv[:], func=AF.Silu), reads=['cv'], writes=['sc'])
        for i in range(DEPTH * NMCH):
            wb, wbt = r_wb.next()
            P.op('sp', lambda e, i=i, wb=wb: e.dma_start(out=wb[:], in_=w_d[i]), writes=[wbt], chan=wbt)
            pp, ppt = r_ps.next()
            for k in range(KC):
                P.op('pe', lambda e, k=k, pp=pp, wb=wb: e.matmul(pp[0:2, 0:MCH], sc[:, k, :], wb[:, k, :],
                                                                start=(k == 0), stop=(k == KC - 1)),
                     reads=[wbt, 'sc'], writes=[ppt])
            P.op('dve', lambda e, i=i, pp=pp: e.tensor_tensor(out=mo[:, i * MCH:(i + 1) * MCH], in0=pp[0:2, 0:MCH],
                                                             in1=bsb[:, i * MCH:(i + 1) * MCH], op=ALU.add),
                 reads=[ppt, 'bsb'], pwrites=['mo'])
        P.op('sp', lambda e: e.dma_start(out=o_d, in_=mo[:]), reads=['mo'], writes=['o_d'], chan='o_d')
        P.finish('sp')
        P.emit()
    return nc


def run_mod(c, c_ctx, w_mod, b_mod):
    nc = build_mod_prog()
    cvec = np.ascontiguousarray(np.stack([c.reshape(KC, 128).T, c_ctx.reshape(KC, 128).T], axis=2)).astype(np.float32)
    in_maps = []
    for r in range(NCORE):
        ws = w_mod[:, :, r * MODC:(r + 1) * MODC]
        ws = ws.reshape(DEPTH, KC, 128, NMCH, MCH).transpose(0, 3, 2, 1, 4).reshape(DEPTH * NMCH, 128, KC, MCH)
        bs = b_mod[:, r * MODC:(r + 1) * MODC].reshape(1, DEPTH * MODC)
        in_maps.append({"cvec": cvec, "wm": np.ascontiguousarray(ws),
                        "bm": np.ascontiguousarray(np.concatenate([bs, bs], axis=0))})
    res = run_bass_kernel_spmd(nc, in_maps, core_ids=list(range(NCORE)))
    mod = np.zeros((DEPTH, 2, 9 * D), np.float32)
    for r in range(NCORE):
        o = res.results[r]["mod_out"].reshape(2, DEPTH, MODC)
        mod[:, :, r * MODC:(r + 1) * MODC] = o.transpose(1, 0, 2)
    return mod


TALL = SEQ + CTX
NTL = TALL // 128
QBW = 512
NQB = SEQ // QBW
NEG = -30000.0


def v3(ap, a, b):
    l = [list(x) for x in ap.ap]
    st, n = l[-1]
    assert n == a * b, (n, a, b)
    return bass.AP(ap.tensor, ap.offset, l[:-1] + [[st * b, a], [st, b]])


def token_blocks():
    return [(i * QBW, QBW) for i in range(NQB)] + [(SEQ, CTX)]


def build_odd_prog(debug=False):
    nc = bass.Bass("TRN2", target_bir_lowering=False)
    dq_d = nc.dram_tensor("dq", [128, 2, TALL], BF16, kind="ExternalOutput").ap() if debug else None
    dk_d = nc.dram_tensor("dk", [128, TALL], BF16, kind="ExternalOutput").ap() if debug else None
    uT_d = nc.dram_tensor("uT", [128, KC, TALL], BF16, kind="ExternalInput").ap()
    wq_d = nc.dram_tensor("wq", [128, KC, 4, 128], F32, kind="ExternalInput").ap()
    wk_d = nc.dram_tensor("wk", [128, KC, 2, 128], F32, kind="ExternalInput").ap()
    wv_d = nc.dram_tensor("wv", [128, KC, 128], F32, kind="ExternalInput").ap()
    rtab_d = nc.dram_tensor("rtab", [128, 2, 128], F32, kind="ExternalInput").ap()
    mask_d = nc.dram_tensor("masks", [128, 6, QBW], BF16, kind="ExternalInput").ap()
    ident_d = nc.dram_tensor("ident", [128, 128], BF16, kind="ExternalInput").ap()
    sink_d = nc.dram_tensor("sink", [128, 2], F32, kind="ExternalInput").ap()
    y_d = nc.dram_tensor("y_out", [128, 2, TALL], BF16, kind="ExternalOutput").ap()
    SCALE = 128.0 ** -0.5
    with ExitStack() as es:
        P = Prog(nc, es)
        sb = lambda name, shape, dt: es.enter_context(nc.sbuf_tensor(name, shape, dt))
        wq = sb("wq_sb", [128, KC, 4, 128], BF16)
        wk = sb("wk_sb", [128, KC, 2, 128], BF16)
        wv = sb("wv_sb", [128, KC, 128], BF16)
        rtab = sb("rtab_sb", [128, 2, 128], F32)
        masks = sb("masks_sb", [128, 6, QBW], BF16)
        ident = sb("ident_sb", [128, 128], BF16)
        onesb = sb("onesb", [128, 128], BF16)
        sink = sb("sink_sb", [128, 2], F32)
        esink = sb("esink", [128, 2], F32)
        qT = sb("qT", [128, 2, TALL], BF16)
        kT = sb("kT", [128, TALL], BF16)
        vtm = sb("vtm", [128, NTL, 128], BF16)
        yT = sb("yT", [128, 2, TALL], BF16)
        ubs = [sb(f"ub{i}", [128, KC, QBW], BF16) for i in range(2)]
        t1s = [sb(f"t1_{i}", [128, QBW], F32) for i in range(2)]
        t2s = [sb(f"t2_{i}", [128, QBW], F32) for i in range(2)]
        pts = [sb(f"pt{i}", [128, QBW], BF16) for i in range(3)]
        rcs = [sb(f"rc{i}", [128, QBW], F32) for i in range(2)]
        ps = [es.enter_context(nc.psum_tensor(f"ps{i}", [128, 512], F32)) for i in range(8)]
        r_ub = Ring(ubs, "ub"); r_t1 = Ring(t1s, "t1"); r_t2 = Ring(t2s, "t2"); r_pt = Ring(pts, "pt"); r_rc = Ring(rcs, "rc")
        r_psA = Ring(ps[0:4], "psA")
        r_psO = Ring(ps[4:6], "psO")
        r_psR = Ring(ps[6:8], "psR")

        P.op('pool', lambda e: e.dma_start(out=wq[:], in_=wq_d), writes=['wq'], chan='wq')
        P.op('pool', lambda e: e.dma_start(out=wk[:], in_=wk_d), writes=['wk'], chan='wk')
        P.op('pool', lambda e: e.dma_start(out=wv[:], in_=wv_d), writes=['wv'], chan='wv')
        P.op('sp', lambda e: e.dma_start(out=rtab[:], in_=rtab_d), writes=['rtab'], chan='rtab')
        P.op('sp', lambda e: e.dma_start(out=masks[:], in_=mask_d), writes=['masks'], chan='masks')
        P.op('sp', lambda e: e.dma_start(out=ident[:], in_=ident_d), writes=['ident'], chan='ident')
        P.op('sp', lambda e: e.dma_start(out=sink[:], in_=sink_d), writes=['sink'], chan='sink')
        P.op('pool', lambda e: e.memset(onesb[:], 1.0), writes=['onesb'])
        P.op('act', lambda e: e.activation(out=esink[:], in_=sink[:], func=AF.Exp), reads=['sink'], writes=['esink'])

        def proj_fm(ub, ubt, w_ap, wtag, width):
            pp, ppt = r_psA.next()
            for k in range(KC):
                P.op('pe', lambda e, k=k, pp=pp: e.matmul(pp[:, 0:width], w_ap(k), ub[:, k, 0:width],
                                                         start=(k == 0), stop=(k == KC - 1)),
                     reads=[ubt, wtag], writes=[ppt])
            return pp, ppt

        def rope_store(px, pxt, pw, pwt, dst, dtag, r0, scale):
            t1, t1t = r_t1.next()
            t2, t2t = r_t2.next()
            for half in range(2):
                p0, p1 = half * 64, half * 64 + 64
                if half == 0:
                    ctab = bc_last(rtab[p0:p1, 0, r0:r0 + 8], 64)
                    stab = bc_last(rtab[p0:p1, 1, r0:r0 + 8], 64)
                else:
                    ctab = bc_mid(rtab[p0:p1, 0, 0:64], 8)
                    stab = bc_mid(rtab[p0:p1, 1, 0:64], 8)
                P.op('dve', lambda e, p0=p0, p1=p1, ctab=ctab, t1=t1: e.scalar_tensor_tensor(
                    out=v3(t1[p0:p1, :], 8, 64), in0=v3(px[p0:p1, :], 8, 64), scalar=scale, in1=ctab,
                    op0=ALU.mult, op1=ALU.mult), reads=[pxt, 'rtab'], pwrites=[t1t])
                P.op('dve', lambda e, p0=p0, p1=p1, stab=stab, t2=t2: e.scalar_tensor_tensor(
                    out=v3(t2[p0:p1, :], 8, 64), in0=v3(pw[p0:p1, :], 8, 64), scalar=scale, in1=stab,
                    op0=ALU.mult, op1=ALU.mult), reads=[pwt, 'rtab'], pwrites=[t2t])
            P.op('pool', lambda e, t1=t1, t2=t2: e.tensor_tensor(out=dst, in0=t1[:, :], in1=t2[:, :], op=ALU.add),
                 reads=[t1t, t2t], pwrites=[dtag])

        for bi, (t0, width) in enumerate(token_blocks()):
            ub, ubt = r_ub.next()
            P.op('sp', lambda e, ub=ub, t0=t0, width=width: e.dma_start(out=ub[:, :, 0:width], in_=uT_d[:, :, t0:t0 + width]),
                 writes=[ubt], chan=ubt)
            is_ctx = t0 >= SEQ
            r0 = t0 // 64
            for h in range(2):
                px, pxt = proj_fm(ub, ubt, lambda k, h=h: wq[:, k, 2 * h, :], 'wq', width)
                if is_ctx:
                    P.op('act', lambda e, px=px, h=h, t0=t0, width=width: e.activation(
                        out=qT[:, h, t0:t0 + width], in_=px[:, 0:width], func=AF.Identity, scale=SCALE),
                        reads=[pxt], pwrites=['qT'])
                else:
                    pw, pwt = proj_fm(ub, ubt, lambda k, h=h: wq[:, k, 2 * h + 1, :], 'wq', width)
                    rope_store(px, pxt, pw, pwt, qT[:, h, t0:t0 + width], 'qT', r0, SCALE)
            px, pxt = proj_fm(ub, ubt, lambda k: wk[:, k, 0, :], 'wk', width)
            if is_ctx:
                P.op('act', lambda e, px=px, t0=t0, width=width: e.activation(
                    out=kT[:, t0:t0 + width], in_=px[:, 0:width], func=AF.Identity), reads=[pxt], pwrites=['kT'])
            else:
                pw, pwt = proj_fm(ub, ubt, lambda k: wk[:, k, 1, :], 'wk', width)
                rope_store(px, pxt, pw, pwt, kT[:, t0:t0 + width], 'kT', r0, 1.0)
            pp, ppt = r_psA.next()
            nti = width // 128
            for i in range(nti):
                for k in range(KC):
                    P.op('pe', lambda e, k=k, i=i, pp=pp, ub=ub: e.matmul(pp[:, i * 128:(i + 1) * 128], ub[:, k, i * 128:(i + 1) * 128],
                                                                         wv[:, k, :], start=(k == 0), stop=(k == KC - 1)),
                         reads=[ubt, 'wv'], pwrites=[ppt])
            tt0 = t0 // 128
            P.op('act', lambda e, pp=pp, tt0=tt0, nti=nti: e.activation(
                out=vtm[:, tt0:tt0 + nti, :], in_=v3(pp[:, 0:nti * 128], nti, 128), func=AF.Identity),
                reads=[ppt], pwrites=['vtm'])

        if debug:
            P.op('sp', lambda e: e.dma_start(out=dq_d, in_=qT[:]), reads=['qT'], writes=['dq_d'], chan='dq')
            P.op('sp', lambda e: e.dma_start(out=dk_d, in_=kT[:]), reads=['kT'], writes=['dk_d'], chan='dk')
        def attend(h, q0, qw, keytiles):
            po, pot = r_psO.next()
            pr, prt = r_psR.next()
            n = len(keytiles)
            for i, (kt, mi) in enumerate(keytiles):
                pS, pst = r_psA.next()
                P.op('pe', lambda e, pS=pS, kt=kt, mi=mi: e.matmul(pS[:, 0:qw], kT[:, kt * 128:(kt + 1) * 128], qT[:, h, q0:q0 + qw],
                                                                  start=True, stop=(mi is None)),
                     reads=['kT', 'qT'], writes=[pst])
                if mi is not None:
                    P.op('pe', lambda e, pS=pS, mi=mi: e.matmul(pS[:, 0:qw], ident[:, :], masks[:, mi, 0:qw], start=False, stop=True),
                         reads=['ident', 'masks'], writes=[pst])
                pt, ptt = r_pt.next()
                P.op('act', lambda e, pS=pS, pt=pt: e.activation(out=pt[:, 0:qw], in_=pS[:, 0:qw], func=AF.Exp),
                     reads=[pst], writes=[ptt])
                P.op('pe', lambda e, pt=pt, kt=kt, i=i: e.matmul(po[:, 0:qw], vtm[:, kt, :], pt[:, 0:qw], start=(i == 0), stop=(i == n - 1)),
                     reads=[ptt, 'vtm'], writes=[pot])
                P.op('pe', lambda e, pt=pt, i=i: e.matmul(pr[:, 0:qw], onesb[:, :], pt[:, 0:qw], start=(i == 0), stop=(i == n - 1)),
                     reads=[ptt, 'onesb'], writes=[prt])
            rc, rct = r_rc.next()
            P.op('dve', lambda e, rc=rc: e.tensor_scalar(out=rc[:, 0:qw], in0=pr[:, 0:qw], scalar1=esink[:, h:h + 1], scalar2=None, op0=ALU.add),
                 reads=[prt, 'esink'], writes=[rct])
            P.op('dve', lambda e, rc=rc: e.reciprocal(out=rc[:, 0:qw], in_=rc[:, 0:qw]), reads=[rct], writes=[rct])
            P.op('dve', lambda e, rc=rc: e.tensor_tensor(out=yT[:, h, q0:q0 + qw], in0=po[:, 0:qw], in1=rc[:, 0:qw], op=ALU.mult),
                 reads=[pot, rct], pwrites=['yT'])

        for h in range(2):
            for qb in range(NQB):
                kts = []
                for rel in range(-1, 5):
                    kt = qb * 4 + rel
                    if 0 <= kt < SEQ // 128:
                        kts.append((kt, rel + 1))
                kts += [(64, None), (65, None)]
                attend(h, qb * QBW, QBW, kts)
            attend(h, SEQ, CTX, [(64, None), (65, None)])
            P.op('sp', lambda e, h=h: e.dma_start(out=y_d[:, h, :], in_=yT[:, h, :]), reads=['yT'], pwrites=['y_d'], chan=f'y{h}')
        P.finish('sp')
        P.emit()
    return nc


def odd_consts():
    inv = 10000.0 ** (-np.arange(0, 64, 2, dtype=np.float32) / 64.0)
    rtab = np.zeros((128, 2, 128), np.float32)
    for p in range(128):
        f = inv[p % 32]
        sign = -1.0 if (p % 64) < 32 else 1.0
        pos = np.arange(128, dtype=np.float32)
        ang = pos * f
        rtab[p, 0, :] = np.cos(ang)
        rtab[p, 1, :] = sign * np.sin(ang)
    masks = np.zeros((128, 6, QBW), np.float32)
    for mi in range(6):
        rel = mi - 1
        kpos = rel * 128 + np.arange(128)[:, None]
        qpos = np.arange(QBW)[None, :]
        masks[:, mi, :] = np.where(np.abs(kpos - qpos) <= 128, 0.0, NEG)
    ident = np.eye(128, dtype=np.float32)
    return rtab, masks.astype(NPBF), ident.astype(NPBF)


def swap_perm():
    p = np.arange(128)
    return np.where((p % 64) < 32, p + 32, p - 32)


def lay_w(w):
    n = w.shape[1]
    return np.ascontiguousarray(w.reshape(KC, 128, n).transpose(1, 0, 2))


def run_odd(uT_all, w_qkv, sink, debug=False):
    nc = build_odd_prog(debug)
    rtab, masks, ident = odd_consts()
    perm = swap_perm()
    in_maps = []
    for j in range(NCORE):
        kv = j // 2
        wq = []
        for h in (2 * j, 2 * j + 1):
            cols = w_qkv[:, h * 128:(h + 1) * 128]
            wq += [cols, cols[:, perm]]
        wq = np.stack([lay_w(c) for c in wq], axis=2)
        kc = w_qkv[:, D + kv * 128: D + (kv + 1) * 128]
        wk = np.stack([lay_w(kc), lay_w(kc[:, perm])], axis=2)
        wv = lay_w(w_qkv[:, D + 512 + kv * 128: D + 512 + (kv + 1) * 128])
        sk = np.broadcast_to(sink[2 * j:2 * j + 2][None, :], (128, 2)).astype(np.float32)
        in_maps.append({"uT": uT_all, "wq": np.ascontiguousarray(wq), "wk": np.ascontiguousarray(wk), "wv": wv,
                        "rtab": rtab, "masks": masks, "ident": ident, "sink": np.ascontiguousarray(sk)})
    res = run_bass_kernel_spmd(nc, in_maps, core_ids=list(range(NCORE)))
    yT = np.zeros((128, KC, TALL), NPBF)
    for j in range(NCORE):
        yT[:, 2 * j:2 * j + 2, :] = res.results[j]["y_out"]
    if debug:
        return yT, res.results
    return yT


NNB = 20


def na_bias_mats(rpb_h):
    def mat(R, kt):
        kr = 2 * kt + np.arange(2)[:, None, None, None]
        kc = np.arange(64)[None, :, None, None]
        r = R + np.arange(8)[None, None, :, None]
        qc = np.arange(64)[None, None, None, :]
        r0 = np.clip(r - 4, 0, 120)
        c0 = np.clip(qc - 8, 0, 48)
        valid = (kr >= r0) & (kr < r0 + 8) & (kc >= c0) & (kc < c0 + 16)
        ri = np.clip(kr - r + 7, 0, 14)
        ci = np.clip(kc - qc + 15, 0, 30)
        ri, ci, valid = np.broadcast_arrays(ri, ci, valid)
        return np.where(valid, rpb_h[ri, ci], NEG).reshape(128, 512)
    mats = [mat(8, 4 + rel) for rel in range(-2, 6)]
    mats += [mat(0, kt) for kt in range(0, 6)]
    mats += [mat(120, kt) for kt in range(58, 64)]
    return np.ascontiguousarray(np.stack(mats, axis=1)).astype(NPBF)


def na_keytiles(qb):
    if qb == 0:
        return [(kt, 8 + kt) for kt in range(0, 6)]
    if qb == NQB - 1:
        return [(kt, 14 + kt - 58) for kt in range(58, 64)]
    return [(4 * qb + rel, rel + 2) for rel in range(-2, 6)]


def build_na_prog():
    nc = bass.Bass("TRN2", target_bir_lowering=False)
    uT_d = nc.dram_tensor("uT", [128, KC, TALL], BF16, kind="ExternalInput").ap()
    w_d = nc.dram_tensor("w3", [128, KC, 3, 128], F32, kind="ExternalInput").ap()
    bias_d = nc.dram_tensor("nabias", [128, NNB, QBW], BF16, kind="ExternalInput").ap()
    ident_d = nc.dram_tensor("ident", [128, 128], BF16, kind="ExternalInput").ap()
    y_d = nc.dram_tensor("y_out", [128, TALL], BF16, kind="ExternalOutput").ap()
    SCALE = 128.0 ** -0.5
    with ExitStack() as es:
        P = Prog(nc, es)
        sb = lambda name, shape, dt: es.enter_context(nc.sbuf_tensor(name, shape, dt))
        w3 = sb("w3_sb", [128, KC, 3, 128], BF16)
        biasm = sb("bias_sb", [128, NNB, QBW], BF16)
        ident = sb("ident_sb", [128, 128], BF16)
        onesb = sb("onesb", [128, 128], BF16)
        qT = sb("qT", [128, TALL], BF16)
        kT = sb("kT", [128, TALL], BF16)
        vtm = sb("vtm", [128, NTL, 128], BF16)
        yT = sb("yT", [128, TALL], BF16)
        ubs = [sb(f"ub{i}", [128, KC, QBW], BF16) for i in range(2)]
        pts = [sb(f"pt{i}", [128, QBW], BF16) for i in range(3)]
        rcs = [sb(f"rc{i}", [128, QBW], F32) for i in range(2)]
        ps = [es.enter_context(nc.psum_tensor(f"ps{i}", [128, 512], F32)) for i in range(8)]
        r_ub = Ring(ubs, "ub"); r_pt = Ring(pts, "pt"); r_rc = Ring(rcs, "rc")
        r_psA = Ring(ps[0:4], "psA"); r_psO = Ring(ps[4:6], "psO"); r_psR = Ring(ps[6:8], "psR")
        P.op('pool', lambda e: e.dma_start(out=w3[:], in_=w_d), writes=['w3'], chan='w3')
        P.op('sp', lambda e: e.dma_start(out=biasm[:], in_=bias_d), writes=['biasm'], chan='biasm')
        P.op('sp', lambda e: e.dma_start(out=ident[:], in_=ident_d), writes=['ident'], chan='ident')
        P.op('pool', lambda e: e.memset(onesb[:], 1.0), writes=['onesb'])
        for bi, (t0, width) in enumerate(token_blocks()):
            ub, ubt = r_ub.next()
            P.op('sp', lambda e, ub=ub, t0=t0, width=width: e.dma_start(out=ub[:, :, 0:width], in_=uT_d[:, :, t0:t0 + width]),
                 writes=[ubt], chan=ubt)
            for g, (dst, dtag, sc) in enumerate(((qT, 'qT', SCALE), (kT, 'kT', 1.0))):
                pp, ppt = r_psA.next()
                for k in range(KC):
                    P.op('pe', lambda e, k=k, pp=pp, g=g, ub=ub, width=width: e.matmul(
                        pp[:, 0:width], w3[:, k, g, :], ub[:, k, 0:width], start=(k == 0), stop=(k == KC - 1)),
                        reads=[ubt, 'w3'], writes=[ppt])
                P.op('act', lambda e, pp=pp, dst=dst, sc=sc, t0=t0, width=width: e.activation(
                    out=dst[:, t0:t0 + width], in_=pp[:, 0:width], func=AF.Identity, scale=sc), reads=[ppt], pwrites=[dtag])
            pp, ppt = r_psA.next()
            nti = width // 128
            for i in range(nti):
                for k in range(KC):
                    P.op('pe', lambda e, k=k, i=i, pp=pp, ub=ub: e.matmul(pp[:, i * 128:(i + 1) * 128], ub[:, k, i * 128:(i + 1) * 128],
                                                                         w3[:, k, 2, :], start=(k == 0), stop=(k == KC - 1)),
                         reads=[ubt, 'w3'], pwrites=[ppt])
            tt0 = t0 // 128
            P.op('dve', lambda e, pp=pp, tt0=tt0, nti=nti: e.tensor_copy(
                out=vtm[:, tt0:tt0 + nti, :], in_=v3(pp[:, 0:nti * 128], nti, 128)), reads=[ppt], pwrites=['vtm'])

        def attend(q0, qw, keytiles):
            po, pot = r_psO.next()
            pr, prt = r_psR.next()
            n = len(keytiles)
            for i, (kt, mi) in enumerate(keytiles):
                pS, pst = r_psA.next()
                P.op('pe', lambda e, pS=pS, kt=kt, mi=mi: e.matmul(pS[:, 0:qw], kT[:, kt * 128:(kt + 1) * 128], qT[:, q0:q0 + qw],
                                                                  start=True, stop=(mi is None)),
                     reads=['kT', 'qT'], writes=[pst])
                if mi is not None:
                    P.op('pe', lambda e, pS=pS, mi=mi: e.matmul(pS[:, 0:qw], ident[:, :], biasm[:, mi, 0:qw], start=False, stop=True),
                         reads=['ident', 'biasm'], writes=[pst])
                pt, ptt = r_pt.next()
                P.op('act', lambda e, pS=pS, pt=pt: e.activation(out=pt[:, 0:qw], in_=pS[:, 0:qw], func=AF.Exp),
                     reads=[pst], writes=[ptt])
                P.op('pe', lambda e, pt=pt, kt=kt, i=i: e.matmul(po[:, 0:qw], vtm[:, kt, :], pt[:, 0:qw], start=(i == 0), stop=(i == n - 1)),
                     reads=[ptt, 'vtm'], writes=[pot])
                P.op('pe', lambda e, pt=pt, i=i: e.matmul(pr[:, 0:qw], onesb[:, :], pt[:, 0:qw], start=(i == 0), stop=(i == n - 1)),
                     reads=[ptt, 'onesb'], writes=[prt])
            rc, rct = r_rc.next()
            P.op('dve', lambda e, rc=rc: e.reciprocal(out=rc[:, 0:qw], in_=pr[:, 0:qw]), reads=[prt], writes=[rct])
            P.op('dve', lambda e, rc=rc: e.tensor_tensor(out=yT[:, q0:q0 + qw], in0=po[:, 0:qw], in1=rc[:, 0:qw], op=ALU.mult),
                 reads=[pot, rct], pwrites=['yT'])

        for qb in range(NQB):
            attend(qb * QBW, QBW, na_keytiles(qb) + [(64, None), (65, None)])
        attend(SEQ, CTX, [(64, None), (65, None)])
        P.op('sp', lambda e: e.dma_start(out=y_d, in_=yT[:]), reads=['yT'], writes=['y_d'], chan='y')
        P.finish('sp')
        P.emit()
    return nc


def run_na(uT_all, w_in, rpb):
    nc = build_na_prog()
    ident = np.eye(128, dtype=np.float32).astype(NPBF)
    in_maps = []
    for j in range(NCORE):
        cols = [w_in[:, g * 1024 + j * 128: g * 1024 + (j + 1) * 128] for g in range(3)]
        w3 = np.stack([lay_w(c) for c in cols], axis=2)
        in_maps.append({"uT": uT_all, "w3": np.ascontiguousarray(w3), "nabias": na_bias_mats(rpb[j]), "ident": ident})
    res = run_bass_kernel_spmd(nc, in_maps, core_ids=list(range(NCORE)))
    return np.stack([res.results[j]["y_out"] for j in range(NCORE)], axis=1)


HC = 64
NCH = TALL // HC
NCTXCH = CTX // HC


def chunk_col(ap2d, nch, idx):
    l = [list(x) for x in ap2d.ap]
    st, n = l[-1]
    assert n == nch * HC
    return bass.AP(ap2d.tensor, ap2d.offset + idx * st, l[:-1] + [[st * HC, nch], [0, HC]])


def chunk_pick(ap2d, nch, idx):
    l = [list(x) for x in ap2d.ap]
    st, n = l[-1]
    return bass.AP(ap2d.tensor, ap2d.offset + idx * st, l[:-1] + [[st * HC, nch]])


def sub_view(ap2d, nch, sub0, nsub, col, bcast):
    l = [list(x) for x in ap2d.ap]
    st, n = l[-1]
    assert n == nch * HC
    if bcast:
        return bass.AP(ap2d.tensor, ap2d.offset + (sub0 * SUB + col) * st, l[:-1] + [[st * HC, nch], [st * SUB, nsub], [0, SUB]])
    return bass.AP(ap2d.tensor, ap2d.offset + sub0 * SUB * st, l[:-1] + [[st * HC, nch], [st * SUB, nsub], [st, SUB]])


SUB = 16
NSUB = HC // SUB
CLAMP = 40.0


def build_hgrn_prog(e_layer, stop=99, nblk=None, limit=None):
    nc = bass.Bass("TRN2", target_bir_lowering=False)
    uT_d = nc.dram_tensor("uT", [128, KC, TALL], BF16, kind="ExternalInput").ap()
    w_d = nc.dram_tensor("w5", [128, KC, 5, 128], F32, kind="ExternalInput").ap()
    lbl_d = nc.dram_tensor("lbl", [128, 2, 2], F32, kind="ExternalInput").ap()
    hgn_d = nc.dram_tensor("hgn", [128, 1], F32, kind="ExternalInput").ap()
    ident_d = nc.dram_tensor("ident", [128, 128], BF16, kind="ExternalInput").ap()
    tri_d = nc.dram_tensor("tri", [128, 4, HC], F32, kind="ExternalInput").ap()
    rmask_d = nc.dram_tensor("rmask", [128, QBW], F32, kind="ExternalInput").ap()
    y_d = nc.dram_tensor("y_out", [128, TALL], BF16, kind="ExternalOutput").ap()
    with ExitStack() as es:
        P = Prog(nc, es)
        sb = lambda name, shape, dt: es.enter_context(nc.sbuf_tensor(name, shape, dt))
        P.limit = limit
        w5 = sb("w5_sb", [128, KC, 5, 128], BF16)
        lbl = sb("lbl_sb", [128, 2, 2], F32)
        lb = sb("lb_sb", [128, 2], F32)
        oml = sb("oml_sb", [128, 2], F32)
        hgn = sb("hgn_sb", [128, 1], F32)
        ident = sb("ident_sb", [128, 128], BF16)
        tri = sb("tri_sb", [128, 4, HC], F32)
        rmask = sb("rmask_sb", [128, QBW], F32)
        ones32 = sb("ones32", [128, 128], F32)
        itm = sb("itm", [128, NTL, 128], BF16)
        Qp = sb("Qp", [128, TALL], BF16)
        attmEO = [sb("attmE", [128, NTL, HC], BF16), sb("attmO", [128, NTL, HC], BF16)]
        dS = sb("dS", [128, NCH, 128], BF16)
        Dall = sb("Dall", [128, NCH], F32)
        oacc = sb("oacc", [128, TALL], F32)
        ub = sb("ub", [128, KC, QBW], BF16)
        T = [sb(f"T{i}", [128, QBW], F32) for i in range(12)]
        KtT = sb("KtT", [128, QBW], BF16)
        Kj = [sb(f"Kj{i}", [128, QBW], BF16) for i in range(3)]
        Kd = sb("Kd", [128, QBW], BF16)
        Qd = sb("Qd", [128, QBW], BF16)
        Qsub = sb("Qsub", [128, QBW], BF16)
        KtmEO = [sb("KtmE", [128, 4, 128], BF16), sb("KtmO", [128, 4, 128], BF16)]
        yb = sb("yb", [128, QBW], BF16)
        ps = [es.enter_context(nc.psum_tensor(f"ps{i}", [128, 512], F32)) for i in range(7)]
        psT = es.enter_context(nc.psum_tensor("psT", [128, 1024], BF16))
        r_psA = Ring(ps[0:3], "psA")
        r_psB = Ring(ps[3:7], "psB")

        P.op('pool', lambda e: e.dma_start(out=w5[:], in_=w_d), writes=['w5'], chan='w5')
        for nm, t_, d_ in (('lbl', lbl, lbl_d), ('hgn', hgn, hgn_d), ('ident', ident, ident_d), ('tri', tri, tri_d), ('rmask', rmask, rmask_d)):
            P.op('sp', lambda e, t_=t_, d_=d_: e.dma_start(out=t_[:], in_=d_), writes=[nm], chan=nm)
        P.op('pool', lambda e: e.memset(ones32[:], 1.0), writes=['ones32'])
        for t_ in KtmEO:
            P.op('pool', lambda e, t_=t_: e.memset(t_[:], 0.0), writes=['Ktm'])
        for t_ in attmEO:
            P.op('pool', lambda e, t_=t_: e.memset(t_[:], 0.0), writes=['attm'])
        P.op('pool', lambda e: e.memset(Qsub[:], 0.0), writes=['Qsub'])
        if e_layer == 1:
            P.op('dve', lambda e: e.tensor_tensor(out=lb[:], in0=lbl[:, :, 1], in1=lbl[:, :, 0], op=ALU.subtract), reads=['lbl'], writes=['lb'])
            P.op('act', lambda e: e.activation(out=lb[:], in_=lb[:], func=AF.Sigmoid), reads=['lb'], writes=['lb'])
            P.op('dve', lambda e: e.tensor_scalar(out=oml[:], in0=lb[:], scalar1=-1.0, scalar2=1.0, op0=ALU.mult, op1=ALU.add),
                 reads=['lb'], writes=['oml'])

        blocks = token_blocks()
        if nblk is not None:
            blocks = blocks[:nblk]

        def load_ub(t0, width):
            P.op('sp', lambda e: e.dma_start(out=ub[:, :, 0:width], in_=uT_d[:, :, t0:t0 + width]), writes=['ub'], chan='ub')

        def proj(g, width):
            pp, ppt = r_psA.next()
            for k in range(KC):
                P.op('pe', lambda e, k=k, pp=pp: e.matmul(pp[:, 0:width], w5[:, k, g, :], ub[:, k, 0:width],
                                                         start=(k == 0), stop=(k == KC - 1)),
                     reads=['ub', 'w5'], writes=[ppt])
            return pp, ppt

        def fslot(ch, d):
            return (ch + NCTXCH) % NCH if d == 0 else ch

        for d in range(2):
            for (t0, w) in blocks:
                nch = w // HC
                nti = w // 128
                ch0 = t0 // HC
                tt0 = t0 // 128
                load_ub(t0, w)
                pq, pqt = proj(0, w)
                pf, pft = proj(1 + d, w)
                if d == 0:
                    pi, pit = r_psB.next()
                    for i in range(nti):
                        for k in range(KC):
                            P.op('pe', lambda e, k=k, i=i, pi=pi: e.matmul(pi[:, i * 128:(i + 1) * 128], ub[:, k, i * 128:(i + 1) * 128],
                                                                         w5[:, k, 3, :], start=(k == 0), stop=(k == KC - 1)),
                                 reads=['ub', 'w5'], pwrites=[pit])
                    P.op('act', lambda e, pi=pi, tt0=tt0, nti=nti: e.activation(
                        out=itm[:, tt0:tt0 + nti, :], in_=v3(pi[:, 0:nti * 128], nti, 128), func=AF.Identity),
                        reads=[pit], pwrites=['itm'])
                P.op('act', lambda e, pf=pf, w=w: e.activation(out=T[0][:, 0:w], in_=pf[:, 0:w], func=AF.Sigmoid), reads=[pft], writes=['T0'])
                if e_layer == 1:
                    P.op('dve', lambda e, w=w, d=d: e.tensor_scalar(out=T[0][:, 0:w], in0=T[0][:, 0:w], scalar1=oml[:, d:d + 1],
                                                                   scalar2=lb[:, d:d + 1], op0=ALU.mult, op1=ALU.add),
                         reads=['T0', 'oml', 'lb'], writes=['T0'])
                P.op('act', lambda e, w=w: e.activation(out=T[1][:, 0:w], in_=T[0][:, 0:w], func=AF.Ln), reads=['T0'], writes=['T1'])
                P.op('dve', lambda e, w=w: e.tensor_scalar(out=T[2][:, 0:w], in0=T[0][:, 0:w], scalar1=-1.0, scalar2=1.0,
                                                          op0=ALU.mult, op1=ALU.add), reads=['T0'], writes=['T2'])
                P.op('dve', lambda e, w=w: e.tensor_tensor_scan(out=T[3][:, 0:w], data0=rmask[:, 0:w], data1=T[1][:, 0:w], initial=0.0,
                                                               op0=ALU.mult, op1=ALU.add), reads=['T1', 'rmask'], writes=['T3'])
                if d == 0:
                    c, ct = T[3], 'T3'
                else:
                    P.op('dve', lambda e, w=w: e.tensor_tensor(out=T[4][:, 0:w], in0=T[1][:, 0:w], in1=T[3][:, 0:w], op=ALU.subtract),
                         reads=['T1', 'T3'], writes=['T4'])
                    P.op('dve', lambda e, w=w, nch=nch: e.tensor_tensor(out=v3(T[4][:, 0:w], nch, HC), in0=v3(T[4][:, 0:w], nch, HC),
                                                                       in1=chunk_col(T[3][:, 0:w], nch, HC - 1), op=ALU.add),
                         reads=['T4', 'T3'], writes=['T4'])
                    c, ct = T[4], 'T4'
                cw = c[:, 0:w]
                P.op('act', lambda e, w=w, cw=cw: e.activation(out=T[5][:, 0:w], in_=cw, func=AF.Exp), reads=[ct], writes=['T5'])
                P.op('dve', lambda e, pq=pq, t0=t0, w=w: e.tensor_tensor(out=Qp[:, t0:t0 + w], in0=pq[:, 0:w], in1=T[5][:, 0:w], op=ALU.mult),
                     reads=[pqt, 'T5'], pwrites=['Qp'])
                P.op('dve', lambda e, w=w, nch=nch, cw=cw: e.tensor_tensor(out=v3(T[6][:, 0:w], nch, HC), in0=chunk_col(T[3][:, 0:w], nch, HC - 1),
                                                                          in1=v3(cw, nch, HC), op=ALU.subtract),
                     reads=[ct, 'T3'], writes=['T6'])
                P.op('act', lambda e, w=w: e.activation(out=T[6][:, 0:w], in_=T[6][:, 0:w], func=AF.Exp), reads=['T6'], writes=['T6'])
                P.op('pool', lambda e, w=w: e.tensor_tensor(out=KtT[:, 0:w], in0=T[2][:, 0:w], in1=T[6][:, 0:w], op=ALU.mult),
                     reads=['T2', 'T6'], writes=['KtT'])
                s0 = fslot(ch0, d)
                P.op('act', lambda e, w=w, nch=nch, s0=s0: e.activation(out=Dall[:, s0:s0 + nch], in_=chunk_pick(T[3][:, 0:w], nch, HC - 1), func=AF.Exp),
                     reads=['T3'], pwrites=['Dall'])
                if d == 0:
                    qs0, bcol = 1, -1
                else:
                    qs0, bcol = 0, SUB
                P.op('dve', lambda e, w=w, nch=nch, cw=cw, qs0=qs0, bcol=bcol: e.tensor_tensor(
                    out=sub_view(T[7][:, 0:w], nch, qs0, 3, 0, False), in0=sub_view(cw, nch, qs0, 3, 0, False),
                    in1=sub_view(cw, nch, qs0, 3, bcol, True), op=ALU.subtract), reads=[ct], writes=['T7'])
                P.op('act', lambda e, w=w, nch=nch, qs0=qs0: e.activation(out=sub_view(T[7][:, 0:w], nch, qs0, 3, 0, False),
                                                                         in_=sub_view(T[7][:, 0:w], nch, qs0, 3, 0, False), func=AF.Exp),
                     reads=['T7'], writes=['T7'])
                P.op('dve', lambda e, w=w, nch=nch, qs0=qs0, pq=pq: e.scalar_tensor_tensor(
                    out=sub_view(Qsub[:, 0:w], nch, qs0, 3, 0, False), in0=sub_view(T[7][:, 0:w], nch, qs0, 3, 0, False), scalar=1.0,
                    in1=sub_view(pq[:, 0:w], nch, qs0, 3, 0, False), op0=ALU.min, op1=ALU.mult), reads=['T7', pqt], writes=['Qsub'])
                for jj in range(3):
                    bj = (jj + 1) * SUB - 1 if d == 0 else (jj + 1) * SUB
                    P.op('dve', lambda e, w=w, nch=nch, cw=cw, bj=bj: e.tensor_tensor(out=v3(T[8][:, 0:w], nch, HC), in0=chunk_col(cw, nch, bj),
                                                                                   in1=v3(cw, nch, HC), op=ALU.subtract),
                         reads=[ct], writes=['T8'])
                    P.op('dve', lambda e, w=w: e.tensor_scalar(out=T[8][:, 0:w], in0=T[8][:, 0:w], scalar1=0.0, scalar2=None, op0=ALU.min),
                         reads=['T8'], writes=['T8'])
                    P.op('act', lambda e, w=w: e.activation(out=T[8][:, 0:w], in_=T[8][:, 0:w], func=AF.Exp), reads=['T8'], writes=['T8'])
                    P.op('pool', lambda e, w=w, jj=jj: e.tensor_tensor(out=Kj[jj][:, 0:w], in0=T[8][:, 0:w], in1=T[2][:, 0:w], op=ALU.mult),
                         reads=['T8', 'T2'], writes=[f'Kj{jj}'])
                nsb = w // SUB
                P.op('dve', lambda e, w=w, nsb=nsb, cw=cw: e.tensor_tensor(
                    out=v3(T[9][:, 0:w], nsb, SUB), in0=v3(cw, nsb, SUB),
                    in1=bass.AP(cw.tensor, cw.offset + SUB // 2, [list(cw.ap[0]), [SUB, nsb], [0, SUB]]), op=ALU.subtract),
                    reads=[ct], writes=['T9'])
                P.op('dve', lambda e, w=w: e.tensor_scalar(out=T[9][:, 0:w], in0=T[9][:, 0:w], scalar1=-CLAMP, scalar2=CLAMP,
                                                          op0=ALU.max, op1=ALU.min), reads=['T9'], writes=['T9'])
                P.op('act', lambda e, w=w: e.activation(out=T[10][:, 0:w], in_=T[9][:, 0:w], func=AF.Exp), reads=['T9'], writes=['T10'])
                P.op('act', lambda e, w=w: e.activation(out=T[11][:, 0:w], in_=T[9][:, 0:w], func=AF.Exp, scale=-1.0), reads=['T9'], writes=['T11'])
                P.op('dve', lambda e, pq=pq, w=w: e.tensor_tensor(out=Qd[:, 0:w], in0=pq[:, 0:w], in1=T[10][:, 0:w], op=ALU.mult),
                     reads=[pqt, 'T10'], writes=['Qd'])
                P.op('pool', lambda e, w=w: e.tensor_tensor(out=Kd[:, 0:w], in0=T[2][:, 0:w], in1=T[11][:, 0:w], op=ALU.mult),
                     reads=['T2', 'T11'], writes=['Kd'])
                pa, pat = r_psB.next()
                for cl in range(nch):
                    i, p0 = cl // 2, (cl % 2) * HC
                    tk = cl * HC
                    P.op('pe', lambda e, pa=pa, i=i, p0=p0, tk=tk: e.matmul(pa[p0:p0 + HC, i * HC:(i + 1) * HC], Kd[:, tk:tk + HC], Qd[:, tk:tk + HC],
                                                                          start=True, stop=True), reads=['Kd', 'Qd'], pwrites=[pat])
                    dummy = 0 if d == 0 else NSUB - 1
                    for sj in range(NSUB):
                        c0 = tk + sj * SUB
                        if sj == dummy:
                            lh, rh, lt, rt = Kd, Qd, 'Kd', 'Qd'
                        else:
                            jj = sj - 1 if d == 0 else sj
                            lh, rh, lt, rt = Kj[jj], Qsub, f'Kj{jj}', 'Qsub'
                        P.op('pe', lambda e, pa=pa, i=i, p0=p0, tk=tk, c0=c0, sj=sj, lh=lh, rh=rh: e.matmul(
                            pa[p0:p0 + HC, 256 + i * HC + sj * SUB:256 + i * HC + (sj + 1) * SUB], lh[:, tk:tk + HC], rh[:, c0:c0 + SUB],
                            start=True, stop=True), reads=[lt, rt], pwrites=[pat])
                P.op('dve', lambda e, pa=pa, nti=nti, d=d: e.tensor_tensor(out=v3(T[0][:, 0:nti * HC], nti, HC), in0=v3(pa[:, 0:nti * HC], nti, HC),
                                                                          in1=bc_mid(tri[:, 2 * d, :], nti), op=ALU.mult),
                     reads=[pat, 'tri'], writes=['T0'])
                P.op('dve', lambda e, pa=pa, nti=nti, d=d: e.tensor_tensor(out=v3(T[1][:, 0:nti * HC], nti, HC), in0=v3(pa[:, 256:256 + nti * HC], nti, HC),
                                                                          in1=bc_mid(tri[:, 2 * d + 1, :], nti), op=ALU.mult),
                     reads=[pat, 'tri'], writes=['T1'])
                for hf in range(2):
                    P.op('pool', lambda e, nti=nti, hf=hf, tt0=tt0: e.tensor_tensor(
                        out=attmEO[hf][hf * HC:(hf + 1) * HC, tt0:tt0 + nti, :], in0=v3(T[0][hf * HC:(hf + 1) * HC, 0:nti * HC], nti, HC),
                        in1=v3(T[1][hf * HC:(hf + 1) * HC, 0:nti * HC], nti, HC), op=ALU.add),
                        reads=['T0', 'T1'], pwrites=['attm'])
                for i in range(nti):
                    P.op('pe', lambda e, i=i: e.transpose(psT[:, i * 128:(i + 1) * 128], KtT[:, i * 128:(i + 1) * 128], ident[:, :]),
                         reads=['KtT', 'ident'], pwrites=['psT'])
                for hf in range(2):
                    P.op('dve', lambda e, nti=nti, hf=hf: e.tensor_copy(out=KtmEO[hf][hf * HC:(hf + 1) * HC, 0:nti, :],
                                                                       in_=v3(psT[hf * HC:(hf + 1) * HC, 0:nti * 128], nti, 128)),
                         reads=['psT'], pwrites=['Ktm'])
                for b0 in range(0, nch, 4):
                    pd, pdt = r_psB.next()
                    for cl in range(b0, b0 + 4):
                        i = cl // 2
                        P.op('pe', lambda e, pd=pd, cl=cl, i=i, b0=b0, tt0=tt0: e.matmul(
                            pd[:, (cl - b0) * 128:(cl - b0 + 1) * 128], KtmEO[cl % 2][:, i, :], itm[:, tt0 + i, :],
                            start=True, stop=True), reads=['Ktm', 'itm'], pwrites=[pdt])
                    P.op('act', lambda e, pd=pd, s0=s0, b0=b0: e.activation(out=dS[:, s0 + b0:s0 + b0 + 4, :], in_=v3(pd[:, 0:512], 4, 128),
                                                                            func=AF.Identity), reads=[pdt], pwrites=['dS'])
            if stop <= 0:
                break
            for ee in range(128):
                if d == 0:
                    a1 = bass.AP(dS, ee, [[NCH * 128, 128], [128, NCH]])
                    a0 = Dall[:, :]
                else:
                    a1 = bass.AP(dS, (NCH - 1) * 128 + ee, [[NCH * 128, 128], [-128, NCH]])
                    a0 = bass.AP(Dall, NCH - 1, [[NCH, 128], [-1, NCH]])
                P.op('dve', lambda e, a1=a1, a0=a0: e.tensor_tensor_scan(out=a1, data0=a0, data1=a1, initial=0.0, op0=ALU.mult, op1=ALU.add),
                     reads=['dS', 'Dall'], pwrites=['dS'])
            if stop <= 1:
                break
            for (t0, w) in blocks:
                nch = w // HC
                ch0 = t0 // HC
                tt0 = t0 // 128
                po, pot = r_psB.next()
                for cl in range(nch):
                    i = cl // 2
                    tk = t0 + cl * HC
                    ch = ch0 + cl
                    if d == 0:
                        s = fslot(ch, 0)
                        prev = s - 1 if s > 0 else None
                    else:
                        prev = ch + 1 if ch < NCH - 1 else None
                    if prev is not None:
                        P.op('pe', lambda e, po=po, cl=cl, prev=prev, tk=tk: e.matmul(po[:, cl * HC:(cl + 1) * HC], dS[:, prev, :], Qp[:, tk:tk + HC],
                                                                                    start=True, stop=False), reads=['dS', 'Qp'], pwrites=[pot])
                    P.op('pe', lambda e, po=po, cl=cl, i=i, tt0=tt0, prev=prev: e.matmul(
                        po[:, cl * HC:(cl + 1) * HC], itm[:, tt0 + i, :], attmEO[cl % 2][:, tt0 + i, :],
                        start=(prev is None), stop=True), reads=['itm', 'attm'], pwrites=[pot])
                if d == 0:
                    P.op('act', lambda e, po=po, t0=t0, w=w: e.activation(out=oacc[:, t0:t0 + w], in_=po[:, 0:w], func=AF.Identity),
                         reads=[pot], pwrites=['oacc'])
                else:
                    P.op('dve', lambda e, po=po, t0=t0, w=w: e.tensor_tensor(out=oacc[:, t0:t0 + w], in0=po[:, 0:w], in1=oacc[:, t0:t0 + w], op=ALU.add),
                         reads=[pot, 'oacc'], pwrites=['oacc'])
        for (t0, w) in (blocks if stop > 2 else []):
            load_ub(t0, w)
            pg, pgt = proj(4, w)
            P.op('act', lambda e, pg=pg, w=w: e.activation(out=T[0][:, 0:w], in_=pg[:, 0:w], func=AF.Silu), reads=[pgt], writes=['T0'])
            P.op('act', lambda e, t0=t0, w=w: e.activation(out=T[1][:, 0:w], in_=oacc[:, t0:t0 + w], func=AF.Square), reads=['oacc'], writes=['T1'])
            pn, pnt = r_psB.next()
            P.op('pe', lambda e, pn=pn, w=w: e.matmul(pn[:, 0:w], ones32[:, :], T[1][:, 0:w], start=True, stop=True), reads=['T1', 'ones32'], writes=[pnt])
            P.op('dve', lambda e, pn=pn, w=w: e.tensor_scalar(out=T[2][:, 0:w], in0=pn[:, 0:w], scalar1=1.0 / 128, scalar2=EPS,
                                                             op0=ALU.mult, op1=ALU.add), reads=[pnt], writes=['T2'])
            P.op('act', lambda e, w=w: e.activation(out=T[2][:, 0:w], in_=T[2][:, 0:w], func=AF.Sqrt), reads=['T2'], writes=['T2'])
            P.op('dve', lambda e, w=w: e.reciprocal(out=T[2][:, 0:w], in_=T[2][:, 0:w]), reads=['T2'], writes=['T2'])
            P.op('dve', lambda e, t0=t0, w=w: e.scalar_tensor_tensor(out=T[3][:, 0:w], in0=oacc[:, t0:t0 + w], scalar=hgn[:, 0:1], in1=T[2][:, 0:w],
                                                                    op0=ALU.mult, op1=ALU.mult), reads=['oacc', 'T2', 'hgn'], writes=['T3'])
            P.op('pool', lambda e, w=w: e.tensor_tensor(out=yb[:, 0:w], in0=T[3][:, 0:w], in1=T[0][:, 0:w], op=ALU.mult),
                 reads=['T3', 'T0'], writes=['yb'])
            P.op('sp', lambda e, t0=t0, w=w: e.dma_start(out=y_d[:, t0:t0 + w], in_=yb[:, 0:w]), reads=['yb'], pwrites=['y_d'], chan='yb')
        P.finish('sp')
        P.emit()
    return nc


def hgrn_consts():
    p = np.arange(128)[:, None] % HC
    t = np.arange(HC)[None, :]
    same = (p // SUB) == (t // SUB)
    tri = np.stack([same & (p <= t), (p // SUB) < (t // SUB), same & (p >= t), (p // SUB) > (t // SUB)], axis=1).astype(np.float32)
    rmask = np.broadcast_to((np.arange(QBW) % HC != 0).astype(np.float32)[None, :], (128, QBW))
    return np.ascontiguousarray(tri), np.ascontiguousarray(rmask)


def run_hgrn(uT_all, w_in, lbl, hgn, e_layer, stop=99, ncores=NCORE):
    nc = build_hgrn_prog(e_layer, stop)
    ident = np.eye(128, dtype=np.float32).astype(NPBF)
    tri, rmask = hgrn_consts()
    in_maps = []
    for j in range(NCORE):
        cols = [w_in[:, (3 + g) * 1024 + j * 128: (3 + g) * 1024 + (j + 1) * 128] for g in range(5)]
        w5 = np.stack([lay_w(c) for c in cols], axis=2)
        lj = np.ascontiguousarray(lbl[:, :, j * 128:(j + 1) * 128].transpose(2, 0, 1)).astype(np.float32)
        hj = np.ascontiguousarray(hgn[j * 128:(j + 1) * 128].reshape(128, 1)).astype(np.float32)
        in_maps.append({"uT": uT_all, "w5": np.ascontiguousarray(w5), "lbl": lj, "hgn": hj, "ident": ident, "tri": tri, "rmask": rmask})
    in_maps = in_maps[:ncores]
    res = run_bass_kernel_spmd(nc, in_maps, core_ids=list(range(ncores)))
    return np.stack([res.results[j]["y_out"] for j in range(ncores)], axis=1)


def _fm(x):
    T = x.shape[0]
    return np.ascontiguousarray(x.T.reshape(KC, 128, T).transpose(1, 0, 2))


def _unfm(a):
    T = a.shape[2]
    return np.ascontiguousarray(a.transpose(1, 0, 2).reshape(D, T).T)


def _vec(v):
    return np.ascontiguousarray(v.reshape(KC, 128).T)


def _lay_win(w):
    g = w[:, :DFF].reshape(KC, 128, JC, 128)
    u = w[:, DFF:].reshape(KC, 128, JC, 128)
    return np.ascontiguousarray(np.concatenate([g.transpose(2, 1, 0, 3), u.transpose(2, 1, 0, 3)], axis=3))


def _lay_wout(w):
    return np.ascontiguousarray(w.reshape(JC, 128, KC, 128).transpose(2, 1, 0, 3))


def _lay_wo(w):
    return np.ascontiguousarray(w.reshape(KC, 128, KC, 128).transpose(2, 1, 0, 3))


def _run_token(stages_spec, h_list, y_all, vec_list, weights, want_u, final):
    stages = [(s, i) for i, s in enumerate(stages_spec)]
    nc = build_token_prog(stages, want_u=want_u, final=final)
    vecs = np.ascontiguousarray(np.stack([_vec(v) for v in vec_list], axis=1)).astype(np.float32)
    shared = {"vecs": vecs}
    for si, s in enumerate(stages_spec):
        if s == 'proj':
            shared[f"wo{si}"] = _lay_wo(weights[si])
        else:
            shared[f"wi{si}"] = _lay_win(weights[si][0])
            shared[f"wf{si}"] = _lay_wout(weights[si][1])
    in_maps = []
    for r in range(NCORE):
        m = dict(shared)
        m["h_in"] = h_list[r]
        if y_all is not None:
            m["y_in"] = np.ascontiguousarray(np.concatenate(
                [y_all[:, :, r * TLAT:(r + 1) * TLAT], y_all[:, :, SEQ + r * TCTX: SEQ + (r + 1) * TCTX]], axis=2))
        in_maps.append(m)
    res = run_bass_kernel_spmd(nc, in_maps, core_ids=list(range(NCORE)))
    return res.results


def _gather_u(results):
    u_all = np.zeros((128, KC, TALL), NPBF)
    for r in range(NCORE):
        uo = results[r]["u_out"]
        u_all[:, :, r * TLAT:(r + 1) * TLAT] = uo[:, :, :TLAT]
        u_all[:, :, SEQ + r * TCTX: SEQ + (r + 1) * TCTX] = uo[:, :, TLAT:]
    return u_all


def kernel(x, c, ctx, c_ctx, w_mod, b_mod, norm_g, w_ff_in, w_ff_out, w_in_even, w_out_even,
           na_rpb, hg_lb_logits, hg_norm_g, w_qkv_odd, w_o_odd, sink_odd, final_norm_g):
    f = lambda a: np.asarray(a, dtype=np.float32)
    x, c, ctx, c_ctx, w_mod, b_mod, norm_g = f(x), f(c), f(ctx), f(c_ctx), f(w_mod), f(b_mod), f(norm_g)
    w_ff_in, w_ff_out, w_in_even, w_out_even = f(w_ff_in), f(w_ff_out), f(w_in_even), f(w_out_even)
    na_rpb, hg_lb_logits, hg_norm_g = f(na_rpb), f(hg_lb_logits), f(hg_norm_g)
    w_qkv_odd, w_o_odd, sink_odd, final_norm_g = f(w_qkv_odd), f(w_o_odd), f(sink_odd), f(final_norm_g)

    mod = run_mod(c[0], c_ctx, w_mod, b_mod).reshape(DEPTH, 2, 3, 3, D)

    def ffn_vecs(l, sub):
        return [norm_g[l, sub], mod[l, 0, sub, 0], mod[l, 0, sub, 1], mod[l, 0, sub, 2],
                mod[l, 1, sub, 0], mod[l, 1, sub, 1], mod[l, 1, sub, 2]]

    def u_vecs(l):
        return [norm_g[l, 1], mod[l, 0, 1, 0], mod[l, 0, 1, 1], mod[l, 1, 1, 0], mod[l, 1, 1, 1]]

    h_list = []
    for r in range(NCORE):
        tok = np.concatenate([x[0, r * TLAT:(r + 1) * TLAT], ctx[0, r * TCTX:(r + 1) * TCTX]], axis=0)
        h_list.append(_fm(tok))

    res = _run_token(['ffn'], h_list, None, ffn_vecs(0, 0) + u_vecs(0), [(w_ff_in[0, 0], w_ff_out[0, 0])], True, False)
    out = None
    for l in range(DEPTH):
        h_list = [res[r]["h_out"] for r in range(NCORE)]
        u_all = _gather_u(res)
        if l % 2 == 0:
            e = l // 2
            ya = run_na(u_all, w_in_even[e], na_rpb[e])
            yb = run_hgrn(u_all, w_in_even[e], hg_lb_logits, hg_norm_g[e], e)
            y_all = np.ascontiguousarray(np.concatenate([ya, yb], axis=1))
            w_o = w_out_even[e]
        else:
            o = l // 2
            y_all = run_odd(u_all, w_qkv_odd[o], sink_odd[o])
            w_o = w_o_odd[o]
        gate_vecs = [mod[l, 0, 1, 2], mod[l, 1, 1, 2]]
        if l < DEPTH - 1:
            res = _run_token(['proj', 'ffn', 'ffn'], h_list, y_all,
                             gate_vecs + ffn_vecs(l, 2) + ffn_vecs(l + 1, 0) + u_vecs(l + 1),
                             [w_o, (w_ff_in[l, 1], w_ff_out[l, 1]), (w_ff_in[l + 1, 0], w_ff_out[l + 1, 0])], True, False)
        else:
            res = _run_token(['proj', 'ffn'], h_list, y_all, gate_vecs + ffn_vecs(l, 2) + [final_norm_g],
                             [w_o, (w_ff_in[l, 1], w_ff_out[l, 1])], False, True)
            out = np.zeros((1, SEQ, D), np.float32)
            for r in range(NCORE):
                out[0, r * TLAT:(r + 1) * TLAT] = _unfm(res[r]["o_out"])[:TLAT]
    return out
```

```python
import numpy as np
import ml_dtypes
from contextlib import ExitStack
import concourse.bass as bass
import concourse.mybir as mybir
from concourse.bass_utils import run_bass_kernel_spmd

F32 = mybir.dt.float32
BF16 = mybir.dt.bfloat16
AF = mybir.ActivationFunctionType
ALU = mybir.AluOpType
NPBF = ml_dtypes.bfloat16

D = 2048
KC = 16
DFF = 5632
JC = 44
NCORE = 8
SEQ = 8192
CTX = 256
TLAT = SEQ // NCORE
TCTX = CTX // NCORE
TT = TLAT + TCTX
TW = 352
NT = 3
EPS = 1e-6
DEPTH = 4


class Prog:
    ENGS = {'pe': 'tensor', 'act': 'scalar', 'dve': 'vector', 'pool': 'gpsimd', 'sp': 'sync'}

    def __init__(self, nc, es):
        self.nc = nc
        self.es = es
        self.q = {e: [] for e in self.ENGS}
        self.sems = {}
        self.val = {}
        self.w = {}
        self.r = {}
        self.waited = {}
        self.limit = None
        self.nops = 0

    def sem(self, key):
        if key not in self.sems:
            self.sems[key] = self.es.enter_context(self.nc.semaphore("s_" + key))
            self.val[key] = 0
        return self.sems[key]

    def op(self, eng, fn, reads=(), writes=(), pwrites=(), chan=None):
        self.nops += 1
        if self.limit is not None and self.nops > self.limit:
            return None
        waits = {}

        def need(evs):
            for k, (v, e) in evs.items():
                if e == 'pe' and eng == 'pe':
                    continue
                if waits.get(k, 0) < v:
                    waits[k] = v
        for t in reads:
            need(self.w.get(t, {}))
        for t in list(writes) + list(pwrites):
            need(self.w.get(t, {}))
            need(self.r.get(t, {}))
        wl = []
        for k, v in waits.items():
            if self.waited.get((eng, k), 0) >= v:
                continue
            self.waited[(eng, k)] = v
            wl.append((k, v))
        if chan is None:
            key, inc, ee = eng, 1, eng
        else:
            key, inc, ee = 'd_' + chan, 16, 'dma'
        self.sem(key)
        self.val[key] += inc
        v = self.val[key]
        for t in reads:
            self.r.setdefault(t, {})[key] = (v, ee)
        for t in writes:
            self.w[t] = {key: (v, ee)}
            self.r[t] = {}
        for t in pwrites:
            self.w.setdefault(t, {})[key] = (v, ee)
            self.r[t] = {}
        self.q[eng].append((wl, fn, key, inc))
        return (key, v)

    def finish(self, eng='sp'):
        wl = [(k, v) for k, v in self.val.items() if k.startswith('d_')]
        self.q[eng].append((wl, None, None, 0))

    def emit(self):
        nc = self.nc
        with nc.Block() as block:
            for e, attr in self.ENGS.items():
                items = self.q[e]

                def body(engine, items=items):
                    for wl, fn, key, inc in items:
                        for k, v in wl:
                            engine.wait_ge(self.sems[k], v)
                        if fn is not None:
                            ins = fn(engine)
                            ins.then_inc(self.sems[key], inc)
                getattr(block, attr)(body)


def bc_mid(ap, n):
    a = [list(x) for x in ap.ap]
    return bass.AP(ap.tensor, ap.offset, [a[0], [0, n]] + a[1:])


def bc_last(ap, n):
    a = [list(x) for x in ap.ap]
    return bass.AP(ap.tensor, ap.offset, a + [[0, n]])


class Ring:
    def __init__(self, bufs, name):
        self.bufs = bufs
        self.name = name
        self.i = 0

    def next(self):
        k = self.i % len(self.bufs)
        self.i += 1
        return self.bufs[k], f"{self.name}{k}"


def build_token_prog(stages, want_u=False, final=False):
    nc = bass.Bass("TRN2", target_bir_lowering=False)
    nv = 0
    for s in stages:
        nv += 2 if s[0] == 'proj' else 7
    if want_u:
        nv += 5
    if final:
        nv += 1
    h_in = nc.dram_tensor("h_in", [128, KC, TT], F32, kind="ExternalInput").ap()
    vecs_d = nc.dram_tensor("vecs", [128, nv, KC], F32, kind="ExternalInput").ap()
    wd = {}
    y_in = None
    for si, s in enumerate(stages):
        if s[0] == 'proj':
            wd[si] = nc.dram_tensor(f"wo{si}", [KC, 128, KC, 128], F32, kind="ExternalInput").ap()
            y_in = nc.dram_tensor("y_in", [128, KC, TT], BF16, kind="ExternalInput").ap()
        else:
            wd[si] = (nc.dram_tensor(f"wi{si}", [JC, 128, KC, 256], F32, kind="ExternalInput").ap(),
                      nc.dram_tensor(f"wf{si}", [KC, 128, JC, 128], F32, kind="ExternalInput").ap())
    hd = [h_in]
    for si in range(len(stages)):
        last = si == len(stages) - 1
        if last and not final:
            hd.append(nc.dram_tensor("h_out", [128, KC, TT], F32, kind="ExternalOutput").ap())
        else:
            hd.append(nc.dram_tensor(f"h_s{si}", [128, KC, TT], F32).ap())
    u_out = nc.dram_tensor("u_out", [128, KC, TT], BF16, kind="ExternalOutput").ap() if want_u else None
    o_out = nc.dram_tensor("o_out", [128, KC, TT], F32, kind="ExternalOutput").ap() if final else None

    tiles = [(i * TW, TW) for i in range(NT)]
    segs = []
    for (t0, w) in tiles:
        sg = []
        lat_end = min(max(TLAT - t0, 0), w)
        if lat_end > 0:
            sg.append((0, lat_end, 0))
        if lat_end < w:
            sg.append((lat_end, w, 1))
        segs.append(sg)

    with ExitStack() as es:
        P = Prog(nc, es)
        sb = lambda name, shape, dt: es.enter_context(nc.sbuf_tensor(name, shape, dt))
        vecs = sb("vecs_sb", [128, nv, KC], F32)
        dvec = sb("dvec", [128, 8, KC], F32)
        ones = sb("ones", [128, 128], F32)
        uT = sb("uT", [128, KC, TT], BF16)
        big = sb("big", [128, JC, TT], BF16)
        rstd = sb("rstd", [128, TW], F32)
        tmpn = [sb(f"tmpn{i}", [128, TW], F32) for i in range(6)]
        wib = [sb(f"wib{i}", [128, KC, 256], BF16) for i in range(3)]
        wob = [sb(f"wob{i}", [128, JC, 128], BF16) for i in range(2)]
        sgb = [sb(f"sgb{i}", [128, TW], F32) for i in range(3)]
        hres = [sb(f"hres{i}", [128, TW], F32) for i in range(8)]
        hout = [sb(f"hout{i}", [128, TW], F32) for i in range(3)]
        ps = [es.enter_context(nc.psum_tensor(f"ps{i}", [128, 512], F32)) for i in range(8)]
        r_tmpn = Ring(tmpn, "tmpn")
        r_wib = Ring(wib, "wib")
        r_wob = Ring(wob, "wob")
        r_sgb = Ring(sgb, "sgb")
        r_hres = Ring(hres, "hres")
        r_hout = Ring(hout, "hout")
        r_psA = Ring(ps[0:5], "ps")
        psn = ps[5:8]

        P.op('sp', lambda e: e.dma_start(out=vecs[:], in_=vecs_d), writes=['vecs'], chan='vecs')
        P.op('pool', lambda e: e.memset(ones[:], 1.0), writes=['ones'])

        def norm_stage(src, vbase, has_mod, dst_dram, presummed=False):
            gi, shl, scl, shc, scc = vbase
            if has_mod:
                for which, sc_i in ((0, scl), (1, scc)):
                    P.op('dve', lambda e, which=which, sc_i=sc_i: e.scalar_tensor_tensor(
                        out=dvec[:, which, :], in0=vecs[:, sc_i, :], scalar=1.0, in1=vecs[:, gi, :],
                        op0=ALU.add, op1=ALU.mult), reads=['vecs'], writes=[f'dvA{which}'])
            for ti, (t0, w) in enumerate(tiles):
                pb, pbt = psn[ti], f'psn{ti}'
                for k in (range(KC) if not presummed else []):
                    hr, hrt = r_hres.next()
                    P.op('sp', lambda e, hr=hr, k=k, t0=t0, w=w: e.dma_start(out=hr[:, 0:w], in_=src[:, k, t0:t0 + w]),
                         reads=[src.tensor.name], writes=[hrt], chan=hrt)
                    sq, sqt = r_sgb.next()
                    P.op('act', lambda e, hr=hr, sq=sq, w=w: e.activation(out=sq[:, 0:w], in_=hr[:, 0:w], func=AF.Square),
                         reads=[hrt], writes=[sqt])
                    P.op('pe', lambda e, k=k, pb=pb, sq=sq, w=w: e.matmul(pb[:, 0:w], ones[:, :], sq[:, 0:w],
                                                                        start=(k == 0), stop=(k == KC - 1)),
                         reads=[sqt, 'ones'], writes=[pbt])
                P.op('dve', lambda e, pb=pb, w=w: e.tensor_scalar(out=rstd[:, 0:w], in0=pb[:, 0:w], scalar1=1.0 / D,
                                                                 scalar2=EPS, op0=ALU.mult, op1=ALU.add),
                     reads=[pbt], writes=['rstd'])
                P.op('act', lambda e, w=w: e.activation(out=rstd[:, 0:w], in_=rstd[:, 0:w], func=AF.Sqrt),
                     reads=['rstd'], writes=['rstd'])
                P.op('dve', lambda e, w=w: e.reciprocal(out=rstd[:, 0:w], in_=rstd[:, 0:w]),
                     reads=['rstd'], writes=['rstd'])
                for k in range(KC):
                    hr, hrt = r_hres.next()
                    P.op('sp', lambda e, hr=hr, k=k, t0=t0, w=w: e.dma_start(out=hr[:, 0:w], in_=src[:, k, t0:t0 + w]),
                         reads=[src.tensor.name], writes=[hrt], chan=hrt)
                    if has_mod:
                        tb, tbt = r_tmpn.next()
                        for (c0, c1, which) in segs[ti]:
                            a_ap = dvec[:, which, k:k + 1]
                            b_ap = vecs[:, (shl if which == 0 else shc), k:k + 1]
                            P.op('dve', lambda e, tb=tb, hr=hr, c0=c0, c1=c1, a_ap=a_ap: e.scalar_tensor_tensor(
                                out=tb[:, c0:c1], in0=hr[:, c0:c1], scalar=a_ap, in1=rstd[:, c0:c1],
                                op0=ALU.mult, op1=ALU.mult), reads=[hrt, 'rstd', f'dvA{which}'], pwrites=[tbt])
                            P.op('act', lambda e, tb=tb, c0=c0, c1=c1, b_ap=b_ap, k=k, t0=t0: e.activation(
                                out=uT[:, k, t0 + c0:t0 + c1], in_=tb[:, c0:c1], func=AF.Identity, bias=b_ap, scale=1.0),
                                reads=[tbt, 'vecs'], pwrites=[f'u_t{ti}'])
                    else:
                        ho, hot = r_hout.next()
                        P.op('dve', lambda e, ho=ho, hr=hr, k=k, w=w: e.scalar_tensor_tensor(
                            out=ho[:, 0:w], in0=hr[:, 0:w], scalar=vecs[:, gi, k:k + 1], in1=rstd[:, 0:w],
                            op0=ALU.mult, op1=ALU.mult), reads=[hrt, 'rstd', 'vecs'], writes=[hot])
                        P.op('sp', lambda e, ho=ho, k=k, t0=t0, w=w: e.dma_start(out=dst_dram[:, k, t0:t0 + w], in_=ho[:, 0:w]),
                             reads=[hot], pwrites=[dst_dram.tensor.name], chan='st_' + hot)
                if has_mod and dst_dram is not None:
                    P.op('sp', lambda e, t0=t0, w=w: e.dma_start(out=dst_dram[:, :, t0:t0 + w], in_=uT[:, :, t0:t0 + w]),
                         reads=[f'u_t{ti}'], pwrites=[dst_dram.tensor.name], chan=f'ust{ti}')

        def ffn1_stage(w_in_d):
            for j in range(JC):
                wb, wbt = r_wib.next()
                P.op('pool', lambda e, j=j, wb=wb: e.dma_start(out=wb[:], in_=w_in_d[j]), writes=[wbt], chan=wbt)
                for ti, (t0, w) in enumerate(tiles):
                    pg, pgt = r_psA.next()
                    pu, put = r_psA.next()
                    for half, (pp, ppt) in enumerate(((pg, pgt), (pu, put))):
                        for k in range(KC):
                            P.op('pe', lambda e, k=k, pp=pp, wb=wb, half=half, t0=t0, w=w: e.matmul(
                                pp[:, 0:w], wb[:, k, half * 128:(half + 1) * 128], uT[:, k, t0:t0 + w],
                                start=(k == 0), stop=(k == KC - 1)),
                                reads=[wbt, f'u_t{ti}'], writes=[ppt])
                    sg, sgt = r_sgb.next()
                    P.op('act', lambda e, sg=sg, pg=pg, w=w: e.activation(out=sg[:, 0:w], in_=pg[:, 0:w], func=AF.Silu),
                         reads=[pgt], writes=[sgt])
                    P.op('dve', lambda e, sg=sg, pu=pu, j=j, t0=t0, w=w: e.tensor_tensor(
                        out=big[:, j, t0:t0 + w], in0=sg[:, 0:w], in1=pu[:, 0:w], op=ALU.mult),
                        reads=[sgt, put], pwrites=[f'big_t{ti}'])

        def proj_stage(w_d, kc, src, dst, gl, gc, half):
            mul = 0.5 if half else 1.0
            for which, gi_ in ((0, gl), (1, gc)):
                P.op('dve', lambda e, which=which, gi_=gi_: e.tensor_scalar(
                    out=dvec[:, 2 + which, :], in0=vecs[:, gi_, :], scalar1=mul, scalar2=None, op0=ALU.mult),
                    reads=['vecs'], writes=[f'dvG{which}'])
            deferred = []
            for dc in range(KC):
                wb, wbt = r_wob.next()
                P.op('pool', lambda e, dc=dc, wb=wb: e.dma_start(out=wb[:, 0:kc, :], in_=w_d[dc]), writes=[wbt], chan=wbt)
                for ti, (t0, w) in enumerate(tiles):
                    hr, hrt = r_hres.next()
                    P.op('sp', lambda e, hr=hr, dc=dc, t0=t0, w=w: e.dma_start(out=hr[:, 0:w], in_=src[:, dc, t0:t0 + w]),
                         reads=[src.tensor.name], writes=[hrt], chan=hrt)
                    pp, ppt = r_psA.next()
                    for j in range(kc):
                        P.op('pe', lambda e, j=j, pp=pp, wb=wb, t0=t0, w=w: e.matmul(
                            pp[:, 0:w], wb[:, j, :], big[:, j, t0:t0 + w], start=(j == 0), stop=(j == kc - 1)),
                            reads=[wbt, f'big_t{ti}'], writes=[ppt])
                    while deferred:
                        deferred.pop(0)()
                    ho, hot = r_hout.next()
                    for (c0, c1, which) in segs[ti]:
                        P.op('dve', lambda e, ho=ho, pp=pp, hr=hr, c0=c0, c1=c1, which=which, dc=dc: e.scalar_tensor_tensor(
                            out=ho[:, c0:c1], in0=pp[:, c0:c1], scalar=dvec[:, 2 + which, dc:dc + 1], in1=hr[:, c0:c1],
                            op0=ALU.mult, op1=ALU.add), reads=[ppt, hrt, f'dvG{which}'], pwrites=[hot])
                    P.op('sp', lambda e, ho=ho, dc=dc, t0=t0, w=w: e.dma_start(out=dst[:, dc, t0:t0 + w], in_=ho[:, 0:w]),
                         reads=[hot], pwrites=[dst.tensor.name], chan='st_' + hot)
                    sq, sqt = r_sgb.next()
                    P.op('act', lambda e, ho=ho, sq=sq, w=w: e.activation(out=sq[:, 0:w], in_=ho[:, 0:w], func=AF.Square),
                         reads=[hot], writes=[sqt])

                    def _ssq(dc=dc, ti=ti, sq=sq, sqt=sqt, w=w):
                        P.op('pe', lambda e: e.matmul(psn[ti][:, 0:w], ones[:, :], sq[:, 0:w], start=(dc == 0), stop=(dc == KC - 1)),
                             reads=[sqt, 'ones'], writes=[f'psn{ti}'])
                    deferred.append(_ssq)
            while deferred:
                deferred.pop(0)()

        vb = 0
        for si, s in enumerate(stages):
            src, dst = hd[si], hd[si + 1]
            if s[0] == 'proj':
                for ti, (t0, w) in enumerate(tiles):
                    P.op('sp', lambda e, t0=t0, w=w: e.dma_start(out=big[:, 0:KC, t0:t0 + w], in_=y_in[:, :, t0:t0 + w]),
                         writes=[f'big_t{ti}'], chan=f'yin{ti}')
                proj_stage(wd[si], KC, src, dst, vb, vb + 1, False)
                vb += 2
            else:
                g, shl, scl, gl, shc, scc, gc = range(vb, vb + 7)
                norm_stage(src, (g, shl, scl, shc, scc), True, None, presummed=(si > 0))
                ffn1_stage(wd[si][0])
                proj_stage(wd[si][1], JC, src, dst, gl, gc, True)
                vb += 7
        if want_u:
            g, shl, scl, shc, scc = range(vb, vb + 5)
            norm_stage(hd[-1], (g, shl, scl, shc, scc), True, u_out, presummed=True)
            vb += 5
        if final:
            norm_stage(hd[-1], (vb, 0, 0, 0, 0), False, o_out, presummed=True)
            vb += 1
        P.finish('sp')
        P.emit()
    return nc


MODC = 9 * D // NCORE
MCH = 384
NMCH = MODC // MCH


def build_mod_prog():
    nc = bass.Bass("TRN2", target_bir_lowering=False)
    cvec_d = nc.dram_tensor("cvec", [128, KC, 2], F32, kind="ExternalInput").ap()
    w_d = nc.dram_tensor("wm", [DEPTH * NMCH, 128, KC, MCH], F32, kind="ExternalInput").ap()
    b_d = nc.dram_tensor("bm", [2, DEPTH * MODC], F32, kind="ExternalInput").ap()
    o_d = nc.dram_tensor("mod_out", [2, DEPTH * MODC], F32, kind="ExternalOutput").ap()
    with ExitStack() as es:
        P = Prog(nc, es)
        sb = lambda name, shape, dt: es.enter_context(nc.sbuf_tensor(name, shape, dt))
        cv = sb("cv", [128, KC, 2], F32)
        sc = sb("sc", [128, KC, 2], F32)
        bsb = sb("bsb", [2, DEPTH * MODC], F32)
        mo = sb("mo", [2, DEPTH * MODC], F32)
        wbs = [sb(f"wb{i}", [128, KC, MCH], F32) for i in range(3)]
        ps = [es.enter_context(nc.psum_tensor(f"ps{i}", [128, 512], F32)) for i in range(4)]
        r_wb = Ring(wbs, "wb")
        r_ps = Ring(ps, "ps")
        P.op('sp', lambda e: e.dma_start(out=cv[:], in_=cvec_d), writes=['cv'], chan='cv')
        P.op('sp', lambda e: e.dma_start(out=bsb[:], in_=b_d), writes=['bsb'], chan='bsb')
        P.op('act', lambda e: e.activation(out=sc[:], in_=cv[:], func=AF.Silu), reads=['cv'], writes=['sc'])
        for i in range(DEPTH * NMCH):
            wb, wbt = r_wb.next()
            P.op('sp', lambda e, i=i, wb=wb: e.dma_start(out=wb[:], in_=w_d[i]), writes=[wbt], chan=wbt)
            pp, ppt = r_ps.next()
            for k in range(KC):
                P.op('pe', lambda e, k=k, pp=pp, wb=wb: e.matmul(pp[0:2, 0:MCH], sc[:, k, :], wb[:, k, :],
                                                                start=(k == 0), stop=(k == KC - 1)),
                     reads=[wbt, 'sc'], writes=[ppt])
            P.op('dve', lambda e, i=i, pp=pp: e.tensor_tensor(out=mo[:, i * MCH:(i + 1) * MCH], in0=pp[0:2, 0:MCH],
                                                             in1=bsb[:, i * MCH:(i + 1) * MCH], op=ALU.add),
                 reads=[ppt, 'bsb'], pwrites=['mo'])
        P.op('sp', lambda e: e.dma_start(out=o_d, in_=mo[:]), reads=['mo'], writes=['o_d'], chan='o_d')
        P.finish('sp')
        P.emit()
    return nc


def run_mod(c, c_ctx, w_mod, b_mod):
    nc = build_mod_prog()
    cvec = np.ascontiguousarray(np.stack([c.reshape(KC, 128).T, c_ctx.reshape(KC, 128).T], axis=2)).astype(np.float32)
    in_maps = []
    for r in range(NCORE):
        ws = w_mod[:, :, r * MODC:(r + 1) * MODC]
        ws = ws.reshape(DEPTH, KC, 128, NMCH, MCH).transpose(0, 3, 2, 1, 4).reshape(DEPTH * NMCH, 128, KC, MCH)
        bs = b_mod[:, r * MODC:(r + 1) * MODC].reshape(1, DEPTH * MODC)
        in_maps.append({"cvec": cvec, "wm": np.ascontiguousarray(ws),
                        "bm": np.ascontiguousarray(np.concatenate([bs, bs], axis=0))})
    res = run_bass_kernel_spmd(nc, in_maps, core_ids=list(range(NCORE)))
    mod = np.zeros((DEPTH, 2, 9 * D), np.float32)
    for r in range(NCORE):
        o = res.results[r]["mod_out"].reshape(2, DEPTH, MODC)
        mod[:, :, r * MODC:(r + 1) * MODC] = o.transpose(1, 0, 2)
    return mod


TALL = SEQ + CTX
NTL = TALL // 128
QBW = 512
NQB = SEQ // QBW
NEG = -30000.0


def v3(ap, a, b):
    l = [list(x) for x in ap.ap]
    st, n = l[-1]
    assert n == a * b, (n, a, b)
    return bass.AP(ap.tensor, ap.offset, l[:-1] + [[st * b, a], [st, b]])


def token_blocks():
    return [(i * QBW, QBW) for i in range(NQB)] + [(SEQ, CTX)]


def build_odd_prog(debug=False):
    nc = bass.Bass("TRN2", target_bir_lowering=False)
    dq_d = nc.dram_tensor("dq", [128, 2, TALL], BF16, kind="ExternalOutput").ap() if debug else None
    dk_d = nc.dram_tensor("dk", [128, TALL], BF16, kind="ExternalOutput").ap() if debug else None
    uT_d = nc.dram_tensor("uT", [128, KC, TALL], BF16, kind="ExternalInput").ap()
    wq_d = nc.dram_tensor("wq", [128, KC, 4, 128], F32, kind="ExternalInput").ap()
    wk_d = nc.dram_tensor("wk", [128, KC, 2, 128], F32, kind="ExternalInput").ap()
    wv_d = nc.dram_tensor("wv", [128, KC, 128], F32, kind="ExternalInput").ap()
    rtab_d = nc.dram_tensor("rtab", [128, 2, 128], F32, kind="ExternalInput").ap()
    mask_d = nc.dram_tensor("masks", [128, 6, QBW], BF16, kind="ExternalInput").ap()
    ident_d = nc.dram_tensor("ident", [128, 128], BF16, kind="ExternalInput").ap()
    sink_d = nc.dram_tensor("sink", [128, 2], F32, kind="ExternalInput").ap()
    y_d = nc.dram_tensor("y_out", [128, 2, TALL], BF16, kind="ExternalOutput").ap()
    SCALE = 128.0 ** -0.5
    with ExitStack() as es:
        P = Prog(nc, es)
        sb = lambda name, shape, dt: es.enter_context(nc.sbuf_tensor(name, shape, dt))
        wq = sb("wq_sb", [128, KC, 4, 128], BF16)
        wk = sb("wk_sb", [128, KC, 2, 128], BF16)
        wv = sb("wv_sb", [128, KC, 128], BF16)
        rtab = sb("rtab_sb", [128, 2, 128], F32)
        masks = sb("masks_sb", [128, 6, QBW], BF16)
        ident = sb("ident_sb", [128, 128], BF16)
        onesb = sb("onesb", [128, 128], BF16)
        sink = sb("sink_sb", [128, 2], F32)
        esink = sb("esink", [128, 2], F32)
        qT = sb("qT", [128, 2, TALL], BF16)
        kT = sb("kT", [128, TALL], BF16)
        vtm = sb("vtm", [128, NTL, 128], BF16)
        yT = sb("yT", [128, 2, TALL], BF16)
        ubs = [sb(f"ub{i}", [128, KC, QBW], BF16) for i in range(2)]
        t1s = [sb(f"t1_{i}", [128, QBW], F32) for i in range(2)]
        t2s = [sb(f"t2_{i}", [128, QBW], F32) for i in range(2)]
        pts = [sb(f"pt{i}", [128, QBW], BF16) for i in range(3)]
        rcs = [sb(f"rc{i}", [128, QBW], F32) for i in range(2)]
        ps = [es.enter_context(nc.psum_tensor(f"ps{i}", [128, 512], F32)) for i in range(8)]
        r_ub = Ring(ubs, "ub"); r_t1 = Ring(t1s, "t1"); r_t2 = Ring(t2s, "t2"); r_pt = Ring(pts, "pt"); r_rc = Ring(rcs, "rc")
        r_psA = Ring(ps[0:4], "psA")
        r_psO = Ring(ps[4:6], "psO")
        r_psR = Ring(ps[6:8], "psR")

        P.op('pool', lambda e: e.dma_start(out=wq[:], in_=wq_d), writes=['wq'], chan='wq')
        P.op('pool', lambda e: e.dma_start(out=wk[:], in_=wk_d), writes=['wk'], chan='wk')
        P.op('pool', lambda e: e.dma_start(out=wv[:], in_=wv_d), writes=['wv'], chan='wv')
        P.op('sp', lambda e: e.dma_start(out=rtab[:], in_=rtab_d), writes=['rtab'], chan='rtab')
        P.op('sp', lambda e: e.dma_start(out=masks[:], in_=mask_d), writes=['masks'], chan='masks')
        P.op('sp', lambda e: e.dma_start(out=ident[:], in_=ident_d), writes=['ident'], chan='ident')
        P.op('sp', lambda e: e.dma_start(out=sink[:], in_=sink_d), writes=['sink'], chan='sink')
        P.op('pool', lambda e: e.memset(onesb[:], 1.0), writes=['onesb'])
        P.op('act', lambda e: e.activation(out=esink[:], in_=sink[:], func=AF.Exp), reads=['sink'], writes=['esink'])

        def proj_fm(ub, ubt, w_ap, wtag, width):
            pp, ppt = r_psA.next()
            for k in range(KC):
                P.op('pe', lambda e, k=k, pp=pp: e.matmul(pp[:, 0:width], w_ap(k), ub[:, k, 0:width],
                                                         start=(k == 0), stop=(k == KC - 1)),
                     reads=[ubt, wtag], writes=[ppt])
            return pp, ppt

        def rope_store(px, pxt, pw, pwt, dst, dtag, r0, scale):
            t1, t1t = r_t1.next()
            t2, t2t = r_t2.next()
            for half in range(2):
                p0, p1 = half * 64, half * 64 + 64
                if half == 0:
                    ctab = bc_last(rtab[p0:p1, 0, r0:r0 + 8], 64)
                    stab = bc_last(rtab[p0:p1, 1, r0:r0 + 8], 64)
                else:
                    ctab = bc_mid(rtab[p0:p1, 0, 0:64], 8)
                    stab = bc_mid(rtab[p0:p1, 1, 0:64], 8)
                P.op('dve', lambda e, p0=p0, p1=p1, ctab=ctab, t1=t1: e.scalar_tensor_tensor(
                    out=v3(t1[p0:p1, :], 8, 64), in0=v3(px[p0:p1, :], 8, 64), scalar=scale, in1=ctab,
                    op0=ALU.mult, op1=ALU.mult), reads=[pxt, 'rtab'], pwrites=[t1t])
                P.op('dve', lambda e, p0=p0, p1=p1, stab=stab, t2=t2: e.scalar_tensor_tensor(
                    out=v3(t2[p0:p1, :], 8, 64), in0=v3(pw[p0:p1, :], 8, 64), scalar=scale, in1=stab,
                    op0=ALU.mult, op1=ALU.mult), reads=[pwt, 'rtab'], pwrites=[t2t])
            P.op('pool', lambda e, t1=t1, t2=t2: e.tensor_tensor(out=dst, in0=t1[:, :], in1=t2[:, :], op=ALU.add),
                 reads=[t1t, t2t], pwrites=[dtag])

        for bi, (t0, width) in enumerate(token_blocks()):
            ub, ubt = r_ub.next()
            P.op('sp', lambda e, ub=ub, t0=t0, width=width: e.dma_start(out=ub[:, :, 0:width], in_=uT_d[:, :, t0:t0 + width]),
                 writes=[ubt], chan=ubt)
            is_ctx = t0 >= SEQ
            r0 = t0 // 64
            for h in range(2):
                px, pxt = proj_fm(ub, ubt, lambda k, h=h: wq[:, k, 2 * h, :], 'wq', width)
                if is_ctx:
                    P.op('act', lambda e, px=px, h=h, t0=t0, width=width: e.activation(
                        out=qT[:, h, t0:t0 + width], in_=px[:, 0:width], func=AF.Identity, scale=SCALE),
                        reads=[pxt], pwrites=['qT'])
                else:
                    pw, pwt = proj_fm(ub, ubt, lambda k, h=h: wq[:, k, 2 * h + 1, :], 'wq', width)
                    rope_store(px, pxt, pw, pwt, qT[:, h, t0:t0 + width], 'qT', r0, SCALE)
            px, pxt = proj_fm(ub, ubt, lambda k: wk[:, k, 0, :], 'wk', width)
            if is_ctx:
                P.op('act', lambda e, px=px, t0=t0, width=width: e.activation(
                    out=kT[:, t0:t0 + width], in_=px[:, 0:width], func=AF.Identity), reads=[pxt], pwrites=['kT'])
            else:
                pw, pwt = proj_fm(ub, ubt, lambda k: wk[:, k, 1, :], 'wk', width)
                rope_store(px, pxt, pw, pwt, kT[:, t0:t0 + width], 'kT', r0, 1.0)
            pp, ppt = r_psA.next()
            nti = width // 128
            for i in range(nti):
                for k in range(KC):
                    P.op('pe', lambda e, k=k, i=i, pp=pp, ub=ub: e.matmul(pp[:, i * 128:(i + 1) * 128], ub[:, k, i * 128:(i + 1) * 128],
                                                                         wv[:, k, :], start=(k == 0), stop=(k == KC - 1)),
                         reads=[ubt, 'wv'], pwrites=[ppt])
            tt0 = t0 // 128
            P.op('act', lambda e, pp=pp, tt0=tt0, nti=nti: e.activation(
                out=vtm[:, tt0:tt0 + nti, :], in_=v3(pp[:, 0:nti * 128], nti, 128), func=AF.Identity),
                reads=[ppt], pwrites=['vtm'])

        if debug:
            P.op('sp', lambda e: e.dma_start(out=dq_d, in_=qT[:]), reads=['qT'], writes=['dq_d'], chan='dq')
            P.op('sp', lambda e: e.dma_start(out=dk_d, in_=kT[:]), reads=['kT'], writes=['dk_d'], chan='dk')
        def attend(h, q0, qw, keytiles):
            po, pot = r_psO.next()
            pr, prt = r_psR.next()
            n = len(keytiles)
            for i, (kt, mi) in enumerate(keytiles):
                pS, pst = r_psA.next()
                P.op('pe', lambda e, pS=pS, kt=kt, mi=mi: e.matmul(pS[:, 0:qw], kT[:, kt * 128:(kt + 1) * 128], qT[:, h, q0:q0 + qw],
                                                                  start=True, stop=(mi is None)),
                     reads=['kT', 'qT'], writes=[pst])
                if mi is not None:
                    P.op('pe', lambda e, pS=pS, mi=mi: e.matmul(pS[:, 0:qw], ident[:, :], masks[:, mi, 0:qw], start=False, stop=True),
                         reads=['ident', 'masks'], writes=[pst])
                pt, ptt = r_pt.next()
                P.op('act', lambda e, pS=pS, pt=pt: e.activation(out=pt[:, 0:qw], in_=pS[:, 0:qw], func=AF.Exp),
                     reads=[pst], writes=[ptt])
                P.op('pe', lambda e, pt=pt, kt=kt, i=i: e.matmul(po[:, 0:qw], vtm[:, kt, :], pt[:, 0:qw], start=(i == 0), stop=(i == n - 1)),
                     reads=[ptt, 'vtm'], writes=[pot])
                P.op('pe', lambda e, pt=pt, i=i: e.matmul(pr[:, 0:qw], onesb[:, :], pt[:, 0:qw], start=(i == 0), stop=(i == n - 1)),
                     reads=[ptt, 'onesb'], writes=[prt])
            rc, rct = r_rc.next()
            P.op('dve', lambda e, rc=rc: e.tensor_scalar(out=rc[:, 0:qw], in0=pr[:, 0:qw], scalar1=esink[:, h:h + 1], scalar2=None, op0=ALU.add),
                 reads=[prt, 'esink'], writes=[rct])
            P.op('dve', lambda e, rc=rc: e.reciprocal(out=rc[:, 0:qw], in_=rc[:, 0:qw]), reads=[rct], writes=[rct])
            P.op('dve', lambda e, rc=rc: e.tensor_tensor(out=yT[:, h, q0:q0 + qw], in0=po[:, 0:qw], in1=rc[:, 0:qw], op=ALU.mult),
                 reads=[pot, rct], pwrites=['yT'])

        for h in range(2):
            for qb in range(NQB):
                kts = []
                for rel in range(-1, 5):
                    kt = qb * 4 + rel
                    if 0 <= kt < SEQ // 128:
                        kts.append((kt, rel + 1))
                kts += [(64, None), (65, None)]
                attend(h, qb * QBW, QBW, kts)
            attend(h, SEQ, CTX, [(64, None), (65, None)])
            P.op('sp', lambda e, h=h: e.dma_start(out=y_d[:, h, :], in_=yT[:, h, :]), reads=['yT'], pwrites=['y_d'], chan=f'y{h}')
        P.finish('sp')
        P.emit()
    return nc


def odd_consts():
    inv = 10000.0 ** (-np.arange(0, 64, 2, dtype=np.float32) / 64.0)
    rtab = np.zeros((128, 2, 128), np.float32)
    for p in range(128):
        f = inv[p % 32]
        sign = -1.0 if (p % 64) < 32 else 1.0
        pos = np.arange(128, dtype=np.float32)
        ang = pos * f
        rtab[p, 0, :] = np.cos(ang)
        rtab[p, 1, :] = sign * np.sin(ang)
    masks = np.zeros((128, 6, QBW), np.float32)
    for mi in range(6):
        rel = mi - 1
        kpos = rel * 128 + np.arange(128)[:, None]
        qpos = np.arange(QBW)[None, :]
        masks[:, mi, :] = np.where(np.abs(kpos - qpos) <= 128, 0.0, NEG)
    ident = np.eye(128, dtype=np.float32)
    return rtab, masks.astype(NPBF), ident.astype(NPBF)


def swap_perm():
    p = np.arange(128)
    return np.where((p % 64) < 32, p + 32, p - 32)


def lay_w(w):
    n = w.shape[1]
    return np.ascontiguousarray(w.reshape(KC, 128, n).transpose(1, 0, 2))


def run_odd(uT_all, w_qkv, sink, debug=False):
    nc = build_odd_prog(debug)
    rtab, masks, ident = odd_consts()
    perm = swap_perm()
    in_maps = []
    for j in range(NCORE):
        kv = j // 2
        wq = []
        for h in (2 * j, 2 * j + 1):
            cols = w_qkv[:, h * 128:(h + 1) * 128]
            wq += [cols, cols[:, perm]]
        wq = np.stack([lay_w(c) for c in wq], axis=2)
        kc = w_qkv[:, D + kv * 128: D + (kv + 1) * 128]
        wk = np.stack([lay_w(kc), lay_w(kc[:, perm])], axis=2)
        wv = lay_w(w_qkv[:, D + 512 + kv * 128: D + 512 + (kv + 1) * 128])
        sk = np.broadcast_to(sink[2 * j:2 * j + 2][None, :], (128, 2)).astype(np.float32)
        in_maps.append({"uT": uT_all, "wq": np.ascontiguousarray(wq), "wk": np.ascontiguousarray(wk), "wv": wv,
                        "rtab": rtab, "masks": masks, "ident": ident, "sink": np.ascontiguousarray(sk)})
    res = run_bass_kernel_spmd(nc, in_maps, core_ids=list(range(NCORE)))
    yT = np.zeros((128, KC, TALL), NPBF)
    for j in range(NCORE):
        yT[:, 2 * j:2 * j + 2, :] = res.results[j]["y_out"]
    if debug:
        return yT, res.results
    return yT


NNB = 20


def na_bias_mats(rpb_h):
    def mat(R, kt):
        kr = 2 * kt + np.arange(2)[:, None, None, None]
        kc = np.arange(64)[None, :, None, None]
        r = R + np.arange(8)[None, None, :, None]
        qc = np.arange(64)[None, None, None, :]
        r0 = np.clip(r - 4, 0, 120)
        c0 = np.clip(qc - 8, 0, 48)
        valid = (kr >= r0) & (kr < r0 + 8) & (kc >= c0) & (kc < c0 + 16)
        ri = np.clip(kr - r + 7, 0, 14)
        ci = np.clip(kc - qc + 15, 0, 30)
        ri, ci, valid = np.broadcast_arrays(ri, ci, valid)
        return np.where(valid, rpb_h[ri, ci], NEG).reshape(128, 512)
    mats = [mat(8, 4 + rel) for rel in range(-2, 6)]
    mats += [mat(0, kt) for kt in range(0, 6)]
    mats += [mat(120, kt) for kt in range(58, 64)]
    return np.ascontiguousarray(np.stack(mats, axis=1)).astype(NPBF)


def na_keytiles(qb):
    if qb == 0:
        return [(kt, 8 + kt) for kt in range(0, 6)]
    if qb == NQB - 1:
        return [(kt, 14 + kt - 58) for kt in range(58, 64)]
    return [(4 * qb + rel, rel + 2) for rel in range(-2, 6)]


def build_na_prog():
    nc = bass.Bass("TRN2", target_bir_lowering=False)
    uT_d = nc.dram_tensor("uT", [128, KC, TALL], BF16, kind="ExternalInput").ap()
    w_d = nc.dram_tensor("w3", [128, KC, 3, 128], F32, kind="ExternalInput").ap()
    bias_d = nc.dram_tensor("nabias", [128, NNB, QBW], BF16, kind="ExternalInput").ap()
    ident_d = nc.dram_tensor("ident", [128, 128], BF16, kind="ExternalInput").ap()
    y_d = nc.dram_tensor("y_out", [128, TALL], BF16, kind="ExternalOutput").ap()
    SCALE = 128.0 ** -0.5
    with ExitStack() as es:
        P = Prog(nc, es)
        sb = lambda name, shape, dt: es.enter_context(nc.sbuf_tensor(name, shape, dt))
        w3 = sb("w3_sb", [128, KC, 3, 128], BF16)
        biasm = sb("bias_sb", [128, NNB, QBW], BF16)
        ident = sb("ident_sb", [128, 128], BF16)
        onesb = sb("onesb", [128, 128], BF16)
        qT = sb("qT", [128, TALL], BF16)
        kT = sb("kT", [128, TALL], BF16)
        vtm = sb("vtm", [128, NTL, 128], BF16)
        yT = sb("yT", [128, TALL], BF16)
        ubs = [sb(f"ub{i}", [128, KC, QBW], BF16) for i in range(2)]
        pts = [sb(f"pt{i}", [128, QBW], BF16) for i in range(3)]
        rcs = [sb(f"rc{i}", [128, QBW], F32) for i in range(2)]
        ps = [es.enter_context(nc.psum_tensor(f"ps{i}", [128, 512], F32)) for i in range(8)]
        r_ub = Ring(ubs, "ub"); r_pt = Ring(pts, "pt"); r_rc = Ring(rcs, "rc")
        r_psA = Ring(ps[0:4], "psA"); r_psO = Ring(ps[4:6], "psO"); r_psR = Ring(ps[6:8], "psR")
        P.op('pool', lambda e: e.dma_start(out=w3[:], in_=w_d), writes=['w3'], chan='w3')
        P.op('sp', lambda e: e.dma_start(out=biasm[:], in_=bias_d), writes=['biasm'], chan='biasm')
        P.op('sp', lambda e: e.dma_start(out=ident[:], in_=ident_d), writes=['ident'], chan='ident')
        P.op('pool', lambda e: e.memset(onesb[:], 1.0), writes=['onesb'])
        for bi, (t0, width) in enumerate(token_blocks()):
            ub, ubt = r_ub.next()
            P.op('sp', lambda e, ub=ub, t0=t0, width=width: e.dma_start(out=ub[:, :, 0:width], in_=uT_d[:, :, t0:t0 + width]),
                 writes=[ubt], chan=ubt)
            for g, (dst, dtag, sc) in enumerate(((qT, 'qT', SCALE), (kT, 'kT', 1.0))):
                pp, ppt = r_psA.next()
                for k in range(KC):
                    P.op('pe', lambda e, k=k, pp=pp, g=g, ub=ub, width=width: e.matmul(
                        pp[:, 0:width], w3[:, k, g, :], ub[:, k, 0:width], start=(k == 0), stop=(k == KC - 1)),
                        reads=[ubt, 'w3'], writes=[ppt])
                P.op('act', lambda e, pp=pp, dst=dst, sc=sc, t0=t0, width=width: e.activation(
                    out=dst[:, t0:t0 + width], in_=pp[:, 0:width], func=AF.Identity, scale=sc), reads=[ppt], pwrites=[dtag])
            pp, ppt = r_psA.next()
            nti = width // 128
            for i in range(nti):
                for k in range(KC):
                    P.op('pe', lambda e, k=k, i=i, pp=pp, ub=ub: e.matmul(pp[:, i * 128:(i + 1) * 128], ub[:, k, i * 128:(i + 1) * 128],
                                                                         w3[:, k, 2, :], start=(k == 0), stop=(k == KC - 1)),
                         reads=[ubt, 'w3'], pwrites=[ppt])
            tt0 = t0 // 128
            P.op('dve', lambda e, pp=pp, tt0=tt0, nti=nti: e.tensor_copy(
                out=vtm[:, tt0:tt0 + nti, :], in_=v3(pp[:, 0:nti * 128], nti, 128)), reads=[ppt], pwrites=['vtm'])

        def attend(q0, qw, keytiles):
            po, pot = r_psO.next()
            pr, prt = r_psR.next()
            n = len(keytiles)
            for i, (kt, mi) in enumerate(keytiles):
                pS, pst = r_psA.next()
                P.op('pe', lambda e, pS=pS, kt=kt, mi=mi: e.matmul(pS[:, 0:qw], kT[:, kt * 128:(kt + 1) * 128], qT[:, q0:q0 + qw],
                                                                  start=True, stop=(mi is None)),
                     reads=['kT', 'qT'], writes=[pst])
                if mi is not None:
                    P.op('pe', lambda e, pS=pS, mi=mi: e.matmul(pS[:, 0:qw], ident[:, :], biasm[:, mi, 0:qw], start=False, stop=True),
                         reads=['ident', 'biasm'], writes=[pst])
                pt, ptt = r_pt.next()
                P.op('act', lambda e, pS=pS, pt=pt: e.activation(out=pt[:, 0:qw], in_=pS[:, 0:qw], func=AF.Exp),
                     reads=[pst], writes=[ptt])
                P.op('pe', lambda e, pt=pt, kt=kt, i=i: e.matmul(po[:, 0:qw], vtm[:, kt, :], pt[:, 0:qw], start=(i == 0), stop=(i == n - 1)),
                     reads=[ptt, 'vtm'], writes=[pot])
                P.op('pe', lambda e, pt=pt, i=i: e.matmul(pr[:, 0:qw], onesb[:, :], pt[:, 0:qw], start=(i == 0), stop=(i == n - 1)),
                     reads=[ptt, 'onesb'], writes=[prt])
            rc, rct = r_rc.next()
            P.op('dve', lambda e, rc=rc: e.reciprocal(out=rc[:, 0:qw], in_=pr[:, 0:qw]), reads=[prt], writes=[rct])
            P.op('dve', lambda e, rc=rc: e.tensor_tensor(out=yT[:, q0:q0 + qw], in0=po[:, 0:qw], in1=rc[:, 0:qw], op=ALU.mult),
                 reads=[pot, rct], pwrites=['yT'])

        for qb in range(NQB):
            attend(qb * QBW, QBW, na_keytiles(qb) + [(64, None), (65, None)])
        attend(SEQ, CTX, [(64, None), (65, None)])
        P.op('sp', lambda e: e.dma_start(out=y_d, in_=yT[:]), reads=['yT'], writes=['y_d'], chan='y')
        P.finish('sp')
        P.emit()
    return nc


def run_na(uT_all, w_in, rpb):
    nc = build_na_prog()
    ident = np.eye(128, dtype=np.float32).astype(NPBF)
    in_maps = []
    for j in range(NCORE):
        cols = [w_in[:, g * 1024 + j * 128: g * 1024 + (j + 1) * 128] for g in range(3)]
        w3 = np.stack([lay_w(c) for c in cols], axis=2)
        in_maps.append({"uT": uT_all, "w3": np.ascontiguousarray(w3), "nabias": na_bias_mats(rpb[j]), "ident": ident})
    res = run_bass_kernel_spmd(nc, in_maps, core_ids=list(range(NCORE)))
    return np.stack([res.results[j]["y_out"] for j in range(NCORE)], axis=1)


HC = 64
NCH = TALL // HC
NCTXCH = CTX // HC


def chunk_col(ap2d, nch, idx):
    l = [list(x) for x in ap2d.ap]
    st, n = l[-1]
    assert n == nch * HC
    return bass.AP(ap2d.tensor, ap2d.offset + idx * st, l[:-1] + [[st * HC, nch], [0, HC]])


def chunk_pick(ap2d, nch, idx):
    l = [list(x) for x in ap2d.ap]
    st, n = l[-1]
    return bass.AP(ap2d.tensor, ap2d.offset + idx * st, l[:-1] + [[st * HC, nch]])


def sub_view(ap2d, nch, sub0, nsub, col, bcast):
    l = [list(x) for x in ap2d.ap]
    st, n = l[-1]
    assert n == nch * HC
    if bcast:
        return bass.AP(ap2d.tensor, ap2d.offset + (sub0 * SUB + col) * st, l[:-1] + [[st * HC, nch], [st * SUB, nsub], [0, SUB]])
    return bass.AP(ap2d.tensor, ap2d.offset + sub0 * SUB * st, l[:-1] + [[st * HC, nch], [st * SUB, nsub], [st, SUB]])


SUB = 16
NSUB = HC // SUB
CLAMP = 40.0


def build_hgrn_prog(e_layer, stop=99, nblk=None, limit=None):
    nc = bass.Bass("TRN2", target_bir_lowering=False)
    uT_d = nc.dram_tensor("uT", [128, KC, TALL], BF16, kind="ExternalInput").ap()
    w_d = nc.dram_tensor("w5", [128, KC, 5, 128], F32, kind="ExternalInput").ap()
    lbl_d = nc.dram_tensor("lbl", [128, 2, 2], F32, kind="ExternalInput").ap()
    hgn_d = nc.dram_tensor("hgn", [128, 1], F32, kind="ExternalInput").ap()
    ident_d = nc.dram_tensor("ident", [128, 128], BF16, kind="ExternalInput").ap()
    tri_d = nc.dram_tensor("tri", [128, 4, HC], F32, kind="ExternalInput").ap()
    rmask_d = nc.dram_tensor("rmask", [128, QBW], F32, kind="ExternalInput").ap()
    y_d = nc.dram_tensor("y_out", [128, TALL], BF16, kind="ExternalOutput").ap()
    with ExitStack() as es:
        P = Prog(nc, es)
        sb = lambda name, shape, dt: es.enter_context(nc.sbuf_tensor(name, shape, dt))
        P.limit = limit
        w5 = sb("w5_sb", [128, KC, 5, 128], BF16)
        lbl = sb("lbl_sb", [128, 2, 2], F32)
        lb = sb("lb_sb", [128, 2], F32)
        oml = sb("oml_sb", [128, 2], F32)
        hgn = sb("hgn_sb", [128, 1], F32)
        ident = sb("ident_sb", [128, 128], BF16)
        tri = sb("tri_sb", [128, 4, HC], F32)
        rmask = sb("rmask_sb", [128, QBW], F32)
        ones32 = sb("ones32", [128, 128], F32)
        itm = sb("itm", [128, NTL, 128], BF16)
        Qp = sb("Qp", [128, TALL], BF16)
        attmEO = [sb("attmE", [128, NTL, HC], BF16), sb("attmO", [128, NTL, HC], BF16)]
        dS = sb("dS", [128, NCH, 128], BF16)
        Dall = sb("Dall", [128, NCH], F32)
        oacc = sb("oacc", [128, TALL], F32)
        ub = sb("ub", [128, KC, QBW], BF16)
        T = [sb(f"T{i}", [128, QBW], F32) for i in range(12)]
        KtT = sb("KtT", [128, QBW], BF16)
        Kj = [sb(f"Kj{i}", [128, QBW], BF16) for i in range(3)]
        Kd = sb("Kd", [128, QBW], BF16)
        Qd = sb("Qd", [128, QBW], BF16)
        Qsub = sb("Qsub", [128, QBW], BF16)
        KtmEO = [sb("KtmE", [128, 4, 128], BF16), sb("KtmO", [128, 4, 128], BF16)]
        yb = sb("yb", [128, QBW], BF16)
        ps = [es.enter_context(nc.psum_tensor(f"ps{i}", [128, 512], F32)) for i in range(7)]
        psT = es.enter_context(nc.psum_tensor("psT", [128, 1024], BF16))
        r_psA = Ring(ps[0:3], "psA")
        r_psB = Ring(ps[3:7], "psB")

        P.op('pool', lambda e: e.dma_start(out=w5[:], in_=w_d), writes=['w5'], chan='w5')
        for nm, t_, d_ in (('lbl', lbl, lbl_d), ('hgn', hgn, hgn_d), ('ident', ident, ident_d), ('tri', tri, tri_d), ('rmask', rmask, rmask_d)):
            P.op('sp', lambda e, t_=t_, d_=d_: e.dma_start(out=t_[:], in_=d_), writes=[nm], chan=nm)
        P.op('pool', lambda e: e.memset(ones32[:], 1.0), writes=['ones32'])
        for t_ in KtmEO:
            P.op('pool', lambda e, t_=t_: e.memset(t_[:], 0.0), writes=['Ktm'])
        for t_ in attmEO:
            P.op('pool', lambda e, t_=t_: e.memset(t_[:], 0.0), writes=['attm'])
        P.op('pool', lambda e: e.memset(Qsub[:], 0.0), writes=['Qsub'])
        if e_layer == 1:
            P.op('dve', lambda e: e.tensor_tensor(out=lb[:], in0=lbl[:, :, 1], in1=lbl[:, :, 0], op=ALU.subtract), reads=['lbl'], writes=['lb'])
            P.op('act', lambda e: e.activation(out=lb[:], in_=lb[:], func=AF.Sigmoid), reads=['lb'], writes=['lb'])
            P.op('dve', lambda e: e.tensor_scalar(out=oml[:], in0=lb[:], scalar1=-1.0, scalar2=1.0, op0=ALU.mult, op1=ALU.add),
                 reads=['lb'], writes=['oml'])

        blocks = token_blocks()
        if nblk is not None:
            blocks = blocks[:nblk]

        def load_ub(t0, width):
            P.op('sp', lambda e: e.dma_start(out=ub[:, :, 0:width], in_=uT_d[:, :, t0:t0 + width]), writes=['ub'], chan='ub')

        def proj(g, width):
            pp, ppt = r_psA.next()
            for k in range(KC):
                P.op('pe', lambda e, k=k, pp=pp: e.matmul(pp[:, 0:width], w5[:, k, g, :], ub[:, k, 0:width],
                                                         start=(k == 0), stop=(k == KC - 1)),
                     reads=['ub', 'w5'], writes=[ppt])
            return pp, ppt

        def fslot(ch, d):
            return (ch + NCTXCH) % NCH if d == 0 else ch

        for d in range(2):
            for (t0, w) in blocks:
                nch = w // HC
                nti = w // 128
                ch0 = t0 // HC
                tt0 = t0 // 128
                load_ub(t0, w)
                pq, pqt = proj(0, w)
                pf, pft = proj(1 + d, w)
                if d == 0:
                    pi, pit = r_psB.next()
                    for i in range(nti):
                        for k in range(KC):
                            P.op('pe', lambda e, k=k, i=i, pi=pi: e.matmul(pi[:, i * 128:(i + 1) * 128], ub[:, k, i * 128:(i + 1) * 128],
                                                                         w5[:, k, 3, :], start=(k == 0), stop=(k == KC - 1)),
                                 reads=['ub', 'w5'], pwrites=[pit])
                    P.op('act', lambda e, pi=pi, tt0=tt0, nti=nti: e.activation(
                        out=itm[:, tt0:tt0 + nti, :], in_=v3(pi[:, 0:nti * 128], nti, 128), func=AF.Identity),
                        reads=[pit], pwrites=['itm'])
                P.op('act', lambda e, pf=pf, w=w: e.activation(out=T[0][:, 0:w], in_=pf[:, 0:w], func=AF.Sigmoid), reads=[pft], writes=['T0'])
                if e_layer == 1:
                    P.op('dve', lambda e, w=w, d=d: e.tensor_scalar(out=T[0][:, 0:w], in0=T[0][:, 0:w], scalar1=oml[:, d:d + 1],
                                                                   scalar2=lb[:, d:d + 1], op0=ALU.mult, op1=ALU.add),
                         reads=['T0', 'oml', 'lb'], writes=['T0'])
                P.op('act', lambda e, w=w: e.activation(out=T[1][:, 0:w], in_=T[0][:, 0:w], func=AF.Ln), reads=['T0'], writes=['T1'])
                P.op('dve', lambda e, w=w: e.tensor_scalar(out=T[2][:, 0:w], in0=T[0][:, 0:w], scalar1=-1.0, scalar2=1.0,
                                                          op0=ALU.mult, op1=ALU.add), reads=['T0'], writes=['T2'])
                P.op('dve', lambda e, w=w: e.tensor_tensor_scan(out=T[3][:, 0:w], data0=rmask[:, 0:w], data1=T[1][:, 0:w], initial=0.0,
                                                               op0=ALU.mult, op1=ALU.add), reads=['T1', 'rmask'], writes=['T3'])
                if d == 0:
                    c, ct = T[3], 'T3'
                else:
                    P.op('dve', lambda e, w=w: e.tensor_tensor(out=T[4][:, 0:w], in0=T[1][:, 0:w], in1=T[3][:, 0:w], op=ALU.subtract),
                         reads=['T1', 'T3'], writes=['T4'])
                    P.op('dve', lambda e, w=w, nch=nch: e.tensor_tensor(out=v3(T[4][:, 0:w], nch, HC), in0=v3(T[4][:, 0:w], nch, HC),
                                                                       in1=chunk_col(T[3][:, 0:w], nch, HC - 1), op=ALU.add),
                         reads=['T4', 'T3'], writes=['T4'])
                    c, ct = T[4], 'T4'
                cw = c[:, 0:w]
                P.op('act', lambda e, w=w, cw=cw: e.activation(out=T[5][:, 0:w], in_=cw, func=AF.Exp), reads=[ct], writes=['T5'])
                P.op('dve', lambda e, pq=pq, t0=t0, w=w: e.tensor_tensor(out=Qp[:, t0:t0 + w], in0=pq[:, 0:w], in1=T[5][:, 0:w], op=ALU.mult),
                     reads=[pqt, 'T5'], pwrites=['Qp'])
                P.op('dve', lambda e, w=w, nch=nch, cw=cw: e.tensor_tensor(out=v3(T[6][:, 0:w], nch, HC), in0=chunk_col(T[3][:, 0:w], nch, HC - 1),
                                                                          in1=v3(cw, nch, HC), op=ALU.subtract),
                     reads=[ct, 'T3'], writes=['T6'])
                P.op('act', lambda e, w=w: e.activation(out=T[6][:, 0:w], in_=T[6][:, 0:w], func=AF.Exp), reads=['T6'], writes=['T6'])
                P.op('pool', lambda e, w=w: e.tensor_tensor(out=KtT[:, 0:w], in0=T[2][:, 0:w], in1=T[6][:, 0:w], op=ALU.mult),
                     reads=['T2', 'T6'], writes=['KtT'])
                s0 = fslot(ch0, d)
                P.op('act', lambda e, w=w, nch=nch, s0=s0: e.activation(out=Dall[:, s0:s0 + nch], in_=chunk_pick(T[3][:, 0:w], nch, HC - 1), func=AF.Exp),
                     reads=['T3'], pwrites=['Dall'])
                if d == 0:
                    qs0, bcol = 1, -1
                else:
                    qs0, bcol = 0, SUB
                P.op('dve', lambda e, w=w, nch=nch, cw=cw, qs0=qs0, bcol=bcol: e.tensor_tensor(
                    out=sub_view(T[7][:, 0:w], nch, qs0, 3, 0, False), in0=sub_view(cw, nch, qs0, 3, 0, False),
                    in1=sub_view(cw, nch, qs0, 3, bcol, True), op=ALU.subtract), reads=[ct], writes=['T7'])
                P.op('act', lambda e, w=w, nch=nch, qs0=qs0: e.activation(out=sub_view(T[7][:, 0:w], nch, qs0, 3, 0, False),
                                                                         in_=sub_view(T[7][:, 0:w], nch, qs0, 3, 0, False), func=AF.Exp),
                     reads=['T7'], writes=['T7'])
                P.op('dve', lambda e, w=w, nch=nch, qs0=qs0, pq=pq: e.scalar_tensor_tensor(
                    out=sub_view(Qsub[:, 0:w], nch, qs0, 3, 0, False), in0=sub_view(T[7][:, 0:w], nch, qs0, 3, 0, False), scalar=1.0,
                    in1=sub_view(pq[:, 0:w], nch, qs0, 3, 0, False), op0=ALU.min, op1=ALU.mult), reads=['T7', pqt], writes=['Qsub'])
                for jj in range(3):
                    bj = (jj + 1) * SUB - 1 if d == 0 else (jj + 1) * SUB
                    P.op('dve', lambda e, w=w, nch=nch, cw=cw, bj=bj: e.tensor_tensor(out=v3(T[8][:, 0:w], nch, HC), in0=chunk_col(cw, nch, bj),
                                                                                   in1=v3(cw, nch, HC), op=ALU.subtract),
                         reads=[ct], writes=['T8'])
                    P.op('dve', lambda e, w=w: e.tensor_scalar(out=T[8][:, 0:w], in0=T[8][:, 0:w], scalar1=0.0, scalar2=None, op0=ALU.min),
                         reads=['T8'], writes=['T8'])
                    P.op('act', lambda e, w=w: e.activation(out=T[8][:, 0:w], in_=T[8][:, 0:w], func=AF.Exp), reads=['T8'], writes=['T8'])
                    P.op('pool', lambda e, w=w, jj=jj: e.tensor_tensor(out=Kj[jj][:, 0:w], in0=T[8][:, 0:w], in1=T[2][:, 0:w], op=ALU.mult),
                         reads=['T8', 'T2'], writes=[f'Kj{jj}'])
                nsb = w // SUB
                P.op('dve', lambda e, w=w, nsb=nsb, cw=cw: e.tensor_tensor(
                    out=v3(T[9][:, 0:w], nsb, SUB), in0=v3(cw, nsb, SUB),
                    in1=bass.AP(cw.tensor, cw.offset + SUB // 2, [list(cw.ap[0]), [SUB, nsb], [0, SUB]]), op=ALU.subtract),
                    reads=[ct], writes=['T9'])
                P.op('dve', lambda e, w=w: e.tensor_scalar(out=T[9][:, 0:w], in0=T[9][:, 0:w], scalar1=-CLAMP, scalar2=CLAMP,
                                                          op0=ALU.max, op1=ALU.min), reads=['T9'], writes=['T9'])
                P.op('act', lambda e, w=w: e.activation(out=T[10][:, 0:w], in_=T[9][:, 0:w], func=AF.Exp), reads=['T9'], writes=['T10'])
                P.op('act', lambda e, w=w: e.activation(out=T[11][:, 0:w], in_=T[9][:, 0:w], func=AF.Exp, scale=-1.0), reads=['T9'], writes=['T11'])
                P.op('dve', lambda e, pq=pq, w=w: e.tensor_tensor(out=Qd[:, 0:w], in0=pq[:, 0:w], in1=T[10][:, 0:w], op=ALU.mult),
                     reads=[pqt, 'T10'], writes=['Qd'])
                P.op('pool', lambda e, w=w: e.tensor_tensor(out=Kd[:, 0:w], in0=T[2][:, 0:w], in1=T[11][:, 0:w], op=ALU.mult),
                     reads=['T2', 'T11'], writes=['Kd'])
                pa, pat = r_psB.next()
                for cl in range(nch):
                    i, p0 = cl // 2, (cl % 2) * HC
                    tk = cl * HC
                    P.op('pe', lambda e, pa=pa, i=i, p0=p0, tk=tk: e.matmul(pa[p0:p0 + HC, i * HC:(i + 1) * HC], Kd[:, tk:tk + HC], Qd[:, tk:tk + HC],
                                                                          start=True, stop=True), reads=['Kd', 'Qd'], pwrites=[pat])
                    dummy = 0 if d == 0 else NSUB - 1
                    for sj in range(NSUB):
                        c0 = tk + sj * SUB
                        if sj == dummy:
                            lh, rh, lt, rt = Kd, Qd, 'Kd', 'Qd'
                        else:
                            jj = sj - 1 if d == 0 else sj
                            lh, rh, lt, rt = Kj[jj], Qsub, f'Kj{jj}', 'Qsub'
                        P.op('pe', lambda e, pa=pa, i=i, p0=p0, tk=tk, c0=c0, sj=sj, lh=lh, rh=rh: e.matmul(
                            pa[p0:p0 + HC, 256 + i * HC + sj * SUB:256 + i * HC + (sj + 1) * SUB], lh[:, tk:tk + HC], rh[:, c0:c0 + SUB],
                            start=True, stop=True), reads=[lt, rt], pwrites=[pat])
                P.op('dve', lambda e, pa=pa, nti=nti, d=d: e.tensor_tensor(out=v3(T[0][:, 0:nti * HC], nti, HC), in0=v3(pa[:, 0:nti * HC], nti, HC),
                                                                          in1=bc_mid(tri[:, 2 * d, :], nti), op=ALU.mult),
                     reads=[pat, 'tri'], writes=['T0'])
                P.op('dve', lambda e, pa=pa, nti=nti, d=d: e.tensor_tensor(out=v3(T[1][:, 0:nti * HC], nti, HC), in0=v3(pa[:, 256:256 + nti * HC], nti, HC),
                                                                          in1=bc_mid(tri[:, 2 * d + 1, :], nti), op=ALU.mult),
                     reads=[pat, 'tri'], writes=['T1'])
                for hf in range(2):
                    P.op('pool', lambda e, nti=nti, hf=hf, tt0=tt0: e.tensor_tensor(
                        out=attmEO[hf][hf * HC:(hf + 1) * HC, tt0:tt0 + nti, :], in0=v3(T[0][hf * HC:(hf + 1) * HC, 0:nti * HC], nti, HC),
                        in1=v3(T[1][hf * HC:(hf + 1) * HC, 0:nti * HC], nti, HC), op=ALU.add),
                        reads=['T0', 'T1'], pwrites=['attm'])
                for i in range(nti):
                    P.op('pe', lambda e, i=i: e.transpose(psT[:, i * 128:(i + 1) * 128], KtT[:, i * 128:(i + 1) * 128], ident[:, :]),
                         reads=['KtT', 'ident'], pwrites=['psT'])
                for hf in range(2):
                    P.op('dve', lambda e, nti=nti, hf=hf: e.tensor_copy(out=KtmEO[hf][hf * HC:(hf + 1) * HC, 0:nti, :],
                                                                       in_=v3(psT[hf * HC:(hf + 1) * HC, 0:nti * 128], nti, 128)),
                         reads=['psT'], pwrites=['Ktm'])
                for b0 in range(0, nch, 4):
                    pd, pdt = r_psB.next()
                    for cl in range(b0, b0 + 4):
                        i = cl // 2
                        P.op('pe', lambda e, pd=pd, cl=cl, i=i, b0=b0, tt0=tt0: e.matmul(
                            pd[:, (cl - b0) * 128:(cl - b0 + 1) * 128], KtmEO[cl % 2][:, i, :], itm[:, tt0 + i, :],
                            start=True, stop=True), reads=['Ktm', 'itm'], pwrites=[pdt])
                    P.op('act', lambda e, pd=pd, s0=s0, b0=b0: e.activation(out=dS[:, s0 + b0:s0 + b0 + 4, :], in_=v3(pd[:, 0:512], 4, 128),
                                                                            func=AF.Identity), reads=[pdt], pwrites=['dS'])
            if stop <= 0:
                break
            for ee in range(128):
                if d == 0:
                    a1 = bass.AP(dS, ee, [[NCH * 128, 128], [128, NCH]])
                    a0 = Dall[:, :]
                else:
                    a1 = bass.AP(dS, (NCH - 1) * 128 + ee, [[NCH * 128, 128], [-128, NCH]])
                    a0 = bass.AP(Dall, NCH - 1, [[NCH, 128], [-1, NCH]])
                P.op('dve', lambda e, a1=a1, a0=a0: e.tensor_tensor_scan(out=a1, data0=a0, data1=a1, initial=0.0, op0=ALU.mult, op1=ALU.add),
                     reads=['dS', 'Dall'], pwrites=['dS'])
            if stop <= 1:
                break
            for (t0, w) in blocks:
                nch = w // HC
                ch0 = t0 // HC
                tt0 = t0 // 128
                po, pot = r_psB.next()
                for cl in range(nch):
                    i = cl // 2
                    tk = t0 + cl * HC
                    ch = ch0 + cl
                    if d == 0:
                        s = fslot(ch, 0)
                        prev = s - 1 if s > 0 else None
                    else:
                        prev = ch + 1 if ch < NCH - 1 else None
                    if prev is not None:
                        P.op('pe', lambda e, po=po, cl=cl, prev=prev, tk=tk: e.matmul(po[:, cl * HC:(cl + 1) * HC], dS[:, prev, :], Qp[:, tk:tk + HC],
                                                                                    start=True, stop=False), reads=['dS', 'Qp'], pwrites=[pot])
                    P.op('pe', lambda e, po=po, cl=cl, i=i, tt0=tt0, prev=prev: e.matmul(
                        po[:, cl * HC:(cl + 1) * HC], itm[:, tt0 + i, :], attmEO[cl % 2][:, tt0 + i, :],
                        start=(prev is None), stop=True), reads=['itm', 'attm'], pwrites=[pot])
                if d == 0:
                    P.op('act', lambda e, po=po, t0=t0, w=w: e.activation(out=oacc[:, t0:t0 + w], in_=po[:, 0:w], func=AF.Identity),
                         reads=[pot], pwrites=['oacc'])
                else:
                    P.op('dve', lambda e, po=po, t0=t0, w=w: e.tensor_tensor(out=oacc[:, t0:t0 + w], in0=po[:, 0:w], in1=oacc[:, t0:t0 + w], op=ALU.add),
                         reads=[pot, 'oacc'], pwrites=['oacc'])
        for (t0, w) in (blocks if stop > 2 else []):
            load_ub(t0, w)
            pg, pgt = proj(4, w)
            P.op('act', lambda e, pg=pg, w=w: e.activation(out=T[0][:, 0:w], in_=pg[:, 0:w], func=AF.Silu), reads=[pgt], writes=['T0'])
            P.op('act', lambda e, t0=t0, w=w: e.activation(out=T[1][:, 0:w], in_=oacc[:, t0:t0 + w], func=AF.Square), reads=['oacc'], writes=['T1'])
            pn, pnt = r_psB.next()
            P.op('pe', lambda e, pn=pn, w=w: e.matmul(pn[:, 0:w], ones32[:, :], T[1][:, 0:w], start=True, stop=True), reads=['T1', 'ones32'], writes=[pnt])
            P.op('dve', lambda e, pn=pn, w=w: e.tensor_scalar(out=T[2][:, 0:w], in0=pn[:, 0:w], scalar1=1.0 / 128, scalar2=EPS,
                                                             op0=ALU.mult, op1=ALU.add), reads=[pnt], writes=['T2'])
            P.op('act', lambda e, w=w: e.activation(out=T[2][:, 0:w], in_=T[2][:, 0:w], func=AF.Sqrt), reads=['T2'], writes=['T2'])
            P.op('dve', lambda e, w=w: e.reciprocal(out=T[2][:, 0:w], in_=T[2][:, 0:w]), reads=['T2'], writes=['T2'])
            P.op('dve', lambda e, t0=t0, w=w: e.scalar_tensor_tensor(out=T[3][:, 0:w], in0=oacc[:, t0:t0 + w], scalar=hgn[:, 0:1], in1=T[2][:, 0:w],
                                                                    op0=ALU.mult, op1=ALU.mult), reads=['oacc', 'T2', 'hgn'], writes=['T3'])
            P.op('pool', lambda e, w=w: e.tensor_tensor(out=yb[:, 0:w], in0=T[3][:, 0:w], in1=T[0][:, 0:w], op=ALU.mult),
                 reads=['T3', 'T0'], writes=['yb'])
            P.op('sp', lambda e, t0=t0, w=w: e.dma_start(out=y_d[:, t0:t0 + w], in_=yb[:, 0:w]), reads=['yb'], pwrites=['y_d'], chan='yb')
        P.finish('sp')
        P.emit()
    return nc


def hgrn_consts():
    p = np.arange(128)[:, None] % HC
    t = np.arange(HC)[None, :]
    same = (p // SUB) == (t // SUB)
    tri = np.stack([same & (p <= t), (p // SUB) < (t // SUB), same & (p >= t), (p // SUB) > (t // SUB)], axis=1).astype(np.float32)
    rmask = np.broadcast_to((np.arange(QBW) % HC != 0).astype(np.float32)[None, :], (128, QBW))
    return np.ascontiguousarray(tri), np.ascontiguousarray(rmask)


def run_hgrn(uT_all, w_in, lbl, hgn, e_layer, stop=99, ncores=NCORE):
    nc = build_hgrn_prog(e_layer, stop)
    ident = np.eye(128, dtype=np.float32).astype(NPBF)
    tri, rmask = hgrn_consts()
    in_maps = []
    for j in range(NCORE):
        cols = [w_in[:, (3 + g) * 1024 + j * 128: (3 + g) * 1024 + (j + 1) * 128] for g in range(5)]
        w5 = np.stack([lay_w(c) for c in cols], axis=2)
        lj = np.ascontiguousarray(lbl[:, :, j * 128:(j + 1) * 128].transpose(2, 0, 1)).astype(np.float32)
        hj = np.ascontiguousarray(hgn[j * 128:(j + 1) * 128].reshape(128, 1)).astype(np.float32)
        in_maps.append({"uT": uT_all, "w5": np.ascontiguousarray(w5), "lbl": lj, "hgn": hj, "ident": ident, "tri": tri, "rmask": rmask})
    in_maps = in_maps[:ncores]
    res = run_bass_kernel_spmd(nc, in_maps, core_ids=list(range(ncores)))
    return np.stack([res.results[j]["y_out"] for j in range(ncores)], axis=1)


def _fm(x):
    T = x.shape[0]
    return np.ascontiguousarray(x.T.reshape(KC, 128, T).transpose(1, 0, 2))


def _unfm(a):
    T = a.shape[2]
    return np.ascontiguousarray(a.transpose(1, 0, 2).reshape(D, T).T)


def _vec(v):
    return np.ascontiguousarray(v.reshape(KC, 128).T)


def _lay_win(w):
    g = w[:, :DFF].reshape(KC, 128, JC, 128)
    u = w[:, DFF:].reshape(KC, 128, JC, 128)
    return np.ascontiguousarray(np.concatenate([g.transpose(2, 1, 0, 3), u.transpose(2, 1, 0, 3)], axis=3))


def _lay_wout(w):
    return np.ascontiguousarray(w.reshape(JC, 128, KC, 128).transpose(2, 1, 0, 3))


def _lay_wo(w):
    return np.ascontiguousarray(w.reshape(KC, 128, KC, 128).transpose(2, 1, 0, 3))


def _run_token(stages_spec, h_list, y_all, vec_list, weights, want_u, final):
    stages = [(s, i) for i, s in enumerate(stages_spec)]
    nc = build_token_prog(stages, want_u=want_u, final=final)
    vecs = np.ascontiguousarray(np.stack([_vec(v) for v in vec_list], axis=1)).astype(np.float32)
    shared = {"vecs": vecs}
    for si, s in enumerate(stages_spec):
        if s == 'proj':
            shared[f"wo{si}"] = _lay_wo(weights[si])
        else:
            shared[f"wi{si}"] = _lay_win(weights[si][0])
            shared[f"wf{si}"] = _lay_wout(weights[si][1])
    in_maps = []
    for r in range(NCORE):
        m = dict(shared)
        m["h_in"] = h_list[r]
        if y_all is not None:
            m["y_in"] = np.ascontiguousarray(np.concatenate(
                [y_all[:, :, r * TLAT:(r + 1) * TLAT], y_all[:, :, SEQ + r * TCTX: SEQ + (r + 1) * TCTX]], axis=2))
        in_maps.append(m)
    res = run_bass_kernel_spmd(nc, in_maps, core_ids=list(range(NCORE)))
    return res.results


def _gather_u(results):
    u_all = np.zeros((128, KC, TALL), NPBF)
    for r in range(NCORE):
        uo = results[r]["u_out"]
        u_all[:, :, r * TLAT:(r + 1) * TLAT] = uo[:, :, :TLAT]
        u_all[:, :, SEQ + r * TCTX: SEQ + (r + 1) * TCTX] = uo[:, :, TLAT:]
    return u_all


def kernel(x, c, ctx, c_ctx, w_mod, b_mod, norm_g, w_ff_in, w_ff_out, w_in_even, w_out_even,
           na_rpb, hg_lb_logits, hg_norm_g, w_qkv_odd, w_o_odd, sink_odd, final_norm_g):
    f = lambda a: np.asarray(a, dtype=np.float32)
    x, c, ctx, c_ctx, w_mod, b_mod, norm_g = f(x), f(c), f(ctx), f(c_ctx), f(w_mod), f(b_mod), f(norm_g)
    w_ff_in, w_ff_out, w_in_even, w_out_even = f(w_ff_in), f(w_ff_out), f(w_in_even), f(w_out_even)
    na_rpb, hg_lb_logits, hg_norm_g = f(na_rpb), f(hg_lb_logits), f(hg_norm_g)
    w_qkv_odd, w_o_odd, sink_odd, final_norm_g = f(w_qkv_odd), f(w_o_odd), f(sink_odd), f(final_norm_g)

    mod = run_mod(c[0], c_ctx, w_mod, b_mod).reshape(DEPTH, 2, 3, 3, D)

    def ffn_vecs(l, sub):
        return [norm_g[l, sub], mod[l, 0, sub, 0], mod[l, 0, sub, 1], mod[l, 0, sub, 2],
                mod[l, 1, sub, 0], mod[l, 1, sub, 1], mod[l, 1, sub, 2]]

    def u_vecs(l):
        return [norm_g[l, 1], mod[l, 0, 1, 0], mod[l, 0, 1, 1], mod[l, 1, 1, 0], mod[l, 1, 1, 1]]

    h_list = []
    for r in range(NCORE):
        tok = np.concatenate([x[0, r * TLAT:(r + 1) * TLAT], ctx[0, r * TCTX:(r + 1) * TCTX]], axis=0)
        h_list.append(_fm(tok))

    res = _run_token(['ffn'], h_list, None, ffn_vecs(0, 0) + u_vecs(0), [(w_ff_in[0, 0], w_ff_out[0, 0])], True, False)
    out = None
    for l in range(DEPTH):
        h_list = [res[r]["h_out"] for r in range(NCORE)]
        u_all = _gather_u(res)
        if l % 2 == 0:
            e = l // 2
            ya = run_na(u_all, w_in_even[e], na_rpb[e])
            yb = run_hgrn(u_all, w_in_even[e], hg_lb_logits, hg_norm_g[e], e)
            y_all = np.ascontiguousarray(np.concatenate([ya, yb], axis=1))
            w_o = w_out_even[e]
        else:
            o = l // 2
            y_all = run_odd(u_all, w_qkv_odd[o], sink_odd[o])
            w_o = w_o_odd[o]
        gate_vecs = [mod[l, 0, 1, 2], mod[l, 1, 1, 2]]
        if l < DEPTH - 1:
            res = _run_token(['proj', 'ffn', 'ffn'], h_list, y_all,
                             gate_vecs + ffn_vecs(l, 2) + ffn_vecs(l + 1, 0) + u_vecs(l + 1),
                             [w_o, (w_ff_in[l, 1], w_ff_out[l, 1]), (w_ff_in[l + 1, 0], w_ff_out[l + 1, 0])], True, False)
        else:
            res = _run_token(['proj', 'ffn'], h_list, y_all, gate_vecs + ffn_vecs(l, 2) + [final_norm_g],
                             [w_o, (w_ff_in[l, 1], w_ff_out[l, 1])], False, True)
            out = np.zeros((1, SEQ, D), np.float32)
            for r in range(NCORE):
                out[0, r * TLAT:(r + 1) * TLAT] = _unfm(res[r]["o_out"])[:TLAT]
    return out
```

```python
import numpy as np
import ml_dtypes
from contextlib import ExitStack
import concourse.bass as bass
import concourse.mybir as mybir
from concourse.bass_utils import run_bass_kernel_spmd

F32 = mybir.dt.float32
BF16 = mybir.dt.bfloat16
AF = mybir.ActivationFunctionType
ALU = mybir.AluOpType
NPBF = ml_dtypes.bfloat16

D = 2048
KC = 16
DFF = 5632
JC = 44
NCORE = 8
SEQ = 8192
CTX = 256
TLAT = SEQ // NCORE
TCTX = CTX // NCORE
TT = TLAT + TCTX
TW = 352
NT = 3
EPS = 1e-6
DEPTH = 4


SAME_ENGINE_SYNC = False


class Prog:
    ENGS = {'pe': 'tensor', 'act': 'scalar', 'dve': 'vector', 'pool': 'gpsimd', 'sp': 'sync'}

    def __init__(self, nc, es):
        self.nc = nc
        self.es = es
        self.q = {e: [] for e in self.ENGS}
        self.sems = {}
        self.val = {}
        self.w = {}
        self.r = {}
        self.waited = {}
        self.limit = None
        self.nops = 0

    def sem(self, key):
        if key not in self.sems:
            self.sems[key] = self.es.enter_context(self.nc.semaphore("s_" + key))
            self.val[key] = 0
        return self.sems[key]

    def op(self, eng, fn, reads=(), writes=(), pwrites=(), chan=None):
        self.nops += 1
        if self.limit is not None and self.nops > self.limit:
            return None
        waits = {}

        def need(evs):
            for k, (v, e) in evs.items():
                if e == 'pe' and eng == 'pe':
                    continue
                if e == eng and not SAME_ENGINE_SYNC:
                    continue
                if waits.get(k, 0) < v:
                    waits[k] = v
        for t in reads:
            need(self.w.get(t, {}))
        for t in list(writes) + list(pwrites):
            need(self.w.get(t, {}))
            need(self.r.get(t, {}))
        wl = []
        for k, v in waits.items():
            if self.waited.get((eng, k), 0) >= v:
                continue
            self.waited[(eng, k)] = v
            wl.append((k, v))
        if chan is None:
            key, inc, ee = eng, 1, eng
        else:
            key, inc, ee = 'd_' + chan, 16, 'dma'
        self.sem(key)
        self.val[key] += inc
        v = self.val[key]
        for t in reads:
            self.r.setdefault(t, {})[key] = (v, ee)
        for t in writes:
            self.w[t] = {key: (v, ee)}
            self.r[t] = {}
        for t in pwrites:
            self.w.setdefault(t, {})[key] = (v, ee)
            self.r[t] = {}
        self.q[eng].append((wl, fn, key, inc))
        return (key, v)

    def finish(self, eng='sp'):
        wl = [(k, v) for k, v in self.val.items() if k.startswith('d_')]
        self.q[eng].append((wl, None, None, 0))

    def emit(self):
        nc = self.nc
        with nc.Block() as block:
            for e, attr in self.ENGS.items():
                items = self.q[e]

                def body(engine, items=items):
                    for wl, fn, key, inc in items:
                        for k, v in wl:
                            engine.wait_ge(self.sems[k], v)
                        if fn is not None:
                            ins = fn(engine)
                            ins.then_inc(self.sems[key], inc)
                getattr(block, attr)(body)


def bc_mid(ap, n):
    a = [list(x) for x in ap.ap]
    return bass.AP(ap.tensor, ap.offset, [a[0], [0, n]] + a[1:])


def bc_last(ap, n):
    a = [list(x) for x in ap.ap]
    return bass.AP(ap.tensor, ap.offset, a + [[0, n]])


class Ring:
    def __init__(self, bufs, name):
        self.bufs = bufs
        self.name = name
        self.i = 0

    def next(self):
        k = self.i % len(self.bufs)
        self.i += 1
        return self.bufs[k], f"{self.name}{k}"


def build_token_prog(stages, want_u=False, final=False):
    nc = bass.Bass("TRN2", target_bir_lowering=False)
    nv = 0
    for s in stages:
        nv += 2 if s[0] == 'proj' else 7
    if want_u:
        nv += 5
    if final:
        nv += 1
    h_in = nc.dram_tensor("h_in", [128, KC, TT], F32, kind="ExternalInput").ap()
    vecs_d = nc.dram_tensor("vecs", [128, nv, KC], F32, kind="ExternalInput").ap()
    wd = {}
    y_in = None
    for si, s in enumerate(stages):
        if s[0] == 'proj':
            wd[si] = nc.dram_tensor(f"wo{si}", [KC, 128, KC, 128], F32, kind="ExternalInput").ap()
            y_in = nc.dram_tensor("y_in", [128, KC, TT], BF16, kind="ExternalInput").ap()
        else:
            wd[si] = (nc.dram_tensor(f"wi{si}", [JC, 128, KC, 256], F32, kind="ExternalInput").ap(),
                      nc.dram_tensor(f"wf{si}", [KC, 128, JC, 128], F32, kind="ExternalInput").ap())
    hd = [h_in]
    for si in range(len(stages)):
        last = si == len(stages) - 1
        if last and not final:
            hd.append(nc.dram_tensor("h_out", [128, KC, TT], F32, kind="ExternalOutput").ap())
        else:
            hd.append(nc.dram_tensor(f"h_s{si}", [128, KC, TT], F32).ap())
    u_out = nc.dram_tensor("u_out", [128, KC, TT], BF16, kind="ExternalOutput").ap() if want_u else None
    o_out = nc.dram_tensor("o_out", [128, KC, TT], F32, kind="ExternalOutput").ap() if final else None

    tiles = [(i * TW, TW) for i in range(NT)]
    segs = []
    for (t0, w) in tiles:
        sg = []
        lat_end = min(max(TLAT - t0, 0), w)
        if lat_end > 0:
            sg.append((0, lat_end, 0))
        if lat_end < w:
            sg.append((lat_end, w, 1))
        segs.append(sg)

    with ExitStack() as es:
        P = Prog(nc, es)
        sb = lambda name, shape, dt: es.enter_context(nc.sbuf_tensor(name, shape, dt))
        vecs = sb("vecs_sb", [128, nv, KC], F32)
        dvec = sb("dvec", [128, 8, KC], F32)
        ones = sb("ones", [128, 128], F32)
        uT = sb("uT", [128, KC, TT], BF16)
        big = sb("big", [128, JC, TT], BF16)
        rstd = sb("rstd", [128, TW], F32)
        tmpn = [sb(f"tmpn{i}", [128, TW], F32) for i in range(6)]
        wib = [sb(f"wib{i}", [128, KC, 256], BF16) for i in range(3)]
        wob = [sb(f"wob{i}", [128, JC, 128], BF16) for i in range(2)]
        sgb = [sb(f"sgb{i}", [128, TW], F32) for i in range(3)]
        hres = [sb(f"hres{i}", [128, TW], F32) for i in range(8)]
        hout = [sb(f"hout{i}", [128, TW], F32) for i in range(3)]
        ps = [es.enter_context(nc.psum_tensor(f"ps{i}", [128, 512], F32)) for i in range(8)]
        r_tmpn = Ring(tmpn, "tmpn")
        r_wib = Ring(wib, "wib")
        r_wob = Ring(wob, "wob")
        r_sgb = Ring(sgb, "sgb")
        r_hres = Ring(hres, "hres")
        r_hout = Ring(hout, "hout")
        r_psA = Ring(ps[0:5], "ps")
        psn = ps[5:8]

        P.op('sp', lambda e: e.dma_start(out=vecs[:], in_=vecs_d), writes=['vecs'], chan='vecs')
        P.op('pool', lambda e: e.memset(ones[:], 1.0), writes=['ones'])

        def norm_stage(src, vbase, has_mod, dst_dram, presummed=False):
            gi, shl, scl, shc, scc = vbase
            if has_mod:
                for which, sc_i in ((0, scl), (1, scc)):
                    P.op('dve', lambda e, which=which, sc_i=sc_i: e.scalar_tensor_tensor(
                        out=dvec[:, which, :], in0=vecs[:, sc_i, :], scalar=1.0, in1=vecs[:, gi, :],
                        op0=ALU.add, op1=ALU.mult), reads=['vecs'], writes=[f'dvA{which}'])
            for ti, (t0, w) in enumerate(tiles):
                pb, pbt = psn[ti], f'psn{ti}'
                for k in (range(KC) if not presummed else []):
                    hr, hrt = r_hres.next()
                    P.op('sp', lambda e, hr=hr, k=k, t0=t0, w=w: e.dma_start(out=hr[:, 0:w], in_=src[:, k, t0:t0 + w]),
                         reads=[src.tensor.name], writes=[hrt], chan=hrt)
                    sq, sqt = r_sgb.next()
                    P.op('act', lambda e, hr=hr, sq=sq, w=w: e.activation(out=sq[:, 0:w], in_=hr[:, 0:w], func=AF.Square),
                         reads=[hrt], writes=[sqt])
                    P.op('pe', lambda e, k=k, pb=pb, sq=sq, w=w: e.matmul(pb[:, 0:w], ones[:, :], sq[:, 0:w],
                                                                        start=(k == 0), stop=(k == KC - 1)),
                         reads=[sqt, 'ones'], writes=[pbt])
                P.op('dve', lambda e, pb=pb, w=w: e.tensor_scalar(out=rstd[:, 0:w], in0=pb[:, 0:w], scalar1=1.0 / D,
                                                                 scalar2=EPS, op0=ALU.mult, op1=ALU.add),
                     reads=[pbt], writes=['rstd'])
                P.op('act', lambda e, w=w: e.activation(out=rstd[:, 0:w], in_=rstd[:, 0:w], func=AF.Sqrt),
                     reads=['rstd'], writes=['rstd'])
                P.op('dve', lambda e, w=w: e.reciprocal(out=rstd[:, 0:w], in_=rstd[:, 0:w]),
                     reads=['rstd'], writes=['rstd'])
                for k in range(KC):
                    hr, hrt = r_hres.next()
                    P.op('sp', lambda e, hr=hr, k=k, t0=t0, w=w: e.dma_start(out=hr[:, 0:w], in_=src[:, k, t0:t0 + w]),
                         reads=[src.tensor.name], writes=[hrt], chan=hrt)
                    if has_mod:
                        tb, tbt = r_tmpn.next()
                        for (c0, c1, which) in segs[ti]:
                            a_ap = dvec[:, which, k:k + 1]
                            b_ap = vecs[:, (shl if which == 0 else shc), k:k + 1]
                            P.op('dve', lambda e, tb=tb, hr=hr, c0=c0, c1=c1, a_ap=a_ap: e.scalar_tensor_tensor(
                                out=tb[:, c0:c1], in0=hr[:, c0:c1], scalar=a_ap, in1=rstd[:, c0:c1],
                                op0=ALU.mult, op1=ALU.mult), reads=[hrt, 'rstd', f'dvA{which}'], pwrites=[tbt])
                            P.op('act', lambda e, tb=tb, c0=c0, c1=c1, b_ap=b_ap, k=k, t0=t0: e.activation(
                                out=uT[:, k, t0 + c0:t0 + c1], in_=tb[:, c0:c1], func=AF.Identity, bias=b_ap, scale=1.0),
                                reads=[tbt, 'vecs'], pwrites=[f'u_t{ti}'])
                    else:
                        ho, hot = r_hout.next()
                        P.op('dve', lambda e, ho=ho, hr=hr, k=k, w=w: e.scalar_tensor_tensor(
                            out=ho[:, 0:w], in0=hr[:, 0:w], scalar=vecs[:, gi, k:k + 1], in1=rstd[:, 0:w],
                            op0=ALU.mult, op1=ALU.mult), reads=[hrt, 'rstd', 'vecs'], writes=[hot])
                        P.op('sp', lambda e, ho=ho, k=k, t0=t0, w=w: e.dma_start(out=dst_dram[:, k, t0:t0 + w], in_=ho[:, 0:w]),
                             reads=[hot], pwrites=[dst_dram.tensor.name], chan='st_' + hot)
                if has_mod and dst_dram is not None:
                    P.op('sp', lambda e, t0=t0, w=w: e.dma_start(out=dst_dram[:, :, t0:t0 + w], in_=uT[:, :, t0:t0 + w]),
                         reads=[f'u_t{ti}'], pwrites=[dst_dram.tensor.name], chan=f'ust{ti}')

        def ffn1_stage(w_in_d):
            for j in range(JC):
                wb, wbt = r_wib.next()
                P.op('pool', lambda e, j=j, wb=wb: e.dma_start(out=wb[:], in_=w_in_d[j]), writes=[wbt], chan=wbt)
                for ti, (t0, w) in enumerate(tiles):
                    pg, pgt = r_psA.next()
                    pu, put = r_psA.next()
                    for half, (pp, ppt) in enumerate(((pg, pgt), (pu, put))):
                        for k in range(KC):
                            P.op('pe', lambda e, k=k, pp=pp, wb=wb, half=half, t0=t0, w=w: e.matmul(
                                pp[:, 0:w], wb[:, k, half * 128:(half + 1) * 128], uT[:, k, t0:t0 + w],
                                start=(k == 0), stop=(k == KC - 1)),
                                reads=[wbt, f'u_t{ti}'], writes=[ppt])
                    sg, sgt = r_sgb.next()
                    P.op('act', lambda e, sg=sg, pg=pg, w=w: e.activation(out=sg[:, 0:w], in_=pg[:, 0:w], func=AF.Silu),
                         reads=[pgt], writes=[sgt])
                    P.op('dve', lambda e, sg=sg, pu=pu, j=j, t0=t0, w=w: e.tensor_tensor(
                        out=big[:, j, t0:t0 + w], in0=sg[:, 0:w], in1=pu[:, 0:w], op=ALU.mult),
                        reads=[sgt, put], pwrites=[f'big_t{ti}'])

        def proj_stage(w_d, kc, src, dst, gl, gc, half):
            mul = 0.5 if half else 1.0
            for which, gi_ in ((0, gl), (1, gc)):
                P.op('dve', lambda e, which=which, gi_=gi_: e.tensor_scalar(
                    out=dvec[:, 2 + which, :], in0=vecs[:, gi_, :], scalar1=mul, scalar2=None, op0=ALU.mult),
                    reads=['vecs'], writes=[f'dvG{which}'])
            deferred = []
            for dc in range(KC):
                wb, wbt = r_wob.next()
                P.op('pool', lambda e, dc=dc, wb=wb: e.dma_start(out=wb[:, 0:kc, :], in_=w_d[dc]), writes=[wbt], chan=wbt)
                for ti, (t0, w) in enumerate(tiles):
                    hr, hrt = r_hres.next()
                    P.op('sp', lambda e, hr=hr, dc=dc, t0=t0, w=w: e.dma_start(out=hr[:, 0:w], in_=src[:, dc, t0:t0 + w]),
                         reads=[src.tensor.name], writes=[hrt], chan=hrt)
                    pp, ppt = r_psA.next()
                    for j in range(kc):
                        P.op('pe', lambda e, j=j, pp=pp, wb=wb, t0=t0, w=w: e.matmul(
                            pp[:, 0:w], wb[:, j, :], big[:, j, t0:t0 + w], start=(j == 0), stop=(j == kc - 1)),
                            reads=[wbt, f'big_t{ti}'], writes=[ppt])
                    while deferred:
                        deferred.pop(0)()
                    ho, hot = r_hout.next()
                    for (c0, c1, which) in segs[ti]:
                        P.op('dve', lambda e, ho=ho, pp=pp, hr=hr, c0=c0, c1=c1, which=which, dc=dc: e.scalar_tensor_tensor(
                            out=ho[:, c0:c1], in0=pp[:, c0:c1], scalar=dvec[:, 2 + which, dc:dc + 1], in1=hr[:, c0:c1],
                            op0=ALU.mult, op1=ALU.add), reads=[ppt, hrt, f'dvG{which}'], pwrites=[hot])
                    P.op('sp', lambda e, ho=ho, dc=dc, t0=t0, w=w: e.dma_start(out=dst[:, dc, t0:t0 + w], in_=ho[:, 0:w]),
                         reads=[hot], pwrites=[dst.tensor.name], chan='st_' + hot)
                    sq, sqt = r_sgb.next()
                    P.op('act', lambda e, ho=ho, sq=sq, w=w: e.activation(out=sq[:, 0:w], in_=ho[:, 0:w], func=AF.Square),
                         reads=[hot], writes=[sqt])

                    def _ssq(dc=dc, ti=ti, sq=sq, sqt=sqt, w=w):
                        P.op('pe', lambda e: e.matmul(psn[ti][:, 0:w], ones[:, :], sq[:, 0:w], start=(dc == 0), stop=(dc == KC - 1)),
                             reads=[sqt, 'ones'], writes=[f'psn{ti}'])
                    deferred.append(_ssq)
            while deferred:
                deferred.pop(0)()

        vb = 0
        for si, s in enumerate(stages):
            src, dst = hd[si], hd[si + 1]
            if s[0] == 'proj':
                for ti, (t0, w) in enumerate(tiles):
                    P.op('sp', lambda e, t0=t0, w=w: e.dma_start(out=big[:, 0:KC, t0:t0 + w], in_=y_in[:, :, t0:t0 + w]),
                         writes=[f'big_t{ti}'], chan=f'yin{ti}')
                proj_stage(wd[si], KC, src, dst, vb, vb + 1, False)
                vb += 2
            else:
                g, shl, scl, gl, shc, scc, gc = range(vb, vb + 7)
                norm_stage(src, (g, shl, scl, shc, scc), True, None, presummed=(si > 0))
                ffn1_stage(wd[si][0])
                proj_stage(wd[si][1], JC, src, dst, gl, gc, True)
                vb += 7
        if want_u:
            g, shl, scl, shc, scc = range(vb, vb + 5)
            norm_stage(hd[-1], (g, shl, scl, shc, scc), True, u_out, presummed=True)
            vb += 5
        if final:
            norm_stage(hd[-1], (vb, 0, 0, 0, 0), False, o_out, presummed=True)
            vb += 1
        P.finish('sp')
        P.emit()
    return nc


MODC = 9 * D // NCORE
MCH = 384
NMCH = MODC // MCH


def build_mod_prog():
    nc = bass.Bass("TRN2", target_bir_lowering=False)
    cvec_d = nc.dram_tensor("cvec", [128, KC, 2], F32, kind="ExternalInput").ap()
    w_d = nc.dram_tensor("wm", [DEPTH * NMCH, 128, KC, MCH], F32, kind="ExternalInput").ap()
    b_d = nc.dram_tensor("bm", [2, DEPTH * MODC], F32, kind="ExternalInput").ap()
    o_d = nc.dram_tensor("mod_out", [2, DEPTH * MODC], F32, kind="ExternalOutput").ap()
    with ExitStack() as es:
        P = Prog(nc, es)
        sb = lambda name, shape, dt: es.enter_context(nc.sbuf_tensor(name, shape, dt))
        cv = sb("cv", [128, KC, 2], F32)
        sc = sb("sc", [128, KC, 2], F32)
        bsb = sb("bsb", [2, DEPTH * MODC], F32)
        mo = sb("mo", [2, DEPTH * MODC], F32)
        wbs = [sb(f"wb{i}", [128, KC, MCH], F32) for i in range(3)]
        ps = [es.enter_context(nc.psum_tensor(f"ps{i}", [128, 512], F32)) for i in range(4)]
        r_wb = Ring(wbs, "wb")
        r_ps = Ring(ps, "ps")
        P.op('sp', lambda e: e.dma_start(out=cv[:], in_=cvec_d), writes=['cv'], chan='cv')
        P.op('sp', lambda e: e.dma_start(out=bsb[:], in_=b_d), writes=['bsb'], chan='bsb')
        P.op('act', lambda e: e.activation(out=sc[:], in_=cv[:], func=AF.Silu), reads=['cv'], writes=['sc'])
        for i in range(DEPTH * NMCH):
            wb, wbt = r_wb.next()
            P.op('sp', lambda e, i=i, wb=wb: e.dma_start(out=wb[:], in_=w_d[i]), writes=[wbt], chan=wbt)
            pp, ppt = r_ps.next()
            for k in range(KC):
                P.op('pe', lambda e, k=k, pp=pp, wb=wb: e.matmul(pp[0:2, 0:MCH], sc[:, k, :], wb[:, k, :],
                                                                start=(k == 0), stop=(k == KC - 1)),
                     reads=[wbt, 'sc'], writes=[ppt])
            P.op('dve', lambda e, i=i, pp=pp: e.tensor_tensor(out=mo[:, i * MCH:(i + 1) * MCH], in0=pp[0:2, 0:MCH],
                                                             in1=bsb[:, i * MCH:(i + 1) * MCH], op=ALU.add),
                 reads=[ppt, 'bsb'], pwrites=['mo'])
        P.op('sp', lambda e: e.dma_start(out=o_d, in_=mo[:]), reads=['mo'], writes=['o_d'], chan='o_d')
        P.finish('sp')
        P.emit()
    return nc


def run_mod(c, c_ctx, w_mod, b_mod):
    nc = build_mod_prog()
    cvec = np.ascontiguousarray(np.stack([c.reshape(KC, 128).T, c_ctx.reshape(KC, 128).T], axis=2)).astype(np.float32)
    in_maps = []
    for r in range(NCORE):
        ws = w_mod[:, :, r * MODC:(r + 1) * MODC]
        ws = ws.reshape(DEPTH, KC, 128, NMCH, MCH).transpose(0, 3, 2, 1, 4).reshape(DEPTH * NMCH, 128, KC, MCH)
        bs = b_mod[:, r * MODC:(r + 1) * MODC].reshape(1, DEPTH * MODC)
        in_maps.append({"cvec": cvec, "wm": np.ascontiguousarray(ws),
                        "bm": np.ascontiguousarray(np.concatenate([bs, bs], axis=0))})
    res = run_bass_kernel_spmd(nc, in_maps, core_ids=list(range(NCORE)))
    mod = np.zeros((DEPTH, 2, 9 * D), np.float32)
    for r in range(NCORE):
        o = res.results[r]["mod_out"].reshape(2, DEPTH, MODC)
        mod[:, :, r * MODC:(r + 1) * MODC] = o.transpose(1, 0, 2)
    return mod


TALL = SEQ + CTX
NTL = TALL // 128
QBW = 512
NQB = SEQ // QBW
NEG = -30000.0


def v3(ap, a, b):
    l = [list(x) for x in ap.ap]
    st, n = l[-1]
    assert n == a * b, (n, a, b)
    return bass.AP(ap.tensor, ap.offset, l[:-1] + [[st * b, a], [st, b]])


def token_blocks():
    return [(i * QBW, QBW) for i in range(NQB)] + [(SEQ, CTX)]


def build_odd_prog(debug=False):
    nc = bass.Bass("TRN2", target_bir_lowering=False)
    dq_d = nc.dram_tensor("dq", [128, 2, TALL], BF16, kind="ExternalOutput").ap() if debug else None
    dk_d = nc.dram_tensor("dk", [128, TALL], BF16, kind="ExternalOutput").ap() if debug else None
    uT_d = nc.dram_tensor("uT", [128, KC, TALL], BF16, kind="ExternalInput").ap()
    wq_d = nc.dram_tensor("wq", [128, KC, 4, 128], F32, kind="ExternalInput").ap()
    wk_d = nc.dram_tensor("wk", [128, KC, 2, 128], F32, kind="ExternalInput").ap()
    wv_d = nc.dram_tensor("wv", [128, KC, 128], F32, kind="ExternalInput").ap()
    rtab_d = nc.dram_tensor("rtab", [128, 2, 128], F32, kind="ExternalInput").ap()
    mask_d = nc.dram_tensor("masks", [128, 6, QBW], BF16, kind="ExternalInput").ap()
    ident_d = nc.dram_tensor("ident", [128, 128], BF16, kind="ExternalInput").ap()
    sink_d = nc.dram_tensor("sink", [128, 2], F32, kind="ExternalInput").ap()
    y_d = nc.dram_tensor("y_out", [128, 2, TALL], BF16, kind="ExternalOutput").ap()
    SCALE = 128.0 ** -0.5
    with ExitStack() as es:
        P = Prog(nc, es)
        sb = lambda name, shape, dt: es.enter_context(nc.sbuf_tensor(name, shape, dt))
        wq = sb("wq_sb", [128, KC, 4, 128], BF16)
        wk = sb("wk_sb", [128, KC, 2, 128], BF16)
        wv = sb("wv_sb", [128, KC, 128], BF16)
        rtab = sb("rtab_sb", [128, 2, 128], F32)
        masks = sb("masks_sb", [128, 6, QBW], BF16)
        ident = sb("ident_sb", [128, 128], BF16)
        onesb = sb("onesb", [128, 128], BF16)
        sink = sb("sink_sb", [128, 2], F32)
        esink = sb("esink", [128, 2], F32)
        qT = sb("qT", [128, 2, TALL], BF16)
        kT = sb("kT", [128, TALL], BF16)
        vtm = sb("vtm", [128, NTL, 128], BF16)
        yT = sb("yT", [128, 2, TALL], BF16)
        ubs = [sb(f"ub{i}", [128, KC, QBW], BF16) for i in range(2)]
        t1s = [sb(f"t1_{i}", [128, QBW], F32) for i in range(2)]
        t2s = [sb(f"t2_{i}", [128, QBW], F32) for i in range(2)]
        pts = [sb(f"pt{i}", [128, QBW], BF16) for i in range(3)]
        rcs = [sb(f"rc{i}", [128, QBW], F32) for i in range(2)]
        ps = [es.enter_context(nc.psum_tensor(f"ps{i}", [128, 512], F32)) for i in range(8)]
        r_ub = Ring(ubs, "ub"); r_t1 = Ring(t1s, "t1"); r_t2 = Ring(t2s, "t2"); r_pt = Ring(pts, "pt"); r_rc = Ring(rcs, "rc")
        r_psA = Ring(ps[0:4], "psA")
        r_psO = Ring(ps[4:6], "psO")
        r_psR = Ring(ps[6:8], "psR")

        P.op('pool', lambda e: e.dma_start(out=wq[:], in_=wq_d), writes=['wq'], chan='wq')
        P.op('pool', lambda e: e.dma_start(out=wk[:], in_=wk_d), writes=['wk'], chan='wk')
        P.op('pool', lambda e: e.dma_start(out=wv[:], in_=wv_d), writes=['wv'], chan='wv')
        P.op('sp', lambda e: e.dma_start(out=rtab[:], in_=rtab_d), writes=['rtab'], chan='rtab')
        P.op('sp', lambda e: e.dma_start(out=masks[:], in_=mask_d), writes=['masks'], chan='masks')
        P.op('sp', lambda e: e.dma_start(out=ident[:], in_=ident_d), writes=['ident'], chan='ident')
        P.op('sp', lambda e: e.dma_start(out=sink[:], in_=sink_d), writes=['sink'], chan='sink')
        P.op('pool', lambda e: e.memset(onesb[:], 1.0), writes=['onesb'])
        P.op('act', lambda e: e.activation(out=esink[:], in_=sink[:], func=AF.Exp), reads=['sink'], writes=['esink'])

        def proj_fm(ub, ubt, w_ap, wtag, width):
            pp, ppt = r_psA.next()
            for k in range(KC):
                P.op('pe', lambda e, k=k, pp=pp: e.matmul(pp[:, 0:width], w_ap(k), ub[:, k, 0:width],
                                                         start=(k == 0), stop=(k == KC - 1)),
                     reads=[ubt, wtag], writes=[ppt])
            return pp, ppt

        def rope_store(px, pxt, pw, pwt, dst, dtag, r0, scale):
            t1, t1t = r_t1.next()
            t2, t2t = r_t2.next()
            for half in range(2):
                p0, p1 = half * 64, half * 64 + 64
                if half == 0:
                    ctab = bc_last(rtab[p0:p1, 0, r0:r0 + 8], 64)
                    stab = bc_last(rtab[p0:p1, 1, r0:r0 + 8], 64)
                else:
                    ctab = bc_mid(rtab[p0:p1, 0, 0:64], 8)
                    stab = bc_mid(rtab[p0:p1, 1, 0:64], 8)
                P.op('dve', lambda e, p0=p0, p1=p1, ctab=ctab, t1=t1: e.scalar_tensor_tensor(
                    out=v3(t1[p0:p1, :], 8, 64), in0=v3(px[p0:p1, :], 8, 64), scalar=scale, in1=ctab,
                    op0=ALU.mult, op1=ALU.mult), reads=[pxt, 'rtab'], pwrites=[t1t])
                P.op('dve', lambda e, p0=p0, p1=p1, stab=stab, t2=t2: e.scalar_tensor_tensor(
                    out=v3(t2[p0:p1, :], 8, 64), in0=v3(pw[p0:p1, :], 8, 64), scalar=scale, in1=stab,
                    op0=ALU.mult, op1=ALU.mult), reads=[pwt, 'rtab'], pwrites=[t2t])
            P.op('pool', lambda e, t1=t1, t2=t2: e.tensor_tensor(out=dst, in0=t1[:, :], in1=t2[:, :], op=ALU.add),
                 reads=[t1t, t2t], pwrites=[dtag])

        for bi, (t0, width) in enumerate(token_blocks()):
            ub, ubt = r_ub.next()
            P.op('sp', lambda e, ub=ub, t0=t0, width=width: e.dma_start(out=ub[:, :, 0:width], in_=uT_d[:, :, t0:t0 + width]),
                 writes=[ubt], chan=ubt)
            is_ctx = t0 >= SEQ
            r0 = t0 // 64
            for h in range(2):
                px, pxt = proj_fm(ub, ubt, lambda k, h=h: wq[:, k, 2 * h, :], 'wq', width)
                if is_ctx:
                    P.op('act', lambda e, px=px, h=h, t0=t0, width=width: e.activation(
                        out=qT[:, h, t0:t0 + width], in_=px[:, 0:width], func=AF.Identity, scale=SCALE),
                        reads=[pxt], pwrites=['qT'])
                else:
                    pw, pwt = proj_fm(ub, ubt, lambda k, h=h: wq[:, k, 2 * h + 1, :], 'wq', width)
                    rope_store(px, pxt, pw, pwt, qT[:, h, t0:t0 + width], 'qT', r0, SCALE)
            px, pxt = proj_fm(ub, ubt, lambda k: wk[:, k, 0, :], 'wk', width)
            if is_ctx:
                P.op('act', lambda e, px=px, t0=t0, width=width: e.activation(
                    out=kT[:, t0:t0 + width], in_=px[:, 0:width], func=AF.Identity), reads=[pxt], pwrites=['kT'])
            else:
                pw, pwt = proj_fm(ub, ubt, lambda k: wk[:, k, 1, :], 'wk', width)
                rope_store(px, pxt, pw, pwt, kT[:, t0:t0 + width], 'kT', r0, 1.0)
            pp, ppt = r_psA.next()
            nti = width // 128
            for i in range(nti):
                for k in range(KC):
                    P.op('pe', lambda e, k=k, i=i, pp=pp, ub=ub: e.matmul(pp[:, i * 128:(i + 1) * 128], ub[:, k, i * 128:(i + 1) * 128],
                                                                         wv[:, k, :], start=(k == 0), stop=(k == KC - 1)),
                         reads=[ubt, 'wv'], pwrites=[ppt])
            tt0 = t0 // 128
            P.op('act', lambda e, pp=pp, tt0=tt0, nti=nti: e.activation(
                out=vtm[:, tt0:tt0 + nti, :], in_=v3(pp[:, 0:nti * 128], nti, 128), func=AF.Identity),
                reads=[ppt], pwrites=['vtm'])

        if debug:
            P.op('sp', lambda e: e.dma_start(out=dq_d, in_=qT[:]), reads=['qT'], writes=['dq_d'], chan='dq')
            P.op('sp', lambda e: e.dma_start(out=dk_d, in_=kT[:]), reads=['kT'], writes=['dk_d'], chan='dk')
        def attend(h, q0, qw, keytiles):
            po, pot = r_psO.next()
            pr, prt = r_psR.next()
            n = len(keytiles)
            for i, (kt, mi) in enumerate(keytiles):
                pS, pst = r_psA.next()
                P.op('pe', lambda e, pS=pS, kt=kt, mi=mi: e.matmul(pS[:, 0:qw], kT[:, kt * 128:(kt + 1) * 128], qT[:, h, q0:q0 + qw],
                                                                  start=True, stop=(mi is None)),
                     reads=['kT', 'qT'], writes=[pst])
                if mi is not None:
                    P.op('pe', lambda e, pS=pS, mi=mi: e.matmul(pS[:, 0:qw], ident[:, :], masks[:, mi, 0:qw], start=False, stop=True),
                         reads=['ident', 'masks'], writes=[pst])
                pt, ptt = r_pt.next()
                P.op('act', lambda e, pS=pS, pt=pt: e.activation(out=pt[:, 0:qw], in_=pS[:, 0:qw], func=AF.Exp),
                     reads=[pst], writes=[ptt])
                P.op('pe', lambda e, pt=pt, kt=kt, i=i: e.matmul(po[:, 0:qw], vtm[:, kt, :], pt[:, 0:qw], start=(i == 0), stop=(i == n - 1)),
                     reads=[ptt, 'vtm'], writes=[pot])
                P.op('pe', lambda e, pt=pt, i=i: e.matmul(pr[:, 0:qw], onesb[:, :], pt[:, 0:qw], start=(i == 0), stop=(i == n - 1)),
                     reads=[ptt, 'onesb'], writes=[prt])
            rc, rct = r_rc.next()
            P.op('dve', lambda e, rc=rc: e.tensor_scalar(out=rc[:, 0:qw], in0=pr[:, 0:qw], scalar1=esink[:, h:h + 1], scalar2=None, op0=ALU.add),
                 reads=[prt, 'esink'], writes=[rct])
            P.op('dve', lambda e, rc=rc: e.reciprocal(out=rc[:, 0:qw], in_=rc[:, 0:qw]), reads=[rct], writes=[rct])
            P.op('dve', lambda e, rc=rc: e.tensor_tensor(out=yT[:, h, q0:q0 + qw], in0=po[:, 0:qw], in1=rc[:, 0:qw], op=ALU.mult),
                 reads=[pot, rct], pwrites=['yT'])

        for h in range(2):
            for qb in range(NQB):
                kts = []
                for rel in range(-1, 5):
                    kt = qb * 4 + rel
                    if 0 <= kt < SEQ // 128:
                        kts.append((kt, rel + 1))
                kts += [(64, None), (65, None)]
                attend(h, qb * QBW, QBW, kts)
            attend(h, SEQ, CTX, [(64, None), (65, None)])
            P.op('sp', lambda e, h=h: e.dma_start(out=y_d[:, h, :], in_=yT[:, h, :]), reads=['yT'], pwrites=['y_d'], chan=f'y{h}')
        P.finish('sp')
        P.emit()
    return nc


def odd_consts():
    inv = 10000.0 ** (-np.arange(0, 64, 2, dtype=np.float32) / 64.0)
    rtab = np.zeros((128, 2, 128), np.float32)
    for p in range(128):
        f = inv[p % 32]
        sign = -1.0 if (p % 64) < 32 else 1.0
        pos = np.arange(128, dtype=np.float32)
        ang = pos * f
        rtab[p, 0, :] = np.cos(ang)
        rtab[p, 1, :] = sign * np.sin(ang)
    masks = np.zeros((128, 6, QBW), np.float32)
    for mi in range(6):
        rel = mi - 1
        kpos = rel * 128 + np.arange(128)[:, None]
        qpos = np.arange(QBW)[None, :]
        masks[:, mi, :] = np.where(np.abs(kpos - qpos) <= 128, 0.0, NEG)
    ident = np.eye(128, dtype=np.float32)
    return rtab, masks.astype(NPBF), ident.astype(NPBF)


def swap_perm():
    p = np.arange(128)
    return np.where((p % 64) < 32, p + 32, p - 32)


def lay_w(w):
    n = w.shape[1]
    return np.ascontiguousarray(w.reshape(KC, 128, n).transpose(1, 0, 2))


def run_odd(uT_all, w_qkv, sink, debug=False):
    nc = build_odd_prog(debug)
    rtab, masks, ident = odd_consts()
    perm = swap_perm()
    in_maps = []
    for j in range(NCORE):
        kv = j // 2
        wq = []
        for h in (2 * j, 2 * j + 1):
            cols = w_qkv[:, h * 128:(h + 1) * 128]
            wq += [cols, cols[:, perm]]
        wq = np.stack([lay_w(c) for c in wq], axis=2)
        kc = w_qkv[:, D + kv * 128: D + (kv + 1) * 128]
        wk = np.stack([lay_w(kc), lay_w(kc[:, perm])], axis=2)
        wv = lay_w(w_qkv[:, D + 512 + kv * 128: D + 512 + (kv + 1) * 128])
        sk = np.broadcast_to(sink[2 * j:2 * j + 2][None, :], (128, 2)).astype(np.float32)
        in_maps.append({"uT": uT_all, "wq": np.ascontiguousarray(wq), "wk": np.ascontiguousarray(wk), "wv": wv,
                        "rtab": rtab, "masks": masks, "ident": ident, "sink": np.ascontiguousarray(sk)})
    res = run_bass_kernel_spmd(nc, in_maps, core_ids=list(range(NCORE)))
    yT = np.zeros((128, KC, TALL), NPBF)
    for j in range(NCORE):
        yT[:, 2 * j:2 * j + 2, :] = res.results[j]["y_out"]
    if debug:
        return yT, res.results
    return yT


NNB = 20


def na_bias_mats(rpb_h):
    def mat(R, kt):
        kr = 2 * kt + np.arange(2)[:, None, None, None]
        kc = np.arange(64)[None, :, None, None]
        r = R + np.arange(8)[None, None, :, None]
        qc = np.arange(64)[None, None, None, :]
        r0 = np.clip(r - 4, 0, 120)
        c0 = np.clip(qc - 8, 0, 48)
        valid = (kr >= r0) & (kr < r0 + 8) & (kc >= c0) & (kc < c0 + 16)
        ri = np.clip(kr - r + 7, 0, 14)
        ci = np.clip(kc - qc + 15, 0, 30)
        ri, ci, valid = np.broadcast_arrays(ri, ci, valid)
        return np.where(valid, rpb_h[ri, ci], NEG).reshape(128, 512)
    mats = [mat(8, 4 + rel) for rel in range(-2, 6)]
    mats += [mat(0, kt) for kt in range(0, 6)]
    mats += [mat(120, kt) for kt in range(58, 64)]
    return np.ascontiguousarray(np.stack(mats, axis=1)).astype(NPBF)


def na_keytiles(qb):
    if qb == 0:
        return [(kt, 8 + kt) for kt in range(0, 6)]
    if qb == NQB - 1:
        return [(kt, 14 + kt - 58) for kt in range(58, 64)]
    return [(4 * qb + rel, rel + 2) for rel in range(-2, 6)]


def build_na_prog():
    nc = bass.Bass("TRN2", target_bir_lowering=False)
    uT_d = nc.dram_tensor("uT", [128, KC, TALL], BF16, kind="ExternalInput").ap()
    w_d = nc.dram_tensor("w3", [128, KC, 3, 128], F32, kind="ExternalInput").ap()
    bias_d = nc.dram_tensor("nabias", [128, NNB, QBW], BF16, kind="ExternalInput").ap()
    ident_d = nc.dram_tensor("ident", [128, 128], BF16, kind="ExternalInput").ap()
    y_d = nc.dram_tensor("y_out", [128, TALL], BF16, kind="ExternalOutput").ap()
    SCALE = 128.0 ** -0.5
    with ExitStack() as es:
        P = Prog(nc, es)
        sb = lambda name, shape, dt: es.enter_context(nc.sbuf_tensor(name, shape, dt))
        w3 = sb("w3_sb", [128, KC, 3, 128], BF16)
        biasm = sb("bias_sb", [128, NNB, QBW], BF16)
        ident = sb("ident_sb", [128, 128], BF16)
        onesb = sb("onesb", [128, 128], BF16)
        qT = sb("qT", [128, TALL], BF16)
        kT = sb("kT", [128, TALL], BF16)
        vtm = sb("vtm", [128, NTL, 128], BF16)
        yT = sb("yT", [128, TALL], BF16)
        ubs = [sb(f"ub{i}", [128, KC, QBW], BF16) for i in range(2)]
        pts = [sb(f"pt{i}", [128, QBW], BF16) for i in range(3)]
        rcs = [sb(f"rc{i}", [128, QBW], F32) for i in range(2)]
        ps = [es.enter_context(nc.psum_tensor(f"ps{i}", [128, 512], F32)) for i in range(8)]
        r_ub = Ring(ubs, "ub"); r_pt = Ring(pts, "pt"); r_rc = Ring(rcs, "rc")
        r_psA = Ring(ps[0:4], "psA"); r_psO = Ring(ps[4:6], "psO"); r_psR = Ring(ps[6:8], "psR")
        P.op('pool', lambda e: e.dma_start(out=w3[:], in_=w_d), writes=['w3'], chan='w3')
        P.op('sp', lambda e: e.dma_start(out=biasm[:], in_=bias_d), writes=['biasm'], chan='biasm')
        P.op('sp', lambda e: e.dma_start(out=ident[:], in_=ident_d), writes=['ident'], chan='ident')
        P.op('pool', lambda e: e.memset(onesb[:], 1.0), writes=['onesb'])
        for bi, (t0, width) in enumerate(token_blocks()):
            ub, ubt = r_ub.next()
            P.op('sp', lambda e, ub=ub, t0=t0, width=width: e.dma_start(out=ub[:, :, 0:width], in_=uT_d[:, :, t0:t0 + width]),
                 writes=[ubt], chan=ubt)
            for g, (dst, dtag, sc) in enumerate(((qT, 'qT', SCALE), (kT, 'kT', 1.0))):
                pp, ppt = r_psA.next()
                for k in range(KC):
                    P.op('pe', lambda e, k=k, pp=pp, g=g, ub=ub, width=width: e.matmul(
                        pp[:, 0:width], w3[:, k, g, :], ub[:, k, 0:width], start=(k == 0), stop=(k == KC - 1)),
                        reads=[ubt, 'w3'], writes=[ppt])
                P.op('act', lambda e, pp=pp, dst=dst, sc=sc, t0=t0, width=width: e.activation(
                    out=dst[:, t0:t0 + width], in_=pp[:, 0:width], func=AF.Identity, scale=sc), reads=[ppt], pwrites=[dtag])
            pp, ppt = r_psA.next()
            nti = width // 128
            for i in range(nti):
                for k in range(KC):
                    P.op('pe', lambda e, k=k, i=i, pp=pp, ub=ub: e.matmul(pp[:, i * 128:(i + 1) * 128], ub[:, k, i * 128:(i + 1) * 128],
                                                                         w3[:, k, 2, :], start=(k == 0), stop=(k == KC - 1)),
                         reads=[ubt, 'w3'], pwrites=[ppt])
            tt0 = t0 // 128
            P.op('dve', lambda e, pp=pp, tt0=tt0, nti=nti: e.tensor_copy(
                out=vtm[:, tt0:tt0 + nti, :], in_=v3(pp[:, 0:nti * 128], nti, 128)), reads=[ppt], pwrites=['vtm'])

        def attend(q0, qw, keytiles):
            po, pot = r_psO.next()
            pr, prt = r_psR.next()
            n = len(keytiles)
            for i, (kt, mi) in enumerate(keytiles):
                pS, pst = r_psA.next()
                P.op('pe', lambda e, pS=pS, kt=kt, mi=mi: e.matmul(pS[:, 0:qw], kT[:, kt * 128:(kt + 1) * 128], qT[:, q0:q0 + qw],
                                                                  start=True, stop=(mi is None)),
                     reads=['kT', 'qT'], writes=[pst])
                if mi is not None:
                    P.op('pe', lambda e, pS=pS, mi=mi: e.matmul(pS[:, 0:qw], ident[:, :], biasm[:, mi, 0:qw], start=False, stop=True),
                         reads=['ident', 'biasm'], writes=[pst])
                pt, ptt = r_pt.next()
                P.op('act', lambda e, pS=pS, pt=pt: e.activation(out=pt[:, 0:qw], in_=pS[:, 0:qw], func=AF.Exp),
                     reads=[pst], writes=[ptt])
                P.op('pe', lambda e, pt=pt, kt=kt, i=i: e.matmul(po[:, 0:qw], vtm[:, kt, :], pt[:, 0:qw], start=(i == 0), stop=(i == n - 1)),
                     reads=[ptt, 'vtm'], writes=[pot])
                P.op('pe', lambda e, pt=pt, i=i: e.matmul(pr[:, 0:qw], onesb[:, :], pt[:, 0:qw], start=(i == 0), stop=(i == n - 1)),
                     reads=[ptt, 'onesb'], writes=[prt])
            rc, rct = r_rc.next()
            P.op('dve', lambda e, rc=rc: e.reciprocal(out=rc[:, 0:qw], in_=pr[:, 0:qw]), reads=[prt], writes=[rct])
            P.op('dve', lambda e, rc=rc: e.tensor_tensor(out=yT[:, q0:q0 + qw], in0=po[:, 0:qw], in1=rc[:, 0:qw], op=ALU.mult),
                 reads=[pot, rct], pwrites=['yT'])

        for qb in range(NQB):
            attend(qb * QBW, QBW, na_keytiles(qb) + [(64, None), (65, None)])
        attend(SEQ, CTX, [(64, None), (65, None)])
        P.op('sp', lambda e: e.dma_start(out=y_d, in_=yT[:]), reads=['yT'], writes=['y_d'], chan='y')
        P.finish('sp')
        P.emit()
    return nc


def run_na(uT_all, w_in, rpb):
    nc = build_na_prog()
    ident = np.eye(128, dtype=np.float32).astype(NPBF)
    in_maps = []
    for j in range(NCORE):
        cols = [w_in[:, g * 1024 + j * 128: g * 1024 + (j + 1) * 128] for g in range(3)]
        w3 = np.stack([lay_w(c) for c in cols], axis=2)
        in_maps.append({"uT": uT_all, "w3": np.ascontiguousarray(w3), "nabias": na_bias_mats(rpb[j]), "ident": ident})
    res = run_bass_kernel_spmd(nc, in_maps, core_ids=list(range(NCORE)))
    return np.stack([res.results[j]["y_out"] for j in range(NCORE)], axis=1)


HC = 64
NCH = TALL // HC
NCTXCH = CTX // HC


def chunk_col(ap2d, nch, idx):
    l = [list(x) for x in ap2d.ap]
    st, n = l[-1]
    assert n == nch * HC
    return bass.AP(ap2d.tensor, ap2d.offset + idx * st, l[:-1] + [[st * HC, nch], [0, HC]])


def chunk_pick(ap2d, nch, idx):
    l = [list(x) for x in ap2d.ap]
    st, n = l[-1]
    return bass.AP(ap2d.tensor, ap2d.offset + idx * st, l[:-1] + [[st * HC, nch]])


def sub_view(ap2d, nch, sub0, nsub, col, bcast):
    l = [list(x) for x in ap2d.ap]
    st, n = l[-1]
    assert n == nch * HC
    if bcast:
        return bass.AP(ap2d.tensor, ap2d.offset + (sub0 * SUB + col) * st, l[:-1] + [[st * HC, nch], [st * SUB, nsub], [0, SUB]])
    return bass.AP(ap2d.tensor, ap2d.offset + sub0 * SUB * st, l[:-1] + [[st * HC, nch], [st * SUB, nsub], [st, SUB]])


SUB = 16
NSUB = HC // SUB
CLAMP = 40.0


def build_hgrn_prog(e_layer, stop=99, nblk=None, limit=None):
    nc = bass.Bass("TRN2", target_bir_lowering=False)
    uT_d = nc.dram_tensor("uT", [128, KC, TALL], BF16, kind="ExternalInput").ap()
    w_d = nc.dram_tensor("w5", [128, KC, 5, 128], F32, kind="ExternalInput").ap()
    lbl_d = nc.dram_tensor("lbl", [128, 2, 2], F32, kind="ExternalInput").ap()
    hgn_d = nc.dram_tensor("hgn", [128, 1], F32, kind="ExternalInput").ap()
    ident_d = nc.dram_tensor("ident", [128, 128], BF16, kind="ExternalInput").ap()
    tri_d = nc.dram_tensor("tri", [128, 4, HC], F32, kind="ExternalInput").ap()
    rmask_d = nc.dram_tensor("rmask", [128, QBW], F32, kind="ExternalInput").ap()
    y_d = nc.dram_tensor("y_out", [128, TALL], BF16, kind="ExternalOutput").ap()
    with ExitStack() as es:
        P = Prog(nc, es)
        sb = lambda name, shape, dt: es.enter_context(nc.sbuf_tensor(name, shape, dt))
        P.limit = limit
        w5 = sb("w5_sb", [128, KC, 5, 128], BF16)
        lbl = sb("lbl_sb", [128, 2, 2], F32)
        lb = sb("lb_sb", [128, 2], F32)
        oml = sb("oml_sb", [128, 2], F32)
        hgn = sb("hgn_sb", [128, 1], F32)
        ident = sb("ident_sb", [128, 128], BF16)
        tri = sb("tri_sb", [128, 4, HC], F32)
        rmask = sb("rmask_sb", [128, QBW], F32)
        ones32 = sb("ones32", [128, 128], F32)
        itm = sb("itm", [128, NTL, 128], BF16)
        Qp = sb("Qp", [128, TALL], BF16)
        attmEO = [sb("attmE", [128, NTL, HC], BF16), sb("attmO", [128, NTL, HC], BF16)]
        dS = sb("dS", [128, NCH, 128], BF16)
        Dall = sb("Dall", [128, NCH], F32)
        oacc = sb("oacc", [128, TALL], F32)
        ub = sb("ub", [128, KC, QBW], BF16)
        T = [sb(f"T{i}", [128, QBW], F32) for i in range(12)]
        KtT = sb("KtT", [128, QBW], BF16)
        Kj = [sb(f"Kj{i}", [128, QBW], BF16) for i in range(3)]
        Kd = sb("Kd", [128, QBW], BF16)
        Qd = sb("Qd", [128, QBW], BF16)
        Qsub = sb("Qsub", [128, QBW], BF16)
        KtmEO = [sb("KtmE", [128, 4, 128], BF16), sb("KtmO", [128, 4, 128], BF16)]
        yb = sb("yb", [128, QBW], BF16)
        ps = [es.enter_context(nc.psum_tensor(f"ps{i}", [128, 512], F32)) for i in range(7)]
        psT = es.enter_context(nc.psum_tensor("psT", [128, 1024], BF16))
        r_psA = Ring(ps[0:3], "psA")
        r_psB = Ring(ps[3:7], "psB")

        P.op('pool', lambda e: e.dma_start(out=w5[:], in_=w_d), writes=['w5'], chan='w5')
        for nm, t_, d_ in (('lbl', lbl, lbl_d), ('hgn', hgn, hgn_d), ('ident', ident, ident_d), ('tri', tri, tri_d), ('rmask', rmask, rmask_d)):
            P.op('sp', lambda e, t_=t_, d_=d_: e.dma_start(out=t_[:], in_=d_), writes=[nm], chan=nm)
        P.op('pool', lambda e: e.memset(ones32[:], 1.0), writes=['ones32'])
        for t_ in KtmEO:
            P.op('pool', lambda e, t_=t_: e.memset(t_[:], 0.0), writes=['Ktm'])
        for t_ in attmEO:
            P.op('pool', lambda e, t_=t_: e.memset(t_[:], 0.0), writes=['attm'])
        P.op('pool', lambda e: e.memset(Qsub[:], 0.0), writes=['Qsub'])
        if e_layer == 1:
            P.op('dve', lambda e: e.tensor_tensor(out=lb[:], in0=lbl[:, :, 1], in1=lbl[:, :, 0], op=ALU.subtract), reads=['lbl'], writes=['lb'])
            P.op('act', lambda e: e.activation(out=lb[:], in_=lb[:], func=AF.Sigmoid), reads=['lb'], writes=['lb'])
            P.op('dve', lambda e: e.tensor_scalar(out=oml[:], in0=lb[:], scalar1=-1.0, scalar2=1.0, op0=ALU.mult, op1=ALU.add),
                 reads=['lb'], writes=['oml'])

        blocks = token_blocks()
        if nblk is not None:
            blocks = blocks[:nblk]

        def load_ub(t0, width):
            P.op('sp', lambda e: e.dma_start(out=ub[:, :, 0:width], in_=uT_d[:, :, t0:t0 + width]), writes=['ub'], chan='ub')

        def proj(g, width):
            pp, ppt = r_psA.next()
            for k in range(KC):
                P.op('pe', lambda e, k=k, pp=pp: e.matmul(pp[:, 0:width], w5[:, k, g, :], ub[:, k, 0:width],
                                                         start=(k == 0), stop=(k == KC - 1)),
                     reads=['ub', 'w5'], writes=[ppt])
            return pp, ppt

        def fslot(ch, d):
            return (ch + NCTXCH) % NCH if d == 0 else ch

        for d in range(2):
            for (t0, w) in blocks:
                nch = w // HC
                nti = w // 128
                ch0 = t0 // HC
                tt0 = t0 // 128
                load_ub(t0, w)
                pq, pqt = proj(0, w)
                pf, pft = proj(1 + d, w)
                if d == 0:
                    pi, pit = r_psB.next()
                    for i in range(nti):
                        for k in range(KC):
                            P.op('pe', lambda e, k=k, i=i, pi=pi: e.matmul(pi[:, i * 128:(i + 1) * 128], ub[:, k, i * 128:(i + 1) * 128],
                                                                         w5[:, k, 3, :], start=(k == 0), stop=(k == KC - 1)),
                                 reads=['ub', 'w5'], pwrites=[pit])
                    P.op('act', lambda e, pi=pi, tt0=tt0, nti=nti: e.activation(
                        out=itm[:, tt0:tt0 + nti, :], in_=v3(pi[:, 0:nti * 128], nti, 128), func=AF.Identity),
                        reads=[pit], pwrites=['itm'])
                P.op('act', lambda e, pf=pf, w=w: e.activation(out=T[0][:, 0:w], in_=pf[:, 0:w], func=AF.Sigmoid), reads=[pft], writes=['T0'])
                if e_layer == 1:
                    P.op('dve', lambda e, w=w, d=d: e.tensor_scalar(out=T[0][:, 0:w], in0=T[0][:, 0:w], scalar1=oml[:, d:d + 1],
                                                                   scalar2=lb[:, d:d + 1], op0=ALU.mult, op1=ALU.add),
                         reads=['T0', 'oml', 'lb'], writes=['T0'])
                P.op('act', lambda e, w=w: e.activation(out=T[1][:, 0:w], in_=T[0][:, 0:w], func=AF.Ln), reads=['T0'], writes=['T1'])
                P.op('dve', lambda e, w=w: e.tensor_scalar(out=T[2][:, 0:w], in0=T[0][:, 0:w], scalar1=-1.0, scalar2=1.0,
                                                          op0=ALU.mult, op1=ALU.add), reads=['T0'], writes=['T2'])
                P.op('dve', lambda e, w=w: e.tensor_tensor_scan(out=T[3][:, 0:w], data0=rmask[:, 0:w], data1=T[1][:, 0:w], initial=0.0,
                                                               op0=ALU.mult, op1=ALU.add), reads=['T1', 'rmask'], writes=['T3'])
                if d == 0:
                    c, ct = T[3], 'T3'
                else:
                    P.op('dve', lambda e, w=w: e.tensor_tensor(out=T[4][:, 0:w], in0=T[1][:, 0:w], in1=T[3][:, 0:w], op=ALU.subtract),
                         reads=['T1', 'T3'], writes=['T4'])
                    P.op('dve', lambda e, w=w, nch=nch: e.tensor_tensor(out=v3(T[4][:, 0:w], nch, HC), in0=v3(T[4][:, 0:w], nch, HC),
                                                                       in1=chunk_col(T[3][:, 0:w], nch, HC - 1), op=ALU.add),
                         reads=['T4', 'T3'], writes=['T4'])
                    c, ct = T[4], 'T4'
                cw = c[:, 0:w]
                P.op('act', lambda e, w=w, cw=cw: e.activation(out=T[5][:, 0:w], in_=cw, func=AF.Exp), reads=[ct], writes=['T5'])
                P.op('dve', lambda e, pq=pq, t0=t0, w=w: e.tensor_tensor(out=Qp[:, t0:t0 + w], in0=pq[:, 0:w], in1=T[5][:, 0:w], op=ALU.mult),
                     reads=[pqt, 'T5'], pwrites=['Qp'])
                P.op('dve', lambda e, w=w, nch=nch, cw=cw: e.tensor_tensor(out=v3(T[6][:, 0:w], nch, HC), in0=chunk_col(T[3][:, 0:w], nch, HC - 1),
                                                                          in1=v3(cw, nch, HC), op=ALU.subtract),
                     reads=[ct, 'T3'], writes=['T6'])
                P.op('act', lambda e, w=w: e.activation(out=T[6][:, 0:w], in_=T[6][:, 0:w], func=AF.Exp), reads=['T6'], writes=['T6'])
                P.op('pool', lambda e, w=w: e.tensor_tensor(out=KtT[:, 0:w], in0=T[2][:, 0:w], in1=T[6][:, 0:w], op=ALU.mult),
                     reads=['T2', 'T6'], writes=['KtT'])
                s0 = fslot(ch0, d)
                P.op('act', lambda e, w=w, nch=nch, s0=s0: e.activation(out=Dall[:, s0:s0 + nch], in_=chunk_pick(T[3][:, 0:w], nch, HC - 1), func=AF.Exp),
                     reads=['T3'], pwrites=['Dall'])
                if d == 0:
                    qs0, bcol = 1, -1
                else:
                    qs0, bcol = 0, SUB
                P.op('dve', lambda e, w=w, nch=nch, cw=cw, qs0=qs0, bcol=bcol: e.tensor_tensor(
                    out=sub_view(T[7][:, 0:w], nch, qs0, 3, 0, False), in0=sub_view(cw, nch, qs0, 3, 0, False),
                    in1=sub_view(cw, nch, qs0, 3, bcol, True), op=ALU.subtract), reads=[ct], writes=['T7'])
                P.op('act', lambda e, w=w, nch=nch, qs0=qs0: e.activation(out=sub_view(T[7][:, 0:w], nch, qs0, 3, 0, False),
                                                                         in_=sub_view(T[7][:, 0:w], nch, qs0, 3, 0, False), func=AF.Exp),
                     reads=['T7'], writes=['T7'])
                P.op('dve', lambda e, w=w, nch=nch, qs0=qs0, pq=pq: e.scalar_tensor_tensor(
                    out=sub_view(Qsub[:, 0:w], nch, qs0, 3, 0, False), in0=sub_view(T[7][:, 0:w], nch, qs0, 3, 0, False), scalar=1.0,
                    in1=sub_view(pq[:, 0:w], nch, qs0, 3, 0, False), op0=ALU.min, op1=ALU.mult), reads=['T7', pqt], writes=['Qsub'])
                for jj in range(3):
                    bj = (jj + 1) * SUB - 1 if d == 0 else (jj + 1) * SUB
                    P.op('dve', lambda e, w=w, nch=nch, cw=cw, bj=bj: e.tensor_tensor(out=v3(T[8][:, 0:w], nch, HC), in0=chunk_col(cw, nch, bj),
                                                                                   in1=v3(cw, nch, HC), op=ALU.subtract),
                         reads=[ct], writes=['T8'])
                    P.op('dve', lambda e, w=w: e.tensor_scalar(out=T[8][:, 0:w], in0=T[8][:, 0:w], scalar1=0.0, scalar2=None, op0=ALU.min),
                         reads=['T8'], writes=['T8'])
                    P.op('act', lambda e, w=w: e.activation(out=T[8][:, 0:w], in_=T[8][:, 0:w], func=AF.Exp), reads=['T8'], writes=['T8'])
                    P.op('pool', lambda e, w=w, jj=jj: e.tensor_tensor(out=Kj[jj][:, 0:w], in0=T[8][:, 0:w], in1=T[2][:, 0:w], op=ALU.mult),
                         reads=['T8', 'T2'], writes=[f'Kj{jj}'])
                nsb = w // SUB
                P.op('dve', lambda e, w=w, nsb=nsb, cw=cw: e.tensor_tensor(
                    out=v3(T[9][:, 0:w], nsb, SUB), in0=v3(cw, nsb, SUB),
                    in1=bass.AP(cw.tensor, cw.offset + SUB // 2, [list(cw.ap[0]), [SUB, nsb], [0, SUB]]), op=ALU.subtract),
                    reads=[ct], writes=['T9'])
                P.op('dve', lambda e, w=w: e.tensor_scalar(out=T[9][:, 0:w], in0=T[9][:, 0:w], scalar1=-CLAMP, scalar2=CLAMP,
                                                          op0=ALU.max, op1=ALU.min), reads=['T9'], writes=['T9'])
                P.op('act', lambda e, w=w: e.activation(out=T[10][:, 0:w], in_=T[9][:, 0:w], func=AF.Exp), reads=['T9'], writes=['T10'])
                P.op('act', lambda e, w=w: e.activation(out=T[11][:, 0:w], in_=T[9][:, 0:w], func=AF.Exp, scale=-1.0), reads=['T9'], writes=['T11'])
                P.op('dve', lambda e, pq=pq, w=w: e.tensor_tensor(out=Qd[:, 0:w], in0=pq[:, 0:w], in1=T[10][:, 0:w], op=ALU.mult),
                     reads=[pqt, 'T10'], writes=['Qd'])
                P.op('pool', lambda e, w=w: e.tensor_tensor(out=Kd[:, 0:w], in0=T[2][:, 0:w], in1=T[11][:, 0:w], op=ALU.mult),
                     reads=['T2', 'T11'], writes=['Kd'])
                pa, pat = r_psB.next()
                for cl in range(nch):
                    i, p0 = cl // 2, (cl % 2) * HC
                    tk = cl * HC
                    P.op('pe', lambda e, pa=pa, i=i, p0=p0, tk=tk: e.matmul(pa[p0:p0 + HC, i * HC:(i + 1) * HC], Kd[:, tk:tk + HC], Qd[:, tk:tk + HC],
                                                                          start=True, stop=True), reads=['Kd', 'Qd'], pwrites=[pat])
                    dummy = 0 if d == 0 else NSUB - 1
                    for sj in range(NSUB):
                        c0 = tk + sj * SUB
                        if sj == dummy:
                            lh, rh, lt, rt = Kd, Qd, 'Kd', 'Qd'
                        else:
                            jj = sj - 1 if d == 0 else sj
                            lh, rh, lt, rt = Kj[jj], Qsub, f'Kj{jj}', 'Qsub'
                        P.op('pe', lambda e, pa=pa, i=i, p0=p0, tk=tk, c0=c0, sj=sj, lh=lh, rh=rh: e.matmul(
                            pa[p0:p0 + HC, 256 + i * HC + sj * SUB:256 + i * HC + (sj + 1) * SUB], lh[:, tk:tk + HC], rh[:, c0:c0 + SUB],
                            start=True, stop=True), reads=[lt, rt], pwrites=[pat])
                P.op('dve', lambda e, pa=pa, nti=nti, d=d: e.tensor_tensor(out=v3(T[0][:, 0:nti * HC], nti, HC), in0=v3(pa[:, 0:nti * HC], nti, HC),
                                                                          in1=bc_mid(tri[:, 2 * d, :], nti), op=ALU.mult),
                     reads=[pat, 'tri'], writes=['T0'])
                P.op('dve', lambda e, pa=pa, nti=nti, d=d: e.tensor_tensor(out=v3(T[1][:, 0:nti * HC], nti, HC), in0=v3(pa[:, 256:256 + nti * HC], nti, HC),
                                                                          in1=bc_mid(tri[:, 2 * d + 1, :], nti), op=ALU.mult),
                     reads=[pat, 'tri'], writes=['T1'])
                for hf in range(2):
                    P.op('pool', lambda e, nti=nti, hf=hf, tt0=tt0: e.tensor_tensor(
                        out=attmEO[hf][hf * HC:(hf + 1) * HC, tt0:tt0 + nti, :], in0=v3(T[0][hf * HC:(hf + 1) * HC, 0:nti * HC], nti, HC),
                        in1=v3(T[1][hf * HC:(hf + 1) * HC, 0:nti * HC], nti, HC), op=ALU.add),
                        reads=['T0', 'T1'], pwrites=['attm'])
                for i in range(nti):
                    P.op('pe', lambda e, i=i: e.transpose(psT[:, i * 128:(i + 1) * 128], KtT[:, i * 128:(i + 1) * 128], ident[:, :]),
                         reads=['KtT', 'ident'], pwrites=['psT'])
                for hf in range(2):
                    P.op('dve', lambda e, nti=nti, hf=hf: e.tensor_copy(out=KtmEO[hf][hf * HC:(hf + 1) * HC, 0:nti, :],
                                                                       in_=v3(psT[hf * HC:(hf + 1) * HC, 0:nti * 128], nti, 128)),
                         reads=['psT'], pwrites=['Ktm'])
                for b0 in range(0, nch, 4):
                    pd, pdt = r_psB.next()
                    for cl in range(b0, b0 + 4):
                        i = cl // 2
                        P.op('pe', lambda e, pd=pd, cl=cl, i=i, b0=b0, tt0=tt0: e.matmul(
                            pd[:, (cl - b0) * 128:(cl - b0 + 1) * 128], KtmEO[cl % 2][:, i, :], itm[:, tt0 + i, :],
                            start=True, stop=True), reads=['Ktm', 'itm'], pwrites=[pdt])
                    P.op('act', lambda e, pd=pd, s0=s0, b0=b0: e.activation(out=dS[:, s0 + b0:s0 + b0 + 4, :], in_=v3(pd[:, 0:512], 4, 128),
                                                                            func=AF.Identity), reads=[pdt], pwrites=['dS'])
            if stop <= 0:
                break
            for ee in range(128):
                if d == 0:
                    a1 = bass.AP(dS, ee, [[NCH * 128, 128], [128, NCH]])
                    a0 = Dall[:, :]
                else:
                    a1 = bass.AP(dS, (NCH - 1) * 128 + ee, [[NCH * 128, 128], [-128, NCH]])
                    a0 = bass.AP(Dall, NCH - 1, [[NCH, 128], [-1, NCH]])
                P.op('dve', lambda e, a1=a1, a0=a0: e.tensor_tensor_scan(out=a1, data0=a0, data1=a1, initial=0.0, op0=ALU.mult, op1=ALU.add),
                     reads=['dS', 'Dall'], pwrites=['dS'])
            if stop <= 1:
                break
            for (t0, w) in blocks:
                nch = w // HC
                ch0 = t0 // HC
                tt0 = t0 // 128
                po, pot = r_psB.next()
                for cl in range(nch):
                    i = cl // 2
                    tk = t0 + cl * HC
                    ch = ch0 + cl
                    if d == 0:
                        s = fslot(ch, 0)
                        prev = s - 1 if s > 0 else None
                    else:
                        prev = ch + 1 if ch < NCH - 1 else None
                    if prev is not None:
                        P.op('pe', lambda e, po=po, cl=cl, prev=prev, tk=tk: e.matmul(po[:, cl * HC:(cl + 1) * HC], dS[:, prev, :], Qp[:, tk:tk + HC],
                                                                                    start=True, stop=False), reads=['dS', 'Qp'], pwrites=[pot])
                    P.op('pe', lambda e, po=po, cl=cl, i=i, tt0=tt0, prev=prev: e.matmul(
                        po[:, cl * HC:(cl + 1) * HC], itm[:, tt0 + i, :], attmEO[cl % 2][:, tt0 + i, :],
                        start=(prev is None), stop=True), reads=['itm', 'attm'], pwrites=[pot])
                if d == 0:
                    P.op('act', lambda e, po=po, t0=t0, w=w: e.activation(out=oacc[:, t0:t0 + w], in_=po[:, 0:w], func=AF.Identity),
                         reads=[pot], pwrites=['oacc'])
                else:
                    P.op('dve', lambda e, po=po, t0=t0, w=w: e.tensor_tensor(out=oacc[:, t0:t0 + w], in0=po[:, 0:w], in1=oacc[:, t0:t0 + w], op=ALU.add),
                         reads=[pot, 'oacc'], pwrites=['oacc'])
        for (t0, w) in (blocks if stop > 2 else []):
            load_ub(t0, w)
            pg, pgt = proj(4, w)
            P.op('act', lambda e, pg=pg, w=w: e.activation(out=T[0][:, 0:w], in_=pg[:, 0:w], func=AF.Silu), reads=[pgt], writes=['T0'])
            P.op('act', lambda e, t0=t0, w=w: e.activation(out=T[1][:, 0:w], in_=oacc[:, t0:t0 + w], func=AF.Square), reads=['oacc'], writes=['T1'])
            pn, pnt = r_psB.next()
            P.op('pe', lambda e, pn=pn, w=w: e.matmul(pn[:, 0:w], ones32[:, :], T[1][:, 0:w], start=True, stop=True), reads=['T1', 'ones32'], writes=[pnt])
            P.op('dve', lambda e, pn=pn, w=w: e.tensor_scalar(out=T[2][:, 0:w], in0=pn[:, 0:w], scalar1=1.0 / 128, scalar2=EPS,
                                                             op0=ALU.mult, op1=ALU.add), reads=[pnt], writes=['T2'])
            P.op('act', lambda e, w=w: e.activation(out=T[2][:, 0:w], in_=T[2][:, 0:w], func=AF.Sqrt), reads=['T2'], writes=['T2'])
            P.op('dve', lambda e, w=w: e.reciprocal(out=T[2][:, 0:w], in_=T[2][:, 0:w]), reads=['T2'], writes=['T2'])
            P.op('dve', lambda e, t0=t0, w=w: e.scalar_tensor_tensor(out=T[3][:, 0:w], in0=oacc[:, t0:t0 + w], scalar=hgn[:, 0:1], in1=T[2][:, 0:w],
                                                                    op0=ALU.mult, op1=ALU.mult), reads=['oacc', 'T2', 'hgn'], writes=['T3'])
            P.op('pool', lambda e, w=w: e.tensor_tensor(out=yb[:, 0:w], in0=T[3][:, 0:w], in1=T[0][:, 0:w], op=ALU.mult),
                 reads=['T3', 'T0'], writes=['yb'])
            P.op('sp', lambda e, t0=t0, w=w: e.dma_start(out=y_d[:, t0:t0 + w], in_=yb[:, 0:w]), reads=['yb'], pwrites=['y_d'], chan='yb')
        P.finish('sp')
        P.emit()
    return nc


def hgrn_consts():
    p = np.arange(128)[:, None] % HC
    t = np.arange(HC)[None, :]
    same = (p // SUB) == (t // SUB)
    tri = np.stack([same & (p <= t), (p // SUB) < (t // SUB), same & (p >= t), (p // SUB) > (t // SUB)], axis=1).astype(np.float32)
    rmask = np.broadcast_to((np.arange(QBW) % HC != 0).astype(np.float32)[None, :], (128, QBW))
    return np.ascontiguousarray(tri), np.ascontiguousarray(rmask)


def run_hgrn(uT_all, w_in, lbl, hgn, e_layer, stop=99, ncores=NCORE):
    nc = build_hgrn_prog(e_layer, stop)
    ident = np.eye(128, dtype=np.float32).astype(NPBF)
    tri, rmask = hgrn_consts()
    in_maps = []
    for j in range(NCORE):
        cols = [w_in[:, (3 + g) * 1024 + j * 128: (3 + g) * 1024 + (j + 1) * 128] for g in range(5)]
        w5 = np.stack([lay_w(c) for c in cols], axis=2)
        lj = np.ascontiguousarray(lbl[:, :, j * 128:(j + 1) * 128].transpose(2, 0, 1)).astype(np.float32)
        hj = np.ascontiguousarray(hgn[j * 128:(j + 1) * 128].reshape(128, 1)).astype(np.float32)
        in_maps.append({"uT": uT_all, "w5": np.ascontiguousarray(w5), "lbl": lj, "hgn": hj, "ident": ident, "tri": tri, "rmask": rmask})
    in_maps = in_maps[:ncores]
    res = run_bass_kernel_spmd(nc, in_maps, core_ids=list(range(ncores)))
    return np.stack([res.results[j]["y_out"] for j in range(ncores)], axis=1)


def _fm(x):
    T = x.shape[0]
    return np.ascontiguousarray(x.T.reshape(KC, 128, T).transpose(1, 0, 2))


def _unfm(a):
    T = a.shape[2]
    return np.ascontiguousarray(a.transpose(1, 0, 2).reshape(D, T).T)


def _vec(v):
    return np.ascontiguousarray(v.reshape(KC, 128).T)


def _lay_win(w):
    g = w[:, :DFF].reshape(KC, 128, JC, 128)
    u = w[:, DFF:].reshape(KC, 128, JC, 128)
    return np.ascontiguousarray(np.concatenate([g.transpose(2, 1, 0, 3), u.transpose(2, 1, 0, 3)], axis=3))


def _lay_wout(w):
    return np.ascontiguousarray(w.reshape(JC, 128, KC, 128).transpose(2, 1, 0, 3))


def _lay_wo(w):
    return np.ascontiguousarray(w.reshape(KC, 128, KC, 128).transpose(2, 1, 0, 3))


def _run_token(stages_spec, h_list, y_all, vec_list, weights, want_u, final):
    stages = [(s, i) for i, s in enumerate(stages_spec)]
    nc = build_token_prog(stages, want_u=want_u, final=final)
    vecs = np.ascontiguousarray(np.stack([_vec(v) for v in vec_list], axis=1)).astype(np.float32)
    shared = {"vecs": vecs}
    for si, s in enumerate(stages_spec):
        if s == 'proj':
            shared[f"wo{si}"] = _lay_wo(weights[si])
        else:
            shared[f"wi{si}"] = _lay_win(weights[si][0])
            shared[f"wf{si}"] = _lay_wout(weights[si][1])
    in_maps = []
    for r in range(NCORE):
        m = dict(shared)
        m["h_in"] = h_list[r]
        if y_all is not None:
            m["y_in"] = np.ascontiguousarray(np.concatenate(
                [y_all[:, :, r * TLAT:(r + 1) * TLAT], y_all[:, :, SEQ + r * TCTX: SEQ + (r + 1) * TCTX]], axis=2))
        in_maps.append(m)
    res = run_bass_kernel_spmd(nc, in_maps, core_ids=list(range(NCORE)))
    return res.results


def _gather_u(results):
    u_all = np.zeros((128, KC, TALL), NPBF)
    for r in range(NCORE):
        uo = results[r]["u_out"]
        u_all[:, :, r * TLAT:(r + 1) * TLAT] = uo[:, :, :TLAT]
        u_all[:, :, SEQ + r * TCTX: SEQ + (r + 1) * TCTX] = uo[:, :, TLAT:]
    return u_all


def kernel(x, c, ctx, c_ctx, w_mod, b_mod, norm_g, w_ff_in, w_ff_out, w_in_even, w_out_even,
           na_rpb, hg_lb_logits, hg_norm_g, w_qkv_odd, w_o_odd, sink_odd, final_norm_g):
    f = lambda a: np.asarray(a, dtype=np.float32)
    x, c, ctx, c_ctx, w_mod, b_mod, norm_g = f(x), f(c), f(ctx), f(c_ctx), f(w_mod), f(b_mod), f(norm_g)
    w_ff_in, w_ff_out, w_in_even, w_out_even = f(w_ff_in), f(w_ff_out), f(w_in_even), f(w_out_even)
    na_rpb, hg_lb_logits, hg_norm_g = f(na_rpb), f(hg_lb_logits), f(hg_norm_g)
    w_qkv_odd, w_o_odd, sink_odd, final_norm_g = f(w_qkv_odd), f(w_o_odd), f(sink_odd), f(final_norm_g)

    mod = run_mod(c[0], c_ctx, w_mod, b_mod).reshape(DEPTH, 2, 3, 3, D)

    def ffn_vecs(l, sub):
        return [norm_g[l, sub], mod[l, 0, sub, 0], mod[l, 0, sub, 1], mod[l, 0, sub, 2],
                mod[l, 1, sub, 0], mod[l, 1, sub, 1], mod[l, 1, sub, 2]]

    def u_vecs(l):
        return [norm_g[l, 1], mod[l, 0, 1, 0], mod[l, 0, 1, 1], mod[l, 1, 1, 0], mod[l, 1, 1, 1]]

    h_list = []
    for r in range(NCORE):
        tok = np.concatenate([x[0, r * TLAT:(r + 1) * TLAT], ctx[0, r * TCTX:(r + 1) * TCTX]], axis=0)
        h_list.append(_fm(tok))

    res = _run_token(['ffn'], h_list, None, ffn_vecs(0, 0) + u_vecs(0), [(w_ff_in[0, 0], w_ff_out[0, 0])], True, False)
    out = None
    for l in range(DEPTH):
        h_list = [res[r]["h_out"] for r in range(NCORE)]
        u_all = _gather_u(res)
        if l % 2 == 0:
            e = l // 2
            ya = run_na(u_all, w_in_even[e], na_rpb[e])
            yb = run_hgrn(u_all, w_in_even[e], hg_lb_logits, hg_norm_g[e], e)
            y_all = np.ascontiguousarray(np.concatenate([ya, yb], axis=1))
            w_o = w_out_even[e]
        else:
            o = l // 2
            y_all = run_odd(u_all, w_qkv_odd[o], sink_odd[o])
            w_o = w_o_odd[o]
        gate_vecs = [mod[l, 0, 1, 2], mod[l, 1, 1, 2]]
        if l < DEPTH - 1:
            res = _run_token(['proj', 'ffn', 'ffn'], h_list, y_all,
                             gate_vecs + ffn_vecs(l, 2) + ffn_vecs(l + 1, 0) + u_vecs(l + 1),
                             [w_o, (w_ff_in[l, 1], w_ff_out[l, 1]), (w_ff_in[l + 1, 0], w_ff_out[l + 1, 0])], True, False)
        else:
            res = _run_token(['proj', 'ffn'], h_list, y_all, gate_vecs + ffn_vecs(l, 2) + [final_norm_g],
                             [w_o, (w_ff_in[l, 1], w_ff_out[l, 1])], False, True)
            out = np.zeros((1, SEQ, D), np.float32)
            for r in range(NCORE):
                out[0, r * TLAT:(r + 1) * TLAT] = _unfm(res[r]["o_out"])[:TLAT]
    return out
```

```python
import numpy as np
import ml_dtypes
from contextlib import ExitStack
import concourse.bass as bass
import concourse.mybir as mybir
from concourse.bass_utils import run_bass_kernel_spmd

F32 = mybir.dt.float32
BF16 = mybir.dt.bfloat16
AF = mybir.ActivationFunctionType
ALU = mybir.AluOpType
NPBF = ml_dtypes.bfloat16

D = 2048
KC = 16
DFF = 5632
JC = 44
NCORE = 8
SEQ = 8192
CTX = 256
TLAT = SEQ // NCORE
TCTX = CTX // NCORE
TT = TLAT + TCTX
TW = 352
NT = 3
EPS = 1e-6
DEPTH = 4


SAME_ENGINE_SYNC = False


class Prog:
    ENGS = {'pe': 'tensor', 'act': 'scalar', 'dve': 'vector', 'pool': 'gpsimd', 'sp': 'sync'}

    def __init__(self, nc, es):
        self.nc = nc
        self.es = es
        self.q = {e: [] for e in self.ENGS}
        self.sems = {}
        self.val = {}
        self.w = {}
        self.r = {}
        self.waited = {}
        self.limit = None
        self.nops = 0

    def sem(self, key):
        if key not in self.sems:
            self.sems[key] = self.es.enter_context(self.nc.semaphore("s_" + key))
            self.val[key] = 0
        return self.sems[key]

    def op(self, eng, fn, reads=(), writes=(), pwrites=(), chan=None):
        self.nops += 1
        if self.limit is not None and self.nops > self.limit:
            return None
        waits = {}

        def need(evs):
            for k, (v, e) in evs.items():
                if e == 'pe' and eng == 'pe':
                    continue
                if e == eng and not SAME_ENGINE_SYNC and v < self.val.get(eng, 0):
                    continue
                if waits.get(k, 0) < v:
                    waits[k] = v
        for t in reads:
            need(self.w.get(t, {}))
        for t in list(writes) + list(pwrites):
            need(self.w.get(t, {}))
            need(self.r.get(t, {}))
        wl = []
        for k, v in waits.items():
            if self.waited.get((eng, k), 0) >= v:
                continue
            self.waited[(eng, k)] = v
            wl.append((k, v))
        if chan is None:
            key, inc, ee = eng, 1, eng
        else:
            key, inc, ee = 'd_' + chan, 16, 'dma'
        self.sem(key)
        self.val[key] += inc
        v = self.val[key]
        for t in reads:
            self.r.setdefault(t, {})[key] = (v, ee)
        for t in writes:
            self.w[t] = {key: (v, ee)}
            self.r[t] = {}
        for t in pwrites:
            self.w.setdefault(t, {})[key] = (v, ee)
            self.r[t] = {}
        self.q[eng].append((wl, fn, key, inc))
        return (key, v)

    def finish(self, eng='sp'):
        wl = [(k, v) for k, v in self.val.items() if k.startswith('d_')]
        self.q[eng].append((wl, None, None, 0))

    def emit(self):
        nc = self.nc
        with nc.Block() as block:
            for e, attr in self.ENGS.items():
                items = self.q[e]

                def body(engine, items=items):
                    for wl, fn, key, inc in items:
                        for k, v in wl:
                            engine.wait_ge(self.sems[k], v)
                        if fn is not None:
                            ins = fn(engine)
                            ins.then_inc(self.sems[key], inc)
                getattr(block, attr)(body)


def bc_mid(ap, n):
    a = [list(x) for x in ap.ap]
    return bass.AP(ap.tensor, ap.offset, [a[0], [0, n]] + a[1:])


def bc_last(ap, n):
    a = [list(x) for x in ap.ap]
    return bass.AP(ap.tensor, ap.offset, a + [[0, n]])


class Ring:
    def __init__(self, bufs, name):
        self.bufs = bufs
        self.name = name
        self.i = 0

    def next(self):
        k = self.i % len(self.bufs)
        self.i += 1
        return self.bufs[k], f"{self.name}{k}"


def build_token_prog(stages, want_u=False, final=False):
    nc = bass.Bass("TRN2", target_bir_lowering=False)
    nv = 0
    for s in stages:
        nv += 2 if s[0] == 'proj' else 7
    if want_u:
        nv += 5
    if final:
        nv += 1
    h_in = nc.dram_tensor("h_in", [128, KC, TT], F32, kind="ExternalInput").ap()
    vecs_d = nc.dram_tensor("vecs", [128, nv, KC], F32, kind="ExternalInput").ap()
    wd = {}
    y_in = None
    for si, s in enumerate(stages):
        if s[0] == 'proj':
            wd[si] = nc.dram_tensor(f"wo{si}", [KC, 128, KC, 128], F32, kind="ExternalInput").ap()
            y_in = nc.dram_tensor("y_in", [128, KC, TT], BF16, kind="ExternalInput").ap()
        else:
            wd[si] = (nc.dram_tensor(f"wi{si}", [JC, 128, KC, 256], F32, kind="ExternalInput").ap(),
                      nc.dram_tensor(f"wf{si}", [KC, 128, JC, 128], F32, kind="ExternalInput").ap())
    hd = [h_in]
    for si in range(len(stages)):
        last = si == len(stages) - 1
        if last and not final:
            hd.append(nc.dram_tensor("h_out", [128, KC, TT], F32, kind="ExternalOutput").ap())
        else:
            hd.append(nc.dram_tensor(f"h_s{si}", [128, KC, TT], F32).ap())
    u_out = nc.dram_tensor("u_out", [128, KC, TT], BF16, kind="ExternalOutput").ap() if want_u else None
    o_out = nc.dram_tensor("o_out", [128, KC, TT], F32, kind="ExternalOutput").ap() if final else None

    tiles = [(i * TW, TW) for i in range(NT)]
    segs = []
    for (t0, w) in tiles:
        sg = []
        lat_end = min(max(TLAT - t0, 0), w)
        if lat_end > 0:
            sg.append((0, lat_end, 0))
        if lat_end < w:
            sg.append((lat_end, w, 1))
        segs.append(sg)

    with ExitStack() as es:
        P = Prog(nc, es)
        sb = lambda name, shape, dt: es.enter_context(nc.sbuf_tensor(name, shape, dt))
        vecs = sb("vecs_sb", [128, nv, KC], F32)
        dvec = sb("dvec", [128, 8, KC], F32)
        ones = sb("ones", [128, 128], F32)
        uT = sb("uT", [128, KC, TT], BF16)
        big = sb("big", [128, JC, TT], BF16)
        rstd = sb("rstd", [128, TW], F32)
        tmpn = [sb(f"tmpn{i}", [128, TW], F32) for i in range(6)]
        wib = [sb(f"wib{i}", [128, KC, 256], BF16) for i in range(3)]
        wob = [sb(f"wob{i}", [128, JC, 128], BF16) for i in range(2)]
        sgb = [sb(f"sgb{i}", [128, TW], F32) for i in range(3)]
        hres = [sb(f"hres{i}", [128, TW], F32) for i in range(8)]
        hout = [sb(f"hout{i}", [128, TW], F32) for i in range(3)]
        ps = [es.enter_context(nc.psum_tensor(f"ps{i}", [128, 512], F32)) for i in range(8)]
        r_tmpn = Ring(tmpn, "tmpn")
        r_wib = Ring(wib, "wib")
        r_wob = Ring(wob, "wob")
        r_sgb = Ring(sgb, "sgb")
        r_hres = Ring(hres, "hres")
        r_hout = Ring(hout, "hout")
        r_psA = Ring(ps[0:5], "ps")
        psn = ps[5:8]

        P.op('sp', lambda e: e.dma_start(out=vecs[:], in_=vecs_d), writes=['vecs'], chan='vecs')
        P.op('pool', lambda e: e.memset(ones[:], 1.0), writes=['ones'])

        def norm_stage(src, vbase, has_mod, dst_dram, presummed=False):
            gi, shl, scl, shc, scc = vbase
            if has_mod:
                for which, sc_i in ((0, scl), (1, scc)):
                    P.op('dve', lambda e, which=which, sc_i=sc_i: e.scalar_tensor_tensor(
                        out=dvec[:, which, :], in0=vecs[:, sc_i, :], scalar=1.0, in1=vecs[:, gi, :],
                        op0=ALU.add, op1=ALU.mult), reads=['vecs'], writes=[f'dvA{which}'])
            for ti, (t0, w) in enumerate(tiles):
                pb, pbt = psn[ti], f'psn{ti}'
                for k in (range(KC) if not presummed else []):
                    hr, hrt = r_hres.next()
                    P.op('sp', lambda e, hr=hr, k=k, t0=t0, w=w: e.dma_start(out=hr[:, 0:w], in_=src[:, k, t0:t0 + w]),
                         reads=[src.tensor.name], writes=[hrt], chan=hrt)
                    sq, sqt = r_sgb.next()
                    P.op('act', lambda e, hr=hr, sq=sq, w=w: e.activation(out=sq[:, 0:w], in_=hr[:, 0:w], func=AF.Square),
                         reads=[hrt], writes=[sqt])
                    P.op('pe', lambda e, k=k, pb=pb, sq=sq, w=w: e.matmul(pb[:, 0:w], ones[:, :], sq[:, 0:w],
                                                                        start=(k == 0), stop=(k == KC - 1)),
                         reads=[sqt, 'ones'], writes=[pbt])
                P.op('dve', lambda e, pb=pb, w=w: e.tensor_scalar(out=rstd[:, 0:w], in0=pb[:, 0:w], scalar1=1.0 / D,
                                                                 scalar2=EPS, op0=ALU.mult, op1=ALU.add),
                     reads=[pbt], writes=['rstd'])
                P.op('act', lambda e, w=w: e.activation(out=rstd[:, 0:w], in_=rstd[:, 0:w], func=AF.Sqrt),
                     reads=['rstd'], writes=['rstd'])
                P.op('dve', lambda e, w=w: e.reciprocal(out=rstd[:, 0:w], in_=rstd[:, 0:w]),
                     reads=['rstd'], writes=['rstd'])
                for k in range(KC):
                    hr, hrt = r_hres.next()
                    P.op('sp', lambda e, hr=hr, k=k, t0=t0, w=w: e.dma_start(out=hr[:, 0:w], in_=src[:, k, t0:t0 + w]),
                         reads=[src.tensor.name], writes=[hrt], chan=hrt)
                    if has_mod:
                        tb, tbt = r_tmpn.next()
                        for (c0, c1, which) in segs[ti]:
                            a_ap = dvec[:, which, k:k + 1]
                            b_ap = vecs[:, (shl if which == 0 else shc), k:k + 1]
                            P.op('dve', lambda e, tb=tb, hr=hr, c0=c0, c1=c1, a_ap=a_ap: e.scalar_tensor_tensor(
                                out=tb[:, c0:c1], in0=hr[:, c0:c1], scalar=a_ap, in1=rstd[:, c0:c1],
                                op0=ALU.mult, op1=ALU.mult), reads=[hrt, 'rstd', f'dvA{which}'], pwrites=[tbt])
                            P.op('act', lambda e, tb=tb, c0=c0, c1=c1, b_ap=b_ap, k=k, t0=t0: e.activation(
                                out=uT[:, k, t0 + c0:t0 + c1], in_=tb[:, c0:c1], func=AF.Identity, bias=b_ap, scale=1.0),
                                reads=[tbt, 'vecs'], pwrites=[f'u_t{ti}'])
                    else:
                        ho, hot = r_hout.next()
                        P.op('dve', lambda e, ho=ho, hr=hr, k=k, w=w: e.scalar_tensor_tensor(
                            out=ho[:, 0:w], in0=hr[:, 0:w], scalar=vecs[:, gi, k:k + 1], in1=rstd[:, 0:w],
                            op0=ALU.mult, op1=ALU.mult), reads=[hrt, 'rstd', 'vecs'], writes=[hot])
                        P.op('sp', lambda e, ho=ho, k=k, t0=t0, w=w: e.dma_start(out=dst_dram[:, k, t0:t0 + w], in_=ho[:, 0:w]),
                             reads=[hot], pwrites=[dst_dram.tensor.name], chan='st_' + hot)
                if has_mod and dst_dram is not None:
                    P.op('sp', lambda e, t0=t0, w=w: e.dma_start(out=dst_dram[:, :, t0:t0 + w], in_=uT[:, :, t0:t0 + w]),
                         reads=[f'u_t{ti}'], pwrites=[dst_dram.tensor.name], chan=f'ust{ti}')

        def ffn1_stage(w_in_d):
            for j in range(JC):
                wb, wbt = r_wib.next()
                P.op('pool', lambda e, j=j, wb=wb: e.dma_start(out=wb[:], in_=w_in_d[j]), writes=[wbt], chan=wbt)
                for ti, (t0, w) in enumerate(tiles):
                    pg, pgt = r_psA.next()
                    pu, put = r_psA.next()
                    for half, (pp, ppt) in enumerate(((pg, pgt), (pu, put))):
                        for k in range(KC):
                            P.op('pe', lambda e, k=k, pp=pp, wb=wb, half=half, t0=t0, w=w: e.matmul(
                                pp[:, 0:w], wb[:, k, half * 128:(half + 1) * 128], uT[:, k, t0:t0 + w],
                                start=(k == 0), stop=(k == KC - 1)),
                                reads=[wbt, f'u_t{ti}'], writes=[ppt])
                    sg, sgt = r_sgb.next()
                    P.op('act', lambda e, sg=sg, pg=pg, w=w: e.activation(out=sg[:, 0:w], in_=pg[:, 0:w], func=AF.Silu),
                         reads=[pgt], writes=[sgt])
                    P.op('dve', lambda e, sg=sg, pu=pu, j=j, t0=t0, w=w: e.tensor_tensor(
                        out=big[:, j, t0:t0 + w], in0=sg[:, 0:w], in1=pu[:, 0:w], op=ALU.mult),
                        reads=[sgt, put], pwrites=[f'big_t{ti}'])

        def proj_stage(w_d, kc, src, dst, gl, gc, half):
            mul = 0.5 if half else 1.0
            for which, gi_ in ((0, gl), (1, gc)):
                P.op('dve', lambda e, which=which, gi_=gi_: e.tensor_scalar(
                    out=dvec[:, 2 + which, :], in0=vecs[:, gi_, :], scalar1=mul, scalar2=None, op0=ALU.mult),
                    reads=['vecs'], writes=[f'dvG{which}'])
            deferred = []
            for dc in range(KC):
                wb, wbt = r_wob.next()
                P.op('pool', lambda e, dc=dc, wb=wb: e.dma_start(out=wb[:, 0:kc, :], in_=w_d[dc]), writes=[wbt], chan=wbt)
                for ti, (t0, w) in enumerate(tiles):
                    hr, hrt = r_hres.next()
                    P.op('sp', lambda e, hr=hr, dc=dc, t0=t0, w=w: e.dma_start(out=hr[:, 0:w], in_=src[:, dc, t0:t0 + w]),
                         reads=[src.tensor.name], writes=[hrt], chan=hrt)
                    pp, ppt = r_psA.next()
                    for j in range(kc):
                        P.op('pe', lambda e, j=j, pp=pp, wb=wb, t0=t0, w=w: e.matmul(
                            pp[:, 0:w], wb[:, j, :], big[:, j, t0:t0 + w], start=(j == 0), stop=(j == kc - 1)),
                            reads=[wbt, f'big_t{ti}'], writes=[ppt])
                    while deferred:
                        deferred.pop(0)()
                    ho, hot = r_hout.next()
                    for (c0, c1, which) in segs[ti]:
                        P.op('dve', lambda e, ho=ho, pp=pp, hr=hr, c0=c0, c1=c1, which=which, dc=dc: e.scalar_tensor_tensor(
                            out=ho[:, c0:c1], in0=pp[:, c0:c1], scalar=dvec[:, 2 + which, dc:dc + 1], in1=hr[:, c0:c1],
                            op0=ALU.mult, op1=ALU.add), reads=[ppt, hrt, f'dvG{which}'], pwrites=[hot])
                    P.op('sp', lambda e, ho=ho, dc=dc, t0=t0, w=w: e.dma_start(out=dst[:, dc, t0:t0 + w], in_=ho[:, 0:w]),
                         reads=[hot], pwrites=[dst.tensor.name], chan='st_' + hot)
                    sq, sqt = r_sgb.next()
                    P.op('act', lambda e, ho=ho, sq=sq, w=w: e.activation(out=sq[:, 0:w], in_=ho[:, 0:w], func=AF.Square),
                         reads=[hot], writes=[sqt])

                    def _ssq(dc=dc, ti=ti, sq=sq, sqt=sqt, w=w):
                        P.op('pe', lambda e: e.matmul(psn[ti][:, 0:w], ones[:, :], sq[:, 0:w], start=(dc == 0), stop=(dc == KC - 1)),
                             reads=[sqt, 'ones'], writes=[f'psn{ti}'])
                    deferred.append(_ssq)
            while deferred:
                deferred.pop(0)()

        vb = 0
        for si, s in enumerate(stages):
            src, dst = hd[si], hd[si + 1]
            if s[0] == 'proj':
                for ti, (t0, w) in enumerate(tiles):
                    P.op('sp', lambda e, t0=t0, w=w: e.dma_start(out=big[:, 0:KC, t0:t0 + w], in_=y_in[:, :, t0:t0 + w]),
                         writes=[f'big_t{ti}'], chan=f'yin{ti}')
                proj_stage(wd[si], KC, src, dst, vb, vb + 1, False)
                vb += 2
            else:
                g, shl, scl, gl, shc, scc, gc = range(vb, vb + 7)
                norm_stage(src, (g, shl, scl, shc, scc), True, None, presummed=(si > 0))
                ffn1_stage(wd[si][0])
                proj_stage(wd[si][1], JC, src, dst, gl, gc, True)
                vb += 7
        if want_u:
            g, shl, scl, shc, scc = range(vb, vb + 5)
            norm_stage(hd[-1], (g, shl, scl, shc, scc), True, u_out, presummed=True)
            vb += 5
        if final:
            norm_stage(hd[-1], (vb, 0, 0, 0, 0), False, o_out, presummed=True)
            vb += 1
        P.finish('sp')
        P.emit()
    return nc


MODC = 9 * D // NCORE
MCH = 384
NMCH = MODC // MCH


def build_mod_prog():
    nc = bass.Bass("TRN2", target_bir_lowering=False)
    cvec_d = nc.dram_tensor("cvec", [128, KC, 2], F32, kind="ExternalInput").ap()
    w_d = nc.dram_tensor("wm", [DEPTH * NMCH, 128, KC, MCH], F32, kind="ExternalInput").ap()
    b_d = nc.dram_tensor("bm", [2, DEPTH * MODC], F32, kind="ExternalInput").ap()
    o_d = nc.dram_tensor("mod_out", [2, DEPTH * MODC], F32, kind="ExternalOutput").ap()
    with ExitStack() as es:
        P = Prog(nc, es)
        sb = lambda name, shape, dt: es.enter_context(nc.sbuf_tensor(name, shape, dt))
        cv = sb("cv", [128, KC, 2], F32)
        sc = sb("sc", [128, KC, 2], F32)
        bsb = sb("bsb", [2, DEPTH * MODC], F32)
        mo = sb("mo", [2, DEPTH * MODC], F32)
        wbs = [sb(f"wb{i}", [128, KC, MCH], F32) for i in range(3)]
        ps = [es.enter_context(nc.psum_tensor(f"ps{i}", [128, 512], F32)) for i in range(4)]
        r_wb = Ring(wbs, "wb")
        r_ps = Ring(ps, "ps")
        P.op('sp', lambda e: e.dma_start(out=cv[:], in_=cvec_d), writes=['cv'], chan='cv')
        P.op('sp', lambda e: e.dma_start(out=bsb[:], in_=b_d), writes=['bsb'], chan='bsb')
        P.op('act', lambda e: e.activation(out=sc[:], in_=cv[:], func=AF.Silu), reads=['cv'], writes=['sc'])
        for i in range(DEPTH * NMCH):
            wb, wbt = r_wb.next()
            P.op('sp', lambda e, i=i, wb=wb: e.dma_start(out=wb[:], in_=w_d[i]), writes=[wbt], chan=wbt)
            pp, ppt = r_ps.next()
            for k in range(KC):
                P.op('pe', lambda e, k=k, pp=pp, wb=wb: e.matmul(pp[0:2, 0:MCH], sc[:, k, :], wb[:, k, :],
                                                                start=(k == 0), stop=(k == KC - 1)),
                     reads=[wbt, 'sc'], writes=[ppt])
            P.op('dve', lambda e, i=i, pp=pp: e.tensor_tensor(out=mo[:, i * MCH:(i + 1) * MCH], in0=pp[0:2, 0:MCH],
                                                             in1=bsb[:, i * MCH:(i + 1) * MCH], op=ALU.add),
                 reads=[ppt, 'bsb'], pwrites=['mo'])
        P.op('sp', lambda e: e.dma_start(out=o_d, in_=mo[:]), reads=['mo'], writes=['o_d'], chan='o_d')
        P.finish('sp')
        P.emit()
    return nc


def run_mod(c, c_ctx, w_mod, b_mod):
    nc = build_mod_prog()
    cvec = np.ascontiguousarray(np.stack([c.reshape(KC, 128).T, c_ctx.reshape(KC, 128).T], axis=2)).astype(np.float32)
    in_maps = []
    for r in range(NCORE):
        ws = w_mod[:, :, r * MODC:(r + 1) * MODC]
        ws = ws.reshape(DEPTH, KC, 128, NMCH, MCH).transpose(0, 3, 2, 1, 4).reshape(DEPTH * NMCH, 128, KC, MCH)
        bs = b_mod[:, r * MODC:(r + 1) * MODC].reshape(1, DEPTH * MODC)
        in_maps.append({"cvec": cvec, "wm": np.ascontiguousarray(ws),
                        "bm": np.ascontiguousarray(np.concatenate([bs, bs], axis=0))})
    res = run_bass_kernel_spmd(nc, in_maps, core_ids=list(range(NCORE)))
    mod = np.zeros((DEPTH, 2, 9 * D), np.float32)
    for r in range(NCORE):
        o = res.results[r]["mod_out"].reshape(2, DEPTH, MODC)
        mod[:, :, r * MODC:(r + 1) * MODC] = o.transpose(1, 0, 2)
    return mod


TALL = SEQ + CTX
NTL = TALL // 128
QBW = 512
NQB = SEQ // QBW
NEG = -30000.0


def v3(ap, a, b):
    l = [list(x) for x in ap.ap]
    st, n = l[-1]
    assert n == a * b, (n, a, b)
    return bass.AP(ap.tensor, ap.offset, l[:-1] + [[st * b, a], [st, b]])


def token_blocks():
    return [(i * QBW, QBW) for i in range(NQB)] + [(SEQ, CTX)]


def build_odd_prog(debug=False):
    nc = bass.Bass("TRN2", target_bir_lowering=False)
    dq_d = nc.dram_tensor("dq", [128, 2, TALL], BF16, kind="ExternalOutput").ap() if debug else None
    dk_d = nc.dram_tensor("dk", [128, TALL], BF16, kind="ExternalOutput").ap() if debug else None
    uT_d = nc.dram_tensor("uT", [128, KC, TALL], BF16, kind="ExternalInput").ap()
    wq_d = nc.dram_tensor("wq", [128, KC, 4, 128], F32, kind="ExternalInput").ap()
    wk_d = nc.dram_tensor("wk", [128, KC, 2, 128], F32, kind="ExternalInput").ap()
    wv_d = nc.dram_tensor("wv", [128, KC, 128], F32, kind="ExternalInput").ap()
    rtab_d = nc.dram_tensor("rtab", [128, 2, 128], F32, kind="ExternalInput").ap()
    mask_d = nc.dram_tensor("masks", [128, 6, QBW], BF16, kind="ExternalInput").ap()
    ident_d = nc.dram_tensor("ident", [128, 128], BF16, kind="ExternalInput").ap()
    sink_d = nc.dram_tensor("sink", [128, 2], F32, kind="ExternalInput").ap()
    y_d = nc.dram_tensor("y_out", [128, 2, TALL], BF16, kind="ExternalOutput").ap()
    SCALE = 128.0 ** -0.5
    with ExitStack() as es:
        P = Prog(nc, es)
        sb = lambda name, shape, dt: es.enter_context(nc.sbuf_tensor(name, shape, dt))
        wq = sb("wq_sb", [128, KC, 4, 128], BF16)
        wk = sb("wk_sb", [128, KC, 2, 128], BF16)
        wv = sb("wv_sb", [128, KC, 128], BF16)
        rtab = sb("rtab_sb", [128, 2, 128], F32)
        masks = sb("masks_sb", [128, 6, QBW], BF16)
        ident = sb("ident_sb", [128, 128], BF16)
        onesb = sb("onesb", [128, 128], BF16)
        sink = sb("sink_sb", [128, 2], F32)
        esink = sb("esink", [128, 2], F32)
        qT = sb("qT", [128, 2, TALL], BF16)
        kT = sb("kT", [128, TALL], BF16)
        vtm = sb("vtm", [128, NTL, 128], BF16)
        yT = sb("yT", [128, 2, TALL], BF16)
        ubs = [sb(f"ub{i}", [128, KC, QBW], BF16) for i in range(2)]
        t1s = [sb(f"t1_{i}", [128, QBW], F32) for i in range(2)]
        t2s = [sb(f"t2_{i}", [128, QBW], F32) for i in range(2)]
        pts = [sb(f"pt{i}", [128, QBW], BF16) for i in range(3)]
        rcs = [sb(f"rc{i}", [128, QBW], F32) for i in range(2)]
        ps = [es.enter_context(nc.psum_tensor(f"ps{i}", [128, 512], F32)) for i in range(8)]
        r_ub = Ring(ubs, "ub"); r_t1 = Ring(t1s, "t1"); r_t2 = Ring(t2s, "t2"); r_pt = Ring(pts, "pt"); r_rc = Ring(rcs, "rc")
        r_psA = Ring(ps[0:4], "psA")
        r_psO = Ring(ps[4:6], "psO")
        r_psR = Ring(ps[6:8], "psR")

        P.op('pool', lambda e: e.dma_start(out=wq[:], in_=wq_d), writes=['wq'], chan='wq')
        P.op('pool', lambda e: e.dma_start(out=wk[:], in_=wk_d), writes=['wk'], chan='wk')
        P.op('pool', lambda e: e.dma_start(out=wv[:], in_=wv_d), writes=['wv'], chan='wv')
        P.op('sp', lambda e: e.dma_start(out=rtab[:], in_=rtab_d), writes=['rtab'], chan='rtab')
        P.op('sp', lambda e: e.dma_start(out=masks[:], in_=mask_d), writes=['masks'], chan='masks')
        P.op('sp', lambda e: e.dma_start(out=ident[:], in_=ident_d), writes=['ident'], chan='ident')
        P.op('sp', lambda e: e.dma_start(out=sink[:], in_=sink_d), writes=['sink'], chan='sink')
        P.op('pool', lambda e: e.memset(onesb[:], 1.0), writes=['onesb'])
        P.op('act', lambda e: e.activation(out=esink[:], in_=sink[:], func=AF.Exp), reads=['sink'], writes=['esink'])

        def proj_fm(ub, ubt, w_ap, wtag, width):
            pp, ppt = r_psA.next()
            for k in range(KC):
                P.op('pe', lambda e, k=k, pp=pp: e.matmul(pp[:, 0:width], w_ap(k), ub[:, k, 0:width],
                                                         start=(k == 0), stop=(k == KC - 1)),
                     reads=[ubt, wtag], writes=[ppt])
            return pp, ppt

        def rope_store(px, pxt, pw, pwt, dst, dtag, r0, scale):
            t1, t1t = r_t1.next()
            t2, t2t = r_t2.next()
            for half in range(2):
                p0, p1 = half * 64, half * 64 + 64
                if half == 0:
                    ctab = bc_last(rtab[p0:p1, 0, r0:r0 + 8], 64)
                    stab = bc_last(rtab[p0:p1, 1, r0:r0 + 8], 64)
                else:
                    ctab = bc_mid(rtab[p0:p1, 0, 0:64], 8)
                    stab = bc_mid(rtab[p0:p1, 1, 0:64], 8)
                P.op('dve', lambda e, p0=p0, p1=p1, ctab=ctab, t1=t1: e.scalar_tensor_tensor(
                    out=v3(t1[p0:p1, :], 8, 64), in0=v3(px[p0:p1, :], 8, 64), scalar=scale, in1=ctab,
                    op0=ALU.mult, op1=ALU.mult), reads=[pxt, 'rtab'], pwrites=[t1t])
                P.op('dve', lambda e, p0=p0, p1=p1, stab=stab, t2=t2: e.scalar_tensor_tensor(
                    out=v3(t2[p0:p1, :], 8, 64), in0=v3(pw[p0:p1, :], 8, 64), scalar=scale, in1=stab,
                    op0=ALU.mult, op1=ALU.mult), reads=[pwt, 'rtab'], pwrites=[t2t])
            P.op('pool', lambda e, t1=t1, t2=t2: e.tensor_tensor(out=dst, in0=t1[:, :], in1=t2[:, :], op=ALU.add),
                 reads=[t1t, t2t], pwrites=[dtag])

        for bi, (t0, width) in enumerate(token_blocks()):
            ub, ubt = r_ub.next()
            P.op('sp', lambda e, ub=ub, t0=t0, width=width: e.dma_start(out=ub[:, :, 0:width], in_=uT_d[:, :, t0:t0 + width]),
                 writes=[ubt], chan=ubt)
            is_ctx = t0 >= SEQ
            r0 = t0 // 64
            for h in range(2):
                px, pxt = proj_fm(ub, ubt, lambda k, h=h: wq[:, k, 2 * h, :], 'wq', width)
                if is_ctx:
                    P.op('act', lambda e, px=px, h=h, t0=t0, width=width: e.activation(
                        out=qT[:, h, t0:t0 + width], in_=px[:, 0:width], func=AF.Identity, scale=SCALE),
                        reads=[pxt], pwrites=['qT'])
                else:
                    pw, pwt = proj_fm(ub, ubt, lambda k, h=h: wq[:, k, 2 * h + 1, :], 'wq', width)
                    rope_store(px, pxt, pw, pwt, qT[:, h, t0:t0 + width], 'qT', r0, SCALE)
            px, pxt = proj_fm(ub, ubt, lambda k: wk[:, k, 0, :], 'wk', width)
            if is_ctx:
                P.op('act', lambda e, px=px, t0=t0, width=width: e.activation(
                    out=kT[:, t0:t0 + width], in_=px[:, 0:width], func=AF.Identity), reads=[pxt], pwrites=['kT'])
            else:
                pw, pwt = proj_fm(ub, ubt, lambda k: wk[:, k, 1, :], 'wk', width)
                rope_store(px, pxt, pw, pwt, kT[:, t0:t0 + width], 'kT', r0, 1.0)
            pp, ppt = r_psA.next()
            nti = width // 128
            for i in range(nti):
                for k in range(KC):
                    P.op('pe', lambda e, k=k, i=i, pp=pp, ub=ub: e.matmul(pp[:, i * 128:(i + 1) * 128], ub[:, k, i * 128:(i + 1) * 128],
                                                                         wv[:, k, :], start=(k == 0), stop=(k == KC - 1)),
                         reads=[ubt, 'wv'], pwrites=[ppt])
            tt0 = t0 // 128
            P.op('act', lambda e, pp=pp, tt0=tt0, nti=nti: e.activation(
                out=vtm[:, tt0:tt0 + nti, :], in_=v3(pp[:, 0:nti * 128], nti, 128), func=AF.Identity),
                reads=[ppt], pwrites=['vtm'])

        if debug:
            P.op('sp', lambda e: e.dma_start(out=dq_d, in_=qT[:]), reads=['qT'], writes=['dq_d'], chan='dq')
            P.op('sp', lambda e: e.dma_start(out=dk_d, in_=kT[:]), reads=['kT'], writes=['dk_d'], chan='dk')
        def attend(h, q0, qw, keytiles):
            po, pot = r_psO.next()
            pr, prt = r_psR.next()
            n = len(keytiles)
            for i, (kt, mi) in enumerate(keytiles):
                pS, pst = r_psA.next()
                P.op('pe', lambda e, pS=pS, kt=kt, mi=mi: e.matmul(pS[:, 0:qw], kT[:, kt * 128:(kt + 1) * 128], qT[:, h, q0:q0 + qw],
                                                                  start=True, stop=(mi is None)),
                     reads=['kT', 'qT'], writes=[pst])
                if mi is not None:
                    P.op('pe', lambda e, pS=pS, mi=mi: e.matmul(pS[:, 0:qw], ident[:, :], masks[:, mi, 0:qw], start=False, stop=True),
                         reads=['ident', 'masks'], writes=[pst])
                pt, ptt = r_pt.next()
                P.op('act', lambda e, pS=pS, pt=pt: e.activation(out=pt[:, 0:qw], in_=pS[:, 0:qw], func=AF.Exp),
                     reads=[pst], writes=[ptt])
                P.op('pe', lambda e, pt=pt, kt=kt, i=i: e.matmul(po[:, 0:qw], vtm[:, kt, :], pt[:, 0:qw], start=(i == 0), stop=(i == n - 1)),
                     reads=[ptt, 'vtm'], writes=[pot])
                P.op('pe', lambda e, pt=pt, i=i: e.matmul(pr[:, 0:qw], onesb[:, :], pt[:, 0:qw], start=(i == 0), stop=(i == n - 1)),
                     reads=[ptt, 'onesb'], writes=[prt])
            rc, rct = r_rc.next()
            P.op('dve', lambda e, rc=rc: e.tensor_scalar(out=rc[:, 0:qw], in0=pr[:, 0:qw], scalar1=esink[:, h:h + 1], scalar2=None, op0=ALU.add),
                 reads=[prt, 'esink'], writes=[rct])
            P.op('dve', lambda e, rc=rc: e.reciprocal(out=rc[:, 0:qw], in_=rc[:, 0:qw]), reads=[rct], writes=[rct])
            P.op('dve', lambda e, rc=rc: e.tensor_tensor(out=yT[:, h, q0:q0 + qw], in0=po[:, 0:qw], in1=rc[:, 0:qw], op=ALU.mult),
                 reads=[pot, rct], pwrites=['yT'])

        for h in range(2):
            for qb in range(NQB):
                kts = []
                for rel in range(-1, 5):
                    kt = qb * 4 + rel
                    if 0 <= kt < SEQ // 128:
                        kts.append((kt, rel + 1))
                kts += [(64, None), (65, None)]
                attend(h, qb * QBW, QBW, kts)
            attend(h, SEQ, CTX, [(64, None), (65, None)])
            P.op('sp', lambda e, h=h: e.dma_start(out=y_d[:, h, :], in_=yT[:, h, :]), reads=['yT'], pwrites=['y_d'], chan=f'y{h}')
        P.finish('sp')
        P.emit()
    return nc


def odd_consts():
    inv = 10000.0 ** (-np.arange(0, 64, 2, dtype=np.float32) / 64.0)
    rtab = np.zeros((128, 2, 128), np.float32)
    for p in range(128):
        f = inv[p % 32]
        sign = -1.0 if (p % 64) < 32 else 1.0
        pos = np.arange(128, dtype=np.float32)
        ang = pos * f
        rtab[p, 0, :] = np.cos(ang)
        rtab[p, 1, :] = sign * np.sin(ang)
    masks = np.zeros((128, 6, QBW), np.float32)
    for mi in range(6):
        rel = mi - 1
        kpos = rel * 128 + np.arange(128)[:, None]
        qpos = np.arange(QBW)[None, :]
        masks[:, mi, :] = np.where(np.abs(kpos - qpos) <= 128, 0.0, NEG)
    ident = np.eye(128, dtype=np.float32)
    return rtab, masks.astype(NPBF), ident.astype(NPBF)


def swap_perm():
    p = np.arange(128)
    return np.where((p % 64) < 32, p + 32, p - 32)


def lay_w(w):
    n = w.shape[1]
    return np.ascontiguousarray(w.reshape(KC, 128, n).transpose(1, 0, 2))


def run_odd(uT_all, w_qkv, sink, debug=False):
    nc = build_odd_prog(debug)
    rtab, masks, ident = odd_consts()
    perm = swap_perm()
    in_maps = []
    for j in range(NCORE):
        kv = j // 2
        wq = []
        for h in (2 * j, 2 * j + 1):
            cols = w_qkv[:, h * 128:(h + 1) * 128]
            wq += [cols, cols[:, perm]]
        wq = np.stack([lay_w(c) for c in wq], axis=2)
        kc = w_qkv[:, D + kv * 128: D + (kv + 1) * 128]
        wk = np.stack([lay_w(kc), lay_w(kc[:, perm])], axis=2)
        wv = lay_w(w_qkv[:, D + 512 + kv * 128: D + 512 + (kv + 1) * 128])
        sk = np.broadcast_to(sink[2 * j:2 * j + 2][None, :], (128, 2)).astype(np.float32)
        in_maps.append({"uT": uT_all, "wq": np.ascontiguousarray(wq), "wk": np.ascontiguousarray(wk), "wv": wv,
                        "rtab": rtab, "masks": masks, "ident": ident, "sink": np.ascontiguousarray(sk)})
    res = run_bass_kernel_spmd(nc, in_maps, core_ids=list(range(NCORE)))
    yT = np.zeros((128, KC, TALL), NPBF)
    for j in range(NCORE):
        yT[:, 2 * j:2 * j + 2, :] = res.results[j]["y_out"]
    if debug:
        return yT, res.results
    return yT


NNB = 20


def na_bias_mats(rpb_h):
    def mat(R, kt):
        kr = 2 * kt + np.arange(2)[:, None, None, None]
        kc = np.arange(64)[None, :, None, None]
        r = R + np.arange(8)[None, None, :, None]
        qc = np.arange(64)[None, None, None, :]
        r0 = np.clip(r - 4, 0, 120)
        c0 = np.clip(qc - 8, 0, 48)
        valid = (kr >= r0) & (kr < r0 + 8) & (kc >= c0) & (kc < c0 + 16)
        ri = np.clip(kr - r + 7, 0, 14)
        ci = np.clip(kc - qc + 15, 0, 30)
        ri, ci, valid = np.broadcast_arrays(ri, ci, valid)
        return np.where(valid, rpb_h[ri, ci], NEG).reshape(128, 512)
    mats = [mat(8, 4 + rel) for rel in range(-2, 6)]
    mats += [mat(0, kt) for kt in range(0, 6)]
    mats += [mat(120, kt) for kt in range(58, 64)]
    return np.ascontiguousarray(np.stack(mats, axis=1)).astype(NPBF)


def na_keytiles(qb):
    if qb == 0:
        return [(kt, 8 + kt) for kt in range(0, 6)]
    if qb == NQB - 1:
        return [(kt, 14 + kt - 58) for kt in range(58, 64)]
    return [(4 * qb + rel, rel + 2) for rel in range(-2, 6)]


def build_na_prog():
    nc = bass.Bass("TRN2", target_bir_lowering=False)
    uT_d = nc.dram_tensor("uT", [128, KC, TALL], BF16, kind="ExternalInput").ap()
    w_d = nc.dram_tensor("w3", [128, KC, 3, 128], F32, kind="ExternalInput").ap()
    bias_d = nc.dram_tensor("nabias", [128, NNB, QBW], BF16, kind="ExternalInput").ap()
    ident_d = nc.dram_tensor("ident", [128, 128], BF16, kind="ExternalInput").ap()
    y_d = nc.dram_tensor("y_out", [128, TALL], BF16, kind="ExternalOutput").ap()
    SCALE = 128.0 ** -0.5
    with ExitStack() as es:
        P = Prog(nc, es)
        sb = lambda name, shape, dt: es.enter_context(nc.sbuf_tensor(name, shape, dt))
        w3 = sb("w3_sb", [128, KC, 3, 128], BF16)
        biasm = sb("bias_sb", [128, NNB, QBW], BF16)
        ident = sb("ident_sb", [128, 128], BF16)
        onesb = sb("onesb", [128, 128], BF16)
        qT = sb("qT", [128, TALL], BF16)
        kT = sb("kT", [128, TALL], BF16)
        vtm = sb("vtm", [128, NTL, 128], BF16)
        yT = sb("yT", [128, TALL], BF16)
        ubs = [sb(f"ub{i}", [128, KC, QBW], BF16) for i in range(2)]
        pts = [sb(f"pt{i}", [128, QBW], BF16) for i in range(3)]
        rcs = [sb(f"rc{i}", [128, QBW], F32) for i in range(2)]
        ps = [es.enter_context(nc.psum_tensor(f"ps{i}", [128, 512], F32)) for i in range(8)]
        r_ub = Ring(ubs, "ub"); r_pt = Ring(pts, "pt"); r_rc = Ring(rcs, "rc")
        r_psA = Ring(ps[0:4], "psA"); r_psO = Ring(ps[4:6], "psO"); r_psR = Ring(ps[6:8], "psR")
        P.op('pool', lambda e: e.dma_start(out=w3[:], in_=w_d), writes=['w3'], chan='w3')
        P.op('sp', lambda e: e.dma_start(out=biasm[:], in_=bias_d), writes=['biasm'], chan='biasm')
        P.op('sp', lambda e: e.dma_start(out=ident[:], in_=ident_d), writes=['ident'], chan='ident')
        P.op('pool', lambda e: e.memset(onesb[:], 1.0), writes=['onesb'])
        for bi, (t0, width) in enumerate(token_blocks()):
            ub, ubt = r_ub.next()
            P.op('sp', lambda e, ub=ub, t0=t0, width=width: e.dma_start(out=ub[:, :, 0:width], in_=uT_d[:, :, t0:t0 + width]),
                 writes=[ubt], chan=ubt)
            for g, (dst, dtag, sc) in enumerate(((qT, 'qT', SCALE), (kT, 'kT', 1.0))):
                pp, ppt = r_psA.next()
                for k in range(KC):
                    P.op('pe', lambda e, k=k, pp=pp, g=g, ub=ub, width=width: e.matmul(
                        pp[:, 0:width], w3[:, k, g, :], ub[:, k, 0:width], start=(k == 0), stop=(k == KC - 1)),
                        reads=[ubt, 'w3'], writes=[ppt])
                P.op('act', lambda e, pp=pp, dst=dst, sc=sc, t0=t0, width=width: e.activation(
                    out=dst[:, t0:t0 + width], in_=pp[:, 0:width], func=AF.Identity, scale=sc), reads=[ppt], pwrites=[dtag])
            pp, ppt = r_psA.next()
            nti = width // 128
            for i in range(nti):
                for k in range(KC):
                    P.op('pe', lambda e, k=k, i=i, pp=pp, ub=ub: e.matmul(pp[:, i * 128:(i + 1) * 128], ub[:, k, i * 128:(i + 1) * 128],
                                                                         w3[:, k, 2, :], start=(k == 0), stop=(k == KC - 1)),
                         reads=[ubt, 'w3'], pwrites=[ppt])
            tt0 = t0 // 128
            P.op('dve', lambda e, pp=pp, tt0=tt0, nti=nti: e.tensor_copy(
                out=vtm[:, tt0:tt0 + nti, :], in_=v3(pp[:, 0:nti * 128], nti, 128)), reads=[ppt], pwrites=['vtm'])

        def attend(q0, qw, keytiles):
            po, pot = r_psO.next()
            pr, prt = r_psR.next()
            n = len(keytiles)
            for i, (kt, mi) in enumerate(keytiles):
                pS, pst = r_psA.next()
                P.op('pe', lambda e, pS=pS, kt=kt, mi=mi: e.matmul(pS[:, 0:qw], kT[:, kt * 128:(kt + 1) * 128], qT[:, q0:q0 + qw],
                                                                  start=True, stop=(mi is None)),
                     reads=['kT', 'qT'], writes=[pst])
                if mi is not None:
                    P.op('pe', lambda e, pS=pS, mi=mi: e.matmul(pS[:, 0:qw], ident[:, :], biasm[:, mi, 0:qw], start=False, stop=True),
                         reads=['ident', 'biasm'], writes=[pst])
                pt, ptt = r_pt.next()
                P.op('act', lambda e, pS=pS, pt=pt: e.activation(out=pt[:, 0:qw], in_=pS[:, 0:qw], func=AF.Exp),
                     reads=[pst], writes=[ptt])
                P.op('pe', lambda e, pt=pt, kt=kt, i=i: e.matmul(po[:, 0:qw], vtm[:, kt, :], pt[:, 0:qw], start=(i == 0), stop=(i == n - 1)),
                     reads=[ptt, 'vtm'], writes=[pot])
                P.op('pe', lambda e, pt=pt, i=i: e.matmul(pr[:, 0:qw], onesb[:, :], pt[:, 0:qw], start=(i == 0), stop=(i == n - 1)),
                     reads=[ptt, 'onesb'], writes=[prt])
            rc, rct = r_rc.next()
            P.op('dve', lambda e, rc=rc: e.reciprocal(out=rc[:, 0:qw], in_=pr[:, 0:qw]), reads=[prt], writes=[rct])
            P.op('dve', lambda e, rc=rc: e.tensor_tensor(out=yT[:, q0:q0 + qw], in0=po[:, 0:qw], in1=rc[:, 0:qw], op=ALU.mult),
                 reads=[pot, rct], pwrites=['yT'])

        for qb in range(NQB):
            attend(qb * QBW, QBW, na_keytiles(qb) + [(64, None), (65, None)])
        attend(SEQ, CTX, [(64, None), (65, None)])
        P.op('sp', lambda e: e.dma_start(out=y_d, in_=yT[:]), reads=['yT'], writes=['y_d'], chan='y')
        P.finish('sp')
        P.emit()
    return nc


def run_na(uT_all, w_in, rpb):
    nc = build_na_prog()
    ident = np.eye(128, dtype=np.float32).astype(NPBF)
    in_maps = []
    for j in range(NCORE):
        cols = [w_in[:, g * 1024 + j * 128: g * 1024 + (j + 1) * 128] for g in range(3)]
        w3 = np.stack([lay_w(c) for c in cols], axis=2)
        in_maps.append({"uT": uT_all, "w3": np.ascontiguousarray(w3), "nabias": na_bias_mats(rpb[j]), "ident": ident})
    res = run_bass_kernel_spmd(nc, in_maps, core_ids=list(range(NCORE)))
    return np.stack([res.results[j]["y_out"] for j in range(NCORE)], axis=1)


HC = 64
NCH = TALL // HC
NCTXCH = CTX // HC


def chunk_col(ap2d, nch, idx):
    l = [list(x) for x in ap2d.ap]
    st, n = l[-1]
    assert n == nch * HC
    return bass.AP(ap2d.tensor, ap2d.offset + idx * st, l[:-1] + [[st * HC, nch], [0, HC]])


def chunk_pick(ap2d, nch, idx):
    l = [list(x) for x in ap2d.ap]
    st, n = l[-1]
    return bass.AP(ap2d.tensor, ap2d.offset + idx * st, l[:-1] + [[st * HC, nch]])


def sub_view(ap2d, nch, sub0, nsub, col, bcast):
    l = [list(x) for x in ap2d.ap]
    st, n = l[-1]
    assert n == nch * HC
    if bcast:
        return bass.AP(ap2d.tensor, ap2d.offset + (sub0 * SUB + col) * st, l[:-1] + [[st * HC, nch], [st * SUB, nsub], [0, SUB]])
    return bass.AP(ap2d.tensor, ap2d.offset + sub0 * SUB * st, l[:-1] + [[st * HC, nch], [st * SUB, nsub], [st, SUB]])


SUB = 16
NSUB = HC // SUB
CLAMP = 40.0


def build_hgrn_prog(e_layer, stop=99, nblk=None, limit=None):
    nc = bass.Bass("TRN2", target_bir_lowering=False)
    uT_d = nc.dram_tensor("uT", [128, KC, TALL], BF16, kind="ExternalInput").ap()
    w_d = nc.dram_tensor("w5", [128, KC, 5, 128], F32, kind="ExternalInput").ap()
    lbl_d = nc.dram_tensor("lbl", [128, 2, 2], F32, kind="ExternalInput").ap()
    hgn_d = nc.dram_tensor("hgn", [128, 1], F32, kind="ExternalInput").ap()
    ident_d = nc.dram_tensor("ident", [128, 128], BF16, kind="ExternalInput").ap()
    tri_d = nc.dram_tensor("tri", [128, 4, HC], F32, kind="ExternalInput").ap()
    rmask_d = nc.dram_tensor("rmask", [128, QBW], F32, kind="ExternalInput").ap()
    y_d = nc.dram_tensor("y_out", [128, TALL], BF16, kind="ExternalOutput").ap()
    with ExitStack() as es:
        P = Prog(nc, es)
        sb = lambda name, shape, dt: es.enter_context(nc.sbuf_tensor(name, shape, dt))
        P.limit = limit
        w5 = sb("w5_sb", [128, KC, 5, 128], BF16)
        lbl = sb("lbl_sb", [128, 2, 2], F32)
        lb = sb("lb_sb", [128, 2], F32)
        oml = sb("oml_sb", [128, 2], F32)
        hgn = sb("hgn_sb", [128, 1], F32)
        ident = sb("ident_sb", [128, 128], BF16)
        tri = sb("tri_sb", [128, 4, HC], F32)
        rmask = sb("rmask_sb", [128, QBW], F32)
        ones32 = sb("ones32", [128, 128], F32)
        itm = sb("itm", [128, NTL, 128], BF16)
        Qp = sb("Qp", [128, TALL], BF16)
        attmEO = [sb("attmE", [128, NTL, HC], BF16), sb("attmO", [128, NTL, HC], BF16)]
        dS = sb("dS", [128, NCH, 128], BF16)
        Dall = sb("Dall", [128, NCH], F32)
        oacc = sb("oacc", [128, TALL], F32)
        ub = sb("ub", [128, KC, QBW], BF16)
        T = [sb(f"T{i}", [128, QBW], F32) for i in range(14)]
        KtT = sb("KtT", [128, QBW], BF16)
        Kj = [sb(f"Kj{i}", [128, QBW], BF16) for i in range(3)]
        Kd = sb("Kd", [128, QBW], BF16)
        Qd = sb("Qd", [128, QBW], BF16)
        Qsub = sb("Qsub", [128, QBW], BF16)
        KtmEO = [sb("KtmE", [128, 4, 128], BF16), sb("KtmO", [128, 4, 128], BF16)]
        yb = sb("yb", [128, QBW], BF16)
        ps = [es.enter_context(nc.psum_tensor(f"ps{i}", [128, 512], F32)) for i in range(7)]
        psT = es.enter_context(nc.psum_tensor("psT", [128, 1024], BF16))
        r_psA = Ring(ps[0:3], "psA")
        r_psB = Ring(ps[3:7], "psB")

        P.op('pool', lambda e: e.dma_start(out=w5[:], in_=w_d), writes=['w5'], chan='w5')
        for nm, t_, d_ in (('lbl', lbl, lbl_d), ('hgn', hgn, hgn_d), ('ident', ident, ident_d), ('tri', tri, tri_d), ('rmask', rmask, rmask_d)):
            P.op('sp', lambda e, t_=t_, d_=d_: e.dma_start(out=t_[:], in_=d_), writes=[nm], chan=nm)
        P.op('pool', lambda e: e.memset(ones32[:], 1.0), writes=['ones32'])
        for t_ in KtmEO:
            P.op('pool', lambda e, t_=t_: e.memset(t_[:], 0.0), writes=['Ktm'])
        for t_ in attmEO:
            P.op('pool', lambda e, t_=t_: e.memset(t_[:], 0.0), writes=['attm'])
        P.op('pool', lambda e: e.memset(Qsub[:], 0.0), writes=['Qsub'])
        if e_layer == 1:
            P.op('dve', lambda e: e.tensor_tensor(out=lb[:], in0=lbl[:, :, 1], in1=lbl[:, :, 0], op=ALU.subtract), reads=['lbl'], writes=['lb'])
            P.op('act', lambda e: e.activation(out=lb[:], in_=lb[:], func=AF.Sigmoid), reads=['lb'], writes=['lb'])
            P.op('dve', lambda e: e.tensor_scalar(out=oml[:], in0=lb[:], scalar1=-1.0, scalar2=1.0, op0=ALU.mult, op1=ALU.add),
                 reads=['lb'], writes=['oml'])

        blocks = token_blocks()
        if nblk is not None:
            blocks = blocks[:nblk]

        def load_ub(t0, width):
            P.op('sp', lambda e: e.dma_start(out=ub[:, :, 0:width], in_=uT_d[:, :, t0:t0 + width]), writes=['ub'], chan='ub')

        def proj(g, width):
            pp, ppt = r_psA.next()
            for k in range(KC):
                P.op('pe', lambda e, k=k, pp=pp: e.matmul(pp[:, 0:width], w5[:, k, g, :], ub[:, k, 0:width],
                                                         start=(k == 0), stop=(k == KC - 1)),
                     reads=['ub', 'w5'], writes=[ppt])
            return pp, ppt

        def fslot(ch, d):
            return (ch + NCTXCH) % NCH if d == 0 else ch

        for d in range(2):
            for (t0, w) in blocks:
                nch = w // HC
                nti = w // 128
                ch0 = t0 // HC
                tt0 = t0 // 128
                load_ub(t0, w)
                pq, pqt = proj(0, w)
                pf, pft = proj(1 + d, w)
                if d == 0:
                    pi, pit = r_psB.next()
                    for i in range(nti):
                        for k in range(KC):
                            P.op('pe', lambda e, k=k, i=i, pi=pi: e.matmul(pi[:, i * 128:(i + 1) * 128], ub[:, k, i * 128:(i + 1) * 128],
                                                                         w5[:, k, 3, :], start=(k == 0), stop=(k == KC - 1)),
                                 reads=['ub', 'w5'], pwrites=[pit])
                    P.op('act', lambda e, pi=pi, tt0=tt0, nti=nti: e.activation(
                        out=itm[:, tt0:tt0 + nti, :], in_=v3(pi[:, 0:nti * 128], nti, 128), func=AF.Identity),
                        reads=[pit], pwrites=['itm'])
                P.op('act', lambda e, pf=pf, w=w: e.activation(out=T[0][:, 0:w], in_=pf[:, 0:w], func=AF.Sigmoid), reads=[pft], writes=['T0'])
                if e_layer == 1:
                    P.op('dve', lambda e, w=w, d=d: e.tensor_scalar(out=T[0][:, 0:w], in0=T[0][:, 0:w], scalar1=oml[:, d:d + 1],
                                                                   scalar2=lb[:, d:d + 1], op0=ALU.mult, op1=ALU.add),
                         reads=['T0', 'oml', 'lb'], writes=['T0'])
                P.op('act', lambda e, w=w: e.activation(out=T[1][:, 0:w], in_=T[0][:, 0:w], func=AF.Ln), reads=['T0'], writes=['T1'])
                P.op('dve', lambda e, w=w: e.tensor_scalar(out=T[2][:, 0:w], in0=T[0][:, 0:w], scalar1=-1.0, scalar2=1.0,
                                                          op0=ALU.mult, op1=ALU.add), reads=['T0'], writes=['T2'])
                P.op('dve', lambda e, w=w: e.tensor_tensor_scan(out=T[3][:, 0:w], data0=rmask[:, 0:w], data1=T[1][:, 0:w], initial=0.0,
                                                               op0=ALU.mult, op1=ALU.add), reads=['T1', 'rmask'], writes=['T3'])
                if d == 0:
                    c, ct = T[3], 'T3'
                else:
                    P.op('dve', lambda e, w=w: e.tensor_tensor(out=T[4][:, 0:w], in0=T[1][:, 0:w], in1=T[3][:, 0:w], op=ALU.subtract),
                         reads=['T1', 'T3'], writes=['T4'])
                    P.op('dve', lambda e, w=w, nch=nch: e.tensor_tensor(out=v3(T[4][:, 0:w], nch, HC), in0=v3(T[4][:, 0:w], nch, HC),
                                                                       in1=chunk_col(T[3][:, 0:w], nch, HC - 1), op=ALU.add),
                         reads=['T4', 'T3'], writes=['T4'])
                    c, ct = T[4], 'T4'
                cw = c[:, 0:w]
                s0 = fslot(ch0, d)
                if d == 0:
                    qs0, bcol = 1, -1
                else:
                    qs0, bcol = 0, SUB
                nsb = w // SUB
                TK = [T[8], T[12], T[13]]
                TKt = ['T8', 'T12', 'T13']
                P.op('dve', lambda e, w=w, nch=nch, cw=cw: e.tensor_tensor(out=v3(T[6][:, 0:w], nch, HC), in0=chunk_col(T[3][:, 0:w], nch, HC - 1),
                                                                          in1=v3(cw, nch, HC), op=ALU.subtract),
                     reads=[ct, 'T3'], writes=['T6'])
                P.op('dve', lambda e, w=w, nch=nch, cw=cw, qs0=qs0, bcol=bcol: e.tensor_tensor(
                    out=sub_view(T[7][:, 0:w], nch, qs0, 3, 0, False), in0=sub_view(cw, nch, qs0, 3, 0, False),
                    in1=sub_view(cw, nch, qs0, 3, bcol, True), op=ALU.subtract), reads=[ct], writes=['T7'])
                for jj in range(3):
                    bj = (jj + 1) * SUB - 1 if d == 0 else (jj + 1) * SUB
                    P.op('dve', lambda e, w=w, nch=nch, cw=cw, bj=bj, jj=jj: e.tensor_tensor(out=v3(TK[jj][:, 0:w], nch, HC), in0=chunk_col(cw, nch, bj),
                                                                                          in1=v3(cw, nch, HC), op=ALU.subtract),
                         reads=[ct], writes=[TKt[jj]])
                    P.op('dve', lambda e, w=w, jj=jj: e.tensor_scalar(out=TK[jj][:, 0:w], in0=TK[jj][:, 0:w], scalar1=0.0, scalar2=None, op0=ALU.min),
                         reads=[TKt[jj]], writes=[TKt[jj]])
                P.op('dve', lambda e, w=w, nsb=nsb, cw=cw: e.tensor_tensor(
                    out=v3(T[9][:, 0:w], nsb, SUB), in0=v3(cw, nsb, SUB),
                    in1=bass.AP(cw.tensor, cw.offset + SUB // 2, [list(cw.ap[0]), [SUB, nsb], [0, SUB]]), op=ALU.subtract),
                    reads=[ct], writes=['T9'])
                P.op('dve', lambda e, w=w: e.tensor_scalar(out=T[9][:, 0:w], in0=T[9][:, 0:w], scalar1=-CLAMP, scalar2=CLAMP,
                                                          op0=ALU.max, op1=ALU.min), reads=['T9'], writes=['T9'])
                P.op('act', lambda e, w=w, cw=cw: e.activation(out=T[5][:, 0:w], in_=cw, func=AF.Exp), reads=[ct], writes=['T5'])
                P.op('act', lambda e, w=w, nch=nch, s0=s0: e.activation(out=Dall[:, s0:s0 + nch], in_=chunk_pick(T[3][:, 0:w], nch, HC - 1), func=AF.Exp),
                     reads=['T3'], pwrites=['Dall'])
                P.op('act', lambda e, w=w: e.activation(out=T[6][:, 0:w], in_=T[6][:, 0:w], func=AF.Exp), reads=['T6'], writes=['T6'])
                P.op('act', lambda e, w=w, nch=nch, qs0=qs0: e.activation(out=sub_view(T[7][:, 0:w], nch, qs0, 3, 0, False),
                                                                         in_=sub_view(T[7][:, 0:w], nch, qs0, 3, 0, False), func=AF.Exp),
                     reads=['T7'], writes=['T7'])
                for jj in range(3):
                    P.op('act', lambda e, w=w, jj=jj: e.activation(out=TK[jj][:, 0:w], in_=TK[jj][:, 0:w], func=AF.Exp), reads=[TKt[jj]], writes=[TKt[jj]])
                P.op('act', lambda e, w=w: e.activation(out=T[10][:, 0:w], in_=T[9][:, 0:w], func=AF.Exp), reads=['T9'], writes=['T10'])
                P.op('act', lambda e, w=w: e.activation(out=T[11][:, 0:w], in_=T[9][:, 0:w], func=AF.Exp, scale=-1.0), reads=['T9'], writes=['T11'])
                P.op('dve', lambda e, pq=pq, t0=t0, w=w: e.tensor_tensor(out=Qp[:, t0:t0 + w], in0=pq[:, 0:w], in1=T[5][:, 0:w], op=ALU.mult),
                     reads=[pqt, 'T5'], pwrites=['Qp'])
                P.op('pool', lambda e, w=w: e.tensor_tensor(out=KtT[:, 0:w], in0=T[2][:, 0:w], in1=T[6][:, 0:w], op=ALU.mult),
                     reads=['T2', 'T6'], writes=['KtT'])
                P.op('dve', lambda e, w=w, nch=nch, qs0=qs0, pq=pq: e.scalar_tensor_tensor(
                    out=sub_view(Qsub[:, 0:w], nch, qs0, 3, 0, False), in0=sub_view(T[7][:, 0:w], nch, qs0, 3, 0, False), scalar=1.0,
                    in1=sub_view(pq[:, 0:w], nch, qs0, 3, 0, False), op0=ALU.min, op1=ALU.mult), reads=['T7', pqt], writes=['Qsub'])
                for jj in range(3):
                    P.op('pool', lambda e, w=w, jj=jj: e.tensor_tensor(out=Kj[jj][:, 0:w], in0=TK[jj][:, 0:w], in1=T[2][:, 0:w], op=ALU.mult),
                         reads=[TKt[jj], 'T2'], writes=[f'Kj{jj}'])
                P.op('dve', lambda e, pq=pq, w=w: e.tensor_tensor(out=Qd[:, 0:w], in0=pq[:, 0:w], in1=T[10][:, 0:w], op=ALU.mult),
                     reads=[pqt, 'T10'], writes=['Qd'])
                P.op('pool', lambda e, w=w: e.tensor_tensor(out=Kd[:, 0:w], in0=T[2][:, 0:w], in1=T[11][:, 0:w], op=ALU.mult),
                     reads=['T2', 'T11'], writes=['Kd'])
                pa, pat = r_psB.next()
                for cl in range(nch):
                    i, p0 = cl // 2, (cl % 2) * HC
                    tk = cl * HC
                    P.op('pe', lambda e, pa=pa, i=i, p0=p0, tk=tk: e.matmul(pa[p0:p0 + HC, i * HC:(i + 1) * HC], Kd[:, tk:tk + HC], Qd[:, tk:tk + HC],
                                                                          start=True, stop=True), reads=['Kd', 'Qd'], pwrites=[pat])
                    dummy = 0 if d == 0 else NSUB - 1
                    for sj in range(NSUB):
                        c0 = tk + sj * SUB
                        if sj == dummy:
                            lh, rh, lt, rt = Kd, Qd, 'Kd', 'Qd'
                        else:
                            jj = sj - 1 if d == 0 else sj
                            lh, rh, lt, rt = Kj[jj], Qsub, f'Kj{jj}', 'Qsub'
                        P.op('pe', lambda e, pa=pa, i=i, p0=p0, tk=tk, c0=c0, sj=sj, lh=lh, rh=rh: e.matmul(
                            pa[p0:p0 + HC, 256 + i * HC + sj * SUB:256 + i * HC + (sj + 1) * SUB], lh[:, tk:tk + HC], rh[:, c0:c0 + SUB],
                            start=True, stop=True), reads=[lt, rt], pwrites=[pat])
                P.op('dve', lambda e, pa=pa, nti=nti, d=d: e.tensor_tensor(out=v3(T[0][:, 0:nti * HC], nti, HC), in0=v3(pa[:, 0:nti * HC], nti, HC),
                                                                          in1=bc_mid(tri[:, 2 * d, :], nti), op=ALU.mult),
                     reads=[pat, 'tri'], writes=['T0'])
                P.op('dve', lambda e, pa=pa, nti=nti, d=d: e.tensor_tensor(out=v3(T[1][:, 0:nti * HC], nti, HC), in0=v3(pa[:, 256:256 + nti * HC], nti, HC),
                                                                          in1=bc_mid(tri[:, 2 * d + 1, :], nti), op=ALU.mult),
                     reads=[pat, 'tri'], writes=['T1'])
                for hf in range(2):
                    P.op('pool', lambda e, nti=nti, hf=hf, tt0=tt0: e.tensor_tensor(
                        out=attmEO[hf][hf * HC:(hf + 1) * HC, tt0:tt0 + nti, :], in0=v3(T[0][hf * HC:(hf + 1) * HC, 0:nti * HC], nti, HC),
                        in1=v3(T[1][hf * HC:(hf + 1) * HC, 0:nti * HC], nti, HC), op=ALU.add),
                        reads=['T0', 'T1'], pwrites=['attm'])
                for i in range(nti):
                    P.op('pe', lambda e, i=i: e.transpose(psT[:, i * 128:(i + 1) * 128], KtT[:, i * 128:(i + 1) * 128], ident[:, :]),
                         reads=['KtT', 'ident'], pwrites=['psT'])
                for hf in range(2):
                    P.op('dve', lambda e, nti=nti, hf=hf: e.tensor_copy(out=KtmEO[hf][hf * HC:(hf + 1) * HC, 0:nti, :],
                                                                       in_=v3(psT[hf * HC:(hf + 1) * HC, 0:nti * 128], nti, 128)),
                         reads=['psT'], pwrites=['Ktm'])
                for b0 in range(0, nch, 4):
                    pd, pdt = r_psB.next()
                    for cl in range(b0, b0 + 4):
                        i = cl // 2
                        P.op('pe', lambda e, pd=pd, cl=cl, i=i, b0=b0, tt0=tt0: e.matmul(
                            pd[:, (cl - b0) * 128:(cl - b0 + 1) * 128], KtmEO[cl % 2][:, i, :], itm[:, tt0 + i, :],
                            start=True, stop=True), reads=['Ktm', 'itm'], pwrites=[pdt])
                    P.op('act', lambda e, pd=pd, s0=s0, b0=b0: e.activation(out=dS[:, s0 + b0:s0 + b0 + 4, :], in_=v3(pd[:, 0:512], 4, 128),
                                                                            func=AF.Identity), reads=[pdt], pwrites=['dS'])
            if stop <= 0:
                break
            for ee in range(128):
                if d == 0:
                    a1 = bass.AP(dS, ee, [[NCH * 128, 128], [128, NCH]])
                    a0 = Dall[:, :]
                else:
                    a1 = bass.AP(dS, (NCH - 1) * 128 + ee, [[NCH * 128, 128], [-128, NCH]])
                    a0 = bass.AP(Dall, NCH - 1, [[NCH, 128], [-1, NCH]])
                P.op('dve', lambda e, a1=a1, a0=a0: e.tensor_tensor_scan(out=a1, data0=a0, data1=a1, initial=0.0, op0=ALU.mult, op1=ALU.add),
                     reads=['dS', 'Dall'], pwrites=['dS'])
            if stop <= 1:
                break
            for (t0, w) in blocks:
                nch = w // HC
                ch0 = t0 // HC
                tt0 = t0 // 128
                po, pot = r_psB.next()
                for cl in range(nch):
                    i = cl // 2
                    tk = t0 + cl * HC
                    ch = ch0 + cl
                    if d == 0:
                        s = fslot(ch, 0)
                        prev = s - 1 if s > 0 else None
                    else:
                        prev = ch + 1 if ch < NCH - 1 else None
                    if prev is not None:
                        P.op('pe', lambda e, po=po, cl=cl, prev=prev, tk=tk: e.matmul(po[:, cl * HC:(cl + 1) * HC], dS[:, prev, :], Qp[:, tk:tk + HC],
                                                                                    start=True, stop=False), reads=['dS', 'Qp'], pwrites=[pot])
                    P.op('pe', lambda e, po=po, cl=cl, i=i, tt0=tt0, prev=prev: e.matmul(
                        po[:, cl * HC:(cl + 1) * HC], itm[:, tt0 + i, :], attmEO[cl % 2][:, tt0 + i, :],
                        start=(prev is None), stop=True), reads=['itm', 'attm'], pwrites=[pot])
                if d == 0:
                    P.op('act', lambda e, po=po, t0=t0, w=w: e.activation(out=oacc[:, t0:t0 + w], in_=po[:, 0:w], func=AF.Identity),
                         reads=[pot], pwrites=['oacc'])
                else:
                    P.op('dve', lambda e, po=po, t0=t0, w=w: e.tensor_tensor(out=oacc[:, t0:t0 + w], in0=po[:, 0:w], in1=oacc[:, t0:t0 + w], op=ALU.add),
                         reads=[pot, 'oacc'], pwrites=['oacc'])
        for (t0, w) in (blocks if stop > 2 else []):
            load_ub(t0, w)
            pg, pgt = proj(4, w)
            P.op('act', lambda e, pg=pg, w=w: e.activation(out=T[0][:, 0:w], in_=pg[:, 0:w], func=AF.Silu), reads=[pgt], writes=['T0'])
            P.op('act', lambda e, t0=t0, w=w: e.activation(out=T[1][:, 0:w], in_=oacc[:, t0:t0 + w], func=AF.Square), reads=['oacc'], writes=['T1'])
            pn, pnt = r_psB.next()
            P.op('pe', lambda e, pn=pn, w=w: e.matmul(pn[:, 0:w], ones32[:, :], T[1][:, 0:w], start=True, stop=True), reads=['T1', 'ones32'], writes=[pnt])
            P.op('dve', lambda e, pn=pn, w=w: e.tensor_scalar(out=T[2][:, 0:w], in0=pn[:, 0:w], scalar1=1.0 / 128, scalar2=EPS,
                                                             op0=ALU.mult, op1=ALU.add), reads=[pnt], writes=['T2'])
            P.op('act', lambda e, w=w: e.activation(out=T[2][:, 0:w], in_=T[2][:, 0:w], func=AF.Sqrt), reads=['T2'], writes=['T2'])
            P.op('dve', lambda e, w=w: e.reciprocal(out=T[2][:, 0:w], in_=T[2][:, 0:w]), reads=['T2'], writes=['T2'])
            P.op('dve', lambda e, t0=t0, w=w: e.scalar_tensor_tensor(out=T[3][:, 0:w], in0=oacc[:, t0:t0 + w], scalar=hgn[:, 0:1], in1=T[2][:, 0:w],
                                                                    op0=ALU.mult, op1=ALU.mult), reads=['oacc', 'T2', 'hgn'], writes=['T3'])
            P.op('pool', lambda e, w=w: e.tensor_tensor(out=yb[:, 0:w], in0=T[3][:, 0:w], in1=T[0][:, 0:w], op=ALU.mult),
                 reads=['T3', 'T0'], writes=['yb'])
            P.op('sp', lambda e, t0=t0, w=w: e.dma_start(out=y_d[:, t0:t0 + w], in_=yb[:, 0:w]), reads=['yb'], pwrites=['y_d'], chan='yb')
        P.finish('sp')
        P.emit()
    return nc


def hgrn_consts():
    p = np.arange(128)[:, None] % HC
    t = np.arange(HC)[None, :]
    same = (p // SUB) == (t // SUB)
    tri = np.stack([same & (p <= t), (p // SUB) < (t // SUB), same & (p >= t), (p // SUB) > (t // SUB)], axis=1).astype(np.float32)
    rmask = np.broadcast_to((np.arange(QBW) % HC != 0).astype(np.float32)[None, :], (128, QBW))
    return np.ascontiguousarray(tri), np.ascontiguousarray(rmask)


def run_hgrn(uT_all, w_in, lbl, hgn, e_layer, stop=99, ncores=NCORE):
    nc = build_hgrn_prog(e_layer, stop)
    ident = np.eye(128, dtype=np.float32).astype(NPBF)
    tri, rmask = hgrn_consts()
    in_maps = []
    for j in range(NCORE):
        cols = [w_in[:, (3 + g) * 1024 + j * 128: (3 + g) * 1024 + (j + 1) * 128] for g in range(5)]
        w5 = np.stack([lay_w(c) for c in cols], axis=2)
        lj = np.ascontiguousarray(lbl[:, :, j * 128:(j + 1) * 128].transpose(2, 0, 1)).astype(np.float32)
        hj = np.ascontiguousarray(hgn[j * 128:(j + 1) * 128].reshape(128, 1)).astype(np.float32)
        in_maps.append({"uT": uT_all, "w5": np.ascontiguousarray(w5), "lbl": lj, "hgn": hj, "ident": ident, "tri": tri, "rmask": rmask})
    in_maps = in_maps[:ncores]
    res = run_bass_kernel_spmd(nc, in_maps, core_ids=list(range(ncores)))
    return np.stack([res.results[j]["y_out"] for j in range(ncores)], axis=1)


def _fm(x):
    T = x.shape[0]
    return np.ascontiguousarray(x.T.reshape(KC, 128, T).transpose(1, 0, 2))


def _unfm(a):
    T = a.shape[2]
    return np.ascontiguousarray(a.transpose(1, 0, 2).reshape(D, T).T)


def _vec(v):
    return np.ascontiguousarray(v.reshape(KC, 128).T)


def _lay_win(w):
    g = w[:, :DFF].reshape(KC, 128, JC, 128)
    u = w[:, DFF:].reshape(KC, 128, JC, 128)
    return np.ascontiguousarray(np.concatenate([g.transpose(2, 1, 0, 3), u.transpose(2, 1, 0, 3)], axis=3))


def _lay_wout(w):
    return np.ascontiguousarray(w.reshape(JC, 128, KC, 128).transpose(2, 1, 0, 3))


def _lay_wo(w):
    return np.ascontiguousarray(w.reshape(KC, 128, KC, 128).transpose(2, 1, 0, 3))


def _run_token(stages_spec, h_list, y_all, vec_list, weights, want_u, final):
    stages = [(s, i) for i, s in enumerate(stages_spec)]
    nc = build_token_prog(stages, want_u=want_u, final=final)
    vecs = np.ascontiguousarray(np.stack([_vec(v) for v in vec_list], axis=1)).astype(np.float32)
    shared = {"vecs": vecs}
    for si, s in enumerate(stages_spec):
        if s == 'proj':
            shared[f"wo{si}"] = _lay_wo(weights[si])
        else:
            shared[f"wi{si}"] = _lay_win(weights[si][0])
            shared[f"wf{si}"] = _lay_wout(weights[si][1])
    in_maps = []
    for r in range(NCORE):
        m = dict(shared)
        m["h_in"] = h_list[r]
        if y_all is not None:
            m["y_in"] = np.ascontiguousarray(np.concatenate(
                [y_all[:, :, r * TLAT:(r + 1) * TLAT], y_all[:, :, SEQ + r * TCTX: SEQ + (r + 1) * TCTX]], axis=2))
        in_maps.append(m)
    res = run_bass_kernel_spmd(nc, in_maps, core_ids=list(range(NCORE)))
    return res.results


def _gather_u(results):
    u_all = np.zeros((128, KC, TALL), NPBF)
    for r in range(NCORE):
        uo = results[r]["u_out"]
        u_all[:, :, r * TLAT:(r + 1) * TLAT] = uo[:, :, :TLAT]
        u_all[:, :, SEQ + r * TCTX: SEQ + (r + 1) * TCTX] = uo[:, :, TLAT:]
    return u_all


def kernel(x, c, ctx, c_ctx, w_mod, b_mod, norm_g, w_ff_in, w_ff_out, w_in_even, w_out_even,
           na_rpb, hg_lb_logits, hg_norm_g, w_qkv_odd, w_o_odd, sink_odd, final_norm_g):
    f = lambda a: np.asarray(a, dtype=np.float32)
    x, c, ctx, c_ctx, w_mod, b_mod, norm_g = f(x), f(c), f(ctx), f(c_ctx), f(w_mod), f(b_mod), f(norm_g)
    w_ff_in, w_ff_out, w_in_even, w_out_even = f(w_ff_in), f(w_ff_out), f(w_in_even), f(w_out_even)
    na_rpb, hg_lb_logits, hg_norm_g = f(na_rpb), f(hg_lb_logits), f(hg_norm_g)
    w_qkv_odd, w_o_odd, sink_odd, final_norm_g = f(w_qkv_odd), f(w_o_odd), f(sink_odd), f(final_norm_g)

    mod = run_mod(c[0], c_ctx, w_mod, b_mod).reshape(DEPTH, 2, 3, 3, D)

    def ffn_vecs(l, sub):
        return [norm_g[l, sub], mod[l, 0, sub, 0], mod[l, 0, sub, 1], mod[l, 0, sub, 2],
                mod[l, 1, sub, 0], mod[l, 1, sub, 1], mod[l, 1, sub, 2]]

    def u_vecs(l):
        return [norm_g[l, 1], mod[l, 0, 1, 0], mod[l, 0, 1, 1], mod[l, 1, 1, 0], mod[l, 1, 1, 1]]

    h_list = []
    for r in range(NCORE):
        tok = np.concatenate([x[0, r * TLAT:(r + 1) * TLAT], ctx[0, r * TCTX:(r + 1) * TCTX]], axis=0)
        h_list.append(_fm(tok))

    res = _run_token(['ffn'], h_list, None, ffn_vecs(0, 0) + u_vecs(0), [(w_ff_in[0, 0], w_ff_out[0, 0])], True, False)
    out = None
    for l in range(DEPTH):
        h_list = [res[r]["h_out"] for r in range(NCORE)]
        u_all = _gather_u(res)
        if l % 2 == 0:
            e = l // 2
            ya = run_na(u_all, w_in_even[e], na_rpb[e])
            yb = run_hgrn(u_all, w_in_even[e], hg_lb_logits, hg_norm_g[e], e)
            y_all = np.ascontiguousarray(np.concatenate([ya, yb], axis=1))
            w_o = w_out_even[e]
        else:
            o = l // 2
            y_all = run_odd(u_all, w_qkv_odd[o], sink_odd[o])
            w_o = w_o_odd[o]
        gate_vecs = [mod[l, 0, 1, 2], mod[l, 1, 1, 2]]
        if l < DEPTH - 1:
            res = _run_token(['proj', 'ffn', 'ffn'], h_list, y_all,
                             gate_vecs + ffn_vecs(l, 2) + ffn_vecs(l + 1, 0) + u_vecs(l + 1),
                             [w_o, (w_ff_in[l, 1], w_ff_out[l, 1]), (w_ff_in[l + 1, 0], w_ff_out[l + 1, 0])], True, False)
        else:
            res = _run_token(['proj', 'ffn'], h_list, y_all, gate_vecs + ffn_vecs(l, 2) + [final_norm_g],
                             [w_o, (w_ff_in[l, 1], w_ff_out[l, 1])], False, True)
            out = np.zeros((1, SEQ, D), np.float32)
            for r in range(NCORE):
                out[0, r * TLAT:(r + 1) * TLAT] = _unfm(res[r]["o_out"])[:TLAT]
    return out
```

```python
import numpy as np
import ml_dtypes
from contextlib import ExitStack
import concourse.bass as bass
import concourse.mybir as mybir
from concourse.bass_utils import run_bass_kernel_spmd

F32 = mybir.dt.float32
BF16 = mybir.dt.bfloat16
AF = mybir.ActivationFunctionType
ALU = mybir.AluOpType
NPBF = ml_dtypes.bfloat16

D = 2048
KC = 16
DFF = 5632
JC = 44
NCORE = 8
SEQ = 8192
CTX = 256
TLAT = SEQ // NCORE
TCTX = CTX // NCORE
TT = TLAT + TCTX
TW = 352
NT = 3
EPS = 1e-6
DEPTH = 4


SAME_ENGINE_SYNC = False


class Prog:
    ENGS = {'pe': 'tensor', 'act': 'scalar', 'dve': 'vector', 'pool': 'gpsimd', 'sp': 'sync'}

    def __init__(self, nc, es):
        self.nc = nc
        self.es = es
        self.q = {e: [] for e in self.ENGS}
        self.sems = {}
        self.val = {}
        self.w = {}
        self.r = {}
        self.waited = {}
        self.limit = None
        self.nops = 0

    def sem(self, key):
        if key not in self.sems:
            self.sems[key] = self.es.enter_context(self.nc.semaphore("s_" + key))
            self.val[key] = 0
        return self.sems[key]

    def op(self, eng, fn, reads=(), writes=(), pwrites=(), chan=None):
        self.nops += 1
        if self.limit is not None and self.nops > self.limit:
            return None
        waits = {}

        def need(evs):
            for k, (v, e) in evs.items():
                if e == 'pe' and eng == 'pe':
                    continue
                if e == eng and not SAME_ENGINE_SYNC and v < self.val.get(eng, 0):
                    continue
                if waits.get(k, 0) < v:
                    waits[k] = v
        for t in reads:
            need(self.w.get(t, {}))
        for t in list(writes) + list(pwrites):
            need(self.w.get(t, {}))
            need(self.r.get(t, {}))
        wl = []
        for k, v in waits.items():
            if self.waited.get((eng, k), 0) >= v:
                continue
            self.waited[(eng, k)] = v
            wl.append((k, v))
        if chan is None:
            key, inc, ee = eng, 1, eng
        else:
            key, inc, ee = 'd_' + chan, 16, 'dma'
        self.sem(key)
        self.val[key] += inc
        v = self.val[key]
        for t in reads:
            self.r.setdefault(t, {})[key] = (v, ee)
        for t in writes:
            self.w[t] = {key: (v, ee)}
            self.r[t] = {}
        for t in pwrites:
            self.w.setdefault(t, {})[key] = (v, ee)
            self.r[t] = {}
        self.q[eng].append((wl, fn, key, inc))
        return (key, v)

    def finish(self, eng='sp'):
        wl = [(k, v) for k, v in self.val.items() if k.startswith('d_')]
        self.q[eng].append((wl, None, None, 0))

    def emit(self):
        nc = self.nc
        with nc.Block() as block:
            for e, attr in self.ENGS.items():
                items = self.q[e]

                def body(engine, items=items):
                    for wl, fn, key, inc in items:
                        for k, v in wl:
                            engine.wait_ge(self.sems[k], v)
                        if fn is not None:
                            ins = fn(engine)
                            ins.then_inc(self.sems[key], inc)
                getattr(block, attr)(body)


def bc_mid(ap, n):
    a = [list(x) for x in ap.ap]
    return bass.AP(ap.tensor, ap.offset, [a[0], [0, n]] + a[1:])


def bc_last(ap, n):
    a = [list(x) for x in ap.ap]
    return bass.AP(ap.tensor, ap.offset, a + [[0, n]])


class Ring:
    def __init__(self, bufs, name):
        self.bufs = bufs
        self.name = name
        self.i = 0

    def next(self):
        k = self.i % len(self.bufs)
        self.i += 1
        return self.bufs[k], f"{self.name}{k}"


def build_token_prog(stages, want_u=False, final=False):
    nc = bass.Bass("TRN2", target_bir_lowering=False)
    nv = 0
    for s in stages:
        nv += 2 if s[0] == 'proj' else 7
    if want_u:
        nv += 5
    if final:
        nv += 1
    h_in = nc.dram_tensor("h_in", [128, KC, TT], F32, kind="ExternalInput").ap()
    vecs_d = nc.dram_tensor("vecs", [128, nv, KC], F32, kind="ExternalInput").ap()
    wd = {}
    y_in = None
    for si, s in enumerate(stages):
        if s[0] == 'proj':
            wd[si] = nc.dram_tensor(f"wo{si}", [KC, 128, KC, 128], F32, kind="ExternalInput").ap()
            y_in = nc.dram_tensor("y_in", [128, KC, TT], BF16, kind="ExternalInput").ap()
        else:
            wd[si] = (nc.dram_tensor(f"wi{si}", [JC, 128, KC, 256], F32, kind="ExternalInput").ap(),
                      nc.dram_tensor(f"wf{si}", [KC, 128, JC, 128], F32, kind="ExternalInput").ap())
    hd = [h_in]
    for si in range(len(stages)):
        last = si == len(stages) - 1
        if last and not final:
            hd.append(nc.dram_tensor("h_out", [128, KC, TT], F32, kind="ExternalOutput").ap())
        else:
            hd.append(nc.dram_tensor(f"h_s{si}", [128, KC, TT], F32).ap())
    u_out = nc.dram_tensor("u_out", [128, KC, TT], BF16, kind="ExternalOutput").ap() if want_u else None
    o_out = nc.dram_tensor("o_out", [128, KC, TT], F32, kind="ExternalOutput").ap() if final else None

    tiles = [(i * TW, TW) for i in range(NT)]
    segs = []
    for (t0, w) in tiles:
        sg = []
        lat_end = min(max(TLAT - t0, 0), w)
        if lat_end > 0:
            sg.append((0, lat_end, 0))
        if lat_end < w:
            sg.append((lat_end, w, 1))
        segs.append(sg)

    with ExitStack() as es:
        P = Prog(nc, es)
        sb = lambda name, shape, dt: es.enter_context(nc.sbuf_tensor(name, shape, dt))
        vecs = sb("vecs_sb", [128, nv, KC], F32)
        dvec = sb("dvec", [128, 8, KC], F32)
        ones = sb("ones", [128, 128], F32)
        uT = sb("uT", [128, KC, TT], BF16)
        big = sb("big", [128, JC, TT], BF16)
        rstd = sb("rstd", [128, TW], F32)
        tmpn = [sb(f"tmpn{i}", [128, TW], F32) for i in range(6)]
        wib = [sb(f"wib{i}", [128, KC, 256], BF16) for i in range(3)]
        wob = [sb(f"wob{i}", [128, JC, 128], BF16) for i in range(2)]
        sgb = [sb(f"sgb{i}", [128, TW], F32) for i in range(3)]
        hres = [sb(f"hres{i}", [128, TW], F32) for i in range(8)]
        hout = [sb(f"hout{i}", [128, TW], F32) for i in range(3)]
        ps = [es.enter_context(nc.psum_tensor(f"ps{i}", [128, 512], F32)) for i in range(8)]
        r_tmpn = Ring(tmpn, "tmpn")
        r_wib = Ring(wib, "wib")
        r_wob = Ring(wob, "wob")
        r_sgb = Ring(sgb, "sgb")
        r_hres = Ring(hres, "hres")
        r_hout = Ring(hout, "hout")
        r_psA = Ring(ps[0:5], "ps")
        psn = ps[5:8]

        P.op('sp', lambda e: e.dma_start(out=vecs[:], in_=vecs_d), writes=['vecs'], chan='vecs')
        P.op('pool', lambda e: e.memset(ones[:], 1.0), writes=['ones'])

        def norm_stage(src, vbase, has_mod, dst_dram, presummed=False):
            gi, shl, scl, shc, scc = vbase
            if has_mod:
                for which, sc_i in ((0, scl), (1, scc)):
                    P.op('dve', lambda e, which=which, sc_i=sc_i: e.scalar_tensor_tensor(
                        out=dvec[:, which, :], in0=vecs[:, sc_i, :], scalar=1.0, in1=vecs[:, gi, :],
                        op0=ALU.add, op1=ALU.mult), reads=['vecs'], writes=[f'dvA{which}'])
            for ti, (t0, w) in enumerate(tiles):
                pb, pbt = psn[ti], f'psn{ti}'
                for k in (range(KC) if not presummed else []):
                    hr, hrt = r_hres.next()
                    P.op('sp', lambda e, hr=hr, k=k, t0=t0, w=w: e.dma_start(out=hr[:, 0:w], in_=src[:, k, t0:t0 + w]),
                         reads=[src.tensor.name], writes=[hrt], chan=hrt)
                    sq, sqt = r_sgb.next()
                    P.op('act', lambda e, hr=hr, sq=sq, w=w: e.activation(out=sq[:, 0:w], in_=hr[:, 0:w], func=AF.Square),
                         reads=[hrt], writes=[sqt])
                    P.op('pe', lambda e, k=k, pb=pb, sq=sq, w=w: e.matmul(pb[:, 0:w], ones[:, :], sq[:, 0:w],
                                                                        start=(k == 0), stop=(k == KC - 1)),
                         reads=[sqt, 'ones'], writes=[pbt])
                P.op('dve', lambda e, pb=pb, w=w: e.tensor_scalar(out=rstd[:, 0:w], in0=pb[:, 0:w], scalar1=1.0 / D,
                                                                 scalar2=EPS, op0=ALU.mult, op1=ALU.add),
                     reads=[pbt], writes=['rstd'])
                P.op('act', lambda e, w=w: e.activation(out=rstd[:, 0:w], in_=rstd[:, 0:w], func=AF.Sqrt),
                     reads=['rstd'], writes=['rstd'])
                P.op('dve', lambda e, w=w: e.reciprocal(out=rstd[:, 0:w], in_=rstd[:, 0:w]),
                     reads=['rstd'], writes=['rstd'])
                for k in range(KC):
                    hr, hrt = r_hres.next()
                    P.op('sp', lambda e, hr=hr, k=k, t0=t0, w=w: e.dma_start(out=hr[:, 0:w], in_=src[:, k, t0:t0 + w]),
                         reads=[src.tensor.name], writes=[hrt], chan=hrt)
                    if has_mod:
                        tb, tbt = r_tmpn.next()
                        for (c0, c1, which) in segs[ti]:
                            a_ap = dvec[:, which, k:k + 1]
                            b_ap = vecs[:, (shl if which == 0 else shc), k:k + 1]
                            P.op('dve', lambda e, tb=tb, hr=hr, c0=c0, c1=c1, a_ap=a_ap: e.scalar_tensor_tensor(
                                out=tb[:, c0:c1], in0=hr[:, c0:c1], scalar=a_ap, in1=rstd[:, c0:c1],
                                op0=ALU.mult, op1=ALU.mult), reads=[hrt, 'rstd', f'dvA{which}'], pwrites=[tbt])
                            P.op('act', lambda e, tb=tb, c0=c0, c1=c1, b_ap=b_ap, k=k, t0=t0: e.activation(
                                out=uT[:, k, t0 + c0:t0 + c1], in_=tb[:, c0:c1], func=AF.Identity, bias=b_ap, scale=1.0),
                                reads=[tbt, 'vecs'], pwrites=[f'u_t{ti}'])
                    else:
                        ho, hot = r_hout.next()
                        P.op('dve', lambda e, ho=ho, hr=hr, k=k, w=w: e.scalar_tensor_tensor(
                            out=ho[:, 0:w], in0=hr[:, 0:w], scalar=vecs[:, gi, k:k + 1], in1=rstd[:, 0:w],
                            op0=ALU.mult, op1=ALU.mult), reads=[hrt, 'rstd', 'vecs'], writes=[hot])
                        P.op('sp', lambda e, ho=ho, k=k, t0=t0, w=w: e.dma_start(out=dst_dram[:, k, t0:t0 + w], in_=ho[:, 0:w]),
                             reads=[hot], pwrites=[dst_dram.tensor.name], chan='st_' + hot)
                if has_mod and dst_dram is not None:
                    P.op('sp', lambda e, t0=t0, w=w: e.dma_start(out=dst_dram[:, :, t0:t0 + w], in_=uT[:, :, t0:t0 + w]),
                         reads=[f'u_t{ti}'], pwrites=[dst_dram.tensor.name], chan=f'ust{ti}')

        def ffn1_stage(w_in_d):
            for j in range(JC):
                wb, wbt = r_wib.next()
                P.op('pool', lambda e, j=j, wb=wb: e.dma_start(out=wb[:], in_=w_in_d[j]), writes=[wbt], chan=wbt)
                for ti, (t0, w) in enumerate(tiles):
                    pg, pgt = r_psA.next()
                    pu, put = r_psA.next()
                    for half, (pp, ppt) in enumerate(((pg, pgt), (pu, put))):
                        for k in range(KC):
                            P.op('pe', lambda e, k=k, pp=pp, wb=wb, half=half, t0=t0, w=w: e.matmul(
                                pp[:, 0:w], wb[:, k, half * 128:(half + 1) * 128], uT[:, k, t0:t0 + w],
                                start=(k == 0), stop=(k == KC - 1)),
                                reads=[wbt, f'u_t{ti}'], writes=[ppt])
                    sg, sgt = r_sgb.next()
                    P.op('act', lambda e, sg=sg, pg=pg, w=w: e.activation(out=sg[:, 0:w], in_=pg[:, 0:w], func=AF.Silu),
                         reads=[pgt], writes=[sgt])
                    P.op('dve', lambda e, sg=sg, pu=pu, j=j, t0=t0, w=w: e.tensor_tensor(
                        out=big[:, j, t0:t0 + w], in0=sg[:, 0:w], in1=pu[:, 0:w], op=ALU.mult),
                        reads=[sgt, put], pwrites=[f'big_t{ti}'])

        def proj_stage(w_d, kc, src, dst, gl, gc, half):
            mul = 0.5 if half else 1.0
            for which, gi_ in ((0, gl), (1, gc)):
                P.op('dve', lambda e, which=which, gi_=gi_: e.tensor_scalar(
                    out=dvec[:, 2 + which, :], in0=vecs[:, gi_, :], scalar1=mul, scalar2=None, op0=ALU.mult),
                    reads=['vecs'], writes=[f'dvG{which}'])
            deferred = []
            for dc in range(KC):
                wb, wbt = r_wob.next()
                P.op('pool', lambda e, dc=dc, wb=wb: e.dma_start(out=wb[:, 0:kc, :], in_=w_d[dc]), writes=[wbt], chan=wbt)
                for ti, (t0, w) in enumerate(tiles):
                    hr, hrt = r_hres.next()
                    P.op('sp', lambda e, hr=hr, dc=dc, t0=t0, w=w: e.dma_start(out=hr[:, 0:w], in_=src[:, dc, t0:t0 + w]),
                         reads=[src.tensor.name], writes=[hrt], chan=hrt)
                    pp, ppt = r_psA.next()
                    for j in range(kc):
                        P.op('pe', lambda e, j=j, pp=pp, wb=wb, t0=t0, w=w: e.matmul(
                            pp[:, 0:w], wb[:, j, :], big[:, j, t0:t0 + w], start=(j == 0), stop=(j == kc - 1)),
                            reads=[wbt, f'big_t{ti}'], writes=[ppt])
                    while deferred:
                        deferred.pop(0)()
                    ho, hot = r_hout.next()
                    for (c0, c1, which) in segs[ti]:
                        P.op('dve', lambda e, ho=ho, pp=pp, hr=hr, c0=c0, c1=c1, which=which, dc=dc: e.scalar_tensor_tensor(
                            out=ho[:, c0:c1], in0=pp[:, c0:c1], scalar=dvec[:, 2 + which, dc:dc + 1], in1=hr[:, c0:c1],
                            op0=ALU.mult, op1=ALU.add), reads=[ppt, hrt, f'dvG{which}'], pwrites=[hot])
                    P.op('sp', lambda e, ho=ho, dc=dc, t0=t0, w=w: e.dma_start(out=dst[:, dc, t0:t0 + w], in_=ho[:, 0:w]),
                         reads=[hot], pwrites=[dst.tensor.name], chan='st_' + hot)
                    sq, sqt = r_sgb.next()
                    P.op('act', lambda e, ho=ho, sq=sq, w=w: e.activation(out=sq[:, 0:w], in_=ho[:, 0:w], func=AF.Square),
                         reads=[hot], writes=[sqt])

                    def _ssq(dc=dc, ti=ti, sq=sq, sqt=sqt, w=w):
                        P.op('pe', lambda e: e.matmul(psn[ti][:, 0:w], ones[:, :], sq[:, 0:w], start=(dc == 0), stop=(dc == KC - 1)),
                             reads=[sqt, 'ones'], writes=[f'psn{ti}'])
                    deferred.append(_ssq)
            while deferred:
                deferred.pop(0)()

        vb = 0
        for si, s in enumerate(stages):
            src, dst = hd[si], hd[si + 1]
            if s[0] == 'proj':
                for ti, (t0, w) in enumerate(tiles):
                    P.op('sp', lambda e, t0=t0, w=w: e.dma_start(out=big[:, 0:KC, t0:t0 + w], in_=y_in[:, :, t0:t0 + w]),
                         writes=[f'big_t{ti}'], chan=f'yin{ti}')
                proj_stage(wd[si], KC, src, dst, vb, vb + 1, False)
                vb += 2
            else:
                g, shl, scl, gl, shc, scc, gc = range(vb, vb + 7)
                norm_stage(src, (g, shl, scl, shc, scc), True, None, presummed=(si > 0))
                ffn1_stage(wd[si][0])
                proj_stage(wd[si][1], JC, src, dst, gl, gc, True)
                vb += 7
        if want_u:
            g, shl, scl, shc, scc = range(vb, vb + 5)
            norm_stage(hd[-1], (g, shl, scl, shc, scc), True, u_out, presummed=True)
            vb += 5
        if final:
            norm_stage(hd[-1], (vb, 0, 0, 0, 0), False, o_out, presummed=True)
            vb += 1
        P.finish('sp')
        P.emit()
    return nc


MODC = 9 * D // NCORE
MCH = 384
NMCH = MODC // MCH


def build_mod_prog():
    nc = bass.Bass("TRN2", target_bir_lowering=False)
    cvec_d = nc.dram_tensor("cvec", [128, KC, 2], F32, kind="ExternalInput").ap()
    w_d = nc.dram_tensor("wm", [DEPTH * NMCH, 128, KC, MCH], F32, kind="ExternalInput").ap()
    b_d = nc.dram_tensor("bm", [2, DEPTH * MODC], F32, kind="ExternalInput").ap()
    o_d = nc.dram_tensor("mod_out", [2, DEPTH * MODC], F32, kind="ExternalOutput").ap()
    with ExitStack() as es:
        P = Prog(nc, es)
        sb = lambda name, shape, dt: es.enter_context(nc.sbuf_tensor(name, shape, dt))
        cv = sb("cv", [128, KC, 2], F32)
        sc = sb("sc", [128, KC, 2], F32)
        bsb = sb("bsb", [2, DEPTH * MODC], F32)
        mo = sb("mo", [2, DEPTH * MODC], F32)
        wbs = [sb(f"wb{i}", [128, KC, MCH], F32) for i in range(3)]
        ps = [es.enter_context(nc.psum_tensor(f"ps{i}", [128, 512], F32)) for i in range(4)]
        r_wb = Ring(wbs, "wb")
        r_ps = Ring(ps, "ps")
        P.op('sp', lambda e: e.dma_start(out=cv[:], in_=cvec_d), writes=['cv'], chan='cv')
        P.op('sp', lambda e: e.dma_start(out=bsb[:], in_=b_d), writes=['bsb'], chan='bsb')
        P.op('act', lambda e: e.activation(out=sc[:], in_=cv[:], func=AF.Silu), reads=['cv'], writes=['sc'])
        for i in range(DEPTH * NMCH):
            wb, wbt = r_wb.next()
            P.op('sp', lambda e, i=i, wb=wb: e.dma_start(out=wb[:], in_=w_d[i]), writes=[wbt], chan=wbt)
            pp, ppt = r_ps.next()
            for k in range(KC):
                P.op('pe', lambda e, k=k, pp=pp, wb=wb: e.matmul(pp[0:2, 0:MCH], sc[:, k, :], wb[:, k, :],
                                                                start=(k == 0), stop=(k == KC - 1)),
                     reads=[wbt, 'sc'], writes=[ppt])
            P.op('dve', lambda e, i=i, pp=pp: e.tensor_tensor(out=mo[:, i * MCH:(i + 1) * MCH], in0=pp[0:2, 0:MCH],
                                                             in1=bsb[:, i * MCH:(i + 1) * MCH], op=ALU.add),
                 reads=[ppt, 'bsb'], pwrites=['mo'])
        P.op('sp', lambda e: e.dma_start(out=o_d, in_=mo[:]), reads=['mo'], writes=['o_d'], chan='o_d')
        P.finish('sp')
        P.emit()
    return nc


def run_mod(c, c_ctx, w_mod, b_mod):
    nc = build_mod_prog()
    cvec = np.ascontiguousarray(np.stack([c.reshape(KC, 128).T, c_ctx.reshape(KC, 128).T], axis=2)).astype(np.float32)
    in_maps = []
    for r in range(NCORE):
        ws = w_mod[:, :, r * MODC:(r + 1) * MODC]
        ws = ws.reshape(DEPTH, KC, 128, NMCH, MCH).transpose(0, 3, 2, 1, 4).reshape(DEPTH * NMCH, 128, KC, MCH)
        bs = b_mod[:, r * MODC:(r + 1) * MODC].reshape(1, DEPTH * MODC)
        in_maps.append({"cvec": cvec, "wm": np.ascontiguousarray(ws),
                        "bm": np.ascontiguousarray(np.concatenate([bs, bs], axis=0))})
    res = run_bass_kernel_spmd(nc, in_maps, core_ids=list(range(NCORE)))
    mod = np.zeros((DEPTH, 2, 9 * D), np.float32)
    for r in range(NCORE):
        o = res.results[r]["mod_out"].reshape(2, DEPTH, MODC)
        mod[:, :, r * MODC:(r + 1) * MODC] = o.transpose(1, 0, 2)
    return mod


TALL = SEQ + CTX
NTL = TALL // 128
QBW = 512
NQB = SEQ // QBW
NEG = -30000.0


def v3(ap, a, b):
    l = [list(x) for x in ap.ap]
    st, n = l[-1]
    assert n == a * b, (n, a, b)
    return bass.AP(ap.tensor, ap.offset, l[:-1] + [[st * b, a], [st, b]])


def token_blocks():
    return [(i * QBW, QBW) for i in range(NQB)] + [(SEQ, CTX)]


def build_odd_prog(debug=False):
    nc = bass.Bass("TRN2", target_bir_lowering=False)
    dq_d = nc.dram_tensor("dq", [128, 2, TALL], BF16, kind="ExternalOutput").ap() if debug else None
    dk_d = nc.dram_tensor("dk", [128, TALL], BF16, kind="ExternalOutput").ap() if debug else None
    uT_d = nc.dram_tensor("uT", [128, KC, TALL], BF16, kind="ExternalInput").ap()
    wq_d = nc.dram_tensor("wq", [128, KC, 4, 128], F32, kind="ExternalInput").ap()
    wk_d = nc.dram_tensor("wk", [128, KC, 2, 128], F32, kind="ExternalInput").ap()
    wv_d = nc.dram_tensor("wv", [128, KC, 128], F32, kind="ExternalInput").ap()
    rtab_d = nc.dram_tensor("rtab", [128, 2, 128], F32, kind="ExternalInput").ap()
    mask_d = nc.dram_tensor("masks", [128, 6, QBW], BF16, kind="ExternalInput").ap()
    ident_d = nc.dram_tensor("ident", [128, 128], BF16, kind="ExternalInput").ap()
    sink_d = nc.dram_tensor("sink", [128, 2], F32, kind="ExternalInput").ap()
    y_d = nc.dram_tensor("y_out", [128, 2, TALL], BF16, kind="ExternalOutput").ap()
    SCALE = 128.0 ** -0.5
    with ExitStack() as es:
        P = Prog(nc, es)
        sb = lambda name, shape, dt: es.enter_context(nc.sbuf_tensor(name, shape, dt))
        wq = sb("wq_sb", [128, KC, 4, 128], BF16)
        wk = sb("wk_sb", [128, KC, 2, 128], BF16)
        wv = sb("wv_sb", [128, KC, 128], BF16)
        rtab = sb("rtab_sb", [128, 2, 128], F32)
        masks = sb("masks_sb", [128, 6, QBW], BF16)
        ident = sb("ident_sb", [128, 128], BF16)
        onesb = sb("onesb", [128, 128], BF16)
        sink = sb("sink_sb", [128, 2], F32)
        esink = sb("esink", [128, 2], F32)
        qT = sb("qT", [128, 2, TALL], BF16)
        kT = sb("kT", [128, TALL], BF16)
        vtm = sb("vtm", [128, NTL, 128], BF16)
        yT = sb("yT", [128, 2, TALL], BF16)
        ubs = [sb(f"ub{i}", [128, KC, QBW], BF16) for i in range(2)]
        t1s = [sb(f"t1_{i}", [128, QBW], F32) for i in range(2)]
        t2s = [sb(f"t2_{i}", [128, QBW], F32) for i in range(2)]
        pts = [sb(f"pt{i}", [128, QBW], BF16) for i in range(3)]
        rcs = [sb(f"rc{i}", [128, QBW], F32) for i in range(2)]
        ps = [es.enter_context(nc.psum_tensor(f"ps{i}", [128, 512], F32)) for i in range(8)]
        r_ub = Ring(ubs, "ub"); r_t1 = Ring(t1s, "t1"); r_t2 = Ring(t2s, "t2"); r_pt = Ring(pts, "pt"); r_rc = Ring(rcs, "rc")
        r_psA = Ring(ps[0:4], "psA")
        r_psO = Ring(ps[4:6], "psO")
        r_psR = Ring(ps[6:8], "psR")

        P.op('pool', lambda e: e.dma_start(out=wq[:], in_=wq_d), writes=['wq'], chan='wq')
        P.op('pool', lambda e: e.dma_start(out=wk[:], in_=wk_d), writes=['wk'], chan='wk')
        P.op('pool', lambda e: e.dma_start(out=wv[:], in_=wv_d), writes=['wv'], chan='wv')
        P.op('sp', lambda e: e.dma_start(out=rtab[:], in_=rtab_d), writes=['rtab'], chan='rtab')
        P.op('sp', lambda e: e.dma_start(out=masks[:], in_=mask_d), writes=['masks'], chan='masks')
        P.op('sp', lambda e: e.dma_start(out=ident[:], in_=ident_d), writes=['ident'], chan='ident')
        P.op('sp', lambda e: e.dma_start(out=sink[:], in_=sink_d), writes=['sink'], chan='sink')
        P.op('pool', lambda e: e.memset(onesb[:], 1.0), writes=['onesb'])
        P.op('act', lambda e: e.activation(out=esink[:], in_=sink[:], func=AF.Exp), reads=['sink'], writes=['esink'])

        def proj_fm(ub, ubt, w_ap, wtag, width):
            pp, ppt = r_psA.next()
            for k in range(KC):
                P.op('pe', lambda e, k=k, pp=pp: e.matmul(pp[:, 0:width], w_ap(k), ub[:, k, 0:width],
                                                         start=(k == 0), stop=(k == KC - 1)),
                     reads=[ubt, wtag], writes=[ppt])
            return pp, ppt

        def rope_store(px, pxt, pw, pwt, dst, dtag, r0, scale):
            t1, t1t = r_t1.next()
            t2, t2t = r_t2.next()
            for half in range(2):
                p0, p1 = half * 64, half * 64 + 64
                if half == 0:
                    ctab = bc_last(rtab[p0:p1, 0, r0:r0 + 8], 64)
                    stab = bc_last(rtab[p0:p1, 1, r0:r0 + 8], 64)
                else:
                    ctab = bc_mid(rtab[p0:p1, 0, 0:64], 8)
                    stab = bc_mid(rtab[p0:p1, 1, 0:64], 8)
                P.op('dve', lambda e, p0=p0, p1=p1, ctab=ctab, t1=t1: e.scalar_tensor_tensor(
                    out=v3(t1[p0:p1, :], 8, 64), in0=v3(px[p0:p1, :], 8, 64), scalar=scale, in1=ctab,
                    op0=ALU.mult, op1=ALU.mult), reads=[pxt, 'rtab'], pwrites=[t1t])
                P.op('dve', lambda e, p0=p0, p1=p1, stab=stab, t2=t2: e.scalar_tensor_tensor(
                    out=v3(t2[p0:p1, :], 8, 64), in0=v3(pw[p0:p1, :], 8, 64), scalar=scale, in1=stab,
                    op0=ALU.mult, op1=ALU.mult), reads=[pwt, 'rtab'], pwrites=[t2t])
            P.op('pool', lambda e, t1=t1, t2=t2: e.tensor_tensor(out=dst, in0=t1[:, :], in1=t2[:, :], op=ALU.add),
                 reads=[t1t, t2t], pwrites=[dtag])

        def proj_block(bi, t0, width):
            ub, ubt = r_ub.next()
            P.op('sp', lambda e, ub=ub, t0=t0, width=width: e.dma_start(out=ub[:, :, 0:width], in_=uT_d[:, :, t0:t0 + width]),
                 writes=[ubt], chan=ubt)
            is_ctx = t0 >= SEQ
            r0 = t0 // 64
            for h in range(2):
                px, pxt = proj_fm(ub, ubt, lambda k, h=h: wq[:, k, 2 * h, :], 'wq', width)
                if is_ctx:
                    P.op('act', lambda e, px=px, h=h, t0=t0, width=width: e.activation(
                        out=qT[:, h, t0:t0 + width], in_=px[:, 0:width], func=AF.Identity, scale=SCALE),
                        reads=[pxt], pwrites=[f'qT{bi}'])
                else:
                    pw, pwt = proj_fm(ub, ubt, lambda k, h=h: wq[:, k, 2 * h + 1, :], 'wq', width)
                    rope_store(px, pxt, pw, pwt, qT[:, h, t0:t0 + width], f'qT{bi}', r0, SCALE)
            px, pxt = proj_fm(ub, ubt, lambda k: wk[:, k, 0, :], 'wk', width)
            if is_ctx:
                P.op('act', lambda e, px=px, t0=t0, width=width: e.activation(
                    out=kT[:, t0:t0 + width], in_=px[:, 0:width], func=AF.Identity), reads=[pxt], pwrites=[f'kT{bi}'])
            else:
                pw, pwt = proj_fm(ub, ubt, lambda k: wk[:, k, 1, :], 'wk', width)
                rope_store(px, pxt, pw, pwt, kT[:, t0:t0 + width], f'kT{bi}', r0, 1.0)
            pp, ppt = r_psA.next()
            nti = width // 128
            for i in range(nti):
                for k in range(KC):
                    P.op('pe', lambda e, k=k, i=i, pp=pp, ub=ub: e.matmul(pp[:, i * 128:(i + 1) * 128], ub[:, k, i * 128:(i + 1) * 128],
                                                                         wv[:, k, :], start=(k == 0), stop=(k == KC - 1)),
                         reads=[ubt, 'wv'], pwrites=[ppt])
            tt0 = t0 // 128
            P.op('act', lambda e, pp=pp, tt0=tt0, nti=nti: e.activation(
                out=vtm[:, tt0:tt0 + nti, :], in_=v3(pp[:, 0:nti * 128], nti, 128), func=AF.Identity),
                reads=[ppt], pwrites=[f'vtm{bi}'])

        def attend(h, q0, qw, keytiles):
            qbi = q0 // QBW if q0 < SEQ else NQB
            kb = lambda kt: kt // 4 if kt < SEQ // 128 else NQB
            po, pot = r_psO.next()
            pr, prt = r_psR.next()
            n = len(keytiles)
            for i, (kt, mi) in enumerate(keytiles):
                pS, pst = r_psA.next()
                P.op('pe', lambda e, pS=pS, kt=kt, mi=mi: e.matmul(pS[:, 0:qw], kT[:, kt * 128:(kt + 1) * 128], qT[:, h, q0:q0 + qw],
                                                                  start=True, stop=(mi is None)),
                     reads=[f'kT{kb(kt)}', f'qT{qbi}'], writes=[pst])
                if mi is not None:
                    P.op('pe', lambda e, pS=pS, mi=mi: e.matmul(pS[:, 0:qw], ident[:, :], masks[:, mi, 0:qw], start=False, stop=True),
                         reads=['ident', 'masks'], writes=[pst])
                pt, ptt = r_pt.next()
                P.op('act', lambda e, pS=pS, pt=pt: e.activation(out=pt[:, 0:qw], in_=pS[:, 0:qw], func=AF.Exp),
                     reads=[pst], writes=[ptt])
                P.op('pe', lambda e, pt=pt, kt=kt, i=i: e.matmul(po[:, 0:qw], vtm[:, kt, :], pt[:, 0:qw], start=(i == 0), stop=(i == n - 1)),
                     reads=[ptt, f'vtm{kb(kt)}'], writes=[pot])
                P.op('pe', lambda e, pt=pt, i=i: e.matmul(pr[:, 0:qw], onesb[:, :], pt[:, 0:qw], start=(i == 0), stop=(i == n - 1)),
                     reads=[ptt, 'onesb'], writes=[prt])
            rc, rct = r_rc.next()
            P.op('dve', lambda e, rc=rc: e.tensor_scalar(out=rc[:, 0:qw], in0=pr[:, 0:qw], scalar1=esink[:, h:h + 1], scalar2=None, op0=ALU.add),
                 reads=[prt, 'esink'], writes=[rct])
            P.op('dve', lambda e, rc=rc: e.reciprocal(out=rc[:, 0:qw], in_=rc[:, 0:qw]), reads=[rct], writes=[rct])
            P.op('dve', lambda e, rc=rc: e.tensor_tensor(out=yT[:, h, q0:q0 + qw], in0=po[:, 0:qw], in1=rc[:, 0:qw], op=ALU.mult),
                 reads=[pot, rct], pwrites=['yT'])

        blks = token_blocks()
        if debug:
            for bi, (t0, width) in enumerate(blks):
                proj_block(bi, t0, width)
        else:
            proj_block(NQB, *blks[NQB])
            proj_block(0, *blks[0])
            proj_block(1, *blks[1])

        def keytiles_for(qb):
            kts = []
            for rel in range(-1, 5):
                kt = qb * 4 + rel
                if 0 <= kt < SEQ // 128:
                    kts.append((kt, rel + 1))
            return kts + [(64, None), (65, None)]
        for h in range(2):
            attend(h, SEQ, CTX, [(64, None), (65, None)])
        for qb in range(NQB):
            for h in range(2):
                attend(h, qb * QBW, QBW, keytiles_for(qb))
            if not debug and qb + 2 < NQB:
                proj_block(qb + 2, *blks[qb + 2])
        for h in range(2):
            P.op('sp', lambda e, h=h: e.dma_start(out=y_d[:, h, :], in_=yT[:, h, :]), reads=['yT'], pwrites=['y_d'], chan=f'y{h}')
        if debug:
            P.op('sp', lambda e: e.dma_start(out=dq_d, in_=qT[:]), reads=[f'qT{i}' for i in range(17)], writes=['dq_d'], chan='dq')
            P.op('sp', lambda e: e.dma_start(out=dk_d, in_=kT[:]), reads=[f'kT{i}' for i in range(17)], writes=['dk_d'], chan='dk')
        P.finish('sp')
        P.emit()
    return nc


def odd_consts():
    inv = 10000.0 ** (-np.arange(0, 64, 2, dtype=np.float32) / 64.0)
    rtab = np.zeros((128, 2, 128), np.float32)
    for p in range(128):
        f = inv[p % 32]
        sign = -1.0 if (p % 64) < 32 else 1.0
        pos = np.arange(128, dtype=np.float32)
        ang = pos * f
        rtab[p, 0, :] = np.cos(ang)
        rtab[p, 1, :] = sign * np.sin(ang)
    masks = np.zeros((128, 6, QBW), np.float32)
    for mi in range(6):
        rel = mi - 1
        kpos = rel * 128 + np.arange(128)[:, None]
        qpos = np.arange(QBW)[None, :]
        masks[:, mi, :] = np.where(np.abs(kpos - qpos) <= 128, 0.0, NEG)
    ident = np.eye(128, dtype=np.float32)
    return rtab, masks.astype(NPBF), ident.astype(NPBF)


def swap_perm():
    p = np.arange(128)
    return np.where((p % 64) < 32, p + 32, p - 32)


def lay_w(w):
    n = w.shape[1]
    return np.ascontiguousarray(w.reshape(KC, 128, n).transpose(1, 0, 2))


def run_odd(uT_all, w_qkv, sink, debug=False):
    nc = build_odd_prog(debug)
    rtab, masks, ident = odd_consts()
    perm = swap_perm()
    in_maps = []
    for j in range(NCORE):
        kv = j // 2
        wq = []
        for h in (2 * j, 2 * j + 1):
            cols = w_qkv[:, h * 128:(h + 1) * 128]
            wq += [cols, cols[:, perm]]
        wq = np.stack([lay_w(c) for c in wq], axis=2)
        kc = w_qkv[:, D + kv * 128: D + (kv + 1) * 128]
        wk = np.stack([lay_w(kc), lay_w(kc[:, perm])], axis=2)
        wv = lay_w(w_qkv[:, D + 512 + kv * 128: D + 512 + (kv + 1) * 128])
        sk = np.broadcast_to(sink[2 * j:2 * j + 2][None, :], (128, 2)).astype(np.float32)
        in_maps.append({"uT": uT_all, "wq": np.ascontiguousarray(wq), "wk": np.ascontiguousarray(wk), "wv": wv,
                        "rtab": rtab, "masks": masks, "ident": ident, "sink": np.ascontiguousarray(sk)})
    res = run_bass_kernel_spmd(nc, in_maps, core_ids=list(range(NCORE)))
    yT = np.zeros((128, KC, TALL), NPBF)
    for j in range(NCORE):
        yT[:, 2 * j:2 * j + 2, :] = res.results[j]["y_out"]
    if debug:
        return yT, res.results
    return yT


NNB = 20


def na_bias_mats(rpb_h):
    def mat(R, kt):
        kr = 2 * kt + np.arange(2)[:, None, None, None]
        kc = np.arange(64)[None, :, None, None]
        r = R + np.arange(8)[None, None, :, None]
        qc = np.arange(64)[None, None, None, :]
        r0 = np.clip(r - 4, 0, 120)
        c0 = np.clip(qc - 8, 0, 48)
        valid = (kr >= r0) & (kr < r0 + 8) & (kc >= c0) & (kc < c0 + 16)
        ri = np.clip(kr - r + 7, 0, 14)
        ci = np.clip(kc - qc + 15, 0, 30)
        ri, ci, valid = np.broadcast_arrays(ri, ci, valid)
        return np.where(valid, rpb_h[ri, ci], NEG).reshape(128, 512)
    mats = [mat(8, 4 + rel) for rel in range(-2, 6)]
    mats += [mat(0, kt) for kt in range(0, 6)]
    mats += [mat(120, kt) for kt in range(58, 64)]
    return np.ascontiguousarray(np.stack(mats, axis=1)).astype(NPBF)


def na_keytiles(qb):
    if qb == 0:
        return [(kt, 8 + kt) for kt in range(0, 6)]
    if qb == NQB - 1:
        return [(kt, 14 + kt - 58) for kt in range(58, 64)]
    return [(4 * qb + rel, rel + 2) for rel in range(-2, 6)]


def build_na_prog():
    nc = bass.Bass("TRN2", target_bir_lowering=False)
    uT_d = nc.dram_tensor("uT", [128, KC, TALL], BF16, kind="ExternalInput").ap()
    w_d = nc.dram_tensor("w3", [128, KC, 3, 128], F32, kind="ExternalInput").ap()
    bias_d = nc.dram_tensor("nabias", [128, NNB, QBW], BF16, kind="ExternalInput").ap()
    ident_d = nc.dram_tensor("ident", [128, 128], BF16, kind="ExternalInput").ap()
    y_d = nc.dram_tensor("y_out", [128, TALL], BF16, kind="ExternalOutput").ap()
    SCALE = 128.0 ** -0.5
    with ExitStack() as es:
        P = Prog(nc, es)
        sb = lambda name, shape, dt: es.enter_context(nc.sbuf_tensor(name, shape, dt))
        w3 = sb("w3_sb", [128, KC, 3, 128], BF16)
        biasm = sb("bias_sb", [128, NNB, QBW], BF16)
        ident = sb("ident_sb", [128, 128], BF16)
        onesb = sb("onesb", [128, 128], BF16)
        qT = sb("qT", [128, TALL], BF16)
        kT = sb("kT", [128, TALL], BF16)
        vtm = sb("vtm", [128, NTL, 128], BF16)
        yT = sb("yT", [128, TALL], BF16)
        ubs = [sb(f"ub{i}", [128, KC, QBW], BF16) for i in range(2)]
        pts = [sb(f"pt{i}", [128, QBW], BF16) for i in range(3)]
        rcs = [sb(f"rc{i}", [128, QBW], F32) for i in range(2)]
        ps = [es.enter_context(nc.psum_tensor(f"ps{i}", [128, 512], F32)) for i in range(8)]
        r_ub = Ring(ubs, "ub"); r_pt = Ring(pts, "pt"); r_rc = Ring(rcs, "rc")
        r_psA = Ring(ps[0:4], "psA"); r_psO = Ring(ps[4:6], "psO"); r_psR = Ring(ps[6:8], "psR")
        P.op('pool', lambda e: e.dma_start(out=w3[:], in_=w_d), writes=['w3'], chan='w3')
        P.op('sp', lambda e: e.dma_start(out=biasm[:], in_=bias_d), writes=['biasm'], chan='biasm')
        P.op('sp', lambda e: e.dma_start(out=ident[:], in_=ident_d), writes=['ident'], chan='ident')
        P.op('pool', lambda e: e.memset(onesb[:], 1.0), writes=['onesb'])
        def proj_block(bi, t0, width):
            ub, ubt = r_ub.next()
            P.op('sp', lambda e, ub=ub, t0=t0, width=width: e.dma_start(out=ub[:, :, 0:width], in_=uT_d[:, :, t0:t0 + width]),
                 writes=[ubt], chan=ubt)
            for g, (dst, dtag, sc) in enumerate(((qT, f'qT{bi}', SCALE), (kT, f'kT{bi}', 1.0))):
                pp, ppt = r_psA.next()
                for k in range(KC):
                    P.op('pe', lambda e, k=k, pp=pp, g=g, ub=ub, width=width: e.matmul(
                        pp[:, 0:width], w3[:, k, g, :], ub[:, k, 0:width], start=(k == 0), stop=(k == KC - 1)),
                        reads=[ubt, 'w3'], writes=[ppt])
                P.op('act', lambda e, pp=pp, dst=dst, sc=sc, t0=t0, width=width: e.activation(
                    out=dst[:, t0:t0 + width], in_=pp[:, 0:width], func=AF.Identity, scale=sc), reads=[ppt], pwrites=[dtag])
            pp, ppt = r_psA.next()
            nti = width // 128
            for i in range(nti):
                for k in range(KC):
                    P.op('pe', lambda e, k=k, i=i, pp=pp, ub=ub: e.matmul(pp[:, i * 128:(i + 1) * 128], ub[:, k, i * 128:(i + 1) * 128],
                                                                         w3[:, k, 2, :], start=(k == 0), stop=(k == KC - 1)),
                         reads=[ubt, 'w3'], pwrites=[ppt])
            tt0 = t0 // 128
            P.op('dve', lambda e, pp=pp, tt0=tt0, nti=nti: e.tensor_copy(
                out=vtm[:, tt0:tt0 + nti, :], in_=v3(pp[:, 0:nti * 128], nti, 128)), reads=[ppt], pwrites=[f'vtm{bi}'])

        def attend(q0, qw, keytiles):
            qbi = q0 // QBW if q0 < SEQ else NQB
            kb = lambda kt: kt // 4 if kt < SEQ // 128 else NQB
            po, pot = r_psO.next()
            pr, prt = r_psR.next()
            n = len(keytiles)
            for i, (kt, mi) in enumerate(keytiles):
                pS, pst = r_psA.next()
                P.op('pe', lambda e, pS=pS, kt=kt, mi=mi: e.matmul(pS[:, 0:qw], kT[:, kt * 128:(kt + 1) * 128], qT[:, q0:q0 + qw],
                                                                  start=True, stop=(mi is None)),
                     reads=[f'kT{kb(kt)}', f'qT{qbi}'], writes=[pst])
                if mi is not None:
                    P.op('pe', lambda e, pS=pS, mi=mi: e.matmul(pS[:, 0:qw], ident[:, :], biasm[:, mi, 0:qw], start=False, stop=True),
                         reads=['ident', 'biasm'], writes=[pst])
                pt, ptt = r_pt.next()
                P.op('act', lambda e, pS=pS, pt=pt: e.activation(out=pt[:, 0:qw], in_=pS[:, 0:qw], func=AF.Exp),
                     reads=[pst], writes=[ptt])
                P.op('pe', lambda e, pt=pt, kt=kt, i=i: e.matmul(po[:, 0:qw], vtm[:, kt, :], pt[:, 0:qw], start=(i == 0), stop=(i == n - 1)),
                     reads=[ptt, f'vtm{kb(kt)}'], writes=[pot])
                P.op('pe', lambda e, pt=pt, i=i: e.matmul(pr[:, 0:qw], onesb[:, :], pt[:, 0:qw], start=(i == 0), stop=(i == n - 1)),
                     reads=[ptt, 'onesb'], writes=[prt])
            rc, rct = r_rc.next()
            P.op('dve', lambda e, rc=rc: e.reciprocal(out=rc[:, 0:qw], in_=pr[:, 0:qw]), reads=[prt], writes=[rct])
            P.op('dve', lambda e, rc=rc: e.tensor_tensor(out=yT[:, q0:q0 + qw], in0=po[:, 0:qw], in1=rc[:, 0:qw], op=ALU.mult),
                 reads=[pot, rct], pwrites=['yT'])

        blks = token_blocks()
        proj_block(NQB, *blks[NQB])
        proj_block(0, *blks[0])
        proj_block(1, *blks[1])
        attend(SEQ, CTX, [(64, None), (65, None)])
        for qb in range(NQB):
            attend(qb * QBW, QBW, na_keytiles(qb) + [(64, None), (65, None)])
            if qb + 2 < NQB:
                proj_block(qb + 2, *blks[qb + 2])
        P.op('sp', lambda e: e.dma_start(out=y_d, in_=yT[:]), reads=['yT'], writes=['y_d'], chan='y')
        P.finish('sp')
        P.emit()
    return nc


def run_na(uT_all, w_in, rpb):
    nc = build_na_prog()
    ident = np.eye(128, dtype=np.float32).astype(NPBF)
    in_maps = []
    for j in range(NCORE):
        cols = [w_in[:, g * 1024 + j * 128: g * 1024 + (j + 1) * 128] for g in range(3)]
        w3 = np.stack([lay_w(c) for c in cols], axis=2)
        in_maps.append({"uT": uT_all, "w3": np.ascontiguousarray(w3), "nabias": na_bias_mats(rpb[j]), "ident": ident})
    res = run_bass_kernel_spmd(nc, in_maps, core_ids=list(range(NCORE)))
    return np.stack([res.results[j]["y_out"] for j in range(NCORE)], axis=1)


HC = 64
NCH = TALL // HC
NCTXCH = CTX // HC


def chunk_col(ap2d, nch, idx):
    l = [list(x) for x in ap2d.ap]
    st, n = l[-1]
    assert n == nch * HC
    return bass.AP(ap2d.tensor, ap2d.offset + idx * st, l[:-1] + [[st * HC, nch], [0, HC]])


def chunk_pick(ap2d, nch, idx):
    l = [list(x) for x in ap2d.ap]
    st, n = l[-1]
    return bass.AP(ap2d.tensor, ap2d.offset + idx * st, l[:-1] + [[st * HC, nch]])


def sub_view(ap2d, nch, sub0, nsub, col, bcast):
    l = [list(x) for x in ap2d.ap]
    st, n = l[-1]
    assert n == nch * HC
    if bcast:
        return bass.AP(ap2d.tensor, ap2d.offset + (sub0 * SUB + col) * st, l[:-1] + [[st * HC, nch], [st * SUB, nsub], [0, SUB]])
    return bass.AP(ap2d.tensor, ap2d.offset + sub0 * SUB * st, l[:-1] + [[st * HC, nch], [st * SUB, nsub], [st, SUB]])


SUB = 16
NSUB = HC // SUB
CLAMP = 40.0


def build_hgrn_prog(e_layer, stop=99, nblk=None, limit=None):
    nc = bass.Bass("TRN2", target_bir_lowering=False)
    uT_d = nc.dram_tensor("uT", [128, KC, TALL], BF16, kind="ExternalInput").ap()
    w_d = nc.dram_tensor("w5", [128, KC, 5, 128], F32, kind="ExternalInput").ap()
    lbl_d = nc.dram_tensor("lbl", [128, 2, 2], F32, kind="ExternalInput").ap()
    hgn_d = nc.dram_tensor("hgn", [128, 1], F32, kind="ExternalInput").ap()
    ident_d = nc.dram_tensor("ident", [128, 128], BF16, kind="ExternalInput").ap()
    tri_d = nc.dram_tensor("tri", [128, 4, HC], F32, kind="ExternalInput").ap()
    rmask_d = nc.dram_tensor("rmask", [128, QBW], F32, kind="ExternalInput").ap()
    y_d = nc.dram_tensor("y_out", [128, TALL], BF16, kind="ExternalOutput").ap()
    with ExitStack() as es:
        P = Prog(nc, es)
        sb = lambda name, shape, dt: es.enter_context(nc.sbuf_tensor(name, shape, dt))
        P.limit = limit
        w5 = sb("w5_sb", [128, KC, 5, 128], BF16)
        lbl = sb("lbl_sb", [128, 2, 2], F32)
        lb = sb("lb_sb", [128, 2], F32)
        oml = sb("oml_sb", [128, 2], F32)
        hgn = sb("hgn_sb", [128, 1], F32)
        ident = sb("ident_sb", [128, 128], BF16)
        tri = sb("tri_sb", [128, 4, HC], F32)
        rmask = sb("rmask_sb", [128, QBW], F32)
        ones32 = sb("ones32", [128, 128], F32)
        itm = sb("itm", [128, NTL, 128], BF16)
        Qp = sb("Qp", [128, TALL], BF16)
        attmEO = [sb("attmE", [128, NTL, HC], BF16), sb("attmO", [128, NTL, HC], BF16)]
        dS = sb("dS", [128, NCH, 128], BF16)
        Dall = sb("Dall", [128, NCH], F32)
        oacc = sb("oacc", [128, TALL], F32)
        ub = sb("ub", [128, KC, QBW], BF16)
        T = [sb(f"T{i}", [128, QBW], F32) for i in range(14)]
        KtT = sb("KtT", [128, QBW], BF16)
        Kj = [sb(f"Kj{i}", [128, QBW], BF16) for i in range(3)]
        Kd = sb("Kd", [128, QBW], BF16)
        Qd = sb("Qd", [128, QBW], BF16)
        Qsub = sb("Qsub", [128, QBW], BF16)
        KtmEO = [sb("KtmE", [128, 4, 128], BF16), sb("KtmO", [128, 4, 128], BF16)]
        yb = sb("yb", [128, QBW], BF16)
        ps = [es.enter_context(nc.psum_tensor(f"ps{i}", [128, 512], F32)) for i in range(7)]
        psT = es.enter_context(nc.psum_tensor("psT", [128, 1024], BF16))
        r_psA = Ring(ps[0:3], "psA")
        r_psB = Ring(ps[3:7], "psB")

        P.op('pool', lambda e: e.dma_start(out=w5[:], in_=w_d), writes=['w5'], chan='w5')
        for nm, t_, d_ in (('lbl', lbl, lbl_d), ('hgn', hgn, hgn_d), ('ident', ident, ident_d), ('tri', tri, tri_d), ('rmask', rmask, rmask_d)):
            P.op('sp', lambda e, t_=t_, d_=d_: e.dma_start(out=t_[:], in_=d_), writes=[nm], chan=nm)
        P.op('pool', lambda e: e.memset(ones32[:], 1.0), writes=['ones32'])
        for t_ in KtmEO:
            P.op('pool', lambda e, t_=t_: e.memset(t_[:], 0.0), writes=['Ktm'])
        for t_ in attmEO:
            P.op('pool', lambda e, t_=t_: e.memset(t_[:], 0.0), writes=['attm'])
        P.op('pool', lambda e: e.memset(Qsub[:], 0.0), writes=['Qsub'])
        if e_layer == 1:
            P.op('dve', lambda e: e.tensor_tensor(out=lb[:], in0=lbl[:, :, 1], in1=lbl[:, :, 0], op=ALU.subtract), reads=['lbl'], writes=['lb'])
            P.op('act', lambda e: e.activation(out=lb[:], in_=lb[:], func=AF.Sigmoid), reads=['lb'], writes=['lb'])
            P.op('dve', lambda e: e.tensor_scalar(out=oml[:], in0=lb[:], scalar1=-1.0, scalar2=1.0, op0=ALU.mult, op1=ALU.add),
                 reads=['lb'], writes=['oml'])

        blocks = token_blocks()
        if nblk is not None:
            blocks = blocks[:nblk]

        def load_ub(t0, width):
            P.op('sp', lambda e: e.dma_start(out=ub[:, :, 0:width], in_=uT_d[:, :, t0:t0 + width]), writes=['ub'], chan='ub')

        def proj(g, width):
            pp, ppt = r_psA.next()
            for k in range(KC):
                P.op('pe', lambda e, k=k, pp=pp: e.matmul(pp[:, 0:width], w5[:, k, g, :], ub[:, k, 0:width],
                                                         start=(k == 0), stop=(k == KC - 1)),
                     reads=['ub', 'w5'], writes=[ppt])
            return pp, ppt

        def fslot(ch, d):
            return (ch + NCTXCH) % NCH if d == 0 else ch

        for d in range(2):
            for (t0, w) in blocks:
                nch = w // HC
                nti = w // 128
                ch0 = t0 // HC
                tt0 = t0 // 128
                load_ub(t0, w)
                pq, pqt = proj(0, w)
                pf, pft = proj(1 + d, w)
                if d == 0:
                    pi, pit = r_psB.next()
                    for i in range(nti):
                        for k in range(KC):
                            P.op('pe', lambda e, k=k, i=i, pi=pi: e.matmul(pi[:, i * 128:(i + 1) * 128], ub[:, k, i * 128:(i + 1) * 128],
                                                                         w5[:, k, 3, :], start=(k == 0), stop=(k == KC - 1)),
                                 reads=['ub', 'w5'], pwrites=[pit])
                    P.op('act', lambda e, pi=pi, tt0=tt0, nti=nti: e.activation(
                        out=itm[:, tt0:tt0 + nti, :], in_=v3(pi[:, 0:nti * 128], nti, 128), func=AF.Identity),
                        reads=[pit], pwrites=['itm'])
                P.op('act', lambda e, pf=pf, w=w: e.activation(out=T[0][:, 0:w], in_=pf[:, 0:w], func=AF.Sigmoid), reads=[pft], writes=['T0'])
                if e_layer == 1:
                    P.op('dve', lambda e, w=w, d=d: e.tensor_scalar(out=T[0][:, 0:w], in0=T[0][:, 0:w], scalar1=oml[:, d:d + 1],
                                                                   scalar2=lb[:, d:d + 1], op0=ALU.mult, op1=ALU.add),
                         reads=['T0', 'oml', 'lb'], writes=['T0'])
                P.op('act', lambda e, w=w: e.activation(out=T[1][:, 0:w], in_=T[0][:, 0:w], func=AF.Ln), reads=['T0'], writes=['T1'])
                P.op('dve', lambda e, w=w: e.tensor_scalar(out=T[2][:, 0:w], in0=T[0][:, 0:w], scalar1=-1.0, scalar2=1.0,
                                                          op0=ALU.mult, op1=ALU.add), reads=['T0'], writes=['T2'])
                P.op('dve', lambda e, w=w: e.tensor_tensor_scan(out=T[3][:, 0:w], data0=rmask[:, 0:w], data1=T[1][:, 0:w], initial=0.0,
                                                               op0=ALU.mult, op1=ALU.add), reads=['T1', 'rmask'], writes=['T3'])
                if d == 0:
                    c, ct = T[3], 'T3'
                else:
                    P.op('dve', lambda e, w=w: e.tensor_tensor(out=T[4][:, 0:w], in0=T[1][:, 0:w], in1=T[3][:, 0:w], op=ALU.subtract),
                         reads=['T1', 'T3'], writes=['T4'])
                    P.op('dve', lambda e, w=w, nch=nch: e.tensor_tensor(out=v3(T[4][:, 0:w], nch, HC), in0=v3(T[4][:, 0:w], nch, HC),
                                                                       in1=chunk_col(T[3][:, 0:w], nch, HC - 1), op=ALU.add),
                         reads=['T4', 'T3'], writes=['T4'])
                    c, ct = T[4], 'T4'
                cw = c[:, 0:w]
                s0 = fslot(ch0, d)
                if d == 0:
                    qs0, bcol = 1, -1
                else:
                    qs0, bcol = 0, SUB
                nsb = w // SUB
                TK = [T[8], T[12], T[13]]
                TKt = ['T8', 'T12', 'T13']
                P.op('dve', lambda e, w=w, nch=nch, cw=cw: e.tensor_tensor(out=v3(T[6][:, 0:w], nch, HC), in0=chunk_col(T[3][:, 0:w], nch, HC - 1),
                                                                          in1=v3(cw, nch, HC), op=ALU.subtract),
                     reads=[ct, 'T3'], writes=['T6'])
                P.op('dve', lambda e, w=w, nch=nch, cw=cw, qs0=qs0, bcol=bcol: e.tensor_tensor(
                    out=sub_view(T[7][:, 0:w], nch, qs0, 3, 0, False), in0=sub_view(cw, nch, qs0, 3, 0, False),
                    in1=sub_view(cw, nch, qs0, 3, bcol, True), op=ALU.subtract), reads=[ct], writes=['T7'])
                for jj in range(3):
                    bj = (jj + 1) * SUB - 1 if d == 0 else (jj + 1) * SUB
                    P.op('dve', lambda e, w=w, nch=nch, cw=cw, bj=bj, jj=jj: e.tensor_tensor(out=v3(TK[jj][:, 0:w], nch, HC), in0=chunk_col(cw, nch, bj),
                                                                                          in1=v3(cw, nch, HC), op=ALU.subtract),
                         reads=[ct], writes=[TKt[jj]])
                    P.op('dve', lambda e, w=w, jj=jj: e.tensor_scalar(out=TK[jj][:, 0:w], in0=TK[jj][:, 0:w], scalar1=0.0, scalar2=None, op0=ALU.min),
                         reads=[TKt[jj]], writes=[TKt[jj]])
                P.op('dve', lambda e, w=w, nsb=nsb, cw=cw: e.tensor_tensor(
                    out=v3(T[9][:, 0:w], nsb, SUB), in0=v3(cw, nsb, SUB),
                    in1=bass.AP(cw.tensor, cw.offset + SUB // 2, [list(cw.ap[0]), [SUB, nsb], [0, SUB]]), op=ALU.subtract),
                    reads=[ct], writes=['T9'])
                P.op('dve', lambda e, w=w: e.tensor_scalar(out=T[9][:, 0:w], in0=T[9][:, 0:w], scalar1=-CLAMP, scalar2=CLAMP,
                                                          op0=ALU.max, op1=ALU.min), reads=['T9'], writes=['T9'])
                P.op('act', lambda e, w=w, cw=cw: e.activation(out=T[5][:, 0:w], in_=cw, func=AF.Exp), reads=[ct], writes=['T5'])
                P.op('act', lambda e, w=w, nch=nch, s0=s0: e.activation(out=Dall[:, s0:s0 + nch], in_=chunk_pick(T[3][:, 0:w], nch, HC - 1), func=AF.Exp),
                     reads=['T3'], pwrites=['Dall'])
                P.op('act', lambda e, w=w: e.activation(out=T[6][:, 0:w], in_=T[6][:, 0:w], func=AF.Exp), reads=['T6'], writes=['T6'])
                P.op('act', lambda e, w=w, nch=nch, qs0=qs0: e.activation(out=sub_view(T[7][:, 0:w], nch, qs0, 3, 0, False),
                                                                         in_=sub_view(T[7][:, 0:w], nch, qs0, 3, 0, False), func=AF.Exp),
                     reads=['T7'], writes=['T7'])
                for jj in range(3):
                    P.op('act', lambda e, w=w, jj=jj: e.activation(out=TK[jj][:, 0:w], in_=TK[jj][:, 0:w], func=AF.Exp), reads=[TKt[jj]], writes=[TKt[jj]])
                P.op('act', lambda e, w=w: e.activation(out=T[10][:, 0:w], in_=T[9][:, 0:w], func=AF.Exp), reads=['T9'], writes=['T10'])
                P.op('act', lambda e, w=w: e.activation(out=T[11][:, 0:w], in_=T[9][:, 0:w], func=AF.Exp, scale=-1.0), reads=['T9'], writes=['T11'])
                P.op('dve', lambda e, pq=pq, t0=t0, w=w: e.tensor_tensor(out=Qp[:, t0:t0 + w], in0=pq[:, 0:w], in1=T[5][:, 0:w], op=ALU.mult),
                     reads=[pqt, 'T5'], pwrites=['Qp'])
                P.op('pool', lambda e, w=w: e.tensor_tensor(out=KtT[:, 0:w], in0=T[2][:, 0:w], in1=T[6][:, 0:w], op=ALU.mult),
                     reads=['T2', 'T6'], writes=['KtT'])
                P.op('dve', lambda e, w=w, nch=nch, qs0=qs0, pq=pq: e.scalar_tensor_tensor(
                    out=sub_view(Qsub[:, 0:w], nch, qs0, 3, 0, False), in0=sub_view(T[7][:, 0:w], nch, qs0, 3, 0, False), scalar=1.0,
                    in1=sub_view(pq[:, 0:w], nch, qs0, 3, 0, False), op0=ALU.min, op1=ALU.mult), reads=['T7', pqt], writes=['Qsub'])
                for jj in range(3):
                    P.op('pool', lambda e, w=w, jj=jj: e.tensor_tensor(out=Kj[jj][:, 0:w], in0=TK[jj][:, 0:w], in1=T[2][:, 0:w], op=ALU.mult),
                         reads=[TKt[jj], 'T2'], writes=[f'Kj{jj}'])
                P.op('dve', lambda e, pq=pq, w=w: e.tensor_tensor(out=Qd[:, 0:w], in0=pq[:, 0:w], in1=T[10][:, 0:w], op=ALU.mult),
                     reads=[pqt, 'T10'], writes=['Qd'])
                P.op('pool', lambda e, w=w: e.tensor_tensor(out=Kd[:, 0:w], in0=T[2][:, 0:w], in1=T[11][:, 0:w], op=ALU.mult),
                     reads=['T2', 'T11'], writes=['Kd'])
                pa, pat = r_psB.next()
                for cl in range(nch):
                    i, p0 = cl // 2, (cl % 2) * HC
                    tk = cl * HC
                    P.op('pe', lambda e, pa=pa, i=i, p0=p0, tk=tk: e.matmul(pa[p0:p0 + HC, i * HC:(i + 1) * HC], Kd[:, tk:tk + HC], Qd[:, tk:tk + HC],
                                                                          start=True, stop=True), reads=['Kd', 'Qd'], pwrites=[pat])
                    dummy = 0 if d == 0 else NSUB - 1
                    for sj in range(NSUB):
                        c0 = tk + sj * SUB
                        if sj == dummy:
                            lh, rh, lt, rt = Kd, Qd, 'Kd', 'Qd'
                        else:
                            jj = sj - 1 if d == 0 else sj
                            lh, rh, lt, rt = Kj[jj], Qsub, f'Kj{jj}', 'Qsub'
                        P.op('pe', lambda e, pa=pa, i=i, p0=p0, tk=tk, c0=c0, sj=sj, lh=lh, rh=rh: e.matmul(
                            pa[p0:p0 + HC, 256 + i * HC + sj * SUB:256 + i * HC + (sj + 1) * SUB], lh[:, tk:tk + HC], rh[:, c0:c0 + SUB],
                            start=True, stop=True), reads=[lt, rt], pwrites=[pat])
                P.op('dve', lambda e, pa=pa, nti=nti, d=d: e.tensor_tensor(out=v3(T[0][:, 0:nti * HC], nti, HC), in0=v3(pa[:, 0:nti * HC], nti, HC),
                                                                          in1=bc_mid(tri[:, 2 * d, :], nti), op=ALU.mult),
                     reads=[pat, 'tri'], writes=['T0'])
                P.op('dve', lambda e, pa=pa, nti=nti, d=d: e.tensor_tensor(out=v3(T[1][:, 0:nti * HC], nti, HC), in0=v3(pa[:, 256:256 + nti * HC], nti, HC),
                                                                          in1=bc_mid(tri[:, 2 * d + 1, :], nti), op=ALU.mult),
                     reads=[pat, 'tri'], writes=['T1'])
                for hf in range(2):
                    P.op('pool', lambda e, nti=nti, hf=hf, tt0=tt0: e.tensor_tensor(
                        out=attmEO[hf][hf * HC:(hf + 1) * HC, tt0:tt0 + nti, :], in0=v3(T[0][hf * HC:(hf + 1) * HC, 0:nti * HC], nti, HC),
                        in1=v3(T[1][hf * HC:(hf + 1) * HC, 0:nti * HC], nti, HC), op=ALU.add),
                        reads=['T0', 'T1'], pwrites=['attm'])
                for i in range(nti):
                    P.op('pe', lambda e, i=i: e.transpose(psT[:, i * 128:(i + 1) * 128], KtT[:, i * 128:(i + 1) * 128], ident[:, :]),
                         reads=['KtT', 'ident'], pwrites=['psT'])
                for hf in range(2):
                    P.op('dve', lambda e, nti=nti, hf=hf: e.tensor_copy(out=KtmEO[hf][hf * HC:(hf + 1) * HC, 0:nti, :],
                                                                       in_=v3(psT[hf * HC:(hf + 1) * HC, 0:nti * 128], nti, 128)),
                         reads=['psT'], pwrites=['Ktm'])
                for b0 in range(0, nch, 4):
                    pd, pdt = r_psB.next()
                    for cl in range(b0, b0 + 4):
                        i = cl // 2
                        P.op('pe', lambda e, pd=pd, cl=cl, i=i, b0=b0, tt0=tt0: e.matmul(
                            pd[:, (cl - b0) * 128:(cl - b0 + 1) * 128], KtmEO[cl % 2][:, i, :], itm[:, tt0 + i, :],
                            start=True, stop=True), reads=['Ktm', 'itm'], pwrites=[pdt])
                    P.op('act', lambda e, pd=pd, s0=s0, b0=b0: e.activation(out=dS[:, s0 + b0:s0 + b0 + 4, :], in_=v3(pd[:, 0:512], 4, 128),
                                                                            func=AF.Identity), reads=[pdt], pwrites=['dS'])
            if stop <= 0:
                break
            for ee in range(128):
                if d == 0:
                    a1 = bass.AP(dS, ee, [[NCH * 128, 128], [128, NCH]])
                    a0 = Dall[:, :]
                else:
                    a1 = bass.AP(dS, (NCH - 1) * 128 + ee, [[NCH * 128, 128], [-128, NCH]])
                    a0 = bass.AP(Dall, NCH - 1, [[NCH, 128], [-1, NCH]])
                P.op('dve', lambda e, a1=a1, a0=a0: e.tensor_tensor_scan(out=a1, data0=a0, data1=a1, initial=0.0, op0=ALU.mult, op1=ALU.add),
                     reads=['dS', 'Dall'], pwrites=['dS'])
            if stop <= 1:
                break
            for (t0, w) in blocks:
                nch = w // HC
                ch0 = t0 // HC
                tt0 = t0 // 128
                po, pot = r_psB.next()
                for cl in range(nch):
                    i = cl // 2
                    tk = t0 + cl * HC
                    ch = ch0 + cl
                    if d == 0:
                        s = fslot(ch, 0)
                        prev = s - 1 if s > 0 else None
                    else:
                        prev = ch + 1 if ch < NCH - 1 else None
                    if prev is not None:
                        P.op('pe', lambda e, po=po, cl=cl, prev=prev, tk=tk: e.matmul(po[:, cl * HC:(cl + 1) * HC], dS[:, prev, :], Qp[:, tk:tk + HC],
                                                                                    start=True, stop=False), reads=['dS', 'Qp'], pwrites=[pot])
                    P.op('pe', lambda e, po=po, cl=cl, i=i, tt0=tt0, prev=prev: e.matmul(
                        po[:, cl * HC:(cl + 1) * HC], itm[:, tt0 + i, :], attmEO[cl % 2][:, tt0 + i, :],
                        start=(prev is None), stop=True), reads=['itm', 'attm'], pwrites=[pot])
                if d == 0:
                    P.op('act', lambda e, po=po, t0=t0, w=w: e.activation(out=oacc[:, t0:t0 + w], in_=po[:, 0:w], func=AF.Identity),
                         reads=[pot], pwrites=['oacc'])
                else:
                    P.op('dve', lambda e, po=po, t0=t0, w=w: e.tensor_tensor(out=oacc[:, t0:t0 + w], in0=po[:, 0:w], in1=oacc[:, t0:t0 + w], op=ALU.add),
                         reads=[pot, 'oacc'], pwrites=['oacc'])
        for (t0, w) in (blocks if stop > 2 else []):
            load_ub(t0, w)
            pg, pgt = proj(4, w)
            P.op('act', lambda e, pg=pg, w=w: e.activation(out=T[0][:, 0:w], in_=pg[:, 0:w], func=AF.Silu), reads=[pgt], writes=['T0'])
            P.op('act', lambda e, t0=t0, w=w: e.activation(out=T[1][:, 0:w], in_=oacc[:, t0:t0 + w], func=AF.Square), reads=['oacc'], writes=['T1'])
            pn, pnt = r_psB.next()
            P.op('pe', lambda e, pn=pn, w=w: e.matmul(pn[:, 0:w], ones32[:, :], T[1][:, 0:w], start=True, stop=True), reads=['T1', 'ones32'], writes=[pnt])
            P.op('dve', lambda e, pn=pn, w=w: e.tensor_scalar(out=T[2][:, 0:w], in0=pn[:, 0:w], scalar1=1.0 / 128, scalar2=EPS,
                                                             op0=ALU.mult, op1=ALU.add), reads=[pnt], writes=['T2'])
            P.op('act', lambda e, w=w: e.activation(out=T[2][:, 0:w], in_=T[2][:, 0:w], func=AF.Sqrt), reads=['T2'], writes=['T2'])
            P.op('dve', lambda e, w=w: e.reciprocal(out=T[2][:, 0:w], in_=T[2][:, 0:w]), reads=['T2'], writes=['T2'])
            P.op('dve', lambda e, t0=t0, w=w: e.scalar_tensor_tensor(out=T[3][:, 0:w], in0=oacc[:, t0:t0 + w], scalar=hgn[:, 0:1], in1=T[2][:, 0:w],
                                                                    op0=ALU.mult, op1=ALU.mult), reads=['oacc', 'T2', 'hgn'], writes=['T3'])
            P.op('pool', lambda e, w=w: e.tensor_tensor(out=yb[:, 0:w], in0=T[3][:, 0:w], in1=T[0][:, 0:w], op=ALU.mult),
                 reads=['T3', 'T0'], writes=['yb'])
            P.op('sp', lambda e, t0=t0, w=w: e.dma_start(out=y_d[:, t0:t0 + w], in_=yb[:, 0:w]), reads=['yb'], pwrites=['y_d'], chan='yb')
        P.finish('sp')
        P.emit()
    return nc


def hgrn_consts():
    p = np.arange(128)[:, None] % HC
    t = np.arange(HC)[None, :]
    same = (p // SUB) == (t // SUB)
    tri = np.stack([same & (p <= t), (p // SUB) < (t // SUB), same & (p >= t), (p // SUB) > (t // SUB)], axis=1).astype(np.float32)
    rmask = np.broadcast_to((np.arange(QBW) % HC != 0).astype(np.float32)[None, :], (128, QBW))
    return np.ascontiguousarray(tri), np.ascontiguousarray(rmask)


def run_hgrn(uT_all, w_in, lbl, hgn, e_layer, stop=99, ncores=NCORE):
    nc = build_hgrn_prog(e_layer, stop)
    ident = np.eye(128, dtype=np.float32).astype(NPBF)
    tri, rmask = hgrn_consts()
    in_maps = []
    for j in range(NCORE):
        cols = [w_in[:, (3 + g) * 1024 + j * 128: (3 + g) * 1024 + (j + 1) * 128] for g in range(5)]
        w5 = np.stack([lay_w(c) for c in cols], axis=2)
        lj = np.ascontiguousarray(lbl[:, :, j * 128:(j + 1) * 128].transpose(2, 0, 1)).astype(np.float32)
        hj = np.ascontiguousarray(hgn[j * 128:(j + 1) * 128].reshape(128, 1)).astype(np.float32)
        in_maps.append({"uT": uT_all, "w5": np.ascontiguousarray(w5), "lbl": lj, "hgn": hj, "ident": ident, "tri": tri, "rmask": rmask})
    in_maps = in_maps[:ncores]
    res = run_bass_kernel_spmd(nc, in_maps, core_ids=list(range(ncores)))
    return np.stack([res.results[j]["y_out"] for j in range(ncores)], axis=1)


def _fm(x):
    T = x.shape[0]
    return np.ascontiguousarray(x.T.reshape(KC, 128, T).transpose(1, 0, 2))


def _unfm(a):
    T = a.shape[2]
    return np.ascontiguousarray(a.transpose(1, 0, 2).reshape(D, T).T)


def _vec(v):
    return np.ascontiguousarray(v.reshape(KC, 128).T)


def _lay_win(w):
    g = w[:, :DFF].reshape(KC, 128, JC, 128)
    u = w[:, DFF:].reshape(KC, 128, JC, 128)
    return np.ascontiguousarray(np.concatenate([g.transpose(2, 1, 0, 3), u.transpose(2, 1, 0, 3)], axis=3))


def _lay_wout(w):
    return np.ascontiguousarray(w.reshape(JC, 128, KC, 128).transpose(2, 1, 0, 3))


def _lay_wo(w):
    return np.ascontiguousarray(w.reshape(KC, 128, KC, 128).transpose(2, 1, 0, 3))


def _run_token(stages_spec, h_list, y_all, vec_list, weights, want_u, final):
    stages = [(s, i) for i, s in enumerate(stages_spec)]
    nc = build_token_prog(stages, want_u=want_u, final=final)
    vecs = np.ascontiguousarray(np.stack([_vec(v) for v in vec_list], axis=1)).astype(np.float32)
    shared = {"vecs": vecs}
    for si, s in enumerate(stages_spec):
        if s == 'proj':
            shared[f"wo{si}"] = _lay_wo(weights[si])
        else:
            shared[f"wi{si}"] = _lay_win(weights[si][0])
            shared[f"wf{si}"] = _lay_wout(weights[si][1])
    in_maps = []
    for r in range(NCORE):
        m = dict(shared)
        m["h_in"] = h_list[r]
        if y_all is not None:
            m["y_in"] = np.ascontiguousarray(np.concatenate(
                [y_all[:, :, r * TLAT:(r + 1) * TLAT], y_all[:, :, SEQ + r * TCTX: SEQ + (r + 1) * TCTX]], axis=2))
        in_maps.append(m)
    res = run_bass_kernel_spmd(nc, in_maps, core_ids=list(range(NCORE)))
    return res.results


def _gather_u(results):
    u_all = np.zeros((128, KC, TALL), NPBF)
    for r in range(NCORE):
        uo = results[r]["u_out"]
        u_all[:, :, r * TLAT:(r + 1) * TLAT] = uo[:, :, :TLAT]
        u_all[:, :, SEQ + r * TCTX: SEQ + (r + 1) * TCTX] = uo[:, :, TLAT:]
    return u_all


def kernel(x, c, ctx, c_ctx, w_mod, b_mod, norm_g, w_ff_in, w_ff_out, w_in_even, w_out_even,
           na_rpb, hg_lb_logits, hg_norm_g, w_qkv_odd, w_o_odd, sink_odd, final_norm_g):
    f = lambda a: np.asarray(a, dtype=np.float32)
    x, c, ctx, c_ctx, w_mod, b_mod, norm_g = f(x), f(c), f(ctx), f(c_ctx), f(w_mod), f(b_mod), f(norm_g)
    w_ff_in, w_ff_out, w_in_even, w_out_even = f(w_ff_in), f(w_ff_out), f(w_in_even), f(w_out_even)
    na_rpb, hg_lb_logits, hg_norm_g = f(na_rpb), f(hg_lb_logits), f(hg_norm_g)
    w_qkv_odd, w_o_odd, sink_odd, final_norm_g = f(w_qkv_odd), f(w_o_odd), f(sink_odd), f(final_norm_g)

    mod = run_mod(c[0], c_ctx, w_mod, b_mod).reshape(DEPTH, 2, 3, 3, D)

    def ffn_vecs(l, sub):
        return [norm_g[l, sub], mod[l, 0, sub, 0], mod[l, 0, sub, 1], mod[l, 0, sub, 2],
                mod[l, 1, sub, 0], mod[l, 1, sub, 1], mod[l, 1, sub, 2]]

    def u_vecs(l):
        return [norm_g[l, 1], mod[l, 0, 1, 0], mod[l, 0, 1, 1], mod[l, 1, 1, 0], mod[l, 1, 1, 1]]

    h_list = []
    for r in range(NCORE):
        tok = np.concatenate([x[0, r * TLAT:(r + 1) * TLAT], ctx[0, r * TCTX:(r + 1) * TCTX]], axis=0)
        h_list.append(_fm(tok))

    res = _run_token(['ffn'], h_list, None, ffn_vecs(0, 0) + u_vecs(0), [(w_ff_in[0, 0], w_ff_out[0, 0])], True, False)
    out = None
    for l in range(DEPTH):
        h_list = [res[r]["h_out"] for r in range(NCORE)]
        u_all = _gather_u(res)
        if l % 2 == 0:
            e = l // 2
            ya = run_na(u_all, w_in_even[e], na_rpb[e])
            yb = run_hgrn(u_all, w_in_even[e], hg_lb_logits, hg_norm_g[e], e)
            y_all = np.ascontiguousarray(np.concatenate([ya, yb], axis=1))
            w_o = w_out_even[e]
        else:
            o = l // 2
            y_all = run_odd(u_all, w_qkv_odd[o], sink_odd[o])
            w_o = w_o_odd[o]
        gate_vecs = [mod[l, 0, 1, 2], mod[l, 1, 1, 2]]
        if l < DEPTH - 1:
            res = _run_token(['proj', 'ffn', 'ffn'], h_list, y_all,
                             gate_vecs + ffn_vecs(l, 2) + ffn_vecs(l + 1, 0) + u_vecs(l + 1),
                             [w_o, (w_ff_in[l, 1], w_ff_out[l, 1]), (w_ff_in[l + 1, 0], w_ff_out[l + 1, 0])], True, False)
        else:
            res = _run_token(['proj', 'ffn'], h_list, y_all, gate_vecs + ffn_vecs(l, 2) + [final_norm_g],
                             [w_o, (w_ff_in[l, 1], w_ff_out[l, 1])], False, True)
            out = np.zeros((1, SEQ, D), np.float32)
            for r in range(NCORE):
                out[0, r * TLAT:(r + 1) * TLAT] = _unfm(res[r]["o_out"])[:TLAT]
    return out
```

```python
import numpy as np
import ml_dtypes
from contextlib import ExitStack
import concourse.bass as bass
import concourse.mybir as mybir
from concourse.bass_utils import run_bass_kernel_spmd

F32 = mybir.dt.float32
BF16 = mybir.dt.bfloat16
AF = mybir.ActivationFunctionType
ALU = mybir.AluOpType
NPBF = ml_dtypes.bfloat16

D = 2048
KC = 16
DFF = 5632
JC = 44
NCORE = 8
SEQ = 8192
CTX = 256
TLAT = SEQ // NCORE
TCTX = CTX // NCORE
TT = TLAT + TCTX
TW = 352
NT = 3
EPS = 1e-6
DEPTH = 4


SAME_ENGINE_SYNC = False


class Prog:
    ENGS = {'pe': 'tensor', 'act': 'scalar', 'dve': 'vector', 'pool': 'gpsimd', 'sp': 'sync'}

    def __init__(self, nc, es):
        self.nc = nc
        self.es = es
        self.q = {e: [] for e in self.ENGS}
        self.sems = {}
        self.val = {}
        self.w = {}
        self.r = {}
        self.waited = {}
        self.limit = None
        self.nops = 0

    def sem(self, key):
        if key not in self.sems:
            self.sems[key] = self.es.enter_context(self.nc.semaphore("s_" + key))
            self.val[key] = 0
        return self.sems[key]

    def op(self, eng, fn, reads=(), writes=(), pwrites=(), chan=None):
        self.nops += 1
        if self.limit is not None and self.nops > self.limit:
            return None
        waits = {}

        def need(evs):
            for k, (v, e) in evs.items():
                if e == 'pe' and eng == 'pe':
                    continue
                if e == eng and not SAME_ENGINE_SYNC and v < self.val.get(eng, 0):
                    continue
                if waits.get(k, 0) < v:
                    waits[k] = v
        for t in reads:
            need(self.w.get(t, {}))
        for t in list(writes) + list(pwrites):
            need(self.w.get(t, {}))
            need(self.r.get(t, {}))
        wl = []
        for k, v in waits.items():
            if self.waited.get((eng, k), 0) >= v:
                continue
            self.waited[(eng, k)] = v
            wl.append((k, v))
        if chan is None:
            key, inc, ee = eng, 1, eng
        else:
            key, inc, ee = 'd_' + chan, 16, 'dma'
        self.sem(key)
        self.val[key] += inc
        v = self.val[key]
        for t in reads:
            self.r.setdefault(t, {})[key] = (v, ee)
        for t in writes:
            self.w[t] = {key: (v, ee)}
            self.r[t] = {}
        for t in pwrites:
            self.w.setdefault(t, {})[key] = (v, ee)
            self.r[t] = {}
        self.q[eng].append((wl, fn, key, inc))
        return (key, v)

    def finish(self, eng='sp'):
        wl = [(k, v) for k, v in self.val.items() if k.startswith('d_')]
        self.q[eng].append((wl, None, None, 0))

    def emit(self):
        nc = self.nc
        with nc.Block() as block:
            for e, attr in self.ENGS.items():
                items = self.q[e]

                def body(engine, items=items):
                    for wl, fn, key, inc in items:
                        for k, v in wl:
                            engine.wait_ge(self.sems[k], v)
                        if fn is not None:
                            ins = fn(engine)
                            ins.then_inc(self.sems[key], inc)
                getattr(block, attr)(body)


def bc_mid(ap, n):
    a = [list(x) for x in ap.ap]
    return bass.AP(ap.tensor, ap.offset, [a[0], [0, n]] + a[1:])


def bc_last(ap, n):
    a = [list(x) for x in ap.ap]
    return bass.AP(ap.tensor, ap.offset, a + [[0, n]])


class Ring:
    def __init__(self, bufs, name):
        self.bufs = bufs
        self.name = name
        self.i = 0

    def next(self):
        k = self.i % len(self.bufs)
        self.i += 1
        return self.bufs[k], f"{self.name}{k}"


def build_token_prog(stages, want_u=False, final=False):
    nc = bass.Bass("TRN2", target_bir_lowering=False)
    nv = 0
    for s in stages:
        nv += 2 if s[0] == 'proj' else 7
    if want_u:
        nv += 5
    if final:
        nv += 1
    h_in = nc.dram_tensor("h_in", [128, KC, TT], F32, kind="ExternalInput").ap()
    vecs_d = nc.dram_tensor("vecs", [128, nv, KC], F32, kind="ExternalInput").ap()
    wd = {}
    y_in = None
    for si, s in enumerate(stages):
        if s[0] == 'proj':
            wd[si] = nc.dram_tensor(f"wo{si}", [KC, 128, KC, 128], F32, kind="ExternalInput").ap()
            y_in = nc.dram_tensor("y_in", [128, KC, TT], BF16, kind="ExternalInput").ap()
        else:
            wd[si] = (nc.dram_tensor(f"wi{si}", [JC, 128, KC, 256], F32, kind="ExternalInput").ap(),
                      nc.dram_tensor(f"wf{si}", [KC, 128, JC, 128], F32, kind="ExternalInput").ap())
    hd = [h_in]
    for si in range(len(stages)):
        last = si == len(stages) - 1
        if last and not final:
            hd.append(nc.dram_tensor("h_out", [128, KC, TT], F32, kind="ExternalOutput").ap())
        else:
            hd.append(nc.dram_tensor(f"h_s{si}", [128, KC, TT], F32).ap())
    u_out = nc.dram_tensor("u_out", [128, KC, TT], BF16, kind="ExternalOutput").ap() if want_u else None
    o_out = nc.dram_tensor("o_out", [128, KC, TT], F32, kind="ExternalOutput").ap() if final else None

    tiles = [(i * TW, TW) for i in range(NT)]
    segs = []
    for (t0, w) in tiles:
        sg = []
        lat_end = min(max(TLAT - t0, 0), w)
        if lat_end > 0:
            sg.append((0, lat_end, 0))
        if lat_end < w:
            sg.append((lat_end, w, 1))
        segs.append(sg)

    with ExitStack() as es:
        P = Prog(nc, es)
        sb = lambda name, shape, dt: es.enter_context(nc.sbuf_tensor(name, shape, dt))
        vecs = sb("vecs_sb", [128, nv, KC], F32)
        dvec = sb("dvec", [128, 8, KC], F32)
        ones = sb("ones", [128, 128], F32)
        uT = sb("uT", [128, KC, TT], BF16)
        big = sb("big", [128, JC, TT], BF16)
        rstd = sb("rstd", [128, TW], F32)
        tmpn = [sb(f"tmpn{i}", [128, TW], F32) for i in range(6)]
        wib = [sb(f"wib{i}", [128, KC, 256], BF16) for i in range(3)]
        wob = [sb(f"wob{i}", [128, JC, 128], BF16) for i in range(2)]
        sgb = [sb(f"sgb{i}", [128, TW], F32) for i in range(3)]
        hres = [sb(f"hres{i}", [128, TW], F32) for i in range(8)]
        hout = [sb(f"hout{i}", [128, TW], F32) for i in range(3)]
        ps = [es.enter_context(nc.psum_tensor(f"ps{i}", [128, 512], F32)) for i in range(8)]
        r_tmpn = Ring(tmpn, "tmpn")
        r_wib = Ring(wib, "wib")
        r_wob = Ring(wob, "wob")
        r_sgb = Ring(sgb, "sgb")
        r_hres = Ring(hres, "hres")
        r_hout = Ring(hout, "hout")
        r_psA = Ring(ps[0:5], "ps")
        psn = ps[5:8]

        P.op('sp', lambda e: e.dma_start(out=vecs[:], in_=vecs_d), writes=['vecs'], chan='vecs')
        P.op('pool', lambda e: e.memset(ones[:], 1.0), writes=['ones'])

        def norm_stage(src, vbase, has_mod, dst_dram, presummed=False):
            gi, shl, scl, shc, scc = vbase
            if has_mod:
                for which, sc_i in ((0, scl), (1, scc)):
                    P.op('dve', lambda e, which=which, sc_i=sc_i: e.scalar_tensor_tensor(
                        out=dvec[:, which, :], in0=vecs[:, sc_i, :], scalar=1.0, in1=vecs[:, gi, :],
                        op0=ALU.add, op1=ALU.mult), reads=['vecs'], writes=[f'dvA{which}'])
            for ti, (t0, w) in enumerate(tiles):
                pb, pbt = psn[ti], f'psn{ti}'
                for k in (range(KC) if not presummed else []):
                    hr, hrt = r_hres.next()
                    P.op('sp', lambda e, hr=hr, k=k, t0=t0, w=w: e.dma_start(out=hr[:, 0:w], in_=src[:, k, t0:t0 + w]),
                         reads=[src.tensor.name], writes=[hrt], chan=hrt)
                    sq, sqt = r_sgb.next()
                    P.op('act', lambda e, hr=hr, sq=sq, w=w: e.activation(out=sq[:, 0:w], in_=hr[:, 0:w], func=AF.Square),
                         reads=[hrt], writes=[sqt])
                    P.op('pe', lambda e, k=k, pb=pb, sq=sq, w=w: e.matmul(pb[:, 0:w], ones[:, :], sq[:, 0:w],
                                                                        start=(k == 0), stop=(k == KC - 1)),
                         reads=[sqt, 'ones'], writes=[pbt])
                P.op('dve', lambda e, pb=pb, w=w: e.tensor_scalar(out=rstd[:, 0:w], in0=pb[:, 0:w], scalar1=1.0 / D,
                                                                 scalar2=EPS, op0=ALU.mult, op1=ALU.add),
                     reads=[pbt], writes=['rstd'])
                P.op('act', lambda e, w=w: e.activation(out=rstd[:, 0:w], in_=rstd[:, 0:w], func=AF.Sqrt),
                     reads=['rstd'], writes=['rstd'])
                P.op('dve', lambda e, w=w: e.reciprocal(out=rstd[:, 0:w], in_=rstd[:, 0:w]),
                     reads=['rstd'], writes=['rstd'])
                for k in range(KC):
                    hr, hrt = r_hres.next()
                    P.op('sp', lambda e, hr=hr, k=k, t0=t0, w=w: e.dma_start(out=hr[:, 0:w], in_=src[:, k, t0:t0 + w]),
                         reads=[src.tensor.name], writes=[hrt], chan=hrt)
                    if has_mod:
                        tb, tbt = r_tmpn.next()
                        for (c0, c1, which) in segs[ti]:
                            a_ap = dvec[:, which, k:k + 1]
                            b_ap = vecs[:, (shl if which == 0 else shc), k:k + 1]
                            P.op('dve', lambda e, tb=tb, hr=hr, c0=c0, c1=c1, a_ap=a_ap: e.scalar_tensor_tensor(
                                out=tb[:, c0:c1], in0=hr[:, c0:c1], scalar=a_ap, in1=rstd[:, c0:c1],
                                op0=ALU.mult, op1=ALU.mult), reads=[hrt, 'rstd', f'dvA{which}'], pwrites=[tbt])
                            P.op('act', lambda e, tb=tb, c0=c0, c1=c1, b_ap=b_ap, k=k, t0=t0: e.activation(
                                out=uT[:, k, t0 + c0:t0 + c1], in_=tb[:, c0:c1], func=AF.Identity, bias=b_ap, scale=1.0),
                                reads=[tbt, 'vecs'], pwrites=[f'u_t{ti}'])
                    else:
                        ho, hot = r_hout.next()
                        P.op('dve', lambda e, ho=ho, hr=hr, k=k, w=w: e.scalar_tensor_tensor(
                            out=ho[:, 0:w], in0=hr[:, 0:w], scalar=vecs[:, gi, k:k + 1], in1=rstd[:, 0:w],
                            op0=ALU.mult, op1=ALU.mult), reads=[hrt, 'rstd', 'vecs'], writes=[hot])
                        P.op('sp', lambda e, ho=ho, k=k, t0=t0, w=w: e.dma_start(out=dst_dram[:, k, t0:t0 + w], in_=ho[:, 0:w]),
                             reads=[hot], pwrites=[dst_dram.tensor.name], chan='st_' + hot)
                if has_mod and dst_dram is not None:
                    P.op('sp', lambda e, t0=t0, w=w: e.dma_start(out=dst_dram[:, :, t0:t0 + w], in_=uT[:, :, t0:t0 + w]),
                         reads=[f'u_t{ti}'], pwrites=[dst_dram.tensor.name], chan=f'ust{ti}')

        def ffn1_stage(w_in_d):
            for j in range(JC):
                wb, wbt = r_wib.next()
                P.op('pool', lambda e, j=j, wb=wb: e.dma_start(out=wb[:], in_=w_in_d[j]), writes=[wbt], chan=wbt)
                for ti, (t0, w) in enumerate(tiles):
                    pg, pgt = r_psA.next()
                    pu, put = r_psA.next()
                    for half, (pp, ppt) in enumerate(((pg, pgt), (pu, put))):
                        for k in range(KC):
                            P.op('pe', lambda e, k=k, pp=pp, wb=wb, half=half, t0=t0, w=w: e.matmul(
                                pp[:, 0:w], wb[:, k, half * 128:(half + 1) * 128], uT[:, k, t0:t0 + w],
                                start=(k == 0), stop=(k == KC - 1)),
                                reads=[wbt, f'u_t{ti}'], writes=[ppt])
                    sg, sgt = r_sgb.next()
                    P.op('act', lambda e, sg=sg, pg=pg, w=w: e.activation(out=sg[:, 0:w], in_=pg[:, 0:w], func=AF.Silu),
                         reads=[pgt], writes=[sgt])
                    P.op('dve', lambda e, sg=sg, pu=pu, j=j, t0=t0, w=w: e.tensor_tensor(
                        out=big[:, j, t0:t0 + w], in0=sg[:, 0:w], in1=pu[:, 0:w], op=ALU.mult),
                        reads=[sgt, put], pwrites=[f'big_t{ti}'])

        def proj_stage(w_d, kc, src, dst, gl, gc, half):
            mul = 0.5 if half else 1.0
            for which, gi_ in ((0, gl), (1, gc)):
                P.op('dve', lambda e, which=which, gi_=gi_: e.tensor_scalar(
                    out=dvec[:, 2 + which, :], in0=vecs[:, gi_, :], scalar1=mul, scalar2=None, op0=ALU.mult),
                    reads=['vecs'], writes=[f'dvG{which}'])
            deferred = []
            for dc in range(KC):
                wb, wbt = r_wob.next()
                P.op('pool', lambda e, dc=dc, wb=wb: e.dma_start(out=wb[:, 0:kc, :], in_=w_d[dc]), writes=[wbt], chan=wbt)
                for ti, (t0, w) in enumerate(tiles):
                    hr, hrt = r_hres.next()
                    P.op('sp', lambda e, hr=hr, dc=dc, t0=t0, w=w: e.dma_start(out=hr[:, 0:w], in_=src[:, dc, t0:t0 + w]),
                         reads=[src.tensor.name], writes=[hrt], chan=hrt)
                    pp, ppt = r_psA.next()
                    for j in range(kc):
                        P.op('pe', lambda e, j=j, pp=pp, wb=wb, t0=t0, w=w: e.matmul(
                            pp[:, 0:w], wb[:, j, :], big[:, j, t0:t0 + w], start=(j == 0), stop=(j == kc - 1)),
                            reads=[wbt, f'big_t{ti}'], writes=[ppt])
                    while deferred:
                        deferred.pop(0)()
                    ho, hot = r_hout.next()
                    for (c0, c1, which) in segs[ti]:
                        P.op('dve', lambda e, ho=ho, pp=pp, hr=hr, c0=c0, c1=c1, which=which, dc=dc: e.scalar_tensor_tensor(
                            out=ho[:, c0:c1], in0=pp[:, c0:c1], scalar=dvec[:, 2 + which, dc:dc + 1], in1=hr[:, c0:c1],
                            op0=ALU.mult, op1=ALU.add), reads=[ppt, hrt, f'dvG{which}'], pwrites=[hot])
                    P.op('sp', lambda e, ho=ho, dc=dc, t0=t0, w=w: e.dma_start(out=dst[:, dc, t0:t0 + w], in_=ho[:, 0:w]),
                         reads=[hot], pwrites=[dst.tensor.name], chan='st_' + hot)
                    sq, sqt = r_sgb.next()
                    P.op('act', lambda e, ho=ho, sq=sq, w=w: e.activation(out=sq[:, 0:w], in_=ho[:, 0:w], func=AF.Square),
                         reads=[hot], writes=[sqt])

                    def _ssq(dc=dc, ti=ti, sq=sq, sqt=sqt, w=w):
                        P.op('pe', lambda e: e.matmul(psn[ti][:, 0:w], ones[:, :], sq[:, 0:w], start=(dc == 0), stop=(dc == KC - 1)),
                             reads=[sqt, 'ones'], writes=[f'psn{ti}'])
                    deferred.append(_ssq)
            while deferred:
                deferred.pop(0)()

        vb = 0
        for si, s in enumerate(stages):
            src, dst = hd[si], hd[si + 1]
            if s[0] == 'proj':
                for ti, (t0, w) in enumerate(tiles):
                    P.op('sp', lambda e, t0=t0, w=w: e.dma_start(out=big[:, 0:KC, t0:t0 + w], in_=y_in[:, :, t0:t0 + w]),
                         writes=[f'big_t{ti}'], chan=f'yin{ti}')
                proj_stage(wd[si], KC, src, dst, vb, vb + 1, False)
                vb += 2
            else:
                g, shl, scl, gl, shc, scc, gc = range(vb, vb + 7)
                norm_stage(src, (g, shl, scl, shc, scc), True, None, presummed=(si > 0))
                ffn1_stage(wd[si][0])
                proj_stage(wd[si][1], JC, src, dst, gl, gc, True)
                vb += 7
        if want_u:
            g, shl, scl, shc, scc = range(vb, vb + 5)
            norm_stage(hd[-1], (g, shl, scl, shc, scc), True, u_out, presummed=True)
            vb += 5
        if final:
            norm_stage(hd[-1], (vb, 0, 0, 0, 0), False, o_out, presummed=True)
            vb += 1
        P.finish('sp')
        P.emit()
    return nc


MODC = 9 * D // NCORE
MCH = 384
NMCH = MODC // MCH


def build_mod_prog():
    nc = bass.Bass("TRN2", target_bir_lowering=False)
    cvec_d = nc.dram_tensor("cvec", [128, KC, 2], F32, kind="ExternalInput").ap()
    w_d = nc.dram_tensor("wm", [DEPTH * NMCH, 128, KC, MCH], F32, kind="ExternalInput").ap()
    b_d = nc.dram_tensor("bm", [2, DEPTH * MODC], F32, kind="ExternalInput").ap()
    o_d = nc.dram_tensor("mod_out", [2, DEPTH * MODC], F32, kind="ExternalOutput").ap()
    with ExitStack() as es:
        P = Prog(nc, es)
        sb = lambda name, shape, dt: es.enter_context(nc.sbuf_tensor(name, shape, dt))
        cv = sb("cv", [128, KC, 2], F32)
        sc = sb("sc", [128, KC, 2], F32)
        bsb = sb("bsb", [2, DEPTH * MODC], F32)
        mo = sb("mo", [2, DEPTH * MODC], F32)
        wbs = [sb(f"wb{i}", [128, KC, MCH], F32) for i in range(3)]
        ps = [es.enter_context(nc.psum_tensor(f"ps{i}", [128, 512], F32)) for i in range(4)]
        r_wb = Ring(wbs, "wb")
        r_ps = Ring(ps, "ps")
        P.op('sp', lambda e: e.dma_start(out=cv[:], in_=cvec_d), writes=['cv'], chan='cv')
        P.op('sp', lambda e: e.dma_start(out=bsb[:], in_=b_d), writes=['bsb'], chan='bsb')
        P.op('act', lambda e: e.activation(out=sc[:], in_=cv[:], func=AF.Silu), reads=['cv'], writes=['sc'])
        for i in range(DEPTH * NMCH):
            wb, wbt = r_wb.next()
            P.op('sp', lambda e, i=i, wb=wb: e.dma_start(out=wb[:], in_=w_d[i]), writes=[wbt], chan=wbt)
            pp, ppt = r_ps.next()
            for k in range(KC):
                P.op('pe', lambda e, k=k, pp=pp, wb=wb: e.matmul(pp[0:2, 0:MCH], sc[:, k, :], wb[:, k, :],
                                                                start=(k == 0), stop=(k == KC - 1)),
                     reads=[wbt, 'sc'], writes=[ppt])
            P.op('dve', lambda e, i=i, pp=pp: e.tensor_tensor(out=mo[:, i * MCH:(i + 1) * MCH], in0=pp[0:2, 0:MCH],
                                                             in1=bsb[:, i * MCH:(i + 1) * MCH], op=ALU.add),
                 reads=[ppt, 'bsb'], pwrites=['mo'])
        P.op('sp', lambda e: e.dma_start(out=o_d, in_=mo[:]), reads=['mo'], writes=['o_d'], chan='o_d')
        P.finish('sp')
        P.emit()
    return nc


def run_mod(c, c_ctx, w_mod, b_mod):
    nc = build_mod_prog()
    cvec = np.ascontiguousarray(np.stack([c.reshape(KC, 128).T, c_ctx.reshape(KC, 128).T], axis=2)).astype(np.float32)
    in_maps = []
    for r in range(NCORE):
        ws = w_mod[:, :, r * MODC:(r + 1) * MODC]
        ws = ws.reshape(DEPTH, KC, 128, NMCH, MCH).transpose(0, 3, 2, 1, 4).reshape(DEPTH * NMCH, 128, KC, MCH)
        bs = b_mod[:, r * MODC:(r + 1) * MODC].reshape(1, DEPTH * MODC)
        in_maps.append({"cvec": cvec, "wm": np.ascontiguousarray(ws),
                        "bm": np.ascontiguousarray(np.concatenate([bs, bs], axis=0))})
    res = run_bass_kernel_spmd(nc, in_maps, core_ids=list(range(NCORE)))
    mod = np.zeros((DEPTH, 2, 9 * D), np.float32)
    for r in range(NCORE):
        o = res.results[r]["mod_out"].reshape(2, DEPTH, MODC)
        mod[:, :, r * MODC:(r + 1) * MODC] = o.transpose(1, 0, 2)
    return mod


TALL = SEQ + CTX
NTL = TALL // 128
QBW = 512
NQB = SEQ // QBW
NEG = -30000.0


def v3(ap, a, b):
    l = [list(x) for x in ap.ap]
    st, n = l[-1]
    assert n == a * b, (n, a, b)
    return bass.AP(ap.tensor, ap.offset, l[:-1] + [[st * b, a], [st, b]])


def token_blocks():
    return [(i * QBW, QBW) for i in range(NQB)] + [(SEQ, CTX)]


def build_odd_prog(debug=False):
    nc = bass.Bass("TRN2", target_bir_lowering=False)
    dq_d = nc.dram_tensor("dq", [128, 2, TALL], BF16, kind="ExternalOutput").ap() if debug else None
    dk_d = nc.dram_tensor("dk", [128, TALL], BF16, kind="ExternalOutput").ap() if debug else None
    uT_d = nc.dram_tensor("uT", [128, KC, TALL], BF16, kind="ExternalInput").ap()
    wq_d = nc.dram_tensor("wq", [128, KC, 4, 128], F32, kind="ExternalInput").ap()
    wk_d = nc.dram_tensor("wk", [128, KC, 2, 128], F32, kind="ExternalInput").ap()
    wv_d = nc.dram_tensor("wv", [128, KC, 128], F32, kind="ExternalInput").ap()
    rtab_d = nc.dram_tensor("rtab", [128, 2, 128], F32, kind="ExternalInput").ap()
    mask_d = nc.dram_tensor("masks", [128, 6, QBW], BF16, kind="ExternalInput").ap()
    ident_d = nc.dram_tensor("ident", [128, 128], BF16, kind="ExternalInput").ap()
    sink_d = nc.dram_tensor("sink", [128, 2], F32, kind="ExternalInput").ap()
    y_d = nc.dram_tensor("y_out", [128, 2, TALL], BF16, kind="ExternalOutput").ap()
    SCALE = 128.0 ** -0.5
    with ExitStack() as es:
        P = Prog(nc, es)
        sb = lambda name, shape, dt: es.enter_context(nc.sbuf_tensor(name, shape, dt))
        wq = sb("wq_sb", [128, KC, 4, 128], BF16)
        wk = sb("wk_sb", [128, KC, 2, 128], BF16)
        wv = sb("wv_sb", [128, KC, 128], BF16)
        rtab = sb("rtab_sb", [128, 2, 128], F32)
        masks = sb("masks_sb", [128, 6, QBW], BF16)
        ident = sb("ident_sb", [128, 128], BF16)
        onesb = sb("onesb", [128, 128], BF16)
        sink = sb("sink_sb", [128, 2], F32)
        esink = sb("esink", [128, 2], F32)
        qT = sb("qT", [128, 2, TALL], BF16)
        kT = sb("kT", [128, TALL], BF16)
        vtm = sb("vtm", [128, NTL, 128], BF16)
        yT = sb("yT", [128, 2, TALL], BF16)
        ubs = [sb(f"ub{i}", [128, KC, QBW], BF16) for i in range(2)]
        t1s = [sb(f"t1_{i}", [128, QBW], F32) for i in range(2)]
        t2s = [sb(f"t2_{i}", [128, QBW], F32) for i in range(2)]
        pts = [sb(f"pt{i}", [128, QBW], BF16) for i in range(3)]
        rcs = [sb(f"rc{i}", [128, QBW], F32) for i in range(2)]
        ps = [es.enter_context(nc.psum_tensor(f"ps{i}", [128, 512], F32)) for i in range(8)]
        r_ub = Ring(ubs, "ub"); r_t1 = Ring(t1s, "t1"); r_t2 = Ring(t2s, "t2"); r_pt = Ring(pts, "pt"); r_rc = Ring(rcs, "rc")
        r_psA = Ring(ps[0:4], "psA")
        r_psO = Ring(ps[4:6], "psO")
        r_psR = Ring(ps[6:8], "psR")

        P.op('pool', lambda e: e.dma_start(out=wq[:], in_=wq_d), writes=['wq'], chan='wq')
        P.op('pool', lambda e: e.dma_start(out=wk[:], in_=wk_d), writes=['wk'], chan='wk')
        P.op('pool', lambda e: e.dma_start(out=wv[:], in_=wv_d), writes=['wv'], chan='wv')
        P.op('sp', lambda e: e.dma_start(out=rtab[:], in_=rtab_d), writes=['rtab'], chan='rtab')
        P.op('sp', lambda e: e.dma_start(out=masks[:], in_=mask_d), writes=['masks'], chan='masks')
        P.op('sp', lambda e: e.dma_start(out=ident[:], in_=ident_d), writes=['ident'], chan='ident')
        P.op('sp', lambda e: e.dma_start(out=sink[:], in_=sink_d), writes=['sink'], chan='sink')
        P.op('pool', lambda e: e.memset(onesb[:], 1.0), writes=['onesb'])
        P.op('act', lambda e: e.activation(out=esink[:], in_=sink[:], func=AF.Exp), reads=['sink'], writes=['esink'])

        def proj_fm(ub, ubt, w_ap, wtag, width):
            pp, ppt = r_psA.next()
            for k in range(KC):
                P.op('pe', lambda e, k=k, pp=pp: e.matmul(pp[:, 0:width], w_ap(k), ub[:, k, 0:width],
                                                         start=(k == 0), stop=(k == KC - 1)),
                     reads=[ubt, wtag], writes=[ppt])
            return pp, ppt

        def rope_store(px, pxt, pw, pwt, dst, dtag, r0, scale):
            t1, t1t = r_t1.next()
            t2, t2t = r_t2.next()
            for half in range(2):
                p0, p1 = half * 64, half * 64 + 64
                if half == 0:
                    ctab = bc_last(rtab[p0:p1, 0, r0:r0 + 8], 64)
                    stab = bc_last(rtab[p0:p1, 1, r0:r0 + 8], 64)
                else:
                    ctab = bc_mid(rtab[p0:p1, 0, 0:64], 8)
                    stab = bc_mid(rtab[p0:p1, 1, 0:64], 8)
                P.op('dve', lambda e, p0=p0, p1=p1, ctab=ctab, t1=t1: e.scalar_tensor_tensor(
                    out=v3(t1[p0:p1, :], 8, 64), in0=v3(px[p0:p1, :], 8, 64), scalar=scale, in1=ctab,
                    op0=ALU.mult, op1=ALU.mult), reads=[pxt, 'rtab'], pwrites=[t1t])
                P.op('dve', lambda e, p0=p0, p1=p1, stab=stab, t2=t2: e.scalar_tensor_tensor(
                    out=v3(t2[p0:p1, :], 8, 64), in0=v3(pw[p0:p1, :], 8, 64), scalar=scale, in1=stab,
                    op0=ALU.mult, op1=ALU.mult), reads=[pwt, 'rtab'], pwrites=[t2t])
            P.op('pool', lambda e, t1=t1, t2=t2: e.tensor_tensor(out=dst, in0=t1[:, :], in1=t2[:, :], op=ALU.add),
                 reads=[t1t, t2t], pwrites=[dtag])

        def proj_block(bi, t0, width):
            ub, ubt = r_ub.next()
            P.op('sp', lambda e, ub=ub, t0=t0, width=width: e.dma_start(out=ub[:, :, 0:width], in_=uT_d[:, :, t0:t0 + width]),
                 writes=[ubt], chan=ubt)
            is_ctx = t0 >= SEQ
            r0 = t0 // 64
            for h in range(2):
                px, pxt = proj_fm(ub, ubt, lambda k, h=h: wq[:, k, 2 * h, :], 'wq', width)
                if is_ctx:
                    P.op('act', lambda e, px=px, h=h, t0=t0, width=width: e.activation(
                        out=qT[:, h, t0:t0 + width], in_=px[:, 0:width], func=AF.Identity, scale=SCALE),
                        reads=[pxt], pwrites=[f'qT{bi}'])
                else:
                    pw, pwt = proj_fm(ub, ubt, lambda k, h=h: wq[:, k, 2 * h + 1, :], 'wq', width)
                    rope_store(px, pxt, pw, pwt, qT[:, h, t0:t0 + width], f'qT{bi}', r0, SCALE)
            px, pxt = proj_fm(ub, ubt, lambda k: wk[:, k, 0, :], 'wk', width)
            if is_ctx:
                P.op('act', lambda e, px=px, t0=t0, width=width: e.activation(
                    out=kT[:, t0:t0 + width], in_=px[:, 0:width], func=AF.Identity), reads=[pxt], pwrites=[f'kT{bi}'])
            else:
                pw, pwt = proj_fm(ub, ubt, lambda k: wk[:, k, 1, :], 'wk', width)
                rope_store(px, pxt, pw, pwt, kT[:, t0:t0 + width], f'kT{bi}', r0, 1.0)
            pp, ppt = r_psA.next()
            nti = width // 128
            for i in range(nti):
                for k in range(KC):
                    P.op('pe', lambda e, k=k, i=i, pp=pp, ub=ub: e.matmul(pp[:, i * 128:(i + 1) * 128], ub[:, k, i * 128:(i + 1) * 128],
                                                                         wv[:, k, :], start=(k == 0), stop=(k == KC - 1)),
                         reads=[ubt, 'wv'], pwrites=[ppt])
            tt0 = t0 // 128
            P.op('act', lambda e, pp=pp, tt0=tt0, nti=nti: e.activation(
                out=vtm[:, tt0:tt0 + nti, :], in_=v3(pp[:, 0:nti * 128], nti, 128), func=AF.Identity),
                reads=[ppt], pwrites=[f'vtm{bi}'])

        def attend(h, q0, qw, keytiles):
            qbi = q0 // QBW if q0 < SEQ else NQB
            kb = lambda kt: kt // 4 if kt < SEQ // 128 else NQB
            po, pot = r_psO.next()
            pr, prt = r_psR.next()
            n = len(keytiles)
            for i, (kt, mi) in enumerate(keytiles):
                pS, pst = r_psA.next()
                P.op('pe', lambda e, pS=pS, kt=kt, mi=mi: e.matmul(pS[:, 0:qw], kT[:, kt * 128:(kt + 1) * 128], qT[:, h, q0:q0 + qw],
                                                                  start=True, stop=(mi is None)),
                     reads=[f'kT{kb(kt)}', f'qT{qbi}'], writes=[pst])
                if mi is not None:
                    P.op('pe', lambda e, pS=pS, mi=mi: e.matmul(pS[:, 0:qw], ident[:, :], masks[:, mi, 0:qw], start=False, stop=True),
                         reads=['ident', 'masks'], writes=[pst])
                pt, ptt = r_pt.next()
                P.op('act', lambda e, pS=pS, pt=pt: e.activation(out=pt[:, 0:qw], in_=pS[:, 0:qw], func=AF.Exp),
                     reads=[pst], writes=[ptt])
                P.op('pe', lambda e, pt=pt, kt=kt, i=i: e.matmul(po[:, 0:qw], vtm[:, kt, :], pt[:, 0:qw], start=(i == 0), stop=(i == n - 1)),
                     reads=[ptt, f'vtm{kb(kt)}'], writes=[pot])
                P.op('pe', lambda e, pt=pt, i=i: e.matmul(pr[:, 0:qw], onesb[:, :], pt[:, 0:qw], start=(i == 0), stop=(i == n - 1)),
                     reads=[ptt, 'onesb'], writes=[prt])
            rc, rct = r_rc.next()
            P.op('dve', lambda e, rc=rc: e.tensor_scalar(out=rc[:, 0:qw], in0=pr[:, 0:qw], scalar1=esink[:, h:h + 1], scalar2=None, op0=ALU.add),
                 reads=[prt, 'esink'], writes=[rct])
            P.op('dve', lambda e, rc=rc: e.reciprocal(out=rc[:, 0:qw], in_=rc[:, 0:qw]), reads=[rct], writes=[rct])
            P.op('dve', lambda e, rc=rc: e.tensor_tensor(out=yT[:, h, q0:q0 + qw], in0=po[:, 0:qw], in1=rc[:, 0:qw], op=ALU.mult),
                 reads=[pot, rct], pwrites=['yT'])

        blks = token_blocks()
        if debug:
            for bi, (t0, width) in enumerate(blks):
                proj_block(bi, t0, width)
        else:
            proj_block(NQB, *blks[NQB])
            proj_block(0, *blks[0])
            proj_block(1, *blks[1])

        def keytiles_for(qb):
            kts = []
            for rel in range(-1, 5):
                kt = qb * 4 + rel
                if 0 <= kt < SEQ // 128:
                    kts.append((kt, rel + 1))
            return kts + [(64, None), (65, None)]
        for h in range(2):
            attend(h, SEQ, CTX, [(64, None), (65, None)])
        for qb in range(NQB):
            for h in range(2):
                attend(h, qb * QBW, QBW, keytiles_for(qb))
            if not debug and qb + 2 < NQB:
                proj_block(qb + 2, *blks[qb + 2])
        for h in range(2):
            P.op('sp', lambda e, h=h: e.dma_start(out=y_d[:, h, :], in_=yT[:, h, :]), reads=['yT'], pwrites=['y_d'], chan=f'y{h}')
        if debug:
            P.op('sp', lambda e: e.dma_start(out=dq_d, in_=qT[:]), reads=[f'qT{i}' for i in range(17)], writes=['dq_d'], chan='dq')
            P.op('sp', lambda e: e.dma_start(out=dk_d, in_=kT[:]), reads=[f'kT{i}' for i in range(17)], writes=['dk_d'], chan='dk')
        P.finish('sp')
        P.emit()
    return nc


def odd_consts():
    inv = 10000.0 ** (-np.arange(0, 64, 2, dtype=np.float32) / 64.0)
    rtab = np.zeros((128, 2, 128), np.float32)
    for p in range(128):
        f = inv[p % 32]
        sign = -1.0 if (p % 64) < 32 else 1.0
        pos = np.arange(128, dtype=np.float32)
        ang = pos * f
        rtab[p, 0, :] = np.cos(ang)
        rtab[p, 1, :] = sign * np.sin(ang)
    masks = np.zeros((128, 6, QBW), np.float32)
    for mi in range(6):
        rel = mi - 1
        kpos = rel * 128 + np.arange(128)[:, None]
        qpos = np.arange(QBW)[None, :]
        masks[:, mi, :] = np.where(np.abs(kpos - qpos) <= 128, 0.0, NEG)
    ident = np.eye(128, dtype=np.float32)
    return rtab, masks.astype(NPBF), ident.astype(NPBF)


def swap_perm():
    p = np.arange(128)
    return np.where((p % 64) < 32, p + 32, p - 32)


def lay_w(w):
    n = w.shape[1]
    return np.ascontiguousarray(w.reshape(KC, 128, n).transpose(1, 0, 2))


def run_odd(uT_all, w_qkv, sink, debug=False):
    nc = build_odd_prog(debug)
    rtab, masks, ident = odd_consts()
    perm = swap_perm()
    in_maps = []
    for j in range(NCORE):
        kv = j // 2
        wq = []
        for h in (2 * j, 2 * j + 1):
            cols = w_qkv[:, h * 128:(h + 1) * 128]
            wq += [cols, cols[:, perm]]
        wq = np.stack([lay_w(c) for c in wq], axis=2)
        kc = w_qkv[:, D + kv * 128: D + (kv + 1) * 128]
        wk = np.stack([lay_w(kc), lay_w(kc[:, perm])], axis=2)
        wv = lay_w(w_qkv[:, D + 512 + kv * 128: D + 512 + (kv + 1) * 128])
        sk = np.broadcast_to(sink[2 * j:2 * j + 2][None, :], (128, 2)).astype(np.float32)
        in_maps.append({"uT": uT_all, "wq": np.ascontiguousarray(wq), "wk": np.ascontiguousarray(wk), "wv": wv,
                        "rtab": rtab, "masks": masks, "ident": ident, "sink": np.ascontiguousarray(sk)})
    res = run_bass_kernel_spmd(nc, in_maps, core_ids=list(range(NCORE)))
    yT = np.zeros((128, KC, TALL), NPBF)
    for j in range(NCORE):
        yT[:, 2 * j:2 * j + 2, :] = res.results[j]["y_out"]
    if debug:
        return yT, res.results
    return yT


NNB = 20


def na_bias_mats(rpb_h):
    def mat(R, kt):
        kr = 2 * kt + np.arange(2)[:, None, None, None]
        kc = np.arange(64)[None, :, None, None]
        r = R + np.arange(8)[None, None, :, None]
        qc = np.arange(64)[None, None, None, :]
        r0 = np.clip(r - 4, 0, 120)
        c0 = np.clip(qc - 8, 0, 48)
        valid = (kr >= r0) & (kr < r0 + 8) & (kc >= c0) & (kc < c0 + 16)
        ri = np.clip(kr - r + 7, 0, 14)
        ci = np.clip(kc - qc + 15, 0, 30)
        ri, ci, valid = np.broadcast_arrays(ri, ci, valid)
        return np.where(valid, rpb_h[ri, ci], NEG).reshape(128, 512)
    mats = [mat(8, 4 + rel) for rel in range(-2, 6)]
    mats += [mat(0, kt) for kt in range(0, 6)]
    mats += [mat(120, kt) for kt in range(58, 64)]
    return np.ascontiguousarray(np.stack(mats, axis=1)).astype(NPBF)


def na_keytiles(qb):
    if qb == 0:
        return [(kt, 8 + kt) for kt in range(0, 6)]
    if qb == NQB - 1:
        return [(kt, 14 + kt - 58) for kt in range(58, 64)]
    return [(4 * qb + rel, rel + 2) for rel in range(-2, 6)]


def build_na_prog():
    nc = bass.Bass("TRN2", target_bir_lowering=False)
    uT_d = nc.dram_tensor("uT", [128, KC, TALL], BF16, kind="ExternalInput").ap()
    w_d = nc.dram_tensor("w3", [128, KC, 3, 128], F32, kind="ExternalInput").ap()
    bias_d = nc.dram_tensor("nabias", [128, NNB, QBW], BF16, kind="ExternalInput").ap()
    ident_d = nc.dram_tensor("ident", [128, 128], BF16, kind="ExternalInput").ap()
    y_d = nc.dram_tensor("y_out", [128, TALL], BF16, kind="ExternalOutput").ap()
    SCALE = 128.0 ** -0.5
    with ExitStack() as es:
        P = Prog(nc, es)
        sb = lambda name, shape, dt: es.enter_context(nc.sbuf_tensor(name, shape, dt))
        w3 = sb("w3_sb", [128, KC, 3, 128], BF16)
        biasm = sb("bias_sb", [128, NNB, QBW], BF16)
        ident = sb("ident_sb", [128, 128], BF16)
        onesb = sb("onesb", [128, 128], BF16)
        qT = sb("qT", [128, TALL], BF16)
        kT = sb("kT", [128, TALL], BF16)
        vtm = sb("vtm", [128, NTL, 128], BF16)
        yT = sb("yT", [128, TALL], BF16)
        ubs = [sb(f"ub{i}", [128, KC, QBW], BF16) for i in range(2)]
        pts = [sb(f"pt{i}", [128, QBW], BF16) for i in range(3)]
        rcs = [sb(f"rc{i}", [128, QBW], F32) for i in range(2)]
        ps = [es.enter_context(nc.psum_tensor(f"ps{i}", [128, 512], F32)) for i in range(8)]
        r_ub = Ring(ubs, "ub"); r_pt = Ring(pts, "pt"); r_rc = Ring(rcs, "rc")
        r_psA = Ring(ps[0:4], "psA"); r_psO = Ring(ps[4:6], "psO"); r_psR = Ring(ps[6:8], "psR")
        P.op('pool', lambda e: e.dma_start(out=w3[:], in_=w_d), writes=['w3'], chan='w3')
        P.op('sp', lambda e: e.dma_start(out=biasm[:], in_=bias_d), writes=['biasm'], chan='biasm')
        P.op('sp', lambda e: e.dma_start(out=ident[:], in_=ident_d), writes=['ident'], chan='ident')
        P.op('pool', lambda e: e.memset(onesb[:], 1.0), writes=['onesb'])
        def proj_block(bi, t0, width):
            ub, ubt = r_ub.next()
            P.op('sp', lambda e, ub=ub, t0=t0, width=width: e.dma_start(out=ub[:, :, 0:width], in_=uT_d[:, :, t0:t0 + width]),
                 writes=[ubt], chan=ubt)
            for g, (dst, dtag, sc) in enumerate(((qT, f'qT{bi}', SCALE), (kT, f'kT{bi}', 1.0))):
                pp, ppt = r_psA.next()
                for k in range(KC):
                    P.op('pe', lambda e, k=k, pp=pp, g=g, ub=ub, width=width: e.matmul(
                        pp[:, 0:width], w3[:, k, g, :], ub[:, k, 0:width], start=(k == 0), stop=(k == KC - 1)),
                        reads=[ubt, 'w3'], writes=[ppt])
                P.op('act', lambda e, pp=pp, dst=dst, sc=sc, t0=t0, width=width: e.activation(
                    out=dst[:, t0:t0 + width], in_=pp[:, 0:width], func=AF.Identity, scale=sc), reads=[ppt], pwrites=[dtag])
            pp, ppt = r_psA.next()
            nti = width // 128
            for i in range(nti):
                for k in range(KC):
                    P.op('pe', lambda e, k=k, i=i, pp=pp, ub=ub: e.matmul(pp[:, i * 128:(i + 1) * 128], ub[:, k, i * 128:(i + 1) * 128],
                                                                         w3[:, k, 2, :], start=(k == 0), stop=(k == KC - 1)),
                         reads=[ubt, 'w3'], pwrites=[ppt])
            tt0 = t0 // 128
            P.op('dve', lambda e, pp=pp, tt0=tt0, nti=nti: e.tensor_copy(
                out=vtm[:, tt0:tt0 + nti, :], in_=v3(pp[:, 0:nti * 128], nti, 128)), reads=[ppt], pwrites=[f'vtm{bi}'])

        def attend(q0, qw, keytiles):
            qbi = q0 // QBW if q0 < SEQ else NQB
            kb = lambda kt: kt // 4 if kt < SEQ // 128 else NQB
            po, pot = r_psO.next()
            pr, prt = r_psR.next()
            n = len(keytiles)
            for i, (kt, mi) in enumerate(keytiles):
                pS, pst = r_psA.next()
                P.op('pe', lambda e, pS=pS, kt=kt, mi=mi: e.matmul(pS[:, 0:qw], kT[:, kt * 128:(kt + 1) * 128], qT[:, q0:q0 + qw],
                                                                  start=True, stop=(mi is None)),
                     reads=[f'kT{kb(kt)}', f'qT{qbi}'], writes=[pst])
                if mi is not None:
                    P.op('pe', lambda e, pS=pS, mi=mi: e.matmul(pS[:, 0:qw], ident[:, :], biasm[:, mi, 0:qw], start=False, stop=True),
                         reads=['ident', 'biasm'], writes=[pst])
                pt, ptt = r_pt.next()
                P.op('act', lambda e, pS=pS, pt=pt: e.activation(out=pt[:, 0:qw], in_=pS[:, 0:qw], func=AF.Exp),
                     reads=[pst], writes=[ptt])
                P.op('pe', lambda e, pt=pt, kt=kt, i=i: e.matmul(po[:, 0:qw], vtm[:, kt, :], pt[:, 0:qw], start=(i == 0), stop=(i == n - 1)),
                     reads=[ptt, f'vtm{kb(kt)}'], writes=[pot])
                P.op('pe', lambda e, pt=pt, i=i: e.matmul(pr[:, 0:qw], onesb[:, :], pt[:, 0:qw], start=(i == 0), stop=(i == n - 1)),
                     reads=[ptt, 'onesb'], writes=[prt])
            rc, rct = r_rc.next()
            P.op('dve', lambda e, rc=rc: e.reciprocal(out=rc[:, 0:qw], in_=pr[:, 0:qw]), reads=[prt], writes=[rct])
            P.op('dve', lambda e, rc=rc: e.tensor_tensor(out=yT[:, q0:q0 + qw], in0=po[:, 0:qw], in1=rc[:, 0:qw], op=ALU.mult),
                 reads=[pot, rct], pwrites=['yT'])

        blks = token_blocks()
        proj_block(NQB, *blks[NQB])
        proj_block(0, *blks[0])
        proj_block(1, *blks[1])
        attend(SEQ, CTX, [(64, None), (65, None)])
        for qb in range(NQB):
            attend(qb * QBW, QBW, na_keytiles(qb) + [(64, None), (65, None)])
            if qb + 2 < NQB:
                proj_block(qb + 2, *blks[qb + 2])
        P.op('sp', lambda e: e.dma_start(out=y_d, in_=yT[:]), reads=['yT'], writes=['y_d'], chan='y')
        P.finish('sp')
        P.emit()
    return nc


def run_na(uT_all, w_in, rpb):
    nc = build_na_prog()
    ident = np.eye(128, dtype=np.float32).astype(NPBF)
    in_maps = []
    for j in range(NCORE):
        cols = [w_in[:, g * 1024 + j * 128: g * 1024 + (j + 1) * 128] for g in range(3)]
        w3 = np.stack([lay_w(c) for c in cols], axis=2)
        in_maps.append({"uT": uT_all, "w3": np.ascontiguousarray(w3), "nabias": na_bias_mats(rpb[j]), "ident": ident})
    res = run_bass_kernel_spmd(nc, in_maps, core_ids=list(range(NCORE)))
    return np.stack([res.results[j]["y_out"] for j in range(NCORE)], axis=1)


HC = 64
NCH = TALL // HC
NCTXCH = CTX // HC


def chunk_col(ap2d, nch, idx):
    l = [list(x) for x in ap2d.ap]
    st, n = l[-1]
    assert n == nch * HC
    return bass.AP(ap2d.tensor, ap2d.offset + idx * st, l[:-1] + [[st * HC, nch], [0, HC]])


def chunk_pick(ap2d, nch, idx):
    l = [list(x) for x in ap2d.ap]
    st, n = l[-1]
    return bass.AP(ap2d.tensor, ap2d.offset + idx * st, l[:-1] + [[st * HC, nch]])


def sub_view(ap2d, nch, sub0, nsub, col, bcast):
    l = [list(x) for x in ap2d.ap]
    st, n = l[-1]
    assert n == nch * HC
    if bcast:
        return bass.AP(ap2d.tensor, ap2d.offset + (sub0 * SUB + col) * st, l[:-1] + [[st * HC, nch], [st * SUB, nsub], [0, SUB]])
    return bass.AP(ap2d.tensor, ap2d.offset + sub0 * SUB * st, l[:-1] + [[st * HC, nch], [st * SUB, nsub], [st, SUB]])


SUB = 16
NSUB = HC // SUB
CLAMP = 40.0


def build_hgrn_prog(e_layer, stop=99, nblk=None, limit=None):
    nc = bass.Bass("TRN2", target_bir_lowering=False)
    uT_d = nc.dram_tensor("uT", [128, KC, TALL], BF16, kind="ExternalInput").ap()
    w_d = nc.dram_tensor("w5", [128, KC, 5, 128], F32, kind="ExternalInput").ap()
    lbl_d = nc.dram_tensor("lbl", [128, 2, 2], F32, kind="ExternalInput").ap()
    hgn_d = nc.dram_tensor("hgn", [128, 1], F32, kind="ExternalInput").ap()
    ident_d = nc.dram_tensor("ident", [128, 128], BF16, kind="ExternalInput").ap()
    tri_d = nc.dram_tensor("tri", [128, 4, HC], F32, kind="ExternalInput").ap()
    rmask_d = nc.dram_tensor("rmask", [128, QBW], F32, kind="ExternalInput").ap()
    y_d = nc.dram_tensor("y_out", [128, TALL], BF16, kind="ExternalOutput").ap()
    with ExitStack() as es:
        P = Prog(nc, es)
        sb = lambda name, shape, dt: es.enter_context(nc.sbuf_tensor(name, shape, dt))
        P.limit = limit
        w5 = sb("w5_sb", [128, KC, 5, 128], BF16)
        lbl = sb("lbl_sb", [128, 2, 2], F32)
        lb = sb("lb_sb", [128, 2], F32)
        oml = sb("oml_sb", [128, 2], F32)
        hgn = sb("hgn_sb", [128, 1], F32)
        ident = sb("ident_sb", [128, 128], BF16)
        tri = sb("tri_sb", [128, 4, HC], F32)
        rmask = sb("rmask_sb", [128, QBW], F32)
        ones32 = sb("ones32", [128, 128], F32)
        itm = sb("itm", [128, NTL, 128], BF16)
        Qp = sb("Qp", [128, TALL], BF16)
        attmEO = [sb("attmE", [128, NTL, HC], BF16), sb("attmO", [128, NTL, HC], BF16)]
        dS = sb("dS", [128, NCH, 128], BF16)
        Dall = sb("Dall", [128, NCH], F32)
        oacc = sb("oacc", [128, TALL], F32)
        ub = sb("ub", [128, KC, QBW], BF16)
        T = [sb(f"T{i}", [128, QBW], F32) for i in range(14)]
        KtT = sb("KtT", [128, QBW], BF16)
        Kj = [sb(f"Kj{i}", [128, QBW], BF16) for i in range(3)]
        Kd = sb("Kd", [128, QBW], BF16)
        Qd = sb("Qd", [128, QBW], BF16)
        Qsub = sb("Qsub", [128, QBW], BF16)
        KtmEO = [sb("KtmE", [128, 4, 128], BF16), sb("KtmO", [128, 4, 128], BF16)]
        yb = sb("yb", [128, QBW], BF16)
        ps = [es.enter_context(nc.psum_tensor(f"ps{i}", [128, 512], F32)) for i in range(7)]
        psT = es.enter_context(nc.psum_tensor("psT", [128, 1024], BF16))
        r_psA = Ring(ps[0:3], "psA")
        r_psB = Ring(ps[3:7], "psB")

        P.op('pool', lambda e: e.dma_start(out=w5[:], in_=w_d), writes=['w5'], chan='w5')
        for nm, t_, d_ in (('lbl', lbl, lbl_d), ('hgn', hgn, hgn_d), ('ident', ident, ident_d), ('tri', tri, tri_d), ('rmask', rmask, rmask_d)):
            P.op('sp', lambda e, t_=t_, d_=d_: e.dma_start(out=t_[:], in_=d_), writes=[nm], chan=nm)
        P.op('pool', lambda e: e.memset(ones32[:], 1.0), writes=['ones32'])
        for t_ in KtmEO:
            P.op('pool', lambda e, t_=t_: e.memset(t_[:], 0.0), writes=['Ktm'])
        for t_ in attmEO:
            P.op('pool', lambda e, t_=t_: e.memset(t_[:], 0.0), writes=['attm'])
        P.op('pool', lambda e: e.memset(Qsub[:], 0.0), writes=['Qsub'])
        if e_layer == 1:
            P.op('dve', lambda e: e.tensor_tensor(out=lb[:], in0=lbl[:, :, 1], in1=lbl[:, :, 0], op=ALU.subtract), reads=['lbl'], writes=['lb'])
            P.op('act', lambda e: e.activation(out=lb[:], in_=lb[:], func=AF.Sigmoid), reads=['lb'], writes=['lb'])
            P.op('dve', lambda e: e.tensor_scalar(out=oml[:], in0=lb[:], scalar1=-1.0, scalar2=1.0, op0=ALU.mult, op1=ALU.add),
                 reads=['lb'], writes=['oml'])

        blocks = token_blocks()
        if nblk is not None:
            blocks = blocks[:nblk]

        def load_ub(t0, width):
            P.op('sp', lambda e: e.dma_start(out=ub[:, :, 0:width], in_=uT_d[:, :, t0:t0 + width]), writes=['ub'], chan='ub')

        def proj(g, width):
            pp, ppt = r_psA.next()
            for k in range(KC):
                P.op('pe', lambda e, k=k, pp=pp: e.matmul(pp[:, 0:width], w5[:, k, g, :], ub[:, k, 0:width],
                                                         start=(k == 0), stop=(k == KC - 1)),
                     reads=['ub', 'w5'], writes=[ppt])
            return pp, ppt

        def fslot(ch, d):
            return (ch + NCTXCH) % NCH if d == 0 else ch

        def out_block(t0, w):
            load_ub(t0, w)
            pg, pgt = proj(4, w)
            P.op('act', lambda e, pg=pg, w=w: e.activation(out=T[0][:, 0:w], in_=pg[:, 0:w], func=AF.Silu), reads=[pgt], writes=['T0'])
            P.op('act', lambda e, t0=t0, w=w: e.activation(out=T[1][:, 0:w], in_=oacc[:, t0:t0 + w], func=AF.Square), reads=['oacc'], writes=['T1'])
            pn, pnt = r_psB.next()
            P.op('pe', lambda e, pn=pn, w=w: e.matmul(pn[:, 0:w], ones32[:, :], T[1][:, 0:w], start=True, stop=True), reads=['T1', 'ones32'], writes=[pnt])
            P.op('dve', lambda e, pn=pn, w=w: e.tensor_scalar(out=T[2][:, 0:w], in0=pn[:, 0:w], scalar1=1.0 / 128, scalar2=EPS,
                                                             op0=ALU.mult, op1=ALU.add), reads=[pnt], writes=['T2'])
            P.op('act', lambda e, w=w: e.activation(out=T[2][:, 0:w], in_=T[2][:, 0:w], func=AF.Sqrt), reads=['T2'], writes=['T2'])
            P.op('dve', lambda e, w=w: e.reciprocal(out=T[2][:, 0:w], in_=T[2][:, 0:w]), reads=['T2'], writes=['T2'])
            P.op('dve', lambda e, t0=t0, w=w: e.scalar_tensor_tensor(out=T[3][:, 0:w], in0=oacc[:, t0:t0 + w], scalar=hgn[:, 0:1], in1=T[2][:, 0:w],
                                                                    op0=ALU.mult, op1=ALU.mult), reads=['oacc', 'T2', 'hgn'], writes=['T3'])
            P.op('pool', lambda e, w=w: e.tensor_tensor(out=yb[:, 0:w], in0=T[3][:, 0:w], in1=T[0][:, 0:w], op=ALU.mult),
                 reads=['T3', 'T0'], writes=['yb'])
            P.op('sp', lambda e, t0=t0, w=w: e.dma_start(out=y_d[:, t0:t0 + w], in_=yb[:, 0:w]), reads=['yb'], pwrites=['y_d'], chan='yb')

        for d in range(2):
            for (t0, w) in blocks:
                nch = w // HC
                nti = w // 128
                ch0 = t0 // HC
                tt0 = t0 // 128
                load_ub(t0, w)
                pq, pqt = proj(0, w)
                pf, pft = proj(1 + d, w)
                if d == 0:
                    pi, pit = r_psB.next()
                    for i in range(nti):
                        for k in range(KC):
                            P.op('pe', lambda e, k=k, i=i, pi=pi: e.matmul(pi[:, i * 128:(i + 1) * 128], ub[:, k, i * 128:(i + 1) * 128],
                                                                         w5[:, k, 3, :], start=(k == 0), stop=(k == KC - 1)),
                                 reads=['ub', 'w5'], pwrites=[pit])
                    P.op('act', lambda e, pi=pi, tt0=tt0, nti=nti: e.activation(
                        out=itm[:, tt0:tt0 + nti, :], in_=v3(pi[:, 0:nti * 128], nti, 128), func=AF.Identity),
                        reads=[pit], pwrites=['itm'])
                P.op('act', lambda e, pf=pf, w=w: e.activation(out=T[0][:, 0:w], in_=pf[:, 0:w], func=AF.Sigmoid), reads=[pft], writes=['T0'])
                if e_layer == 1:
                    P.op('dve', lambda e, w=w, d=d: e.tensor_scalar(out=T[0][:, 0:w], in0=T[0][:, 0:w], scalar1=oml[:, d:d + 1],
                                                                   scalar2=lb[:, d:d + 1], op0=ALU.mult, op1=ALU.add),
                         reads=['T0', 'oml', 'lb'], writes=['T0'])
                P.op('act', lambda e, w=w: e.activation(out=T[1][:, 0:w], in_=T[0][:, 0:w], func=AF.Ln), reads=['T0'], writes=['T1'])
                P.op('dve', lambda e, w=w: e.tensor_scalar(out=T[2][:, 0:w], in0=T[0][:, 0:w], scalar1=-1.0, scalar2=1.0,
                                                          op0=ALU.mult, op1=ALU.add), reads=['T0'], writes=['T2'])
                P.op('dve', lambda e, w=w: e.tensor_tensor_scan(out=T[3][:, 0:w], data0=rmask[:, 0:w], data1=T[1][:, 0:w], initial=0.0,
                                                               op0=ALU.mult, op1=ALU.add), reads=['T1', 'rmask'], writes=['T3'])
                if d == 0:
                    c, ct = T[3], 'T3'
                else:
                    P.op('dve', lambda e, w=w: e.tensor_tensor(out=T[4][:, 0:w], in0=T[1][:, 0:w], in1=T[3][:, 0:w], op=ALU.subtract),
                         reads=['T1', 'T3'], writes=['T4'])
                    P.op('dve', lambda e, w=w, nch=nch: e.tensor_tensor(out=v3(T[4][:, 0:w], nch, HC), in0=v3(T[4][:, 0:w], nch, HC),
                                                                       in1=chunk_col(T[3][:, 0:w], nch, HC - 1), op=ALU.add),
                         reads=['T4', 'T3'], writes=['T4'])
                    c, ct = T[4], 'T4'
                cw = c[:, 0:w]
                s0 = fslot(ch0, d)
                if d == 0:
                    qs0, bcol = 1, -1
                else:
                    qs0, bcol = 0, SUB
                nsb = w // SUB
                TK = [T[8], T[12], T[13]]
                TKt = ['T8', 'T12', 'T13']
                P.op('dve', lambda e, w=w, nch=nch, cw=cw: e.tensor_tensor(out=v3(T[6][:, 0:w], nch, HC), in0=chunk_col(T[3][:, 0:w], nch, HC - 1),
                                                                          in1=v3(cw, nch, HC), op=ALU.subtract),
                     reads=[ct, 'T3'], writes=['T6'])
                P.op('dve', lambda e, w=w, nch=nch, cw=cw, qs0=qs0, bcol=bcol: e.tensor_tensor(
                    out=sub_view(T[7][:, 0:w], nch, qs0, 3, 0, False), in0=sub_view(cw, nch, qs0, 3, 0, False),
                    in1=sub_view(cw, nch, qs0, 3, bcol, True), op=ALU.subtract), reads=[ct], writes=['T7'])
                for jj in range(3):
                    bj = (jj + 1) * SUB - 1 if d == 0 else (jj + 1) * SUB
                    P.op('dve', lambda e, w=w, nch=nch, cw=cw, bj=bj, jj=jj: e.tensor_tensor(out=v3(TK[jj][:, 0:w], nch, HC), in0=chunk_col(cw, nch, bj),
                                                                                          in1=v3(cw, nch, HC), op=ALU.subtract),
                         reads=[ct], writes=[TKt[jj]])
                P.op('dve', lambda e, w=w, nsb=nsb, cw=cw: e.tensor_tensor(
                    out=v3(T[9][:, 0:w], nsb, SUB), in0=v3(cw, nsb, SUB),
                    in1=bass.AP(cw.tensor, cw.offset + SUB // 2, [list(cw.ap[0]), [SUB, nsb], [0, SUB]]), op=ALU.subtract),
                    reads=[ct], writes=['T9'])
                for jj in range(3):
                    P.op('dve', lambda e, w=w, jj=jj: e.tensor_scalar(out=TK[jj][:, 0:w], in0=TK[jj][:, 0:w], scalar1=0.0, scalar2=None, op0=ALU.min),
                         reads=[TKt[jj]], writes=[TKt[jj]])
                P.op('dve', lambda e, w=w: e.tensor_scalar(out=T[9][:, 0:w], in0=T[9][:, 0:w], scalar1=-CLAMP, scalar2=CLAMP,
                                                          op0=ALU.max, op1=ALU.min), reads=['T9'], writes=['T9'])
                P.op('act', lambda e, w=w, cw=cw: e.activation(out=T[5][:, 0:w], in_=cw, func=AF.Exp), reads=[ct], writes=['T5'])
                P.op('act', lambda e, w=w, nch=nch, s0=s0: e.activation(out=Dall[:, s0:s0 + nch], in_=chunk_pick(T[3][:, 0:w], nch, HC - 1), func=AF.Exp),
                     reads=['T3'], pwrites=['Dall'])
                P.op('act', lambda e, w=w: e.activation(out=T[6][:, 0:w], in_=T[6][:, 0:w], func=AF.Exp), reads=['T6'], writes=['T6'])
                P.op('act', lambda e, w=w, nch=nch, qs0=qs0: e.activation(out=sub_view(T[7][:, 0:w], nch, qs0, 3, 0, False),
                                                                         in_=sub_view(T[7][:, 0:w], nch, qs0, 3, 0, False), func=AF.Exp),
                     reads=['T7'], writes=['T7'])
                for jj in range(3):
                    P.op('act', lambda e, w=w, jj=jj: e.activation(out=TK[jj][:, 0:w], in_=TK[jj][:, 0:w], func=AF.Exp), reads=[TKt[jj]], writes=[TKt[jj]])
                P.op('act', lambda e, w=w: e.activation(out=T[10][:, 0:w], in_=T[9][:, 0:w], func=AF.Exp), reads=['T9'], writes=['T10'])
                P.op('act', lambda e, w=w: e.activation(out=T[11][:, 0:w], in_=T[9][:, 0:w], func=AF.Exp, scale=-1.0), reads=['T9'], writes=['T11'])
                P.op('dve', lambda e, pq=pq, t0=t0, w=w: e.tensor_tensor(out=Qp[:, t0:t0 + w], in0=pq[:, 0:w], in1=T[5][:, 0:w], op=ALU.mult),
                     reads=[pqt, 'T5'], pwrites=['Qp'])
                P.op('pool', lambda e, w=w: e.tensor_tensor(out=KtT[:, 0:w], in0=T[2][:, 0:w], in1=T[6][:, 0:w], op=ALU.mult),
                     reads=['T2', 'T6'], writes=['KtT'])
                P.op('dve', lambda e, w=w, nch=nch, qs0=qs0, pq=pq: e.scalar_tensor_tensor(
                    out=sub_view(Qsub[:, 0:w], nch, qs0, 3, 0, False), in0=sub_view(T[7][:, 0:w], nch, qs0, 3, 0, False), scalar=1.0,
                    in1=sub_view(pq[:, 0:w], nch, qs0, 3, 0, False), op0=ALU.min, op1=ALU.mult), reads=['T7', pqt], writes=['Qsub'])
                for jj in range(3):
                    P.op('pool', lambda e, w=w, jj=jj: e.tensor_tensor(out=Kj[jj][:, 0:w], in0=TK[jj][:, 0:w], in1=T[2][:, 0:w], op=ALU.mult),
                         reads=[TKt[jj], 'T2'], writes=[f'Kj{jj}'])
                P.op('dve', lambda e, pq=pq, w=w: e.tensor_tensor(out=Qd[:, 0:w], in0=pq[:, 0:w], in1=T[10][:, 0:w], op=ALU.mult),
                     reads=[pqt, 'T10'], writes=['Qd'])
                P.op('pool', lambda e, w=w: e.tensor_tensor(out=Kd[:, 0:w], in0=T[2][:, 0:w], in1=T[11][:, 0:w], op=ALU.mult),
                     reads=['T2', 'T11'], writes=['Kd'])
                pa, pat = r_psB.next()
                for cl in range(nch):
                    i, p0 = cl // 2, (cl % 2) * HC
                    tk = cl * HC
                    P.op('pe', lambda e, pa=pa, i=i, p0=p0, tk=tk: e.matmul(pa[p0:p0 + HC, i * HC:(i + 1) * HC], Kd[:, tk:tk + HC], Qd[:, tk:tk + HC],
                                                                          start=True, stop=True), reads=['Kd', 'Qd'], pwrites=[pat])
                    dummy = 0 if d == 0 else NSUB - 1
                    for sj in range(NSUB):
                        c0 = tk + sj * SUB
                        if sj == dummy:
                            lh, rh, lt, rt = Kd, Qd, 'Kd', 'Qd'
                        else:
                            jj = sj - 1 if d == 0 else sj
                            lh, rh, lt, rt = Kj[jj], Qsub, f'Kj{jj}', 'Qsub'
                        P.op('pe', lambda e, pa=pa, i=i, p0=p0, tk=tk, c0=c0, sj=sj, lh=lh, rh=rh: e.matmul(
                            pa[p0:p0 + HC, 256 + i * HC + sj * SUB:256 + i * HC + (sj + 1) * SUB], lh[:, tk:tk + HC], rh[:, c0:c0 + SUB],
                            start=True, stop=True), reads=[lt, rt], pwrites=[pat])
                P.op('dve', lambda e, pa=pa, nti=nti, d=d: e.tensor_tensor(out=v3(T[0][:, 0:nti * HC], nti, HC), in0=v3(pa[:, 0:nti * HC], nti, HC),
                                                                          in1=bc_mid(tri[:, 2 * d, :], nti), op=ALU.mult),
                     reads=[pat, 'tri'], writes=['T0'])
                P.op('dve', lambda e, pa=pa, nti=nti, d=d: e.tensor_tensor(out=v3(T[1][:, 0:nti * HC], nti, HC), in0=v3(pa[:, 256:256 + nti * HC], nti, HC),
                                                                          in1=bc_mid(tri[:, 2 * d + 1, :], nti), op=ALU.mult),
                     reads=[pat, 'tri'], writes=['T1'])
                for hf in range(2):
                    P.op('pool', lambda e, nti=nti, hf=hf, tt0=tt0: e.tensor_tensor(
                        out=attmEO[hf][hf * HC:(hf + 1) * HC, tt0:tt0 + nti, :], in0=v3(T[0][hf * HC:(hf + 1) * HC, 0:nti * HC], nti, HC),
                        in1=v3(T[1][hf * HC:(hf + 1) * HC, 0:nti * HC], nti, HC), op=ALU.add),
                        reads=['T0', 'T1'], pwrites=['attm'])
                for i in range(nti):
                    P.op('pe', lambda e, i=i: e.transpose(psT[:, i * 128:(i + 1) * 128], KtT[:, i * 128:(i + 1) * 128], ident[:, :]),
                         reads=['KtT', 'ident'], pwrites=['psT'])
                for hf in range(2):
                    P.op('dve', lambda e, nti=nti, hf=hf: e.tensor_copy(out=KtmEO[hf][hf * HC:(hf + 1) * HC, 0:nti, :],
                                                                       in_=v3(psT[hf * HC:(hf + 1) * HC, 0:nti * 128], nti, 128)),
                         reads=['psT'], pwrites=['Ktm'])
                for b0 in range(0, nch, 4):
                    pd, pdt = r_psB.next()
                    for cl in range(b0, b0 + 4):
                        i = cl // 2
                        P.op('pe', lambda e, pd=pd, cl=cl, i=i, b0=b0, tt0=tt0: e.matmul(
                            pd[:, (cl - b0) * 128:(cl - b0 + 1) * 128], KtmEO[cl % 2][:, i, :], itm[:, tt0 + i, :],
                            start=True, stop=True), reads=['Ktm', 'itm'], pwrites=[pdt])
                    P.op('act', lambda e, pd=pd, s0=s0, b0=b0: e.activation(out=dS[:, s0 + b0:s0 + b0 + 4, :], in_=v3(pd[:, 0:512], 4, 128),
                                                                            func=AF.Identity), reads=[pdt], pwrites=['dS'])
            if stop <= 0:
                break
            for ee in range(128):
                if d == 0:
                    a1 = bass.AP(dS, ee, [[NCH * 128, 128], [128, NCH]])
                    a0 = Dall[:, :]
                else:
                    a1 = bass.AP(dS, (NCH - 1) * 128 + ee, [[NCH * 128, 128], [-128, NCH]])
                    a0 = bass.AP(Dall, NCH - 1, [[NCH, 128], [-1, NCH]])
                P.op('dve', lambda e, a1=a1, a0=a0: e.tensor_tensor_scan(out=a1, data0=a0, data1=a1, initial=0.0, op0=ALU.mult, op1=ALU.add),
                     reads=['dS', 'Dall'], pwrites=['dS'])
            if stop <= 1:
                break
            for (t0, w) in blocks:
                nch = w // HC
                ch0 = t0 // HC
                tt0 = t0 // 128
                po, pot = r_psB.next()
                for cl in range(nch):
                    i = cl // 2
                    tk = t0 + cl * HC
                    ch = ch0 + cl
                    if d == 0:
                        s = fslot(ch, 0)
                        prev = s - 1 if s > 0 else None
                    else:
                        prev = ch + 1 if ch < NCH - 1 else None
                    if prev is not None:
                        P.op('pe', lambda e, po=po, cl=cl, prev=prev, tk=tk: e.matmul(po[:, cl * HC:(cl + 1) * HC], dS[:, prev, :], Qp[:, tk:tk + HC],
                                                                                    start=True, stop=False), reads=['dS', 'Qp'], pwrites=[pot])
                    P.op('pe', lambda e, po=po, cl=cl, i=i, tt0=tt0, prev=prev: e.matmul(
                        po[:, cl * HC:(cl + 1) * HC], itm[:, tt0 + i, :], attmEO[cl % 2][:, tt0 + i, :],
                        start=(prev is None), stop=True), reads=['itm', 'attm'], pwrites=[pot])
                if d == 0:
                    P.op('act', lambda e, po=po, t0=t0, w=w: e.activation(out=oacc[:, t0:t0 + w], in_=po[:, 0:w], func=AF.Identity),
                         reads=[pot], pwrites=['oacc'])
                else:
                    P.op('dve', lambda e, po=po, t0=t0, w=w: e.tensor_tensor(out=oacc[:, t0:t0 + w], in0=po[:, 0:w], in1=oacc[:, t0:t0 + w], op=ALU.add),
                         reads=[pot, 'oacc'], pwrites=['oacc'])
                    if stop > 2:
                        out_block(t0, w)
        P.finish('sp')
        P.emit()
    return nc


def hgrn_consts():
    p = np.arange(128)[:, None] % HC
    t = np.arange(HC)[None, :]
    same = (p // SUB) == (t // SUB)
    tri = np.stack([same & (p <= t), (p // SUB) < (t // SUB), same & (p >= t), (p // SUB) > (t // SUB)], axis=1).astype(np.float32)
    rmask = np.broadcast_to((np.arange(QBW) % HC != 0).astype(np.float32)[None, :], (128, QBW))
    return np.ascontiguousarray(tri), np.ascontiguousarray(rmask)


def run_hgrn(uT_all, w_in, lbl, hgn, e_layer, stop=99, ncores=NCORE):
    nc = build_hgrn_prog(e_layer, stop)
    ident = np.eye(128, dtype=np.float32).astype(NPBF)
    tri, rmask = hgrn_consts()
    in_maps = []
    for j in range(NCORE):
        cols = [w_in[:, (3 + g) * 1024 + j * 128: (3 + g) * 1024 + (j + 1) * 128] for g in range(5)]
        w5 = np.stack([lay_w(c) for c in cols], axis=2)
        lj = np.ascontiguousarray(lbl[:, :, j * 128:(j + 1) * 128].transpose(2, 0, 1)).astype(np.float32)
        hj = np.ascontiguousarray(hgn[j * 128:(j + 1) * 128].reshape(128, 1)).astype(np.float32)
        in_maps.append({"uT": uT_all, "w5": np.ascontiguousarray(w5), "lbl": lj, "hgn": hj, "ident": ident, "tri": tri, "rmask": rmask})
    in_maps = in_maps[:ncores]
    res = run_bass_kernel_spmd(nc, in_maps, core_ids=list(range(ncores)))
    return np.stack([res.results[j]["y_out"] for j in range(ncores)], axis=1)


def _fm(x):
    T = x.shape[0]
    return np.ascontiguousarray(x.T.reshape(KC, 128, T).transpose(1, 0, 2))


def _unfm(a):
    T = a.shape[2]
    return np.ascontiguousarray(a.transpose(1, 0, 2).reshape(D, T).T)


def _vec(v):
    return np.ascontiguousarray(v.reshape(KC, 128).T)


def _lay_win(w):
    g = w[:, :DFF].reshape(KC, 128, JC, 128)
    u = w[:, DFF:].reshape(KC, 128, JC, 128)
    return np.ascontiguousarray(np.concatenate([g.transpose(2, 1, 0, 3), u.transpose(2, 1, 0, 3)], axis=3))


def _lay_wout(w):
    return np.ascontiguousarray(w.reshape(JC, 128, KC, 128).transpose(2, 1, 0, 3))


def _lay_wo(w):
    return np.ascontiguousarray(w.reshape(KC, 128, KC, 128).transpose(2, 1, 0, 3))


def _run_token(stages_spec, h_list, y_all, vec_list, weights, want_u, final):
    stages = [(s, i) for i, s in enumerate(stages_spec)]
    nc = build_token_prog(stages, want_u=want_u, final=final)
    vecs = np.ascontiguousarray(np.stack([_vec(v) for v in vec_list], axis=1)).astype(np.float32)
    shared = {"vecs": vecs}
    for si, s in enumerate(stages_spec):
        if s == 'proj':
            shared[f"wo{si}"] = _lay_wo(weights[si])
        else:
            shared[f"wi{si}"] = _lay_win(weights[si][0])
            shared[f"wf{si}"] = _lay_wout(weights[si][1])
    in_maps = []
    for r in range(NCORE):
        m = dict(shared)
        m["h_in"] = h_list[r]
        if y_all is not None:
            m["y_in"] = np.ascontiguousarray(np.concatenate(
                [y_all[:, :, r * TLAT:(r + 1) * TLAT], y_all[:, :, SEQ + r * TCTX: SEQ + (r + 1) * TCTX]], axis=2))
        in_maps.append(m)
    res = run_bass_kernel_spmd(nc, in_maps, core_ids=list(range(NCORE)))
    return res.results


def _gather_u(results):
    u_all = np.zeros((128, KC, TALL), NPBF)
    for r in range(NCORE):
        uo = results[r]["u_out"]
        u_all[:, :, r * TLAT:(r + 1) * TLAT] = uo[:, :, :TLAT]
        u_all[:, :, SEQ + r * TCTX: SEQ + (r + 1) * TCTX] = uo[:, :, TLAT:]
    return u_all


def kernel(x, c, ctx, c_ctx, w_mod, b_mod, norm_g, w_ff_in, w_ff_out, w_in_even, w_out_even,
           na_rpb, hg_lb_logits, hg_norm_g, w_qkv_odd, w_o_odd, sink_odd, final_norm_g):
    f = lambda a: np.asarray(a, dtype=np.float32)
    x, c, ctx, c_ctx, w_mod, b_mod, norm_g = f(x), f(c), f(ctx), f(c_ctx), f(w_mod), f(b_mod), f(norm_g)
    w_ff_in, w_ff_out, w_in_even, w_out_even = f(w_ff_in), f(w_ff_out), f(w_in_even), f(w_out_even)
    na_rpb, hg_lb_logits, hg_norm_g = f(na_rpb), f(hg_lb_logits), f(hg_norm_g)
    w_qkv_odd, w_o_odd, sink_odd, final_norm_g = f(w_qkv_odd), f(w_o_odd), f(sink_odd), f(final_norm_g)

    mod = run_mod(c[0], c_ctx, w_mod, b_mod).reshape(DEPTH, 2, 3, 3, D)

    def ffn_vecs(l, sub):
        return [norm_g[l, sub], mod[l, 0, sub, 0], mod[l, 0, sub, 1], mod[l, 0, sub, 2],
                mod[l, 1, sub, 0], mod[l, 1, sub, 1], mod[l, 1, sub, 2]]

    def u_vecs(l):
        return [norm_g[l, 1], mod[l, 0, 1, 0], mod[l, 0, 1, 1], mod[l, 1, 1, 0], mod[l, 1, 1, 1]]

    h_list = []
    for r in range(NCORE):
        tok = np.concatenate([x[0, r * TLAT:(r + 1) * TLAT], ctx[0, r * TCTX:(r + 1) * TCTX]], axis=0)
        h_list.append(_fm(tok))

    res = _run_token(['ffn'], h_list, None, ffn_vecs(0, 0) + u_vecs(0), [(w_ff_in[0, 0], w_ff_out[0, 0])], True, False)
    out = None
    for l in range(DEPTH):
        h_list = [res[r]["h_out"] for r in range(NCORE)]
        u_all = _gather_u(res)
        if l % 2 == 0:
            e = l // 2
            ya = run_na(u_all, w_in_even[e], na_rpb[e])
            yb = run_hgrn(u_all, w_in_even[e], hg_lb_logits, hg_norm_g[e], e)
            y_all = np.ascontiguousarray(np.concatenate([ya, yb], axis=1))
            w_o = w_out_even[e]
        else:
            o = l // 2
            y_all = run_odd(u_all, w_qkv_odd[o], sink_odd[o])
            w_o = w_o_odd[o]
        gate_vecs = [mod[l, 0, 1, 2], mod[l, 1, 1, 2]]
        if l < DEPTH - 1:
            res = _run_token(['proj', 'ffn', 'ffn'], h_list, y_all,
                             gate_vecs + ffn_vecs(l, 2) + ffn_vecs(l + 1, 0) + u_vecs(l + 1),
                             [w_o, (w_ff_in[l, 1], w_ff_out[l, 1]), (w_ff_in[l + 1, 0], w_ff_out[l + 1, 0])], True, False)
        else:
            res = _run_token(['proj', 'ffn'], h_list, y_all, gate_vecs + ffn_vecs(l, 2) + [final_norm_g],
                             [w_o, (w_ff_in[l, 1], w_ff_out[l, 1])], False, True)
            out = np.zeros((1, SEQ, D), np.float32)
            for r in range(NCORE):
                out[0, r * TLAT:(r + 1) * TLAT] = _unfm(res[r]["o_out"])[:TLAT]
    return out
```
